# Optimizing a Trainium2 kernel written in Bass

```python
import jax, jax.numpy as jnp
from jax import lax
import numpy as np

D_MODEL = 2048
BATCH = 8
SEQ = 2048
DEPTH = 2

CHUNK = 64
N_MIXERS = 2
HEAD_DIM = 64
N_HEADS = D_MODEL // HEAD_DIM
D_DECAY_LORA = max(32, int(round(1.8 * D_MODEL ** 0.5 / 32)) * 32)
D_AAA_LORA = max(32, int(round(1.8 * D_MODEL ** 0.5 / 32)) * 32)
D_GATE_LORA = max(32, int(round(0.6 * D_MODEL ** 0.8 / 32)) * 32)
D_FF = -(-8 * D_MODEL // (3 * 256)) * 256
Q_BLOCK = 128
RMS_EPS = 1e-6
GN_EPS = 64e-5

kernel_name = "rwkv7_fox_interleaved_hybrid"


def rmsnorm(x, g):
    xf = x.astype(jnp.float32)
    y = xf * lax.rsqrt(jnp.mean(xf * xf, axis=-1, keepdims=True) + RMS_EPS)
    return (y * g.astype(jnp.float32)).astype(x.dtype)


def shift_prev(x):
    pad = [(0, 0)] * x.ndim
    pad[1] = (1, 0)
    return jnp.pad(x[:, :-1], pad)


def wkv7_scan(r, w, k, v, a, b):
    B, T, H, N = r.shape

    def to_chunks(t):
        return jnp.moveaxis(t, 1, 0).reshape(T // CHUNK, CHUNK, B, H, N)

    def step(S, inp):
        r_t, w_t, k_t, v_t, a_t, b_t = inp
        sa = jnp.einsum('bhij,bhj->bhi', S, a_t)
        S = (S * w_t[:, :, None, :] + sa[..., None] * b_t[:, :, None, :]
             + v_t[..., None] * k_t[:, :, None, :])
        return S, jnp.einsum('bhij,bhj->bhi', S, r_t)

    def chunk_step(S, inp):
        return lax.scan(step, S, inp)

    S0 = jnp.zeros((B, H, N, N), jnp.float32)
    _, y = lax.scan(chunk_step, S0, tuple(to_chunks(t) for t in (r, w, k, v, a, b)))
    return jnp.moveaxis(y.reshape(T, B, H, N), 0, 1)


def rwkv7_time_mix(h, mix, w_rkv, w0, w1, w2, a0, a1, a2, g1, g2, k_k, k_a, r_k,
                   lnx_g, lnx_b, w_o):
    B, T, D = h.shape
    H, N = N_HEADS, HEAD_DIM
    f32 = jnp.float32
    xx = shift_prev(h) - h
    xm = h[None] + xx[None] * mix[:, None, None, :]
    rkv = jnp.einsum('pbtc,pcd->pbtd', xm[:3], w_rkv)
    r, k, v = rkv[0], rkv[1], rkv[2]
    xw, xa, xg = xm[3], xm[4], xm[5]
    w_log = -jax.nn.softplus(-(w0 + jnp.tanh(xw @ w1) @ w2).astype(f32)) - 0.5
    decay = jnp.exp(-jnp.exp(w_log))
    a = jax.nn.sigmoid(a0 + (xa @ a1) @ a2)
    g = jax.nn.sigmoid(xg @ g1) @ g2
    kk = (k * k_k).reshape(B, T, H, N).astype(f32)
    kk = kk / jnp.maximum(jnp.sqrt(jnp.sum(kk * kk, axis=-1, keepdims=True)), 1e-12)
    k = k * (1.0 + (a - 1.0) * k_a)
    heads = lambda t: t.reshape(B, T, H, N).astype(f32)
    rh, wh, kh, vh, ah = heads(r), heads(decay), heads(k), heads(v), heads(a)
    y = wkv7_scan(rh, wh, kh, vh, -kk, kk * ah)
    mu = jnp.mean(y, axis=-1, keepdims=True)
    var = jnp.mean(jnp.square(y - mu), axis=-1, keepdims=True)
    y = ((y - mu) * lax.rsqrt(var + GN_EPS)).reshape(B, T, D) * lnx_g.astype(f32) + lnx_b.astype(f32)
    bonus = (jnp.sum(rh * kh * r_k.astype(f32), axis=-1, keepdims=True) * vh).reshape(B, T, D)
    y = (y + bonus).astype(h.dtype)
    return (y * g) @ w_o


def fox_attention(h, w_in, b_f, qn_g, kn_g, on_g, w_o):
    B, T, D = h.shape
    H, N = N_HEADS, HEAD_DIM
    f32 = jnp.float32
    proj = h @ w_in
    q = proj[..., 0 * D:1 * D].reshape(B, T, H, N)
    k = proj[..., 1 * D:2 * D].reshape(B, T, H, N)
    v = proj[..., 2 * D:3 * D].reshape(B, T, H, N)
    gate = proj[..., 3 * D:4 * D]
    f_logit = proj[..., 4 * D:4 * D + H]
    alpha_k = jax.nn.sigmoid(proj[..., 4 * D + H:4 * D + 2 * H])[..., None]
    alpha_v = jax.nn.sigmoid(proj[..., 4 * D + 2 * H:4 * D + 3 * H])[..., None]
    k = alpha_k * shift_prev(k) + (1.0 - alpha_k) * k
    v = alpha_v * shift_prev(v) + (1.0 - alpha_v) * v
    q = rmsnorm(q, qn_g)
    k = rmsnorm(k, kn_g)
    c = jnp.cumsum(jax.nn.log_sigmoid((f_logit + b_f).astype(f32)), axis=1)
    qh, kh, vh = (t.transpose(0, 2, 1, 3) for t in (q, k, v))
    c = c.transpose(0, 2, 1)
    scale = N ** -0.5
    outs = []
    for start in range(0, T, Q_BLOCK):
        end = start + Q_BLOCK
        s = jnp.einsum('bhqd,bhkd->bhqk', qh[:, :, start:end], kh[:, :, :end]).astype(f32) * scale
        s = s + c[:, :, start:end, None] - c[:, :, None, :end]
        mask = jnp.arange(start, end)[:, None] >= jnp.arange(end)[None, :]
        p = jax.nn.softmax(jnp.where(mask, s, -jnp.inf), axis=-1)
        outs.append(jnp.einsum('bhqk,bhkd->bhqd', p.astype(vh.dtype), vh[:, :, :end]))
    o = jnp.concatenate(outs, axis=2).transpose(0, 2, 1, 3)
    o = rmsnorm(o, on_g.reshape(H, N)).reshape(B, T, D)
    return (o * jax.nn.sigmoid(gate)) @ w_o


def swiglu(h, w_gu, w_d):
    gu = h @ w_gu
    return (jax.nn.silu(gu[..., :D_FF]) * gu[..., D_FF:]) @ w_d


def setup_inputs(seed: int = 0) -> dict:
    key = jax.random.key(seed)
    ks = iter(jax.random.split(key, 40))
    D, H, N = D_MODEL, N_HEADS, HEAD_DIM
    nA = (DEPTH + N_MIXERS - 1) // N_MIXERS
    nB = DEPTH // N_MIXERS
    f32 = jnp.float32

    def nrm(shape, fan_in, scale=1.0):
        return scale * fan_in ** -0.5 * jax.random.normal(next(ks), shape, f32)

    def gain(shape):
        return 1.0 + 0.02 * jax.random.normal(next(ks), shape, f32)

    def unif(shape, lo, hi):
        return jax.random.uniform(next(ks), shape, f32, lo, hi)

    def small(shape, scale):
        return scale * jax.random.normal(next(ks), shape, f32)

    return {
        "x": jax.random.normal(next(ks), (BATCH, SEQ, D), f32),
        "a_norm_g": gain((nA, D)),
        "a_mix": unif((nA, 6, D), 0.0, 1.0),
        "a_w_rkv": nrm((nA, 3, D, D), D),
        "a_w0": unif((nA, D), -6.5, -1.5),
        "a_w1": nrm((nA, D, D_DECAY_LORA), D),
        "a_w2": nrm((nA, D_DECAY_LORA, D), D_DECAY_LORA, 0.5),
        "a_a0": small((nA, D), 0.1),
        "a_a1": nrm((nA, D, D_AAA_LORA), D),
        "a_a2": nrm((nA, D_AAA_LORA, D), D_AAA_LORA, 0.5),
        "a_g1": nrm((nA, D, D_GATE_LORA), D),
        "a_g2": nrm((nA, D_GATE_LORA, D), D_GATE_LORA),
        "a_k_k": 0.85 + small((nA, D), 0.05),
        "a_k_a": 1.0 + small((nA, D), 0.05),
        "a_r_k": small((nA, H, N), 0.1),
        "a_lnx_g": gain((nA, D)),
        "a_lnx_b": small((nA, D), 0.02),
        "a_w_o": nrm((nA, D, D), D),
        "b_norm_g": gain((nB, D)),
        "b_w_in": nrm((nB, D, 4 * D + 3 * H), D),
        "b_b_f": unif((nB, H), 1.0, 5.0),
        "b_qn_g": gain((nB, N)),
        "b_kn_g": gain((nB, N)),
        "b_on_g": gain((nB, D)),
        "b_w_o": nrm((nB, D, D), D),
        "f_norm_g": gain((DEPTH, D)),
        "f_w_gu": nrm((DEPTH, D, 2 * D_FF), D),
        "f_w_d": nrm((DEPTH, D_FF, D), D_FF),
        "final_g": gain((D,)),
    }


def reference(x, a_norm_g, a_mix, a_w_rkv, a_w0, a_w1, a_w2, a_a0, a_a1, a_a2, a_g1, a_g2,
              a_k_k, a_k_a, a_r_k, a_lnx_g, a_lnx_b, a_w_o,
              b_norm_g, b_w_in, b_b_f, b_qn_g, b_kn_g, b_on_g, b_w_o,
              f_norm_g, f_w_gu, f_w_d, final_g):
    for i in range(DEPTH):
        j = i // N_MIXERS
        if i % N_MIXERS == 0:
            h = rmsnorm(x, a_norm_g[j])
            x = x + rwkv7_time_mix(h, a_mix[j], a_w_rkv[j], a_w0[j], a_w1[j], a_w2[j],
                                   a_a0[j], a_a1[j], a_a2[j], a_g1[j], a_g2[j],
                                   a_k_k[j], a_k_a[j], a_r_k[j], a_lnx_g[j], a_lnx_b[j], a_w_o[j])
        else:
            h = rmsnorm(x, b_norm_g[j])
            x = x + fox_attention(h, b_w_in[j], b_b_f[j], b_qn_g[j], b_kn_g[j], b_on_g[j], b_w_o[j])
        x = x + swiglu(rmsnorm(x, f_norm_g[i]), f_w_gu[i], f_w_d[i])
    return rmsnorm(x, final_g)
```

```python
import contextlib
import numpy as np
import concourse.bass as bass
import concourse.mybir as mybir
from concourse.bass_utils import run_bass_kernel_spmd

F32 = mybir.dt.float32
BF16 = mybir.dt.bfloat16
AF = mybir.ActivationFunctionType
ALU = mybir.AluOpType
AX = mybir.AxisListType

RMS_EPS = 1e-6
DEBUG = False
GN_EPS = 64e-5


class Cfg:
    def __init__(self, T=2048, D=2048, DFF=5632, LW=96, LA=96, LG=256):
        self.T, self.D, self.DFF, self.LW, self.LA, self.LG = T, D, DFF, LW, LA, LG
        self.H = D // 64
        self.KC = D // 128
        self.FC = DFF // 128
        self.TS = min(512, T)
        self.NTT = T // self.TS
        self.NTB = T // 128
        self.NCH = T // 64
        self.WIN = 4 * D + 3 * self.H


ENGS = ["sp", "act", "dve", "pool", "pe"]


class Prog:
    def __init__(self, nc, es):
        self.nc, self.es = nc, es
        self.streams = {e: [] for e in ENGS}
        self.sems, self.semval = {}, {}
        self.seen = {e: {} for e in ENGS}
        self.last_w, self.readers = {}, {}
        self.n_ops = 0

    def _sem(self, name):
        if name not in self.sems:
            self.sems[name] = self.es.enter_context(self.nc.semaphore("s_" + name.replace(":", "_")))
            self.semval[name] = 0
        return self.sems[name]

    def op(self, eng, fn, rd=(), wr=(), dma=None):
        deps = {}

        def add(d):
            if d is not None:
                deps[d[0]] = max(deps.get(d[0], 0), d[1])

        for k in rd:
            add(self.last_w.get(k))
        for k in wr:
            add(self.last_w.get(k))
            for s, v in self.readers.get(k, {}).items():
                add((s, v))
        own = "eng:" + eng
        waits = []
        for s, v in deps.items():
            if s == own and eng == "pe":
                continue
            if self.seen[eng].get(s, 0) < v:
                self.seen[eng][s] = v
                waits.append((self._sem(s), v))
        sname = ("dma:" + dma) if dma else own
        inc = 16 if dma else 1
        sem = self._sem(sname)
        self.semval[sname] += inc
        nv = self.semval[sname]

        def emit(e, waits=waits, fn=fn, sem=sem, inc=inc):
            for (s, v) in waits:
                e.wait_ge(s, v)
            ins = fn(e)
            ins.then_inc(sem, inc)

        self.streams[eng].append(emit)
        for k in wr:
            self.last_w[k] = (sname, nv)
            self.readers[k] = {}
        for k in rd:
            r = self.readers.setdefault(k, {})
            r[sname] = max(r.get(sname, 0), nv)
        self.n_ops += 1

    def barrier(self):
        for eng in ENGS:
            waits = []
            for s, v in self.semval.items():
                if v > 0 and self.seen[eng].get(s, 0) < v and s != "eng:" + eng:
                    self.seen[eng][s] = v
                    waits.append((self.sems[s], v))

            def emit(e, waits=waits):
                for (s, v) in waits:
                    e.wait_ge(s, v)

            self.streams[eng].append(emit)
        self.last_w, self.readers = {}, {}


class Builder:
    def __init__(self, cfg):
        self.c = cfg
        self.nc = bass.Bass("TRN2", target_bir_lowering=False)
        self.es = contextlib.ExitStack()
        self.P = Prog(self.nc, self.es)
        self.dram = {}
        self._uid = 0
        self.ps_ctr = 0
        self.wb_ctr = 0
        self.stg_ctr = 0

    def din(self, name, shape):
        self.dram[name] = self.nc.dram_tensor(name, list(shape), F32, kind="ExternalInput").ap()
        return self.dram[name]

    def dscr(self, name, shape, dt):
        self.dram[name] = self.nc.dram_tensor(name, list(shape), dt, kind=("ExternalOutput" if DEBUG else "Internal")).ap()
        return self.dram[name]

    def sb(self, name, shape, dt):
        return self.es.enter_context(self.nc.sbuf_tensor(name, list(shape), dt))

    def psum(self, name, shape, dt):
        return self.es.enter_context(self.nc.psum_tensor(name, list(shape), dt))

    def uid(self, p):
        self._uid += 1
        return f"{p}{self._uid}"

    def setup_common(self):
        c = self.c
        P = self.P
        self.psA = self.psum("psA", [128, 6, 512], F32)
        self.psT = self.psum("psT", [128, 2, 1024], BF16)
        self.ident = self.sb("ident", [128, 128], BF16)
        self.identf = self.sb("identf", [128, 128], F32)
        self.consts = self.sb("consts", [128, 8], F32)
        self.small = self.sb("small", [128, 64], F32)
        self.junk = self.sb("junk", [128, 128], F32)
        nc = self.nc

        def mk_ident(e):
            return e.affine_select(out=self.identf[:], in_=self.identf[:], pattern=[[-1, 128]],
                                   compare_op=ALU.not_equal, fill=1.0, base=0, channel_multiplier=1)

        P.op("pool", lambda e: e.memset(self.identf[:], 0.0), wr=["identf"])
        P.op("pool", mk_ident, rd=["identf"], wr=["identf"])
        P.op("dve", lambda e: e.tensor_copy(out=self.ident[:], in_=self.identf[:]), rd=["identf"], wr=["ident"])

        def mk_consts(e):
            e.memset(self.consts[:, 0:1], RMS_EPS)
            e.memset(self.consts[:, 1:2], 1.0)
            e.memset(self.consts[:, 2:3], GN_EPS)
            return e.memset(self.consts[:, 3:4], 0.0)

        P.op("pool", mk_consts, wr=["consts"])
        hsz = c.KC * (c.T + 2)
        self.TH = c.T // (2 if c.T >= 1024 else 1)
        r1 = max(hsz + c.KC * c.T, c.FC * self.TH, 17 * c.T, 16 * c.T + c.KC * c.T)
        self.R1 = self.sb("R1", [128, r1], BF16)
        self.actA = self.R1[:, 0:hsz].rearrange("p (c t) -> p c t", t=c.T + 2)
        self.actB = self.R1[:, hsz:hsz + c.KC * c.T]
        self.WBSZ = max(c.KC * 512, c.FC * 256)
        self.NWB = 2
        r2 = max(self.NWB * self.WBSZ, 8 * c.D, 10 * c.T + 1408, 2 * (4 * c.T + c.NTB * 132) + 3 * c.TS + 384, 7 * c.T + 140, 9 * c.D + 8 * c.H)
        self.R2 = self.sb("R2", [128, r2], BF16)
        self.wb = [self.R2[:, i * self.WBSZ:(i + 1) * self.WBSZ] for i in range(self.NWB)]
        self.xt = [self.R2[:, i * 2 * c.D:(i + 1) * 2 * c.D].bitcast(F32) for i in range(2)]
        self.grep = self.R2[:, 4 * c.D:6 * c.D].bitcast(F32)
        self.hn = [self.R2[:, (6 + i) * c.D:(7 + i) * c.D] for i in range(2)]
        self.NSTG = 3
        self.stg = [self.sb(f"stg{i}", [128, 512], F32) for i in range(self.NSTG)]
        P.op("pool", lambda e: e.memset(self.actA[:, :, 0:2], 0.0), wr=["actA"])

    def next_ps(self):
        b = self.ps_ctr % 6
        self.ps_ctr += 1
        return b

    def next_wb(self):
        b = self.wb_ctr % self.NWB
        self.wb_ctr += 1
        return b

    def next_stg(self):
        b = self.stg_ctr % self.NSTG
        self.stg_ctr += 1
        return b

    def norm_transpose(self, x_ap, g_ap):
        c, P = self.c, self.P
        P.op("sp", lambda e: e.dma_start(out=self.grep[:], in_=g_ap.partition_broadcast(128)),
             wr=["grep"], dma="grep")
        for tb in range(c.NTB):
            s = tb % 2
            xt, hn = self.xt[s], self.hn[s]
            ss = self.small[:, s:s + 1]
            rs = self.small[:, 2 + s:3 + s]
            P.op("sp", lambda e, xt=xt, tb=tb: e.dma_start(out=xt[:], in_=x_ap[tb * 128:(tb + 1) * 128, :]),
                 wr=[f"xt{s}"], dma=f"xt{s}")
            P.op("act", lambda e, xt=xt, hn=hn, ss=ss: e.activation(out=hn[:], in_=xt[:], func=AF.Square, accum_out=ss),
                 rd=[f"xt{s}"], wr=[f"hn{s}", f"ss{s}"])
            P.op("act", lambda e, ss=ss, rs=rs: e.activation(out=rs, in_=ss, func=AF.Sqrt, scale=1.0 / c.D,
                                                                 bias=self.consts[:, 0:1]),
                 rd=[f"ss{s}", "consts"], wr=[f"rs{s}"])
            P.op("dve", lambda e, rs=rs: e.reciprocal(out=rs, in_=rs), rd=[f"rs{s}"], wr=[f"rsd{s}", f"rs{s}"])
            P.op("dve", lambda e, xt=xt, hn=hn, rs=rs: e.scalar_tensor_tensor(
                out=hn[:], in0=xt[:], scalar=rs, in1=self.grep[:], op0=ALU.mult, op1=ALU.mult),
                rd=[f"xt{s}", f"rsd{s}", f"rs{s}", "grep"], wr=[f"hn{s}"])
            for c0 in range(0, c.KC, 8):
                nch = min(8, c.KC - c0)
                tbk = (tb * ((c.KC + 7) // 8) + c0 // 8) % 2

                def tr(e, hn=hn, c0=c0, nch=nch, tbk=tbk):
                    ins = None
                    for i in range(nch):
                        ins = e.transpose(out=self.psT[:, tbk, i * 128:(i + 1) * 128],
                                          in_=hn[:, (c0 + i) * 128:(c0 + i + 1) * 128], identity=self.ident[:])
                    return ins

                P.op("pe", tr, rd=[f"hn{s}", "ident"], wr=[f"psT{tbk}"])
                eng = "act" if (c0 // 8) % 2 == 0 else "dve"

                def ev(e, c0=c0, nch=nch, tbk=tbk, tb=tb, eng=eng):
                    src = self.psT[:, tbk, 0:nch * 128].rearrange("p (c t) -> p c t", t=128)
                    dst = self.actA[:, c0:c0 + nch, 2 + tb * 128:2 + (tb + 1) * 128]
                    if eng == "act":
                        return e.activation(out=dst, in_=src, func=AF.Copy)
                    return e.tensor_copy(out=dst, in_=src)

                P.op(eng, ev, rd=[f"psT{tbk}"], wr=["actA"])

    def hT(self, k, t0, t1):
        return self.actA[:, k, 2 + t0:2 + t1]

    def load_w(self, w_ap, kcn, kp, col0, ncols):
        s = self.next_wb()
        view = self.wb[s][0:kp, 0:kcn * ncols].rearrange("p (c n) -> p c n", n=ncols)
        src = w_ap[:, col0:col0 + ncols].rearrange("(c p) n -> p c n", p=kp)
        self.P.op("pool", lambda e: e.dma_start(out=view, in_=src), wr=[f"wb{s}"], dma=f"wb{s}")
        return s, view

    def gemm_fm(self, xfn, xkeys, kcn, kp, groups, epilogue, t0=0, t1=None):
        c, P = self.c, self.P
        t1 = c.T if t1 is None else t1
        for gi, grp in enumerate(groups):
            wts = [self.load_w(w_ap, kcn, kp, col0, ncols) + (ncols,) for (w_ap, col0, ncols) in grp]
            for tt in range((t1 - t0) // c.TS):
                lo, hi = t0 + tt * c.TS, t0 + (tt + 1) * c.TS
                banks = []
                for (s, view, ncols) in wts:
                    b = self.next_ps()
                    banks.append((b, ncols))

                    def mm(e, view=view, ncols=ncols, b=b, lo=lo, hi=hi):
                        ins = None
                        for k in range(kcn):
                            ins = e.matmul(self.psA[0:ncols, b, 0:hi - lo], lhsT=view[:, k, :], rhs=xfn(k, lo, hi),
                                           start=(k == 0), stop=(k == kcn - 1))
                        return ins

                    P.op("pe", mm, rd=[f"wb{s}"] + xkeys, wr=[f"ps{b}"])
                epilogue(gi, tt, lo, hi, banks)

    def gemm_tm(self, xfn, xkeys, kcn, kp, w_ap, ncols_total, nbw, epilogue, t0=0, t1=None):
        c, P = self.c, self.P
        t1 = c.T if t1 is None else t1
        for nb in range(ncols_total // nbw):
            s, view = self.load_w(w_ap, kcn, kp, nb * nbw, nbw)
            for tb in range((t1 - t0) // 128):
                lo = t0 + tb * 128
                b = self.next_ps()

                def mm(e, view=view, b=b, lo=lo):
                    ins = None
                    for k in range(kcn):
                        ins = e.matmul(self.psA[:, b, 0:nbw], lhsT=xfn(k, lo, lo + 128), rhs=view[:, k, :],
                                       start=(k == 0), stop=(k == kcn - 1))
                    return ins

                P.op("pe", mm, rd=[f"wb{s}"] + xkeys, wr=[f"ps{b}"])
                epilogue(nb, lo, b)

    def resid_epilogue(self, x_ap, nbw):
        P = self.P
        cnt = [0]

        def ep(nb, lo, b):
            s = self.next_stg()
            stg = self.stg[s]
            P.op("sp", lambda e: e.dma_start(out=stg[:, 0:nbw], in_=x_ap[lo:lo + 128, nb * nbw:(nb + 1) * nbw]),
                 wr=[f"stg{s}"], dma=f"stg{s}")
            eng = "dve"
            P.op(eng, lambda e: e.tensor_tensor(out=stg[:, 0:nbw], in0=stg[:, 0:nbw], in1=self.psA[:, b, 0:nbw], op=ALU.add),
                 rd=[f"ps{b}", f"stg{s}"], wr=[f"stg{s}"])
            P.op("sp", lambda e: e.dma_start(out=x_ap[lo:lo + 128, nb * nbw:(nb + 1) * nbw], in_=stg[:, 0:nbw]),
                 rd=[f"stg{s}"], dma=f"stg{s}")
            cnt[0] += 1

        return ep

    def swiglu(self, x_ap, g_ap, wgu_ap, wd_ap, mid_ap):
        c, P = self.c, self.P
        self.norm_transpose(x_ap, g_ap)
        P.barrier()
        groups = [[(wgu_ap, j * 128, 128), (wgu_ap, c.DFF + j * 128, 128)] for j in range(c.FC)]
        sg = [self.sb(self.uid("sg"), [128, c.TS], F32) for _ in range(2)]
        mo = [self.sb(self.uid("mo"), [128, c.TS], BF16) for _ in range(2)]
        it = [0]

        def ep(gi, tt, lo, hi, banks):
            s = it[0] % 2
            it[0] += 1
            (bg, _), (bu, _) = banks
            P.op("act", lambda e: e.activation(out=sg[s][:], in_=self.psA[:, bg, 0:c.TS], func=AF.Silu),
                 rd=[f"ps{bg}"], wr=[f"sg{s}"])
            P.op("dve", lambda e: e.tensor_tensor(out=mo[s][:], in0=sg[s][:], in1=self.psA[:, bu, 0:c.TS], op=ALU.mult),
                 rd=[f"sg{s}", f"ps{bu}"], wr=[f"mo{s}"])
            P.op("sp", lambda e: e.dma_start(out=mid_ap[gi * 128:(gi + 1) * 128, lo:hi], in_=mo[s][:]),
                 rd=[f"mo{s}"], dma=f"mo{s}")

        self.gemm_fm(self.hT, ["actA"], c.KC, 128, groups, ep)
        P.barrier()
        TH = self.TH
        nh = c.T // TH
        nbw = 256
        for h in range(nh):
            midv = self.R1[:, 0:c.FC * TH].rearrange("p (c t) -> p c t", t=TH)
            for cc in range(c.FC):
                P.op("sp", lambda e, cc=cc, h=h: e.dma_start(out=midv[:, cc, :], in_=mid_ap[cc * 128:(cc + 1) * 128, h * TH:(h + 1) * TH]),
                     wr=["actB"], dma=f"actB{cc % 4}")
            xfn = lambda k, a, b_, h=h: midv[:, k, a - h * TH:b_ - h * TH]
            self.gemm_tm(xfn, ["actB"], c.FC, 128, wd_ap, c.D, nbw, self.resid_epilogue(x_ap, nbw), t0=h * TH, t1=(h + 1) * TH)
            P.barrier()

    def store_fm(self, dst_ap, row_of_group):
        P = self.P
        it = [0]

        def ep(gi, tt, lo, hi, banks):
            for bi, (b, ncols) in enumerate(banks):
                s = self.next_stg()
                stg = self.stg[s]
                eng = "act" if it[0] % 2 == 0 else "dve"
                it[0] += 1
                n = hi - lo
                if eng == "act":
                    P.op("act", lambda e, b=b, ncols=ncols, stg=stg, n=n: e.activation(out=stg[0:ncols, 0:n], in_=self.psA[0:ncols, b, 0:n], func=AF.Copy),
                         rd=[f"ps{b}"], wr=[f"stg{s}"])
                else:
                    P.op("dve", lambda e, b=b, ncols=ncols, stg=stg, n=n: e.tensor_copy(out=stg[0:ncols, 0:n], in_=self.psA[0:ncols, b, 0:n]),
                         rd=[f"ps{b}"], wr=[f"stg{s}"])
                r0 = row_of_group(gi, bi)
                P.op("sp", lambda e, r0=r0, ncols=ncols, stg=stg, n=n, lo=lo, hi=hi: e.dma_start(out=dst_ap[r0:r0 + ncols, lo:hi], in_=stg[0:ncols, 0:n]),
                     rd=[f"stg{s}"], dma=f"stg{s}")

        return ep

    def store_tm(self, dst_ap, nbw, col0=0):
        P = self.P
        it = [0]

        def ep(nb, lo, b):
            s = self.next_stg()
            stg = self.stg[s]
            eng = "act" if it[0] % 2 == 0 else "dve"
            it[0] += 1
            if eng == "act":
                P.op("act", lambda e: e.activation(out=stg[:, 0:nbw], in_=self.psA[:, b, 0:nbw], func=AF.Copy), rd=[f"ps{b}"], wr=[f"stg{s}"])
            else:
                P.op("dve", lambda e: e.tensor_copy(out=stg[:, 0:nbw], in_=self.psA[:, b, 0:nbw]), rd=[f"ps{b}"], wr=[f"stg{s}"])
            P.op("sp", lambda e: e.dma_start(out=dst_ap[lo:lo + 128, col0 + nb * nbw:col0 + (nb + 1) * nbw], in_=stg[:, 0:nbw]),
                 rd=[f"stg{s}"], dma=f"stg{s}")

        return ep

    def r2view(self, off, n, dt=BF16, parts=128, arena=None):
        w = n * (2 if dt == F32 else 1)
        arena = self.R2 if arena is None else arena
        v = arena[0:parts, off:off + w]
        if dt == F32:
            v = v.bitcast(F32)
        return v, off + w

    def transpose_to_fm(self, src_fn, dst3):
        c, P = self.c, self.P
        for tb in range(c.NTB):
            for c0 in range(0, c.KC, 8):
                nch = min(8, c.KC - c0)
                tbk = (tb * ((c.KC + 7) // 8) + c0 // 8) % 2

                def tr(e, tb=tb, c0=c0, nch=nch, tbk=tbk):
                    ins = None
                    src = src_fn(tb)
                    for i in range(nch):
                        ins = e.transpose(out=self.psT[:, tbk, i * 128:(i + 1) * 128],
                                          in_=src[:, (c0 + i) * 128:(c0 + i + 1) * 128], identity=self.ident[:])
                    return ins

                P.op("pe", tr, rd=["ztm", "ident"], wr=[f"psT{tbk}"])
                eng = "act" if (tb + c0 // 8) % 2 == 0 else "dve"

                def ev(e, c0=c0, nch=nch, tbk=tbk, tb=tb, eng=eng):
                    src = self.psT[:, tbk, 0:nch * 128].rearrange("p (c t) -> p c t", t=128)
                    dst = dst3[:, c0:c0 + nch, tb * 128:(tb + 1) * 128]
                    if eng == "act":
                        return e.activation(out=dst, in_=src, func=AF.Copy)
                    return e.tensor_copy(out=dst, in_=src)

                P.op(eng, ev, rd=[f"psT{tbk}"], wr=["zT"])

    def fox(self, x_ap, W):
        c, P = self.c, self.P
        D, T, H, KC, TS = c.D, c.T, c.H, c.KC, c.TS
        w_in = W["b_w_in"]
        qk = self.dscr("b_qk", [2 * D, T], F32)
        fa = self.dscr("b_fa", [3 * H, T], F32)
        vg = self.dscr("b_vg", [T, 2 * D], F32)
        fat = self.dscr("b_fat", [T, 3 * H], F32)
        qhat = self.dscr("b_qhat", [2 * D, T], BF16)
        qaug = self.dscr("b_qaug", [H, 4, T], BF16)
        kaug = self.dscr("b_kaug", [H, 4, T], BF16)
        akd = self.dscr("b_akd", [H, T], BF16)
        vpr = self.dscr("b_vpr", [T, D], BF16)
        self.norm_transpose(x_ap, W["b_norm_g"])
        P.barrier()
        groups = [[(w_in, j * 128, 128)] for j in range(2 * KC)]
        self.gemm_fm(self.hT, ["actA"], KC, 128, groups, self.store_fm(qk, lambda gi, bi: gi * 128))
        self.gemm_fm(self.hT, ["actA"], KC, 128, [[(w_in, 4 * D, 3 * H)]], self.store_fm(fa, lambda gi, bi: 0))
        self.gemm_tm(self.hT, ["actA"], KC, 128, w_in[:, 2 * D:4 * D], 2 * D, 512 if D >= 512 else 2 * D,
                     self.store_tm(vg, 512 if D >= 512 else 2 * D))
        self.gemm_tm(self.hT, ["actA"], KC, 128, w_in[:, 4 * D:4 * D + 3 * H], 3 * H, 3 * H, self.store_tm(fat, 3 * H))
        P.barrier()
        off = 0
        ft, off = self.r2view(off, T, F32, H, self.R1)
        f2, off = self.r2view(off, T, F32, H, self.R1)
        ct, off = self.r2view(off, T, F32, H, self.R1)
        qa, off = self.r2view(off, 4 * T, BF16, H, self.R1)
        ka, off = self.r2view(off, 4 * T, BF16, H, self.R1)
        akt, off = self.r2view(off, T, F32, H, self.R1)
        akb, off = self.r2view(off, T, BF16, H, self.R1)
        qa = qa.rearrange("p (r t) -> p r t", t=T)
        ka = ka.rearrange("p (r t) -> p r t", t=T)
        nb = self.small[0:H, 8:9]
        P.op("sp", lambda e: e.dma_start(out=ft, in_=fa[0:H, :]), wr=["ft"], dma="ft")
        P.op("sp", lambda e: e.dma_start(out=akt, in_=fa[H:2 * H, :]), wr=["akt"], dma="akt")
        P.op("sp", lambda e: e.dma_start(out=nb, in_=W["b_b_f"].rearrange("o h -> h o")), wr=["nb"], dma="nb")
        P.op("dve", lambda e: e.tensor_scalar(out=nb, in0=nb, scalar1=-1.0, scalar2=None, op0=ALU.mult), rd=["nb"], wr=["nb"])
        P.barrier()
        P.op("act", lambda e: e.activation(out=f2, in_=ft, func=AF.Exp, scale=-1.0, bias=nb), rd=["ft", "nb"], wr=["f2"])
        P.op("act", lambda e: e.activation(out=f2, in_=f2, func=AF.Ln, scale=1.0, bias=self.consts[0:H, 1:2]), rd=["consts"], wr=["f2"])
        P.op("dve", lambda e: e.tensor_scalar(out=f2, in0=f2, scalar1=-0.5, scalar2=None, op0=ALU.mult), rd=["f2"], wr=["f2"])
        P.op("dve", lambda e: e.tensor_tensor_scan(out=ct, data0=f2, data1=f2, initial=0.0, op0=ALU.add, op1=ALU.add), rd=["f2"], wr=["ct"])
        P.op("dve", lambda e: e.tensor_copy(out=qa[:, 0, :], in_=ct), rd=["ct"], wr=["qa"])
        P.op("dve", lambda e: e.tensor_tensor(out=qa[:, 1, :], in0=ct, in1=qa[:, 0, :], op=ALU.subtract), rd=["ct", "qa"], wr=["qa"])
        P.op("pool", lambda e: e.memset(qa[:, 2:4, :], 1.0), wr=["qa"])
        P.op("pool", lambda e: e.memset(ka[:, 0:2, :], 1.0), wr=["ka"])
        P.op("dve", lambda e: e.tensor_scalar(out=ka[:, 2:4, :], in0=qa[:, 0:2, :], scalar1=-1.0, scalar2=None, op0=ALU.mult), rd=["qa"], wr=["ka"])
        P.op("act", lambda e: e.activation(out=akb, in_=akt, func=AF.Sigmoid), rd=["akt"], wr=["akb"])
        P.op("sp", lambda e: e.dma_start(out=qaug, in_=qa), rd=["qa"], dma="qa")
        P.op("sp", lambda e: e.dma_start(out=kaug, in_=ka), rd=["ka"], dma="ka")
        P.op("sp", lambda e: e.dma_start(out=akd, in_=akb), rd=["akb"], dma="akb")
        P.barrier()
        off = 0
        kt, off = self.r2view(off, T + 2, F32)
        tmpf, off = self.r2view(off, T, F32)
        akx, off = self.r2view(off, T, BF16)
        sq, off = self.r2view(off, T, BF16)
        outb, off = self.r2view(off, T, BF16)
        bones, off = self.r2view(off, 128, BF16)
        gq = self.small[:, 10:11]
        gk = self.small[:, 11:12]

        P.op("dve", lambda e: e.memset(bones, 0.0), wr=["bones"])
        P.op("dve", lambda e: e.memset(bones[0:64, 0:64], 1.0), wr=["bones"])
        P.op("dve", lambda e: e.memset(bones[64:128, 64:128], 1.0), wr=["bones"])
        P.op("pool", lambda e: e.memset(kt[:, 0:2], 0.0), wr=["kt"])
        for hh in range(2):
            P.op("sp", lambda e, hh=hh: e.dma_start(out=gq[hh * 64:(hh + 1) * 64, :], in_=W["b_qn_g"].rearrange("o n -> n o")), wr=["gq"], dma="gq")
            P.op("sp", lambda e, hh=hh: e.dma_start(out=gk[hh * 64:(hh + 1) * 64, :], in_=W["b_kn_g"].rearrange("o n -> n o")), wr=["gk"], dma="gk")
        P.op("dve", lambda e: e.tensor_scalar(out=gq, in0=gq, scalar1=0.125, scalar2=None, op0=ALU.mult), rd=["gq"], wr=["gq"])
        P.barrier()
        for which in range(2):
            for p in range(KC):
                row0 = which * D + p * 128
                P.op("sp", lambda e, row0=row0: e.dma_start(out=kt[:, 2:T + 2], in_=qk[row0:row0 + 128, :]), wr=["kt"], dma="kt")
                if which == 1:
                    for hh in range(2):
                        P.op("sp", lambda e, hh=hh, p=p: e.dma_start(out=akx[hh * 64:(hh + 1) * 64, :], in_=akd[2 * p + hh:2 * p + hh + 1, :].partition_broadcast(64)),
                             wr=["akx"], dma="akx")
                    P.op("dve", lambda e: e.tensor_tensor(out=tmpf, in0=kt[:, 1:T + 1], in1=kt[:, 2:T + 2], op=ALU.subtract), rd=["kt"], wr=["tmpf"])
                    P.op("dve", lambda e: e.tensor_tensor(out=tmpf, in0=tmpf, in1=akx, op=ALU.mult), rd=["akx", "tmpf"], wr=["tmpf"])
                    P.op("dve", lambda e: e.tensor_tensor(out=kt[:, 2:T + 2], in0=kt[:, 2:T + 2], in1=tmpf, op=ALU.add), rd=["tmpf", "kt"], wr=["kt"])
                P.op("act", lambda e: e.activation(out=sq, in_=kt[:, 2:T + 2], func=AF.Square), rd=["kt"], wr=["sq"])
                for tt in range(c.NTT):
                    b = self.next_ps()
                    P.op("pe", lambda e, b=b, tt=tt: e.matmul(self.psA[:, b, 0:TS], lhsT=bones, rhs=sq[:, tt * TS:(tt + 1) * TS], start=True, stop=True),
                         rd=["sq", "bones"], wr=[f"ps{b}"])
                    P.op("act", lambda e, b=b, tt=tt: e.activation(out=tmpf[:, tt * TS:(tt + 1) * TS], in_=self.psA[:, b, 0:TS], func=AF.Sqrt,
                                                                 scale=1.0 / 64, bias=self.consts[:, 0:1]),
                         rd=[f"ps{b}", "consts", "tmpf"], wr=["tmpf"])
                P.op("dve", lambda e: e.reciprocal(out=tmpf, in_=tmpf), rd=["tmpf"], wr=["tmpf"])
                gcol = gq if which == 0 else gk
                P.op("dve", lambda e, gcol=gcol: e.scalar_tensor_tensor(out=outb, in0=kt[:, 2:T + 2], scalar=gcol, in1=tmpf, op0=ALU.mult, op1=ALU.mult),
                     rd=["kt", "tmpf", "gq", "gk"], wr=["outb"])
                P.op("sp", lambda e, row0=row0: e.dma_start(out=qhat[row0:row0 + 128, :], in_=outb), rd=["outb"], dma="outb")
        P.barrier()
        off = 0
        vt, off = self.r2view(off, D, F32)
        vp, off = self.r2view(off, D, F32)
        gt, off = self.r2view(off, D, F32)
        ong, off = self.r2view(off, D, F32)
        vo, off = self.r2view(off, D, BF16)
        al, off = self.r2view(off, 3 * H, F32)
        av, off = self.r2view(off, H, F32)
        G3 = self.actB.rearrange("p (b d) -> p b d", d=D)
        ztm = self.R1[:, 0:c.NTB * D].rearrange("p (b d) -> p b d", d=D)
        P.op("sp", lambda e: e.dma_start(out=ong, in_=W["b_on_g"].partition_broadcast(128)), wr=["ong"], dma="ong")
        for tb in range(c.NTB):
            r0 = tb * 128
            P.op("sp", lambda e, r0=r0: e.dma_start(out=vt, in_=vg[r0:r0 + 128, 0:D]), wr=["vt"], dma="vt")
            P.op("sp", lambda e, r0=r0: e.dma_start(out=gt, in_=vg[r0:r0 + 128, D:2 * D]), wr=["gt"], dma="gt")
            P.op("sp", lambda e, r0=r0: e.dma_start(out=al, in_=fat[r0:r0 + 128, :]), wr=["al"], dma="al")
            if tb == 0:
                P.op("pool", lambda e: e.memset(vp[0:1, :], 0.0), wr=["vp"])
                P.op("sp", lambda e: e.dma_start(out=vp[1:128, :], in_=vg[0:127, 0:D]), wr=["vp"], dma="vp")
            else:
                P.op("sp", lambda e, r0=r0: e.dma_start(out=vp, in_=vg[r0 - 1:r0 + 127, 0:D]), wr=["vp"], dma="vp")
            P.op("act", lambda e: e.activation(out=av, in_=al[:, 2 * H:3 * H], func=AF.Sigmoid), rd=["al"], wr=["av"])
            P.op("dve", lambda e: e.tensor_tensor(out=vp, in0=vp, in1=vt, op=ALU.subtract), rd=["vp", "vt"], wr=["vp"])
            P.op("dve", lambda e: e.tensor_tensor(out=vp.rearrange("p (h n) -> p h n", n=64), in0=vp.rearrange("p (h n) -> p h n", n=64),
                                                  in1=av.unsqueeze(2).broadcast_to([128, H, 64]), op=ALU.mult), rd=["vp", "av"], wr=["vp"])
            P.op("dve", lambda e: e.tensor_tensor(out=vo, in0=vp, in1=vt, op=ALU.add), rd=["vp", "vt"], wr=["vo"])
            P.op("sp", lambda e, r0=r0: e.dma_start(out=vpr[r0:r0 + 128, :], in_=vo), rd=["vo"], dma="vo")
            P.op("act", lambda e: e.activation(out=gt, in_=gt, func=AF.Sigmoid), rd=["gt"], wr=["gt"])
            P.op("dve", lambda e, tb=tb: e.tensor_tensor(out=G3[:, tb, :], in0=gt, in1=ong, op=ALU.mult), rd=["gt", "ong"], wr=["G"])
        P.barrier()
        off = 0
        QA, KA, VV = [], [], []
        for i in range(2):
            qs, ks = [], []
            for hh in range(2):
                v_, off = self.r2view(off, T, BF16)
                qs.append(v_)
                v_, off = self.r2view(off, T, BF16)
                ks.append(v_)
            QA.append(qs)
            KA.append(ks)
            v_, off = self.r2view(off, c.NTB * 2 * 66, BF16)
            VV.append(v_.rearrange("p (b h n) -> p b h n", h=2, n=66))
        PT = []
        for i in range(3):
            v_, off = self.r2view(off, TS, BF16)
            PT.append(v_)
        tri, off = self.r2view(off, 128, BF16)
        trif, off = self.r2view(off, 128, F32)

        P.op("pool", lambda e: e.memset(trif, 1.0), wr=["trif"])

        def mk_tri(e):
            return e.affine_select(out=trif, in_=trif, pattern=[[1, 128]], compare_op=ALU.is_ge, fill=0.0, base=0, channel_multiplier=-1)

        P.op("pool", mk_tri, rd=["trif"], wr=["trif"])
        P.op("dve", lambda e: e.tensor_copy(out=tri, in_=trif), rd=["trif"], wr=["tri"])
        for i in range(2):
            P.op("pool", lambda e, i=i: e.memset(VV[i], 1.0), wr=[f"VV{i}"])
        pt_ctr = 0
        sbank = 0
        obank = 0
        nq = T // TS
        nsub = TS // 128
        for p in range(KC):
            i = p % 2
            for hh in range(2):
                h = 2 * p + hh
                P.op("sp", lambda e, i=i, hh=hh, h=h: e.dma_start(out=QA[i][hh][0:64, :], in_=qhat[h * 64:(h + 1) * 64, :]), wr=[f"QA{i}{hh}"], dma=f"QA{i}{hh}")
                P.op("sp", lambda e, i=i, hh=hh, h=h: e.dma_start(out=QA[i][hh][64:68, :], in_=qaug[h]), wr=[f"QA{i}{hh}"], dma=f"QA{i}{hh}")
                P.op("sp", lambda e, i=i, hh=hh, h=h: e.dma_start(out=KA[i][hh][0:64, :], in_=qhat[D + h * 64:D + (h + 1) * 64, :]), wr=[f"KA{i}{hh}"], dma=f"KA{i}{hh}")
                P.op("sp", lambda e, i=i, hh=hh, h=h: e.dma_start(out=KA[i][hh][64:68, :], in_=kaug[h]), wr=[f"KA{i}{hh}"], dma=f"KA{i}{hh}")
                P.op("sp", lambda e, i=i, hh=hh, h=h: e.dma_start(out=VV[i][:, :, hh, 0:64], in_=vpr[:, h * 64:(h + 1) * 64].rearrange("(b s) n -> s b n", s=128)),
                     wr=[f"VV{i}"], dma=f"VV{i}{hh}")
            for hh in range(2):
                h = 2 * p + hh
                for I in range(nq):
                    ob = 4 + (obank % 2)
                    obank += 1
                    q_hi = (I + 1) * TS
                    nJ = (I + 1) * nsub
                    for J in range(nJ):
                        t_lo = max(I * TS, J * 128)
                        N = q_hi - t_lo
                        sbk = sbank % 4
                        sbank += 1
                        P.op("pe", lambda e, i=i, hh=hh, J=J, t_lo=t_lo, q_hi=q_hi, N=N, sbk=sbk: e.matmul(
                            self.psA[:, sbk, 0:N], lhsT=KA[i][hh][0:68, J * 128:(J + 1) * 128], rhs=QA[i][hh][0:68, t_lo:q_hi], start=True, stop=True),
                            rd=[f"QA{i}{hh}", f"KA{i}{hh}"], wr=[f"ps{sbk}"])
                        pti = pt_ctr % 3
                        pt_ctr += 1
                        pt = PT[pti]
                        P.op("act", lambda e, pt=pt, N=N, sbk=sbk: e.activation(out=pt[:, 0:N], in_=self.psA[:, sbk, 0:N], func=AF.Exp),
                             rd=[f"ps{sbk}"], wr=[f"PT{pti}"])
                        if J * 128 >= I * TS:
                            P.op("dve", lambda e, pt=pt: e.tensor_tensor(out=pt[:, 0:128], in0=pt[:, 0:128], in1=tri, op=ALU.mult),
                                 rd=[f"PT{pti}", "tri"], wr=[f"PT{pti}"])

                        def pv(e, i=i, hh=hh, J=J, t_lo=t_lo, q_hi=q_hi, pt=pt, ob=ob, I=I, nJ=nJ):
                            ins = None
                            for u0 in range(t_lo, q_hi, 128):
                                u = (u0 - I * TS) // 128
                                ins = e.matmul(self.psA[:, ob, u * 66:u * 66 + 65], lhsT=pt[:, u0 - t_lo:u0 - t_lo + 128], rhs=VV[i][:, J, hh, 0:65],
                                               start=(J == 0 and u == 0), stop=(J == nJ - 1 and u == nsub - 1))
                            return ins

                        P.op("pe", pv, rd=[f"PT{pti}", f"VV{i}"], wr=[f"ps{ob}"])
                    for u in range(nsub):
                        tb = I * nsub + u
                        oc = self.psA[:, ob, u * 66:u * 66 + 64]
                        den = self.psA[:, ob, u * 66 + 64:u * 66 + 65]
                        k0 = 16 + (tb % 4) * 4
                        rc = self.small[:, k0:k0 + 1]
                        ssq = self.small[:, k0 + 1:k0 + 2]
                        rst = self.small[:, k0 + 2:k0 + 3]
                        kk = f"ep{tb % 4}"
                        P.op("dve", lambda e, rc=rc, den=den: e.reciprocal(out=rc, in_=den), rd=[f"ps{ob}"], wr=[kk + "rc"])
                        P.op("act", lambda e, oc=oc, rc=rc, ssq=ssq: e.activation(out=self.junk[:, 0:64], in_=oc, func=AF.Square, scale=rc, accum_out=ssq),
                             rd=[f"ps{ob}", kk + "rc"], wr=[kk + "ss", "junk"])
                        P.op("act", lambda e, ssq=ssq, rst=rst: e.activation(out=rst, in_=ssq, func=AF.Sqrt, scale=1.0 / 64, bias=self.consts[:, 0:1]),
                             rd=[kk + "ss", "consts"], wr=[kk + "rst"])
                        P.op("dve", lambda e, rst=rst: e.reciprocal(out=rst, in_=rst), rd=[kk + "rst"], wr=[kk + "rst2"])
                        P.op("dve", lambda e, rst=rst, rc=rc: e.tensor_tensor(out=rst, in0=rst, in1=rc, op=ALU.mult), rd=[kk + "rst2", kk + "rc"], wr=[kk + "sc"])
                        P.op("dve", lambda e, oc=oc, rst=rst, tb=tb, h=h: e.scalar_tensor_tensor(
                            out=ztm[:, tb, h * 64:(h + 1) * 64], in0=oc, scalar=rst, in1=G3[:, tb, h * 64:(h + 1) * 64], op0=ALU.mult, op1=ALU.mult),
                            rd=[f"ps{ob}", kk + "sc", "G"], wr=["ztm"])
        P.barrier()
        if DEBUG:
            dz = self.dscr("dbg_ztm", [128, c.NTB, D], BF16)
            dg = self.dscr("dbg_G", [128, c.NTB, D], BF16)
            P.op("sp", lambda e: e.dma_start(out=dz, in_=ztm), dma="dbg")
            P.op("sp", lambda e: e.dma_start(out=dg, in_=G3), dma="dbg")
            P.barrier()
        zT = self.actB.rearrange("p (c t) -> p c t", t=T)
        self.transpose_to_fm(lambda tb: ztm[:, tb, :], zT)
        P.barrier()
        nbw = 512 if D >= 512 else D
        self.gemm_tm(lambda k, a, b_: zT[:, k, a:b_], ["zT"], KC, 128, W["b_w_o"], D, nbw, self.resid_epilogue(x_ap, nbw))
        P.barrier()

    def rwkv(self, x_ap, W):
        c, P = self.c, self.P
        D, T, H, KC, TS, NCH = c.D, c.T, c.H, c.KC, c.TS, c.NCH
        rr = self.dscr("a_rr", [D, T], F32)
        kkd = self.dscr("a_kk", [D, T], F32)
        vvd = self.dscr("a_vv", [D, T], F32)
        vtm = self.dscr("a_vtm", [T, D], F32)
        lwd = self.dscr("a_lw", [D, T], F32)
        aad = self.dscr("a_aa", [D, T], F32)
        ggd = self.dscr("a_gg", [D, T], F32)
        rowsA = self.sb("rowsA", [128, 128], F32)
        rowsB = self.sb("rowsB", [128, 128], F32)
        cols = self.sb("cols", [128, 256], F32)
        lora = self.sb("lora", [128, 2, T], BF16)
        P.op("dve", lambda e: e.memset(rowsA[:], 0.0), wr=["rowsA"])
        P.op("dve", lambda e: e.memset(rowsB[:], 0.0), wr=["rowsB"])
        P.op("sp", lambda e: e.dma_start(out=rowsA[0:6 * KC, :], in_=W["a_mix"].rearrange("s (c p) -> (s c) p", p=128)), wr=["rowsA"], dma="rowsA")
        vecs = ["a_w0", "a_a0", "a_k_k", "a_k_a", "a_r_k", "a_lnx_g", "a_lnx_b"]
        for i, nm in enumerate(vecs):
            P.op("sp", lambda e, i=i, nm=nm: e.dma_start(out=rowsB[i * KC:(i + 1) * KC, :], in_=W[nm].rearrange("o (c p) -> (o c) p", p=128)),
                 wr=["rowsB"], dma="rowsB")
        b0 = self.next_ps()
        P.op("pe", lambda e: e.transpose(out=self.psA[:, b0, 0:128], in_=rowsA[:], identity=self.identf[:]), rd=["rowsA", "identf"], wr=[f"ps{b0}"])
        P.op("dve", lambda e: e.tensor_copy(out=cols[:, 0:128], in_=self.psA[:, b0, 0:128]), rd=[f"ps{b0}"], wr=["cols"])
        b1 = self.next_ps()
        P.op("pe", lambda e: e.transpose(out=self.psA[:, b1, 0:128], in_=rowsB[:], identity=self.identf[:]), rd=["rowsB", "identf"], wr=[f"ps{b1}"])
        P.op("dve", lambda e: e.tensor_copy(out=cols[:, 128:256], in_=self.psA[:, b1, 0:128]), rd=[f"ps{b1}"], wr=["cols"])
        mixc = lambda s_, k: cols[:, s_ * KC + k:s_ * KC + k + 1]
        vcol = lambda nm, k: cols[:, 128 + vecs.index(nm) * KC + k:128 + vecs.index(nm) * KC + k + 1]
        omk0 = 128 + 7 * KC
        P.op("dve", lambda e: e.tensor_scalar(out=cols[:, omk0:omk0 + KC], in0=cols[:, 128 + 3 * KC:128 + 4 * KC], scalar1=-1.0, scalar2=1.0,
                                              op0=ALU.mult, op1=ALU.add), rd=["cols"], wr=["cols"])
        self.norm_transpose(x_ap, W["a_norm_g"])
        P.barrier()
        xm = self.actB.rearrange("p (c t) -> p c t", t=T)
        xmf = lambda k, a, b_: xm[:, k, a:b_]

        def mix(s_):
            for k in range(KC):
                P.op("dve", lambda e, k=k: e.tensor_tensor(out=xm[:, k, :], in0=self.actA[:, k, 1:T + 1], in1=self.actA[:, k, 2:T + 2], op=ALU.subtract),
                     rd=["actA"], wr=["xm"])
                P.op("dve", lambda e, k=k: e.scalar_tensor_tensor(out=xm[:, k, :], in0=xm[:, k, :], scalar=mixc(s_, k), in1=self.actA[:, k, 2:T + 2],
                                                                  op0=ALU.mult, op1=ALU.add), rd=["actA", "xm", "cols"], wr=["xm"])

        fullg = lambda w_ap, n: [[(w_ap, j * 128, min(128, n - j * 128))] for j in range((n + 127) // 128)]
        for s_, dst in [(0, rr), (1, kkd), (2, vvd)]:
            mix(s_)
            self.gemm_fm(xmf, ["xm"], KC, 128, fullg(W["a_w_rkv"][s_], D), self.store_fm(dst, lambda gi, bi: gi * 128))
            if s_ == 2:
                nbw = 512 if D >= 512 else D
                self.gemm_tm(xmf, ["xm"], KC, 128, W["a_w_rkv"][2], D, nbw, self.store_tm(vtm, nbw))
            P.barrier()

        def lora_ep(func):
            def ep(gi, tt, lo, hi, banks):
                (b, ncols), = banks
                P.op("act", lambda e: e.activation(out=lora[0:ncols, gi, lo:hi], in_=self.psA[0:ncols, b, 0:hi - lo], func=func),
                     rd=[f"ps{b}"], wr=["lora"])
            return ep

        def out_ep(dst, func, bias_nm, post_scale):
            def ep(gi, tt, lo, hi, banks):
                (b, ncols), = banks
                s = self.next_stg()
                stg = self.stg[s]
                n = hi - lo
                if func is None:
                    P.op("act", lambda e: e.activation(out=stg[:, 0:n], in_=self.psA[:, b, 0:n], func=AF.Copy), rd=[f"ps{b}"], wr=[f"stg{s}"])
                else:
                    P.op("act", lambda e: e.activation(out=stg[:, 0:n], in_=self.psA[:, b, 0:n], func=func, bias=vcol(bias_nm, gi), scale=1.0),
                         rd=[f"ps{b}", "cols"], wr=[f"stg{s}"])
                if post_scale is not None:
                    P.op("dve", lambda e: e.tensor_scalar(out=stg[:, 0:n], in0=stg[:, 0:n], scalar1=post_scale, scalar2=None, op0=ALU.mult),
                         rd=[f"stg{s}"], wr=[f"stg{s}"])
                P.op("sp", lambda e: e.dma_start(out=dst[gi * 128:(gi + 1) * 128, lo:hi], in_=stg[:, 0:n]), rd=[f"stg{s}"], dma=f"stg{s}")
            return ep

        for s_, w1n, w2n, L, f1, dst, f2, bnm, psc in [
                (3, "a_w1", "a_w2", c.LW, AF.Tanh, lwd, AF.Sigmoid, "a_w0", -float(np.exp(-0.5))),
                (4, "a_a1", "a_a2", c.LA, AF.Copy, aad, AF.Sigmoid, "a_a0", None),
                (5, "a_g1", "a_g2", c.LG, AF.Sigmoid, ggd, None, None, None)]:
            mix(s_)
            self.gemm_fm(xmf, ["xm"], KC, 128, fullg(W[w1n], L), lora_ep(f1))
            kcn2 = (L + 127) // 128
            kp2 = L if L < 128 else 128
            self.gemm_fm(lambda k, a, b_, kp2=kp2: lora[0:kp2, k, a:b_], ["lora"], kcn2, kp2, fullg(W[w2n], D), out_ep(dst, f2, bnm, psc))
            P.barrier()

        off = 0
        maskA, off = self.r2view(off, 384, F32)
        maskB, off = self.r2view(off, 256, F32)
        bones, off = self.r2view(off, 128, BF16)
        fA = []
        for i in range(5):
            v_, off = self.r2view(off, T, F32)
            fA.append(v_)
        assert off <= self.R2.shape[1], (off, self.R2.shape)
        X1, X2, X3, X4, X5 = fA
        o1 = 0
        eN, o1 = self.r2view(o1, T, F32, 128, self.R1)
        cmask, o1 = self.r2view(o1, T, F32, 128, self.R1)
        BO, o1 = self.r2view(o1, T, F32, 128, self.R1)
        YT, o1 = self.r2view(o1, T, F32, 128, self.R1)
        RT, o1 = self.r2view(o1, T, BF16, 128, self.R1)
        AT, o1 = self.r2view(o1, T, BF16, 128, self.R1)
        BT, o1 = self.r2view(o1, T, BF16, 128, self.R1)
        KT, o1 = self.r2view(o1, T, BF16, 128, self.R1)
        BH, o1 = self.r2view(o1, T, BF16, 128, self.R1)
        KH, o1 = self.r2view(o1, T, BF16, 128, self.R1)
        S1, o1 = self.r2view(o1, T, BF16, 128, self.R1)
        V2p, o1 = self.r2view(o1, T, BF16, 128, self.R1)
        zoff = max(o1, c.KC * (c.T + 2))
        V2p = V2p.rearrange("p (c i) -> p c i", i=64)
        zT = self.R1[:, zoff:zoff + KC * T].rearrange("p (c t) -> p c t", t=T)
        MA = [self.sb(self.uid("MA"), [128, 384], BF16) for _ in range(2)]
        MB = [self.sb(self.uid("MB"), [128, 256], BF16) for _ in range(2)]
        PP = [self.sb(self.uid("PP"), [128, 256], BF16) for _ in range(2)]
        Qt = [self.sb(self.uid("Q"), [128, 128], BF16) for _ in range(2)]
        BK2 = [self.sb(self.uid("BK"), [128, 128], BF16) for _ in range(2)]
        Usb = self.sb("Usb", [128, 64], BF16)
        SAb = self.sb("SAb", [128, 64], BF16)
        ST = self.sb("ST", [128, 64], F32)
        STb = self.sb("STb", [128, 64], BF16)
        P.op("dve", lambda e: e.memset(bones, 0.0), wr=["bones"])
        P.op("dve", lambda e: e.memset(bones[0:64, 0:64], 1.0), wr=["bones"])
        P.op("dve", lambda e: e.memset(bones[64:128, 64:128], 1.0), wr=["bones"])
        P.op("dve", lambda e: e.memset(cmask, 1.0), wr=["cmask"])
        P.op("dve", lambda e: e.memset(cmask.rearrange("p (c t) -> p c t", t=64)[:, :, 0:1], 0.0), wr=["cmask"])
        P.op("pool", lambda e: e.memset(maskA, 1.0), wr=["maskA"])
        P.op("pool", lambda e: e.memset(maskB, 1.0), wr=["maskB"])
        for (mk, o_, cm, base, pat) in [(maskA, 0, -1, -1, 1), (maskA, 128, 1, -1, -1), (maskA, 256, -1, -1, 1), (maskB, 0, -1, 0, 1), (maskB, 128, -1, 0, 1)]:
            P.op("pool", lambda e, mk=mk, o_=o_, cm=cm, base=base, pat=pat: e.affine_select(
                out=mk[:, o_:o_ + 128], in_=mk[:, o_:o_ + 128], pattern=[[pat, 128]], compare_op=ALU.is_ge, fill=0.0, base=base, channel_multiplier=cm),
                rd=["maskA", "maskB"], wr=["maskA", "maskB"])
            P.op("pool", lambda e, mk=mk, o_=o_: e.memset(mk[0:64, o_ + 64:o_ + 128], 0.0), rd=["maskA", "maskB"], wr=["maskA", "maskB"])
            P.op("pool", lambda e, mk=mk, o_=o_: e.memset(mk[64:128, o_:o_ + 64], 0.0), rd=["maskA", "maskB"], wr=["maskA", "maskB"])
        P.op("dve", lambda e: e.memset(self.psA[:], 0.0), wr=[f"ps{i}" for i in range(6)])
        P.barrier()
        itc = [0]

        def do_pair(p):
            r0 = p * 128
            ld = lambda dst, src, key: P.op("sp", lambda e: e.dma_start(out=dst, in_=src[r0:r0 + 128, :]), wr=[key], dma=key)
            tt_ = lambda out, a, b_, op, rd, wr: P.op("dve", lambda e: e.tensor_tensor(out=out, in0=a, in1=b_, op=op), rd=rd, wr=wr)
            v3 = lambda a_: a_.rearrange("p (c t) -> p c t", t=64)
            for hh in range(2):
                P.op("pool", lambda e, hh=hh: e.dma_start(out=V2p[hh * 64:(hh + 1) * 64, :, :],
                                                       in_=vtm[:, r0 + hh * 64:r0 + (hh + 1) * 64].rearrange("(c s) i -> s c i", s=64)),
                     wr=["V2p"], dma=f"V2p{hh}")
            ld(X1, lwd, "X1")
            P.op("dve", lambda e: e.tensor_tensor_scan(out=X2, data0=cmask, data1=X1, initial=0.0, op0=ALU.mult, op1=ALU.add), rd=["cmask", "X1"], wr=["X2"])
            tt_(X1, X2, X1, ALU.subtract, ["X2", "X1"], ["X1"])
            P.op("act", lambda e: e.activation(out=X3, in_=X2, func=AF.Exp), rd=["X2"], wr=["X3"])
            P.op("act", lambda e: e.activation(out=eN, in_=X2, func=AF.Exp, scale=-1.0), rd=["X2"], wr=["eN"])
            P.op("act", lambda e: e.activation(out=X1, in_=X1, func=AF.Exp), rd=["X1"], wr=["X1"])
            ld(X2, kkd, "X2")
            P.op("dve", lambda e: e.tensor_scalar(out=X4, in0=X2, scalar1=vcol("a_k_k", p), scalar2=None, op0=ALU.mult), rd=["X2", "cols"], wr=["X4"])
            P.op("act", lambda e: e.activation(out=S1, in_=X4, func=AF.Square), rd=["X4"], wr=["S1"])
            for t2 in range(c.NTT):
                b = self.next_ps()
                P.op("pe", lambda e, b=b, t2=t2: e.matmul(self.psA[:, b, 0:TS], lhsT=bones, rhs=S1[:, t2 * TS:(t2 + 1) * TS], start=True, stop=True),
                     rd=["S1", "bones"], wr=[f"ps{b}"])
                P.op("act", lambda e, b=b, t2=t2: e.activation(out=X5[:, t2 * TS:(t2 + 1) * TS], in_=self.psA[:, b, 0:TS], func=AF.Sqrt),
                     rd=[f"ps{b}", "X5"], wr=["X5"])
            P.op("dve", lambda e: e.tensor_scalar(out=X5, in0=X5, scalar1=1e-12, scalar2=None, op0=ALU.max), rd=["X5"], wr=["X5"])
            P.op("dve", lambda e: e.reciprocal(out=X5, in_=X5), rd=["X5"], wr=["X5"])
            tt_(X4, X4, X5, ALU.mult, ["X4", "X5"], ["X4"])
            P.op("dve", lambda e: e.scalar_tensor_tensor(out=AT, in0=X4, scalar=-1.0, in1=X1, op0=ALU.mult, op1=ALU.mult), rd=["X4", "X1"], wr=["AT"])
            ld(X1, aad, "X1")
            P.op("dve", lambda e: e.tensor_scalar(out=X5, in0=X1, scalar1=vcol("a_k_a", p), scalar2=cols[:, omk0 + p:omk0 + p + 1], op0=ALU.mult, op1=ALU.add),
                 rd=["X1", "cols"], wr=["X5"])
            tt_(X2, X2, X5, ALU.mult, ["X2", "X5"], ["X2"])
            tt_(X1, X4, X1, ALU.mult, ["X4", "X1"], ["X1"])
            WCb = X3.rearrange("p (c t) -> p c t", t=64)[:, :, 63:64].broadcast_to([128, NCH, 64])
            tt_(X5, X1, eN, ALU.mult, ["X1", "eN"], ["X5"])
            P.op("act", lambda e: e.activation(out=BT, in_=X5, func=AF.Copy), rd=["X5"], wr=["BT"])
            tt_(v3(BH), v3(X5), WCb, ALU.mult, ["X5", "X3"], ["BH"])
            tt_(X5, X2, eN, ALU.mult, ["X2", "eN", "BT", "BH"], ["X5"])
            P.op("act", lambda e: e.activation(out=KT, in_=X5, func=AF.Copy), rd=["X5"], wr=["KT"])
            tt_(v3(KH), v3(X5), WCb, ALU.mult, ["X5", "X3"], ["KH"])
            ld(X1, rr, "X1")
            tt_(RT, X1, X3, ALU.mult, ["X1", "X3"], ["RT"])
            P.op("dve", lambda e: e.scalar_tensor_tensor(out=S1, in0=X1, scalar=vcol("a_r_k", p), in1=X2, op0=ALU.mult, op1=ALU.mult),
                 rd=["X1", "X2", "cols", "S1"], wr=["S1"])
            ld(X4, vvd, "X4")
            for t2 in range(c.NTT):
                b = self.next_ps()
                P.op("pe", lambda e, b=b, t2=t2: e.matmul(self.psA[:, b, 0:TS], lhsT=bones, rhs=S1[:, t2 * TS:(t2 + 1) * TS], start=True, stop=True),
                     rd=["S1", "bones"], wr=[f"ps{b}"])
                P.op("dve", lambda e, b=b, t2=t2: e.tensor_tensor(out=BO[:, t2 * TS:(t2 + 1) * TS], in0=self.psA[:, b, 0:TS], in1=X4[:, t2 * TS:(t2 + 1) * TS], op=ALU.mult),
                     rd=[f"ps{b}", "X4", "BO"], wr=["BO"])
            eP = X3
            P.op("dve", lambda e: e.memset(ST[:], 0.0), wr=["ST"])
            P.op("dve", lambda e: e.memset(STb[:], 0.0), wr=["STb"])
            for ch in range(NCH):
                cs = slice(ch * 64, (ch + 1) * 64)
                i2 = itc[0] % 2
                itc[0] += 1
                ma, mb, q, bk = MA[i2], MB[i2], Qt[i2], BK2[i2]
                kma, kmb, kq, kbk = f"MA{i2}", f"MB{i2}", f"Q{i2}", f"BK{i2}"
                bA = self.next_ps()
                bB = self.next_ps()

                def mmats(e, bA=bA, bB=bB, cs=cs):
                    ins = None
                    for (bank, o_, X, Y) in [(bA, 0, BT, AT), (bA, 128, AT, BT), (bA, 256, KT, AT), (bB, 0, BT, RT), (bB, 128, KT, RT)]:
                        for hh in range(2):
                            ps_ = slice(hh * 64, (hh + 1) * 64)
                            ins = e.matmul(self.psA[ps_, bank, o_ + hh * 64:o_ + (hh + 1) * 64], lhsT=X[ps_, cs], rhs=Y[ps_, cs],
                                           start=True, stop=True, tile_position=(hh * 64, hh * 64))
                    return ins

                P.op("pe", mmats, rd=["BT", "AT", "KT", "RT"], wr=[f"ps{bA}", f"ps{bB}"])
                P.op("dve", lambda e, ma=ma, bA=bA: e.tensor_tensor(out=ma[:], in0=self.psA[:, bA, 0:384], in1=maskA, op=ALU.mult), rd=[f"ps{bA}", "maskA"], wr=[kma])
                P.op("dve", lambda e, mb=mb, bB=bB: e.tensor_tensor(out=mb[:], in0=self.psA[:, bB, 0:256], in1=maskB, op=ALU.mult), rd=[f"ps{bB}", "maskB"], wr=[kmb])
                P.op("dve", lambda e, q=q, ma=ma: e.tensor_tensor(out=q[:], in0=ma[:, 0:128], in1=self.ident[:], op=ALU.add), rd=[kma, "ident"], wr=[kq])
                cur = ma[:, 0:256]
                curk = kma
                for lvl in range(5):
                    bL = self.next_ps()
                    pn = PP[lvl % 2]
                    kpn = f"PP{lvl % 2}"

                    def sqm(e, cur=cur, bL=bL):
                        e.matmul(self.psA[:, bL, 0:128], lhsT=cur[:, 128:256], rhs=cur[:, 0:128], start=True, stop=True)
                        return e.matmul(self.psA[:, bL, 128:256], lhsT=cur[:, 0:128], rhs=cur[:, 128:256], start=True, stop=True)

                    P.op("pe", sqm, rd=[curk], wr=[f"ps{bL}"])
                    P.op("act", lambda e, pn=pn, bL=bL: e.activation(out=pn[:], in_=self.psA[:, bL, 0:256], func=AF.Copy), rd=[f"ps{bL}"], wr=[kpn])
                    bQ = self.next_ps()
                    P.op("pe", lambda e, pn=pn, q=q, bQ=bQ: e.matmul(self.psA[:, bQ, 0:128], lhsT=pn[:, 128:256], rhs=q[:], start=True, stop=True),
                         rd=[kpn, kq], wr=[f"ps{bQ}"])
                    P.op("dve", lambda e, q=q, bQ=bQ: e.tensor_tensor(out=q[:], in0=q[:], in1=self.psA[:, bQ, 0:128], op=ALU.add), rd=[f"ps{bQ}", kq], wr=[kq])
                    cur, curk = pn, kpn
                tbk = itc[0] % 2

                def trs(e, tbk=tbk, cs=cs):
                    ins = None
                    for o_, X in [(0, BH), (64, KH)]:
                        for hh in range(2):
                            ps_ = slice(hh * 64, (hh + 1) * 64)
                            ins = e.transpose(out=self.psT[ps_, tbk, o_:o_ + 64], in_=X[ps_, cs], identity=self.ident[ps_, ps_], tile_position=(hh * 64, hh * 64))
                    return ins

                P.op("pe", trs, rd=["BH", "KH", "ident"], wr=[f"psT{tbk}"])
                P.op("act", lambda e, bk=bk, tbk=tbk: e.activation(out=bk[:], in_=self.psT[:, tbk, 0:128], func=AF.Copy), rd=[f"psT{tbk}"], wr=[kbk])
                bU = self.next_ps()

                def mmU(e, bU=bU, cs=cs, ma=ma, ch=ch):
                    for hh in range(2):
                        ps_ = slice(hh * 64, (hh + 1) * 64)
                        e.matmul(self.psA[ps_, bU, 0:64], lhsT=AT[ps_, cs], rhs=STb[ps_, :], start=(hh == 0), stop=False, tile_position=(hh * 64, hh * 64))
                    return e.matmul(self.psA[:, bU, 0:64], lhsT=ma[:, 256:384], rhs=V2p[:, ch, :], start=False, stop=True)

                P.op("pe", mmU, rd=["AT", "STb", kma, "V2p"], wr=[f"ps{bU}"])
                P.op("act", lambda e, bU=bU: e.activation(out=Usb[:], in_=self.psA[:, bU, 0:64], func=AF.Copy), rd=[f"ps{bU}"], wr=["Usb"])
                bS = self.next_ps()
                P.op("pe", lambda e, bS=bS, q=q: e.matmul(self.psA[:, bS, 0:64], lhsT=q[:], rhs=Usb[:], start=True, stop=True), rd=[kq, "Usb"], wr=[f"ps{bS}"])
                P.op("act", lambda e, bS=bS: e.activation(out=SAb[:], in_=self.psA[:, bS, 0:64], func=AF.Copy), rd=[f"ps{bS}"], wr=["SAb"])
                bY = self.next_ps()

                def mmY(e, bY=bY, cs=cs, mb=mb, ch=ch):
                    ins = None
                    for hh in range(2):
                        ps_ = slice(hh * 64, (hh + 1) * 64)
                        e.matmul(self.psA[ps_, bY, 0:64], lhsT=STb[ps_, :], rhs=RT[ps_, cs], start=(hh == 0), stop=False, tile_position=(hh * 64, hh * 64))
                    for hh in range(2):
                        ps_ = slice(hh * 64, (hh + 1) * 64)
                        e.matmul(self.psA[ps_, bY, 0:64], lhsT=V2p[ps_, ch, :], rhs=mb[ps_, 128 + hh * 64:128 + (hh + 1) * 64], start=False, stop=False,
                                 tile_position=(hh * 64, hh * 64))
                    for hh in range(2):
                        ps_ = slice(hh * 64, (hh + 1) * 64)
                        ins = e.matmul(self.psA[ps_, bY, 0:64], lhsT=SAb[ps_, :], rhs=mb[ps_, hh * 64:(hh + 1) * 64], start=False, stop=(hh == 1),
                                       tile_position=(hh * 64, hh * 64))
                    return ins

                P.op("pe", mmY, rd=["STb", "RT", "V2p", kmb, "SAb"], wr=[f"ps{bY}"])
                P.op("act", lambda e, bY=bY, cs=cs: e.activation(out=YT[:, cs], in_=self.psA[:, bY, 0:64], func=AF.Copy), rd=[f"ps{bY}"], wr=["YT"])
                bN = self.next_ps()

                def mmN(e, bN=bN, bk=bk, ch=ch):
                    ins = None
                    for hh in range(2):
                        ps_ = slice(hh * 64, (hh + 1) * 64)
                        e.matmul(self.psA[ps_, bN, 0:64], lhsT=bk[ps_, 0:64], rhs=SAb[ps_, :], start=(hh == 0), stop=False, tile_position=(hh * 64, hh * 64))
                    for hh in range(2):
                        ps_ = slice(hh * 64, (hh + 1) * 64)
                        ins = e.matmul(self.psA[ps_, bN, 0:64], lhsT=bk[ps_, 64:128], rhs=V2p[ps_, ch, :], start=False, stop=(hh == 1),
                                       tile_position=(hh * 64, hh * 64))
                    return ins

                P.op("pe", mmN, rd=[kbk, "SAb", "V2p"], wr=[f"ps{bN}"])
                P.op("dve", lambda e, bN=bN, ch=ch: e.scalar_tensor_tensor(out=ST[:], in0=ST[:], scalar=eP[:, ch * 64 + 63:ch * 64 + 64], in1=self.psA[:, bN, 0:64],
                                                                          op0=ALU.mult, op1=ALU.add), rd=["ST", "X3", f"ps{bN}"], wr=["ST"])
                P.op("act", lambda e: e.activation(out=STb[:], in_=ST[:], func=AF.Copy), rd=["ST"], wr=["STb"])
            P.op("act", lambda e: e.activation(out=S1, in_=YT, func=AF.Copy), rd=["YT"], wr=["S1"])
            P.op("act", lambda e: e.activation(out=RT, in_=YT, func=AF.Square), rd=["YT"], wr=["RT"])
            mu, var, tG = X1, X2, X5
            for t2 in range(c.NTT):
                sl_ = slice(t2 * TS, (t2 + 1) * TS)
                b = self.next_ps()
                P.op("pe", lambda e, b=b, sl_=sl_: e.matmul(self.psA[:, b, 0:TS], lhsT=bones, rhs=S1[:, sl_], start=True, stop=True), rd=["S1", "bones"], wr=[f"ps{b}"])
                P.op("act", lambda e, b=b, sl_=sl_: e.activation(out=mu[:, sl_], in_=self.psA[:, b, 0:TS], func=AF.Copy, scale=1.0 / 64), rd=[f"ps{b}", "X1"], wr=["X1"])
                b2 = self.next_ps()
                P.op("pe", lambda e, b2=b2, sl_=sl_: e.matmul(self.psA[:, b2, 0:TS], lhsT=bones, rhs=RT[:, sl_], start=True, stop=True), rd=["RT", "bones"], wr=[f"ps{b2}"])
                P.op("dve", lambda e, sl_=sl_: e.tensor_tensor(out=tG[:, sl_], in0=mu[:, sl_], in1=mu[:, sl_], op=ALU.mult), rd=["X1", "X5"], wr=["X5"])
                P.op("dve", lambda e, b2=b2, sl_=sl_: e.scalar_tensor_tensor(out=var[:, sl_], in0=self.psA[:, b2, 0:TS], scalar=1.0 / 64, in1=tG[:, sl_],
                                                                             op0=ALU.mult, op1=ALU.subtract), rd=[f"ps{b2}", "X5", "X2"], wr=["X2"])
            P.op("act", lambda e: e.activation(out=var, in_=var, func=AF.Sqrt, bias=self.consts[:, 2:3], scale=1.0), rd=["X2", "consts"], wr=["X2"])
            P.op("dve", lambda e: e.reciprocal(out=var, in_=var), rd=["X2"], wr=["X2"])
            tt_(YT, YT, mu, ALU.subtract, ["YT", "X1"], ["YT"])
            tt_(YT, YT, var, ALU.mult, ["YT", "X2"], ["YT"])
            P.op("dve", lambda e: e.tensor_scalar(out=YT, in0=YT, scalar1=vcol("a_lnx_g", p), scalar2=vcol("a_lnx_b", p), op0=ALU.mult, op1=ALU.add),
                 rd=["YT", "cols"], wr=["YT"])
            tt_(YT, YT, BO, ALU.add, ["YT", "BO"], ["YT"])
            ld(X4, ggd, "X4")
            tt_(zT[:, p, :], YT, X4, ALU.mult, ["YT", "X4"], ["zT"])

        for p in range(KC):
            do_pair(p)
        P.barrier()
        nbw = 512 if D >= 512 else D
        self.gemm_tm(lambda k, a, b_: zT[:, k, a:b_], ["zT"], KC, 128, W["a_w_o"], D, nbw, self.resid_epilogue(x_ap, nbw))
        P.barrier()

    def final_norm(self, x_ap, g_ap, out_ap):
        c, P = self.c, self.P
        P.op("sp", lambda e: e.dma_start(out=self.grep[:], in_=g_ap.partition_broadcast(128)), wr=["grep"], dma="grep")
        for tb in range(c.NTB):
            s = tb % 2
            xt = self.xt[s]
            ss = self.small[:, s:s + 1]
            rs = self.small[:, 2 + s:3 + s]
            P.op("sp", lambda e, xt=xt, tb=tb: e.dma_start(out=xt[:], in_=x_ap[tb * 128:(tb + 1) * 128, :]),
                 wr=[f"xt{s}"], dma=f"xt{s}")
            P.op("act", lambda e, xt=xt, ss=ss, s=s: e.activation(out=self.hn[s][:], in_=xt[:], func=AF.Square, accum_out=ss),
                 rd=[f"xt{s}"], wr=[f"hn{s}", f"ss{s}"])
            P.op("act", lambda e, ss=ss, rs=rs: e.activation(out=rs, in_=ss, func=AF.Sqrt, scale=1.0 / c.D, bias=self.consts[:, 0:1]),
                 rd=[f"ss{s}", "consts"], wr=[f"rs{s}"])
            P.op("dve", lambda e, rs=rs: e.reciprocal(out=rs, in_=rs), rd=[f"rs{s}"], wr=[f"rsd{s}", f"rs{s}"])
            P.op("dve", lambda e, xt=xt, rs=rs: e.scalar_tensor_tensor(
                out=xt[:], in0=xt[:], scalar=rs, in1=self.grep[:], op0=ALU.mult, op1=ALU.mult),
                rd=[f"rsd{s}", f"rs{s}", "grep"], wr=[f"xt{s}"])
            P.op("sp", lambda e, xt=xt, tb=tb: e.dma_start(out=out_ap[tb * 128:(tb + 1) * 128, :], in_=xt[:]),
                 rd=[f"xt{s}"], dma=f"xt{s}")

    def finish(self):
        P, nc = self.P, self.nc
        P.barrier()
        with nc.Block() as block:
            @block.sync
            def _(e):
                for f in P.streams["sp"]:
                    f(e)

            @block.scalar
            def _(e):
                for f in P.streams["act"]:
                    f(e)

            @block.vector
            def _(e):
                for f in P.streams["dve"]:
                    f(e)

            @block.gpsimd
            def _(e):
                for f in P.streams["pool"]:
                    f(e)

            @block.tensor
            def _(e):
                for f in P.streams["pe"]:
                    f(e)
        self.es.close()
        return nc


def build_program(cfg, layers=("a", "f0", "b", "f1", "final")):
    B = Builder(cfg)
    c = cfg
    x_in = B.din("x", [c.T, c.D])
    f_norm_g = B.din("f_norm_g", [2, c.D])
    f_w_gu = B.din("f_w_gu", [2, c.D, 2 * c.DFF])
    f_w_d = B.din("f_w_d", [2, c.DFF, c.D])
    final_g = B.din("final_g", [1, c.D])
    W = {}
    for nm, shp in [("b_norm_g", [1, c.D]), ("b_w_in", [c.D, c.WIN]), ("b_b_f", [1, c.H]), ("b_qn_g", [1, 64]), ("b_kn_g", [1, 64]),
                    ("b_on_g", [1, c.D]), ("b_w_o", [c.D, c.D])]:
        if "b" in layers:
            W[nm] = B.din(nm, shp)
    for nm, shp in [("a_norm_g", [1, c.D]), ("a_mix", [6, c.D]), ("a_w_rkv", [3, c.D, c.D]), ("a_w0", [1, c.D]), ("a_w1", [c.D, c.LW]),
                    ("a_w2", [c.LW, c.D]), ("a_a0", [1, c.D]), ("a_a1", [c.D, c.LA]), ("a_a2", [c.LA, c.D]), ("a_g1", [c.D, c.LG]),
                    ("a_g2", [c.LG, c.D]), ("a_k_k", [1, c.D]), ("a_k_a", [1, c.D]), ("a_r_k", [1, c.D]), ("a_lnx_g", [1, c.D]),
                    ("a_lnx_b", [1, c.D]), ("a_w_o", [c.D, c.D])]:
        if "a" in layers:
            W[nm] = B.din(nm, shp)
    out = B.nc.dram_tensor("out", [c.T, c.D], F32, kind="ExternalOutput").ap()
    xres = B.dscr("xres", [c.T, c.D], F32)
    mid = B.dscr("mid", [c.DFF, c.T], BF16)
    B.setup_common()
    P = B.P
    P.op("sp", lambda e: e.dma_start(out=xres, in_=x_in), dma="xcopy")
    P.barrier()
    for L in layers:
        if L == "f0" or L == "f1":
            l = int(L[1])
            B.swiglu(xres, f_norm_g[l:l + 1, :], f_w_gu[l], f_w_d[l], mid)
        elif L == "a":
            B.rwkv(xres, W)
        elif L == "b":
            B.fox(xres, W)
        elif L == "final":
            B.final_norm(xres, final_g, out)
    return B.finish()


def make_in_map(cfg, inputs, core, layers=("a", "f0", "b", "f1", "final")):
    m = {"x": np.ascontiguousarray(inputs["x"][core]),
         "f_norm_g": np.ascontiguousarray(inputs["f_norm_g"]),
         "f_w_gu": np.ascontiguousarray(inputs["f_w_gu"]),
         "f_w_d": np.ascontiguousarray(inputs["f_w_d"]),
         "final_g": np.ascontiguousarray(inputs["final_g"]).reshape(1, -1)}
    if "a" in layers:
        for nm in ["a_norm_g", "a_w0", "a_a0", "a_k_k", "a_k_a", "a_r_k", "a_lnx_g", "a_lnx_b"]:
            m[nm] = np.ascontiguousarray(inputs[nm]).reshape(1, -1)
        for nm in ["a_mix", "a_w_rkv", "a_w1", "a_w2", "a_a1", "a_a2", "a_g1", "a_g2", "a_w_o"]:
            m[nm] = np.ascontiguousarray(inputs[nm][0])
    if "b" in layers:
        for nm in ["b_norm_g", "b_b_f", "b_qn_g", "b_kn_g", "b_on_g"]:
            m[nm] = np.ascontiguousarray(inputs[nm]).reshape(1, -1)
        m["b_w_in"] = np.ascontiguousarray(inputs["b_w_in"][0])
        m["b_w_o"] = np.ascontiguousarray(inputs["b_w_o"][0])
    return m


_CACHE = {}


def kernel(**inputs):
    cfg = Cfg()
    layers = ("a", "f0", "b", "f1", "final")
    if "nc" not in _CACHE:
        _CACHE["nc"] = build_program(cfg, layers)
    nc = _CACHE["nc"]
    inputs = {k: np.asarray(v) for k, v in inputs.items()}
    in_maps = [make_in_map(cfg, inputs, core, layers) for core in range(8)]
    res = run_bass_kernel_spmd(nc, in_maps, core_ids=list(range(8)))
    out = np.stack([np.asarray(r["out"]) for r in res.results], axis=0)
    return out.astype(np.float32)
```

```python
import contextlib
import numpy as np
import concourse.bass as bass
import concourse.mybir as mybir
from concourse.bass_utils import run_bass_kernel_spmd

F32 = mybir.dt.float32
BF16 = mybir.dt.bfloat16
AF = mybir.ActivationFunctionType
ALU = mybir.AluOpType
AX = mybir.AxisListType

RMS_EPS = 1e-6
DEBUG = False
GN_EPS = 64e-5


class Cfg:
    def __init__(self, T=2048, D=2048, DFF=5632, LW=96, LA=96, LG=256):
        self.T, self.D, self.DFF, self.LW, self.LA, self.LG = T, D, DFF, LW, LA, LG
        self.H = D // 64
        self.KC = D // 128
        self.FC = DFF // 128
        self.TS = min(512, T)
        self.NTT = T // self.TS
        self.NTB = T // 128
        self.NCH = T // 64
        self.WIN = 4 * D + 3 * self.H


ENGS = ["sp", "act", "dve", "pool", "pe"]


class Prog:
    def __init__(self, nc, es):
        self.nc, self.es = nc, es
        self.streams = {e: [] for e in ENGS}
        self.sems, self.semval = {}, {}
        self.seen = {e: {} for e in ENGS}
        self.last_w, self.readers = {}, {}
        self.n_ops = 0

    def _sem(self, name):
        if name not in self.sems:
            self.sems[name] = self.es.enter_context(self.nc.semaphore("s_" + name.replace(":", "_")))
            self.semval[name] = 0
        return self.sems[name]

    def op(self, eng, fn, rd=(), wr=(), dma=None):
        deps = {}

        def add(d):
            if d is not None:
                deps[d[0]] = max(deps.get(d[0], 0), d[1])

        for k in rd:
            add(self.last_w.get(k))
        for k in wr:
            add(self.last_w.get(k))
            for s, v in self.readers.get(k, {}).items():
                add((s, v))
        own = "eng:" + eng
        waits = []
        for s, v in deps.items():
            if s == own and eng == "pe":
                continue
            if self.seen[eng].get(s, 0) < v:
                self.seen[eng][s] = v
                waits.append((self._sem(s), v))
        sname = ("dma:" + dma) if dma else own
        inc = 16 if dma else 1
        sem = self._sem(sname)
        self.semval[sname] += inc
        nv = self.semval[sname]

        def emit(e, waits=waits, fn=fn, sem=sem, inc=inc):
            for (s, v) in waits:
                e.wait_ge(s, v)
            ins = fn(e)
            ins.then_inc(sem, inc)

        self.streams[eng].append(emit)
        for k in wr:
            self.last_w[k] = (sname, nv)
            self.readers[k] = {}
        for k in rd:
            r = self.readers.setdefault(k, {})
            r[sname] = max(r.get(sname, 0), nv)
        self.n_ops += 1

    def barrier(self):
        for eng in ENGS:
            waits = []
            for s, v in self.semval.items():
                if v > 0 and self.seen[eng].get(s, 0) < v and s != "eng:" + eng:
                    self.seen[eng][s] = v
                    waits.append((self.sems[s], v))

            def emit(e, waits=waits):
                for (s, v) in waits:
                    e.wait_ge(s, v)

            self.streams[eng].append(emit)
        self.last_w, self.readers = {}, {}


class Builder:
    def __init__(self, cfg):
        self.c = cfg
        self.nc = bass.Bass("TRN2", target_bir_lowering=False)
        self.es = contextlib.ExitStack()
        self.P = Prog(self.nc, self.es)
        self.dram = {}
        self._uid = 0
        self.ps_ctr = 0
        self.wb_ctr = 0
        self.stg_ctr = 0

    def din(self, name, shape):
        self.dram[name] = self.nc.dram_tensor(name, list(shape), F32, kind="ExternalInput").ap()
        return self.dram[name]

    def dscr(self, name, shape, dt):
        self.dram[name] = self.nc.dram_tensor(name, list(shape), dt, kind=("ExternalOutput" if DEBUG else "Internal")).ap()
        return self.dram[name]

    def sb(self, name, shape, dt):
        return self.es.enter_context(self.nc.sbuf_tensor(name, list(shape), dt))

    def psum(self, name, shape, dt):
        return self.es.enter_context(self.nc.psum_tensor(name, list(shape), dt))

    def uid(self, p):
        self._uid += 1
        return f"{p}{self._uid}"

    def setup_common(self):
        c = self.c
        P = self.P
        self.psA = self.psum("psA", [128, 6, 512], F32)
        self.psT = self.psum("psT", [128, 2, 1024], BF16)
        self.ident = self.sb("ident", [128, 128], BF16)
        self.identf = self.sb("identf", [128, 128], F32)
        self.consts = self.sb("consts", [128, 8], F32)
        self.small = self.sb("small", [128, 64], F32)
        self.junk = self.sb("junk", [128, 128], F32)
        nc = self.nc

        def mk_ident(e):
            return e.affine_select(out=self.identf[:], in_=self.identf[:], pattern=[[-1, 128]],
                                   compare_op=ALU.not_equal, fill=1.0, base=0, channel_multiplier=1)

        P.op("pool", lambda e: e.memset(self.identf[:], 0.0), wr=["identf"])
        P.op("pool", mk_ident, rd=["identf"], wr=["identf"])
        P.op("dve", lambda e: e.tensor_copy(out=self.ident[:], in_=self.identf[:]), rd=["identf"], wr=["ident"])

        def mk_consts(e):
            e.memset(self.consts[:, 0:1], RMS_EPS)
            e.memset(self.consts[:, 1:2], 1.0)
            e.memset(self.consts[:, 2:3], GN_EPS)
            return e.memset(self.consts[:, 3:4], 0.0)

        P.op("pool", mk_consts, wr=["consts"])
        hsz = c.KC * (c.T + 2)
        self.TH = c.T // (2 if c.T >= 1024 else 1)
        r1 = max(hsz + c.KC * c.T, c.FC * self.TH, 17 * c.T, 16 * c.T + c.KC * c.T)
        self.R1 = self.sb("R1", [128, r1], BF16)
        self.actA = self.R1[:, 0:hsz].rearrange("p (c t) -> p c t", t=c.T + 2)
        self.actB = self.R1[:, hsz:hsz + c.KC * c.T]
        self.WQ = c.KC * 128
        self.NWQ = 12
        r2 = max(self.NWQ * self.WQ, 2 * c.FC * 256, 8 * c.D, 10 * c.T + 1408, 2 * (4 * c.T + c.NTB * 132) + 4 * c.TS + 384, 7 * c.T + 140, 9 * c.D + 8 * c.H)
        self.R2 = self.sb("R2", [128, r2], BF16)
        self.NWQ = r2 // self.WQ
        self.wq_ptr = 0
        self.xt = [self.R2[:, i * 2 * c.D:(i + 1) * 2 * c.D].bitcast(F32) for i in range(2)]
        self.grep = self.R2[:, 4 * c.D:6 * c.D].bitcast(F32)
        self.hn = [self.R2[:, (6 + i) * c.D:(7 + i) * c.D] for i in range(2)]
        self.NSTG = 3
        self.stg = [self.sb(f"stg{i}", [128, 512], F32) for i in range(self.NSTG)]
        P.op("pool", lambda e: e.memset(self.actA[:, :, 0:2], 0.0), wr=["actA"])

    def next_ps(self):
        b = self.ps_ctr % 6
        self.ps_ctr += 1
        return b

    def next_wb(self):
        b = self.wb_ctr % self.NWB
        self.wb_ctr += 1
        return b

    def next_stg(self):
        b = self.stg_ctr % self.NSTG
        self.stg_ctr += 1
        return b

    def norm_transpose(self, x_ap, g_ap):
        c, P = self.c, self.P
        P.op("sp", lambda e: e.dma_start(out=self.grep[:], in_=g_ap.partition_broadcast(128)),
             wr=["grep"], dma="grep")
        for tb in range(c.NTB):
            s = tb % 2
            xt, hn = self.xt[s], self.hn[s]
            ss = self.small[:, s:s + 1]
            rs = self.small[:, 2 + s:3 + s]
            P.op("sp", lambda e, xt=xt, tb=tb: e.dma_start(out=xt[:], in_=x_ap[tb * 128:(tb + 1) * 128, :]),
                 wr=[f"xt{s}"], dma=f"xt{s}")
            P.op("act", lambda e, xt=xt, hn=hn, ss=ss: e.activation(out=hn[:], in_=xt[:], func=AF.Square, accum_out=ss),
                 rd=[f"xt{s}"], wr=[f"hn{s}", f"ss{s}"])
            P.op("act", lambda e, ss=ss, rs=rs: e.activation(out=rs, in_=ss, func=AF.Sqrt, scale=1.0 / c.D,
                                                                 bias=self.consts[:, 0:1]),
                 rd=[f"ss{s}", "consts"], wr=[f"rs{s}"])
            P.op("dve", lambda e, rs=rs: e.reciprocal(out=rs, in_=rs), rd=[f"rs{s}"], wr=[f"rsd{s}", f"rs{s}"])
            P.op("dve", lambda e, xt=xt, hn=hn, rs=rs: e.scalar_tensor_tensor(
                out=hn[:], in0=xt[:], scalar=rs, in1=self.grep[:], op0=ALU.mult, op1=ALU.mult),
                rd=[f"xt{s}", f"rsd{s}", f"rs{s}", "grep"], wr=[f"hn{s}"])
            for c0 in range(0, c.KC, 8):
                nch = min(8, c.KC - c0)
                tbk = (tb * ((c.KC + 7) // 8) + c0 // 8) % 2

                def tr(e, hn=hn, c0=c0, nch=nch, tbk=tbk):
                    ins = None
                    for i in range(nch):
                        ins = e.transpose(out=self.psT[:, tbk, i * 128:(i + 1) * 128],
                                          in_=hn[:, (c0 + i) * 128:(c0 + i + 1) * 128], identity=self.ident[:])
                    return ins

                P.op("pe", tr, rd=[f"hn{s}", "ident"], wr=[f"psT{tbk}"])
                eng = "act" if (c0 // 8) % 2 == 0 else "dve"

                def ev(e, c0=c0, nch=nch, tbk=tbk, tb=tb, eng=eng):
                    src = self.psT[:, tbk, 0:nch * 128].rearrange("p (c t) -> p c t", t=128)
                    dst = self.actA[:, c0:c0 + nch, 2 + tb * 128:2 + (tb + 1) * 128]
                    if eng == "act":
                        return e.activation(out=dst, in_=src, func=AF.Copy)
                    return e.tensor_copy(out=dst, in_=src)

                P.op(eng, ev, rd=[f"psT{tbk}"], wr=["actA"])

    def hT(self, k, t0, t1):
        return self.actA[:, k, 2 + t0:2 + t1]

    def load_w(self, w_ap, kcn, kp, col0, ncols):
        nq = (kcn * ncols + self.WQ - 1) // self.WQ
        if self.wq_ptr + nq > self.NWQ:
            self.wq_ptr = 0
        q0 = self.wq_ptr
        self.wq_ptr += nq
        keys = [f"wq{q0 + i}" for i in range(nq)]
        view = self.R2[0:kp, q0 * self.WQ:q0 * self.WQ + kcn * ncols].rearrange("p (c n) -> p c n", n=ncols)
        src = w_ap[:, col0:col0 + ncols].rearrange("(c p) n -> p c n", p=kp)
        self.P.op("pool", lambda e: e.dma_start(out=view, in_=src), wr=keys, dma=f"wq{q0}")
        return keys, view

    def gemm_fm(self, xfn, xkeys, kcn, kp, groups, epilogue, t0=0, t1=None):
        c, P = self.c, self.P
        t1 = c.T if t1 is None else t1
        for gi, grp in enumerate(groups):
            wts = [self.load_w(w_ap, kcn, kp, col0, ncols) + (ncols,) for (w_ap, col0, ncols) in grp]
            for tt in range((t1 - t0) // c.TS):
                lo, hi = t0 + tt * c.TS, t0 + (tt + 1) * c.TS
                banks = []
                for (s, view, ncols) in wts:
                    b = self.next_ps()
                    banks.append((b, ncols))

                    def mm(e, view=view, ncols=ncols, b=b, lo=lo, hi=hi):
                        ins = None
                        for k in range(kcn):
                            ins = e.matmul(self.psA[0:ncols, b, 0:hi - lo], lhsT=view[:, k, :], rhs=xfn(k, lo, hi),
                                           start=(k == 0), stop=(k == kcn - 1))
                        return ins

                    P.op("pe", mm, rd=s + xkeys, wr=[f"ps{b}"])
                epilogue(gi, tt, lo, hi, banks)

    def gemm_tm(self, xfn, xkeys, kcn, kp, w_ap, ncols_total, nbw, epilogue, t0=0, t1=None):
        c, P = self.c, self.P
        t1 = c.T if t1 is None else t1
        for nb in range(ncols_total // nbw):
            s, view = self.load_w(w_ap, kcn, kp, nb * nbw, nbw)
            for tb in range((t1 - t0) // 128):
                lo = t0 + tb * 128
                b = self.next_ps()

                def mm(e, view=view, b=b, lo=lo):
                    ins = None
                    for k in range(kcn):
                        ins = e.matmul(self.psA[:, b, 0:nbw], lhsT=xfn(k, lo, lo + 128), rhs=view[:, k, :],
                                       start=(k == 0), stop=(k == kcn - 1))
                    return ins

                P.op("pe", mm, rd=s + xkeys, wr=[f"ps{b}"])
                epilogue(nb, lo, b)

    def resid_epilogue(self, x_ap, nbw):
        P = self.P
        cnt = [0]

        def ep(nb, lo, b):
            s = self.next_stg()
            stg = self.stg[s]
            P.op("sp", lambda e: e.dma_start(out=stg[:, 0:nbw], in_=x_ap[lo:lo + 128, nb * nbw:(nb + 1) * nbw]),
                 wr=[f"stg{s}"], dma=f"stg{s}")
            eng = "dve"
            P.op(eng, lambda e: e.tensor_tensor(out=stg[:, 0:nbw], in0=stg[:, 0:nbw], in1=self.psA[:, b, 0:nbw], op=ALU.add),
                 rd=[f"ps{b}", f"stg{s}"], wr=[f"stg{s}"])
            P.op("sp", lambda e: e.dma_start(out=x_ap[lo:lo + 128, nb * nbw:(nb + 1) * nbw], in_=stg[:, 0:nbw]),
                 rd=[f"stg{s}"], dma=f"stg{s}")
            cnt[0] += 1

        return ep

    def swiglu(self, x_ap, g_ap, wgu_ap, wd_ap, mid_ap):
        c, P = self.c, self.P
        self.norm_transpose(x_ap, g_ap)
        P.barrier()
        groups = [[(wgu_ap, j * 128, 128), (wgu_ap, c.DFF + j * 128, 128)] for j in range(c.FC)]
        sg = [self.sb(self.uid("sg"), [128, c.TS], F32) for _ in range(2)]
        mo = [self.sb(self.uid("mo"), [128, c.TS], BF16) for _ in range(2)]
        it = [0]

        def ep(gi, tt, lo, hi, banks):
            s = it[0] % 2
            it[0] += 1
            (bg, _), (bu, _) = banks
            P.op("act", lambda e: e.activation(out=sg[s][:], in_=self.psA[:, bg, 0:c.TS], func=AF.Silu),
                 rd=[f"ps{bg}"], wr=[f"sg{s}"])
            P.op("dve", lambda e: e.tensor_tensor(out=mo[s][:], in0=sg[s][:], in1=self.psA[:, bu, 0:c.TS], op=ALU.mult),
                 rd=[f"sg{s}", f"ps{bu}"], wr=[f"mo{s}"])
            P.op("sp", lambda e: e.dma_start(out=mid_ap[gi * 128:(gi + 1) * 128, lo:hi], in_=mo[s][:]),
                 rd=[f"mo{s}"], dma=f"mo{s}")

        self.gemm_fm(self.hT, ["actA"], c.KC, 128, groups, ep)
        P.barrier()
        TH = self.TH
        nh = c.T // TH
        nbw = 256
        for h in range(nh):
            midv = self.R1[:, 0:c.FC * TH].rearrange("p (c t) -> p c t", t=TH)
            for cc in range(c.FC):
                P.op("sp", lambda e, cc=cc, h=h: e.dma_start(out=midv[:, cc, :], in_=mid_ap[cc * 128:(cc + 1) * 128, h * TH:(h + 1) * TH]),
                     wr=["actB"], dma=f"actB{cc % 4}")
            xfn = lambda k, a, b_, h=h: midv[:, k, a - h * TH:b_ - h * TH]
            self.gemm_tm(xfn, ["actB"], c.FC, 128, wd_ap, c.D, nbw, self.resid_epilogue(x_ap, nbw), t0=h * TH, t1=(h + 1) * TH)
            P.barrier()

    def store_fm(self, dst_ap, row_of_group):
        P = self.P
        it = [0]

        def ep(gi, tt, lo, hi, banks):
            for bi, (b, ncols) in enumerate(banks):
                s = self.next_stg()
                stg = self.stg[s]
                eng = "act" if it[0] % 2 == 0 else "dve"
                it[0] += 1
                n = hi - lo
                if eng == "act":
                    P.op("act", lambda e, b=b, ncols=ncols, stg=stg, n=n: e.activation(out=stg[0:ncols, 0:n], in_=self.psA[0:ncols, b, 0:n], func=AF.Copy),
                         rd=[f"ps{b}"], wr=[f"stg{s}"])
                else:
                    P.op("dve", lambda e, b=b, ncols=ncols, stg=stg, n=n: e.tensor_copy(out=stg[0:ncols, 0:n], in_=self.psA[0:ncols, b, 0:n]),
                         rd=[f"ps{b}"], wr=[f"stg{s}"])
                r0 = row_of_group(gi, bi)
                P.op("sp", lambda e, r0=r0, ncols=ncols, stg=stg, n=n, lo=lo, hi=hi: e.dma_start(out=dst_ap[r0:r0 + ncols, lo:hi], in_=stg[0:ncols, 0:n]),
                     rd=[f"stg{s}"], dma=f"stg{s}")

        return ep

    def store_tm(self, dst_ap, nbw, col0=0):
        P = self.P
        it = [0]

        def ep(nb, lo, b):
            s = self.next_stg()
            stg = self.stg[s]
            eng = "act" if it[0] % 2 == 0 else "dve"
            it[0] += 1
            if eng == "act":
                P.op("act", lambda e: e.activation(out=stg[:, 0:nbw], in_=self.psA[:, b, 0:nbw], func=AF.Copy), rd=[f"ps{b}"], wr=[f"stg{s}"])
            else:
                P.op("dve", lambda e: e.tensor_copy(out=stg[:, 0:nbw], in_=self.psA[:, b, 0:nbw]), rd=[f"ps{b}"], wr=[f"stg{s}"])
            P.op("sp", lambda e: e.dma_start(out=dst_ap[lo:lo + 128, col0 + nb * nbw:col0 + (nb + 1) * nbw], in_=stg[:, 0:nbw]),
                 rd=[f"stg{s}"], dma=f"stg{s}")

        return ep

    def r2view(self, off, n, dt=BF16, parts=128, arena=None):
        w = n * (2 if dt == F32 else 1)
        arena = self.R2 if arena is None else arena
        v = arena[0:parts, off:off + w]
        if dt == F32:
            v = v.bitcast(F32)
        return v, off + w

    def transpose_to_fm(self, src_fn, dst3):
        c, P = self.c, self.P
        for tb in range(c.NTB):
            for c0 in range(0, c.KC, 8):
                nch = min(8, c.KC - c0)
                tbk = (tb * ((c.KC + 7) // 8) + c0 // 8) % 2

                def tr(e, tb=tb, c0=c0, nch=nch, tbk=tbk):
                    ins = None
                    src = src_fn(tb)
                    for i in range(nch):
                        ins = e.transpose(out=self.psT[:, tbk, i * 128:(i + 1) * 128],
                                          in_=src[:, (c0 + i) * 128:(c0 + i + 1) * 128], identity=self.ident[:])
                    return ins

                P.op("pe", tr, rd=["ztm", "ident"], wr=[f"psT{tbk}"])
                eng = "act" if (tb + c0 // 8) % 2 == 0 else "dve"

                def ev(e, c0=c0, nch=nch, tbk=tbk, tb=tb, eng=eng):
                    src = self.psT[:, tbk, 0:nch * 128].rearrange("p (c t) -> p c t", t=128)
                    dst = dst3[:, c0:c0 + nch, tb * 128:(tb + 1) * 128]
                    if eng == "act":
                        return e.activation(out=dst, in_=src, func=AF.Copy)
                    return e.tensor_copy(out=dst, in_=src)

                P.op(eng, ev, rd=[f"psT{tbk}"], wr=["zT"])

    def fox(self, x_ap, W):
        c, P = self.c, self.P
        D, T, H, KC, TS = c.D, c.T, c.H, c.KC, c.TS
        w_in = W["b_w_in"]
        qk = self.dscr("b_qk", [2 * D, T], F32)
        fa = self.dscr("b_fa", [3 * H, T], F32)
        vg = self.dscr("b_vg", [T, 2 * D], F32)
        fat = self.dscr("b_fat", [T, 3 * H], F32)
        qhat = self.dscr("b_qhat", [2 * D, T], BF16)
        qaug = self.dscr("b_qaug", [H, 4, T], BF16)
        kaug = self.dscr("b_kaug", [H, 4, T], BF16)
        akd = self.dscr("b_akd", [H, T], BF16)
        vpr = self.dscr("b_vpr", [T, D], BF16)
        self.norm_transpose(x_ap, W["b_norm_g"])
        P.barrier()
        groups = [[(w_in, j * 128, 128)] for j in range(2 * KC)]
        self.gemm_fm(self.hT, ["actA"], KC, 128, groups, self.store_fm(qk, lambda gi, bi: gi * 128))
        self.gemm_fm(self.hT, ["actA"], KC, 128, [[(w_in, 4 * D, 3 * H)]], self.store_fm(fa, lambda gi, bi: 0))
        self.gemm_tm(self.hT, ["actA"], KC, 128, w_in[:, 2 * D:4 * D], 2 * D, 512 if D >= 512 else 2 * D,
                     self.store_tm(vg, 512 if D >= 512 else 2 * D))
        self.gemm_tm(self.hT, ["actA"], KC, 128, w_in[:, 4 * D:4 * D + 3 * H], 3 * H, 3 * H, self.store_tm(fat, 3 * H))
        P.barrier()
        off = 0
        ft, off = self.r2view(off, T, F32, H, self.R1)
        f2, off = self.r2view(off, T, F32, H, self.R1)
        ct, off = self.r2view(off, T, F32, H, self.R1)
        qa, off = self.r2view(off, 4 * T, BF16, H, self.R1)
        ka, off = self.r2view(off, 4 * T, BF16, H, self.R1)
        akt, off = self.r2view(off, T, F32, H, self.R1)
        akb, off = self.r2view(off, T, BF16, H, self.R1)
        qa = qa.rearrange("p (r t) -> p r t", t=T)
        ka = ka.rearrange("p (r t) -> p r t", t=T)
        nb = self.small[0:H, 8:9]
        P.op("sp", lambda e: e.dma_start(out=ft, in_=fa[0:H, :]), wr=["ft"], dma="ft")
        P.op("sp", lambda e: e.dma_start(out=akt, in_=fa[H:2 * H, :]), wr=["akt"], dma="akt")
        P.op("sp", lambda e: e.dma_start(out=nb, in_=W["b_b_f"].rearrange("o h -> h o")), wr=["nb"], dma="nb")
        P.op("dve", lambda e: e.tensor_scalar(out=nb, in0=nb, scalar1=-1.0, scalar2=None, op0=ALU.mult), rd=["nb"], wr=["nb"])
        P.barrier()
        P.op("act", lambda e: e.activation(out=f2, in_=ft, func=AF.Exp, scale=-1.0, bias=nb), rd=["ft", "nb"], wr=["f2"])
        P.op("act", lambda e: e.activation(out=f2, in_=f2, func=AF.Ln, scale=1.0, bias=self.consts[0:H, 1:2]), rd=["consts"], wr=["f2"])
        P.op("dve", lambda e: e.tensor_scalar(out=f2, in0=f2, scalar1=-0.5, scalar2=None, op0=ALU.mult), rd=["f2"], wr=["f2"])
        P.op("dve", lambda e: e.tensor_tensor_scan(out=ct, data0=f2, data1=f2, initial=0.0, op0=ALU.add, op1=ALU.add), rd=["f2"], wr=["ct"])
        P.op("dve", lambda e: e.tensor_copy(out=qa[:, 0, :], in_=ct), rd=["ct"], wr=["qa"])
        P.op("dve", lambda e: e.tensor_tensor(out=qa[:, 1, :], in0=ct, in1=qa[:, 0, :], op=ALU.subtract), rd=["ct", "qa"], wr=["qa"])
        P.op("pool", lambda e: e.memset(qa[:, 2:4, :], 1.0), wr=["qa"])
        P.op("pool", lambda e: e.memset(ka[:, 0:2, :], 1.0), wr=["ka"])
        P.op("dve", lambda e: e.tensor_scalar(out=ka[:, 2:4, :], in0=qa[:, 0:2, :], scalar1=-1.0, scalar2=None, op0=ALU.mult), rd=["qa"], wr=["ka"])
        P.op("act", lambda e: e.activation(out=akb, in_=akt, func=AF.Sigmoid), rd=["akt"], wr=["akb"])
        P.op("sp", lambda e: e.dma_start(out=qaug, in_=qa), rd=["qa"], dma="qa")
        P.op("sp", lambda e: e.dma_start(out=kaug, in_=ka), rd=["ka"], dma="ka")
        P.op("sp", lambda e: e.dma_start(out=akd, in_=akb), rd=["akb"], dma="akb")
        P.barrier()
        off = 0
        kt, off = self.r2view(off, T + 2, F32)
        tmpf, off = self.r2view(off, T, F32)
        akx, off = self.r2view(off, T, BF16)
        sq, off = self.r2view(off, T, BF16)
        outb, off = self.r2view(off, T, BF16)
        bones, off = self.r2view(off, 128, BF16)
        gq = self.small[:, 10:11]
        gk = self.small[:, 11:12]

        P.op("dve", lambda e: e.memset(bones, 0.0), wr=["bones"])
        P.op("dve", lambda e: e.memset(bones[0:64, 0:64], 1.0), wr=["bones"])
        P.op("dve", lambda e: e.memset(bones[64:128, 64:128], 1.0), wr=["bones"])
        P.op("pool", lambda e: e.memset(kt[:, 0:2], 0.0), wr=["kt"])
        for hh in range(2):
            P.op("sp", lambda e, hh=hh: e.dma_start(out=gq[hh * 64:(hh + 1) * 64, :], in_=W["b_qn_g"].rearrange("o n -> n o")), wr=["gq"], dma="gq")
            P.op("sp", lambda e, hh=hh: e.dma_start(out=gk[hh * 64:(hh + 1) * 64, :], in_=W["b_kn_g"].rearrange("o n -> n o")), wr=["gk"], dma="gk")
        P.op("dve", lambda e: e.tensor_scalar(out=gq, in0=gq, scalar1=0.125, scalar2=None, op0=ALU.mult), rd=["gq"], wr=["gq"])
        P.barrier()
        for which in range(2):
            for p in range(KC):
                row0 = which * D + p * 128
                P.op("sp", lambda e, row0=row0: e.dma_start(out=kt[:, 2:T + 2], in_=qk[row0:row0 + 128, :]), wr=["kt"], dma="kt")
                if which == 1:
                    for hh in range(2):
                        P.op("sp", lambda e, hh=hh, p=p: e.dma_start(out=akx[hh * 64:(hh + 1) * 64, :], in_=akd[2 * p + hh:2 * p + hh + 1, :].partition_broadcast(64)),
                             wr=["akx"], dma="akx")
                    P.op("dve", lambda e: e.tensor_tensor(out=tmpf, in0=kt[:, 1:T + 1], in1=kt[:, 2:T + 2], op=ALU.subtract), rd=["kt"], wr=["tmpf"])
                    P.op("dve", lambda e: e.tensor_tensor(out=tmpf, in0=tmpf, in1=akx, op=ALU.mult), rd=["akx", "tmpf"], wr=["tmpf"])
                    P.op("dve", lambda e: e.tensor_tensor(out=kt[:, 2:T + 2], in0=kt[:, 2:T + 2], in1=tmpf, op=ALU.add), rd=["tmpf", "kt"], wr=["kt"])
                P.op("act", lambda e: e.activation(out=sq, in_=kt[:, 2:T + 2], func=AF.Square), rd=["kt"], wr=["sq"])
                for tt in range(c.NTT):
                    b = self.next_ps()
                    P.op("pe", lambda e, b=b, tt=tt: e.matmul(self.psA[:, b, 0:TS], lhsT=bones, rhs=sq[:, tt * TS:(tt + 1) * TS], start=True, stop=True),
                         rd=["sq", "bones"], wr=[f"ps{b}"])
                    P.op("act", lambda e, b=b, tt=tt: e.activation(out=tmpf[:, tt * TS:(tt + 1) * TS], in_=self.psA[:, b, 0:TS], func=AF.Sqrt,
                                                                 scale=1.0 / 64, bias=self.consts[:, 0:1]),
                         rd=[f"ps{b}", "consts", "tmpf"], wr=["tmpf"])
                P.op("dve", lambda e: e.reciprocal(out=tmpf, in_=tmpf), rd=["tmpf"], wr=["tmpf"])
                gcol = gq if which == 0 else gk
                P.op("dve", lambda e, gcol=gcol: e.scalar_tensor_tensor(out=outb, in0=kt[:, 2:T + 2], scalar=gcol, in1=tmpf, op0=ALU.mult, op1=ALU.mult),
                     rd=["kt", "tmpf", "gq", "gk"], wr=["outb"])
                P.op("sp", lambda e, row0=row0: e.dma_start(out=qhat[row0:row0 + 128, :], in_=outb), rd=["outb"], dma="outb")
        P.barrier()
        off = 0
        vt, off = self.r2view(off, D, F32)
        vp, off = self.r2view(off, D, F32)
        gt, off = self.r2view(off, D, F32)
        ong, off = self.r2view(off, D, F32)
        vo, off = self.r2view(off, D, BF16)
        al, off = self.r2view(off, 3 * H, F32)
        av, off = self.r2view(off, H, F32)
        G3 = self.actB.rearrange("p (b d) -> p b d", d=D)
        ztm = self.R1[:, 0:c.NTB * D].rearrange("p (b d) -> p b d", d=D)
        P.op("sp", lambda e: e.dma_start(out=ong, in_=W["b_on_g"].partition_broadcast(128)), wr=["ong"], dma="ong")
        for tb in range(c.NTB):
            r0 = tb * 128
            P.op("sp", lambda e, r0=r0: e.dma_start(out=vt, in_=vg[r0:r0 + 128, 0:D]), wr=["vt"], dma="vt")
            P.op("sp", lambda e, r0=r0: e.dma_start(out=gt, in_=vg[r0:r0 + 128, D:2 * D]), wr=["gt"], dma="gt")
            P.op("sp", lambda e, r0=r0: e.dma_start(out=al, in_=fat[r0:r0 + 128, :]), wr=["al"], dma="al")
            if tb == 0:
                P.op("pool", lambda e: e.memset(vp[0:1, :], 0.0), wr=["vp"])
                P.op("sp", lambda e: e.dma_start(out=vp[1:128, :], in_=vg[0:127, 0:D]), wr=["vp"], dma="vp")
            else:
                P.op("sp", lambda e, r0=r0: e.dma_start(out=vp, in_=vg[r0 - 1:r0 + 127, 0:D]), wr=["vp"], dma="vp")
            P.op("act", lambda e: e.activation(out=av, in_=al[:, 2 * H:3 * H], func=AF.Sigmoid), rd=["al"], wr=["av"])
            P.op("dve", lambda e: e.tensor_tensor(out=vp, in0=vp, in1=vt, op=ALU.subtract), rd=["vp", "vt"], wr=["vp"])
            P.op("dve", lambda e: e.tensor_tensor(out=vp.rearrange("p (h n) -> p h n", n=64), in0=vp.rearrange("p (h n) -> p h n", n=64),
                                                  in1=av.unsqueeze(2).broadcast_to([128, H, 64]), op=ALU.mult), rd=["vp", "av"], wr=["vp"])
            P.op("dve", lambda e: e.tensor_tensor(out=vo, in0=vp, in1=vt, op=ALU.add), rd=["vp", "vt"], wr=["vo"])
            P.op("sp", lambda e, r0=r0: e.dma_start(out=vpr[r0:r0 + 128, :], in_=vo), rd=["vo"], dma="vo")
            P.op("act", lambda e: e.activation(out=gt, in_=gt, func=AF.Sigmoid), rd=["gt"], wr=["gt"])
            P.op("dve", lambda e, tb=tb: e.tensor_tensor(out=G3[:, tb, :], in0=gt, in1=ong, op=ALU.mult), rd=["gt", "ong"], wr=["G"])
        P.barrier()
        off = 0
        QA, KA, VV = [], [], []
        for i in range(2):
            qs, ks = [], []
            for hh in range(2):
                v_, off = self.r2view(off, T, BF16)
                qs.append(v_)
                v_, off = self.r2view(off, T, BF16)
                ks.append(v_)
            QA.append(qs)
            KA.append(ks)
            v_, off = self.r2view(off, c.NTB * 2 * 66, BF16)
            VV.append(v_.rearrange("p (b h n) -> p b h n", h=2, n=66))
        PT = []
        for i in range(4):
            v_, off = self.r2view(off, TS, BF16)
            PT.append(v_)
        tri, off = self.r2view(off, 128, BF16)
        trif, off = self.r2view(off, 128, F32)

        P.op("pool", lambda e: e.memset(trif, 1.0), wr=["trif"])

        def mk_tri(e):
            return e.affine_select(out=trif, in_=trif, pattern=[[1, 128]], compare_op=ALU.is_ge, fill=0.0, base=0, channel_multiplier=-1)

        P.op("pool", mk_tri, rd=["trif"], wr=["trif"])
        P.op("dve", lambda e: e.tensor_copy(out=tri, in_=trif), rd=["trif"], wr=["tri"])
        for i in range(2):
            P.op("pool", lambda e, i=i: e.memset(VV[i], 1.0), wr=[f"VV{i}"])
        nq = T // TS
        nsub = TS // 128
        NPT = len(PT)

        def loads(p):
            i = p % 2
            for hh in range(2):
                h = 2 * p + hh
                P.op("sp", lambda e, i=i, hh=hh, h=h: e.dma_start(out=QA[i][hh][0:64, :], in_=qhat[h * 64:(h + 1) * 64, :]), wr=[f"QA{i}{hh}"], dma=f"QA{i}{hh}")
                P.op("sp", lambda e, i=i, hh=hh, h=h: e.dma_start(out=QA[i][hh][64:68, :], in_=qaug[h]), wr=[f"QA{i}{hh}"], dma=f"QA{i}{hh}")
                P.op("sp", lambda e, i=i, hh=hh, h=h: e.dma_start(out=KA[i][hh][0:64, :], in_=qhat[D + h * 64:D + (h + 1) * 64, :]), wr=[f"KA{i}{hh}"], dma=f"KA{i}{hh}")
                P.op("sp", lambda e, i=i, hh=hh, h=h: e.dma_start(out=KA[i][hh][64:68, :], in_=kaug[h]), wr=[f"KA{i}{hh}"], dma=f"KA{i}{hh}")
                P.op("sp", lambda e, i=i, hh=hh, h=h: e.dma_start(out=VV[i][:, :, hh, 0:64], in_=vpr[:, h * 64:(h + 1) * 64].rearrange("(b s) n -> s b n", s=128)),
                     wr=[f"VV{i}"], dma=f"VV{i}{hh}")

        def epilogue(h, I, ob):
            for u in range(nsub):
                tb = I * nsub + u
                oc = self.psA[:, ob, u * 66:u * 66 + 64]
                den = self.psA[:, ob, u * 66 + 64:u * 66 + 65]
                k0 = 16 + (tb % 4) * 4
                rc = self.small[:, k0:k0 + 1]
                ssq = self.small[:, k0 + 1:k0 + 2]
                rst = self.small[:, k0 + 2:k0 + 3]
                kk = f"ep{tb % 4}"
                P.op("dve", lambda e, rc=rc, den=den: e.reciprocal(out=rc, in_=den), rd=[f"ps{ob}"], wr=[kk + "rc"])
                P.op("act", lambda e, oc=oc, rc=rc, ssq=ssq: e.activation(out=self.junk[:, 0:64], in_=oc, func=AF.Square, scale=rc, accum_out=ssq),
                     rd=[f"ps{ob}", kk + "rc"], wr=[kk + "ss", "junk"])
                P.op("act", lambda e, ssq=ssq, rst=rst: e.activation(out=rst, in_=ssq, func=AF.Sqrt, scale=1.0 / 64, bias=self.consts[:, 0:1]),
                     rd=[kk + "ss", "consts"], wr=[kk + "rst"])
                P.op("dve", lambda e, rst=rst: e.reciprocal(out=rst, in_=rst), rd=[kk + "rst"], wr=[kk + "rst2"])
                P.op("dve", lambda e, rst=rst, rc=rc: e.tensor_tensor(out=rst, in0=rst, in1=rc, op=ALU.mult), rd=[kk + "rst2", kk + "rc"], wr=[kk + "sc"])
                P.op("dve", lambda e, oc=oc, rst=rst, tb=tb, h=h: e.scalar_tensor_tensor(
                    out=ztm[:, tb, h * 64:(h + 1) * 64], in0=oc, scalar=rst, in1=G3[:, tb, h * 64:(h + 1) * 64], op0=ALU.mult, op1=ALU.mult),
                    rd=[f"ps{ob}", kk + "sc", "G"], wr=["ztm"])

        gctr = [0, 0]
        obank = [0]
        LOOK = 2
        loads(0)
        for p in range(KC):
            i = p % 2
            if p + 1 < KC:
                loads(p + 1)
            steps = []
            for hh in range(2):
                for I in range(nq):
                    ob = 4 + (obank[0] % 2)
                    obank[0] += 1
                    nJ = (I + 1) * nsub
                    for J in range(nJ):
                        steps.append((hh, I, J, nJ, ob))

            def emit_S(st, i=i):
                hh, I, J, nJ, ob = st
                q_hi = (I + 1) * TS
                t_lo = max(I * TS, J * 128)
                N = q_hi - t_lo
                sbk = gctr[0] % 4
                gctr[0] += 1
                pti = gctr[1] % NPT
                gctr[1] += 1
                pt = PT[pti]
                P.op("pe", lambda e: e.matmul(self.psA[:, sbk, 0:N], lhsT=KA[i][hh][0:68, J * 128:(J + 1) * 128], rhs=QA[i][hh][0:68, t_lo:q_hi],
                                              start=True, stop=True), rd=[f"QA{i}{hh}", f"KA{i}{hh}"], wr=[f"ps{sbk}"])
                P.op("act", lambda e: e.activation(out=pt[:, 0:N], in_=self.psA[:, sbk, 0:N], func=AF.Exp), rd=[f"ps{sbk}"], wr=[f"PT{pti}"])
                if J * 128 >= I * TS:
                    P.op("dve", lambda e: e.tensor_tensor(out=pt[:, 0:128], in0=pt[:, 0:128], in1=tri, op=ALU.mult), rd=[f"PT{pti}", "tri"], wr=[f"PT{pti}"])
                return (pt, pti, t_lo, q_hi)

            def emit_PV(st, info, i=i, p=p):
                hh, I, J, nJ, ob = st
                pt, pti, t_lo, q_hi = info

                def pv(e):
                    ins = None
                    for u0 in range(t_lo, q_hi, 128):
                        u = (u0 - I * TS) // 128
                        ins = e.matmul(self.psA[:, ob, u * 66:u * 66 + 65], lhsT=pt[:, u0 - t_lo:u0 - t_lo + 128], rhs=VV[i][:, J, hh, 0:65],
                                       start=(J == 0 and u == 0), stop=(J == nJ - 1 and u == nsub - 1))
                    return ins

                P.op("pe", pv, rd=[f"PT{pti}", f"VV{i}"], wr=[f"ps{ob}"])
                if J == nJ - 1:
                    epilogue(2 * p + hh, I, ob)

            infos = {}
            for idx in range(len(steps) + LOOK):
                if idx < len(steps):
                    infos[idx] = emit_S(steps[idx])
                if idx - LOOK >= 0:
                    emit_PV(steps[idx - LOOK], infos.pop(idx - LOOK))
        P.barrier()
        if DEBUG:
            dz = self.dscr("dbg_ztm", [128, c.NTB, D], BF16)
            dg = self.dscr("dbg_G", [128, c.NTB, D], BF16)
            P.op("sp", lambda e: e.dma_start(out=dz, in_=ztm), dma="dbg")
            P.op("sp", lambda e: e.dma_start(out=dg, in_=G3), dma="dbg")
            P.barrier()
        zT = self.actB.rearrange("p (c t) -> p c t", t=T)
        self.transpose_to_fm(lambda tb: ztm[:, tb, :], zT)
        P.barrier()
        nbw = 512 if D >= 512 else D
        self.gemm_tm(lambda k, a, b_: zT[:, k, a:b_], ["zT"], KC, 128, W["b_w_o"], D, nbw, self.resid_epilogue(x_ap, nbw))
        P.barrier()

    def rwkv(self, x_ap, W):
        c, P = self.c, self.P
        D, T, H, KC, TS, NCH = c.D, c.T, c.H, c.KC, c.TS, c.NCH
        rr = self.dscr("a_rr", [D, T], F32)
        kkd = self.dscr("a_kk", [D, T], F32)
        vvd = self.dscr("a_vv", [D, T], F32)
        vtm = self.dscr("a_vtm", [T, D], F32)
        lwd = self.dscr("a_lw", [D, T], F32)
        aad = self.dscr("a_aa", [D, T], F32)
        ggd = self.dscr("a_gg", [D, T], F32)
        dd = {nm_: self.dscr("a_d" + nm_, [D, T], BF16) for nm_ in ["rt", "at", "bt", "kt", "bh", "kh"]}
        d_bo = self.dscr("a_dbo", [D, T], F32)
        d_wc = self.dscr("a_dwc", [D, NCH], F32)
        d_y = self.dscr("a_dy", [D, T], F32)
        rowsA = self.sb("rowsA", [128, 128], F32)
        rowsB = self.sb("rowsB", [128, 128], F32)
        cols = self.sb("cols", [128, 256], F32)
        lora = self.sb("lora", [128, 2, T], BF16)
        wcs = self.sb("wcs", [128, NCH], F32)
        P.op("dve", lambda e: e.memset(rowsA[:], 0.0), wr=["rowsA"])
        P.op("dve", lambda e: e.memset(rowsB[:], 0.0), wr=["rowsB"])
        P.op("sp", lambda e: e.dma_start(out=rowsA[0:6 * KC, :], in_=W["a_mix"].rearrange("s (c p) -> (s c) p", p=128)), wr=["rowsA"], dma="rowsA")
        vecs = ["a_w0", "a_a0", "a_k_k", "a_k_a", "a_r_k", "a_lnx_g", "a_lnx_b"]
        for i, nm in enumerate(vecs):
            P.op("sp", lambda e, i=i, nm=nm: e.dma_start(out=rowsB[i * KC:(i + 1) * KC, :], in_=W[nm].rearrange("o (c p) -> (o c) p", p=128)),
                 wr=["rowsB"], dma="rowsB")
        b0 = self.next_ps()
        P.op("pe", lambda e: e.transpose(out=self.psA[:, b0, 0:128], in_=rowsA[:], identity=self.identf[:]), rd=["rowsA", "identf"], wr=[f"ps{b0}"])
        P.op("dve", lambda e: e.tensor_copy(out=cols[:, 0:128], in_=self.psA[:, b0, 0:128]), rd=[f"ps{b0}"], wr=["cols"])
        b1 = self.next_ps()
        P.op("pe", lambda e: e.transpose(out=self.psA[:, b1, 0:128], in_=rowsB[:], identity=self.identf[:]), rd=["rowsB", "identf"], wr=[f"ps{b1}"])
        P.op("dve", lambda e: e.tensor_copy(out=cols[:, 128:256], in_=self.psA[:, b1, 0:128]), rd=[f"ps{b1}"], wr=["cols"])
        mixc = lambda s_, k: cols[:, s_ * KC + k:s_ * KC + k + 1]
        vcol = lambda nm, k: cols[:, 128 + vecs.index(nm) * KC + k:128 + vecs.index(nm) * KC + k + 1]
        omk0 = 128 + 7 * KC
        P.op("dve", lambda e: e.tensor_scalar(out=cols[:, omk0:omk0 + KC], in0=cols[:, 128 + 3 * KC:128 + 4 * KC], scalar1=-1.0, scalar2=1.0,
                                              op0=ALU.mult, op1=ALU.add), rd=["cols"], wr=["cols"])
        self.norm_transpose(x_ap, W["a_norm_g"])
        P.barrier()
        xm = self.actB.rearrange("p (c t) -> p c t", t=T)
        xmf = lambda k, a, b_: xm[:, k, a:b_]

        def mix(s_):
            for k in range(KC):
                P.op("dve", lambda e, k=k: e.tensor_tensor(out=xm[:, k, :], in0=self.actA[:, k, 1:T + 1], in1=self.actA[:, k, 2:T + 2], op=ALU.subtract),
                     rd=["actA"], wr=["xm"])
                P.op("dve", lambda e, k=k: e.scalar_tensor_tensor(out=xm[:, k, :], in0=xm[:, k, :], scalar=mixc(s_, k), in1=self.actA[:, k, 2:T + 2],
                                                                  op0=ALU.mult, op1=ALU.add), rd=["actA", "xm", "cols"], wr=["xm"])

        fullg = lambda w_ap, n: [[(w_ap, j * 128, min(128, n - j * 128))] for j in range((n + 127) // 128)]
        for s_, dst in [(0, rr), (1, kkd), (2, vvd)]:
            mix(s_)
            self.gemm_fm(xmf, ["xm"], KC, 128, fullg(W["a_w_rkv"][s_], D), self.store_fm(dst, lambda gi, bi: gi * 128))
            if s_ == 2:
                nbw = 512 if D >= 512 else D
                self.gemm_tm(xmf, ["xm"], KC, 128, W["a_w_rkv"][2], D, nbw, self.store_tm(vtm, nbw))
            P.barrier()

        def lora_ep(func):
            def ep(gi, tt, lo, hi, banks):
                (b, ncols), = banks
                P.op("act", lambda e: e.activation(out=lora[0:ncols, gi, lo:hi], in_=self.psA[0:ncols, b, 0:hi - lo], func=func),
                     rd=[f"ps{b}"], wr=["lora"])
            return ep

        def out_ep(dst, func, bias_nm, post_scale):
            def ep(gi, tt, lo, hi, banks):
                (b, ncols), = banks
                s = self.next_stg()
                stg = self.stg[s]
                n = hi - lo
                if func is None:
                    P.op("act", lambda e: e.activation(out=stg[:, 0:n], in_=self.psA[:, b, 0:n], func=AF.Copy), rd=[f"ps{b}"], wr=[f"stg{s}"])
                else:
                    P.op("act", lambda e: e.activation(out=stg[:, 0:n], in_=self.psA[:, b, 0:n], func=func, bias=vcol(bias_nm, gi), scale=1.0),
                         rd=[f"ps{b}", "cols"], wr=[f"stg{s}"])
                if post_scale is not None:
                    P.op("dve", lambda e: e.tensor_scalar(out=stg[:, 0:n], in0=stg[:, 0:n], scalar1=post_scale, scalar2=None, op0=ALU.mult),
                         rd=[f"stg{s}"], wr=[f"stg{s}"])
                P.op("sp", lambda e: e.dma_start(out=dst[gi * 128:(gi + 1) * 128, lo:hi], in_=stg[:, 0:n]), rd=[f"stg{s}"], dma=f"stg{s}")
            return ep

        for s_, w1n, w2n, L, f1, dst, f2, bnm, psc in [
                (3, "a_w1", "a_w2", c.LW, AF.Tanh, lwd, AF.Sigmoid, "a_w0", -float(np.exp(-0.5))),
                (4, "a_a1", "a_a2", c.LA, AF.Copy, aad, AF.Sigmoid, "a_a0", None),
                (5, "a_g1", "a_g2", c.LG, AF.Sigmoid, ggd, None, None, None)]:
            mix(s_)
            self.gemm_fm(xmf, ["xm"], KC, 128, fullg(W[w1n], L), lora_ep(f1))
            kcn2 = (L + 127) // 128
            kp2 = L if L < 128 else 128
            self.gemm_fm(lambda k, a, b_, kp2=kp2: lora[0:kp2, k, a:b_], ["lora"], kcn2, kp2, fullg(W[w2n], D), out_ep(dst, f2, bnm, psc))
            P.barrier()

        off = 0
        maskA, off = self.r2view(off, 384, F32)
        maskB, off = self.r2view(off, 256, F32)
        bones, off = self.r2view(off, 128, BF16)
        fA = []
        for i in range(5):
            v_, off = self.r2view(off, T, F32)
            fA.append(v_)
        assert off <= self.R2.shape[1], (off, self.R2.shape)
        X1, X2, X3, X4, X5 = fA
        o1 = 0
        eN, o1 = self.r2view(o1, T, F32, 128, self.R1)
        cmask, o1 = self.r2view(o1, T, F32, 128, self.R1)
        BO, o1 = self.r2view(o1, T, F32, 128, self.R1)
        YT, o1 = self.r2view(o1, T, F32, 128, self.R1)
        RT, o1 = self.r2view(o1, T, BF16, 128, self.R1)
        AT, o1 = self.r2view(o1, T, BF16, 128, self.R1)
        BT, o1 = self.r2view(o1, T, BF16, 128, self.R1)
        KT, o1 = self.r2view(o1, T, BF16, 128, self.R1)
        BH, o1 = self.r2view(o1, T, BF16, 128, self.R1)
        KH, o1 = self.r2view(o1, T, BF16, 128, self.R1)
        S1, o1 = self.r2view(o1, T, BF16, 128, self.R1)
        V2p, o1 = self.r2view(o1, T, BF16, 128, self.R1)
        zoff = max(o1, c.KC * (c.T + 2))
        V2p = V2p.rearrange("p (c i) -> p c i", i=64)
        zT = self.R1[:, zoff:zoff + KC * T].rearrange("p (c t) -> p c t", t=T)
        P.op("dve", lambda e: e.memset(bones, 0.0), wr=["bones"])
        P.op("dve", lambda e: e.memset(bones[0:64, 0:64], 1.0), wr=["bones"])
        P.op("dve", lambda e: e.memset(bones[64:128, 64:128], 1.0), wr=["bones"])
        P.op("dve", lambda e: e.memset(cmask, 1.0), wr=["cmask"])
        P.op("dve", lambda e: e.memset(cmask.rearrange("p (c t) -> p c t", t=64)[:, :, 0:1], 0.0), wr=["cmask"])
        P.op("pool", lambda e: e.memset(maskA, 1.0), wr=["maskA"])
        P.op("pool", lambda e: e.memset(maskB, 1.0), wr=["maskB"])
        for (mk, o_, cm, base, pat) in [(maskA, 0, -1, -1, 1), (maskA, 128, 1, -1, -1), (maskA, 256, -1, -1, 1), (maskB, 0, -1, 0, 1), (maskB, 128, -1, 0, 1)]:
            P.op("pool", lambda e, mk=mk, o_=o_, cm=cm, base=base, pat=pat: e.affine_select(
                out=mk[:, o_:o_ + 128], in_=mk[:, o_:o_ + 128], pattern=[[pat, 128]], compare_op=ALU.is_ge, fill=0.0, base=base, channel_multiplier=cm),
                rd=["maskA", "maskB"], wr=["maskA", "maskB"])
            P.op("pool", lambda e, mk=mk, o_=o_: e.memset(mk[0:64, o_ + 64:o_ + 128], 0.0), rd=["maskA", "maskB"], wr=["maskA", "maskB"])
            P.op("pool", lambda e, mk=mk, o_=o_: e.memset(mk[64:128, o_:o_ + 64], 0.0), rd=["maskA", "maskB"], wr=["maskA", "maskB"])
        P.op("dve", lambda e: e.memset(self.psA[:], 0.0), wr=[f"ps{i}" for i in range(6)])
        P.barrier()
        itc = [0]

        def do_pair(p):
            r0 = p * 128
            ld = lambda dst, src, key: P.op("sp", lambda e: e.dma_start(out=dst, in_=src[r0:r0 + 128, :]), wr=[key], dma=key)
            tt_ = lambda out, a, b_, op, rd, wr: P.op("dve", lambda e: e.tensor_tensor(out=out, in0=a, in1=b_, op=op), rd=rd, wr=wr)
            v3 = lambda a_: a_.rearrange("p (c t) -> p c t", t=64)
            ld(X1, lwd, "X1")
            P.op("dve", lambda e: e.tensor_tensor_scan(out=X2, data0=cmask, data1=X1, initial=0.0, op0=ALU.mult, op1=ALU.add), rd=["cmask", "X1"], wr=["X2"])
            tt_(X1, X2, X1, ALU.subtract, ["X2", "X1"], ["X1"])
            P.op("act", lambda e: e.activation(out=X3, in_=X2, func=AF.Exp), rd=["X2"], wr=["X3"])
            P.op("act", lambda e: e.activation(out=eN, in_=X2, func=AF.Exp, scale=-1.0), rd=["X2"], wr=["eN"])
            P.op("act", lambda e: e.activation(out=X1, in_=X1, func=AF.Exp), rd=["X1"], wr=["X1"])
            ld(X2, kkd, "X2")
            P.op("dve", lambda e: e.tensor_scalar(out=X4, in0=X2, scalar1=vcol("a_k_k", p), scalar2=None, op0=ALU.mult), rd=["X2", "cols"], wr=["X4"])
            P.op("act", lambda e: e.activation(out=S1, in_=X4, func=AF.Square), rd=["X4"], wr=["S1"])
            for t2 in range(c.NTT):
                b = self.next_ps()
                P.op("pe", lambda e, b=b, t2=t2: e.matmul(self.psA[:, b, 0:TS], lhsT=bones, rhs=S1[:, t2 * TS:(t2 + 1) * TS], start=True, stop=True),
                     rd=["S1", "bones"], wr=[f"ps{b}"])
                P.op("act", lambda e, b=b, t2=t2: e.activation(out=X5[:, t2 * TS:(t2 + 1) * TS], in_=self.psA[:, b, 0:TS], func=AF.Sqrt),
                     rd=[f"ps{b}", "X5"], wr=["X5"])
            P.op("dve", lambda e: e.tensor_scalar(out=X5, in0=X5, scalar1=1e-12, scalar2=None, op0=ALU.max), rd=["X5"], wr=["X5"])
            P.op("dve", lambda e: e.reciprocal(out=X5, in_=X5), rd=["X5"], wr=["X5"])
            tt_(X4, X4, X5, ALU.mult, ["X4", "X5"], ["X4"])
            P.op("dve", lambda e: e.scalar_tensor_tensor(out=AT, in0=X4, scalar=-1.0, in1=X1, op0=ALU.mult, op1=ALU.mult), rd=["X4", "X1"], wr=["AT"])
            ld(X1, aad, "X1")
            P.op("dve", lambda e: e.tensor_scalar(out=X5, in0=X1, scalar1=vcol("a_k_a", p), scalar2=cols[:, omk0 + p:omk0 + p + 1], op0=ALU.mult, op1=ALU.add),
                 rd=["X1", "cols"], wr=["X5"])
            tt_(X2, X2, X5, ALU.mult, ["X2", "X5"], ["X2"])
            tt_(X1, X4, X1, ALU.mult, ["X4", "X1"], ["X1"])
            WCb = X3.rearrange("p (c t) -> p c t", t=64)[:, :, 63:64].broadcast_to([128, NCH, 64])
            tt_(X5, X1, eN, ALU.mult, ["X1", "eN"], ["X5"])
            P.op("act", lambda e: e.activation(out=BT, in_=X5, func=AF.Copy), rd=["X5"], wr=["BT"])
            tt_(v3(BH), v3(X5), WCb, ALU.mult, ["X5", "X3"], ["BH"])
            tt_(X5, X2, eN, ALU.mult, ["X2", "eN", "BT", "BH"], ["X5"])
            P.op("act", lambda e: e.activation(out=KT, in_=X5, func=AF.Copy), rd=["X5"], wr=["KT"])
            tt_(v3(KH), v3(X5), WCb, ALU.mult, ["X5", "X3"], ["KH"])
            ld(X1, rr, "X1")
            tt_(RT, X1, X3, ALU.mult, ["X1", "X3"], ["RT"])
            P.op("dve", lambda e: e.scalar_tensor_tensor(out=S1, in0=X1, scalar=vcol("a_r_k", p), in1=X2, op0=ALU.mult, op1=ALU.mult),
                 rd=["X1", "X2", "cols", "S1"], wr=["S1"])
            ld(X4, vvd, "X4")
            for t2 in range(c.NTT):
                b = self.next_ps()
                P.op("pe", lambda e, b=b, t2=t2: e.matmul(self.psA[:, b, 0:TS], lhsT=bones, rhs=S1[:, t2 * TS:(t2 + 1) * TS], start=True, stop=True),
                     rd=["S1", "bones"], wr=[f"ps{b}"])
                P.op("dve", lambda e, b=b, t2=t2: e.tensor_tensor(out=BO[:, t2 * TS:(t2 + 1) * TS], in0=self.psA[:, b, 0:TS], in1=X4[:, t2 * TS:(t2 + 1) * TS], op=ALU.mult),
                     rd=[f"ps{b}", "X4", "BO"], wr=["BO"])
            eP = X3
            for nm_, til in [("rt", RT), ("at", AT), ("bt", BT), ("kt", KT), ("bh", BH), ("kh", KH)]:
                P.op("sp", lambda e, nm_=nm_, til=til: e.dma_start(out=dd[nm_][r0:r0 + 128, :], in_=til), rd=[nm_.upper()], dma="st_" + nm_)
            P.op("sp", lambda e: e.dma_start(out=d_bo[r0:r0 + 128, :], in_=BO), rd=["BO"], dma="st_bo")
            P.op("dve", lambda e: e.tensor_copy(out=wcs[:], in_=X3.rearrange("p (c t) -> p c t", t=64)[:, :, 63]), rd=["X3"], wr=["wcs"])
            P.op("sp", lambda e: e.dma_start(out=d_wc[r0:r0 + 128, :], in_=wcs[:]), rd=["wcs"], dma="st_wc")

        for p in range(KC):
            do_pair(p)
        P.barrier()

        NS = 1408
        o1 = 0
        D6 = []
        for bufi in range(2):
            lst = []
            for k6 in range(6):
                v_, o1 = self.r2view(o1, KC * 128, BF16, 128, self.R1)
                lst.append(v_.rearrange("p (q t) -> p q t", t=128))
            D6.append(lst)
        V2b = []
        for bufi in range(2):
            v_, o1 = self.r2view(o1, KC * 128, BF16, 128, self.R1)
            V2b.append(v_.rearrange("p (q c i) -> p q c i", c=2, i=64))
        Yst = []
        for bufi in range(2):
            v_, o1 = self.r2view(o1, KC * 128, F32, 128, self.R1)
            Yst.append(v_.rearrange("p (q t) -> p q t", t=128))
        WCt, o1 = self.r2view(o1, KC * NCH, F32, 128, self.R1)
        WCt = WCt.rearrange("p (q c) -> p q c", c=NCH)
        STt, o1 = self.r2view(o1, KC * 64, F32, 128, self.R1)
        STbt, o1 = self.r2view(o1, KC * 64, BF16, 128, self.R1)
        Ust, o1 = self.r2view(o1, KC * 64, BF16, 128, self.R1)
        SAt, o1 = self.r2view(o1, KC * 64, BF16, 128, self.R1)
        v4 = lambda a_: a_.rearrange("p (q i) -> p q i", i=64)
        STt, STbt, Ust, SAt = v4(STt), v4(STbt), v4(Ust), v4(SAt)
        assert o1 <= self.R1.shape[1], (o1, self.R1.shape)
        sets_off = 384 * 2 + 256 * 2 + 128
        assert sets_off + KC * NS <= self.R2.shape[1], (sets_off + KC * NS, self.R2.shape)

        def pset(p):
            o_ = sets_off + p * NS
            g = lambda a_, n: self.R2[:, o_ + a_:o_ + a_ + n]
            return dict(NN=g(0, 256), MAK=g(256, 128), MB=g(384, 256), PP=[g(640, 256), g(896, 256)], Q=g(1152, 128), BK=g(1280, 128))

        P.op("sp", lambda e: e.dma_start(out=WCt, in_=d_wc.rearrange("(q p) c -> p q c", p=128)), wr=["WCt"], dma="WCt")
        P.op("dve", lambda e: e.memset(STt, 0.0), wr=["STall"])
        P.op("dve", lambda e: e.memset(STbt, 0.0), wr=["STball"])
        P.barrier()
        hctr = [0]

        def next_half():
            b_ = self.next_ps()
            return b_, 0, f"ps{b_}"

        tctr = [0]

        def next_tslot():
            t_ = tctr[0] % 2
            tctr[0] += 1
            return t_, 0, f"psT{t_}"

        nblk = T // 128
        d6n = ["rt", "at", "bt", "kt", "bh", "kh"]

        def load_block(bi):
            bufi = bi % 2
            for k6 in range(6):
                P.op("sp", lambda e, k6=k6: e.dma_start(out=D6[bufi][k6], in_=dd[d6n[k6]][:, bi * 128:(bi + 1) * 128].rearrange("(q p) t -> p q t", p=128)),
                     wr=[f"D6_{bufi}"], dma=f"D6_{bufi}")
            for hh in range(2):
                for cl in range(2):
                    src = vtm[bi * 128 + cl * 64:bi * 128 + (cl + 1) * 64, :].rearrange("s (q h i) -> h s q i", h=2, i=64)[hh]
                    P.op("pool", lambda e, hh=hh, cl=cl, src=src: e.dma_start(out=V2b[bufi][hh * 64:(hh + 1) * 64, :, cl, :], in_=src),
                         wr=[f"V2_{bufi}"], dma=f"V2_{bufi}")

        def chunk_gen(p, ch):
            bi, cl = ch // 2, ch % 2
            bufi = bi % 2
            cs = slice(cl * 64, (cl + 1) * 64)
            RTc, ATc, BTc, KTc, BHc, KHc = [D6[bufi][k6][:, p, :] for k6 in range(6)]
            dk = [f"D6_{bufi}"] * 6
            V2c = V2b[bufi][:, p, cl, :]
            vk = f"V2_{bufi}"
            S_ = pset(p)
            NN, MAK, MBt, PPs, Q, BK = S_["NN"], S_["MAK"], S_["MB"], S_["PP"], S_["Q"], S_["BK"]
            kNN, kMAK, kMB, kQ, kBK = f"NN{p}", f"MAK{p}", f"MB{p}", f"Q{p}", f"BK{p}"
            STp, STbp, Up, SAp = STt[:, p, :], STbt[:, p, :], Ust[:, p, :], SAt[:, p, :]
            kST, kSTb, kU, kSA = f"ST{p}", f"STb{p}", f"U{p}", f"SA{p}"
            hs = [slice(0, 64), slice(64, 128)]
            b1, o1_, k1 = next_half()
            b2, o2_, k2 = b1, 256, k1
            b3, o3_, k3 = next_half()

            def mmats(e):
                ins = None
                for (bank, o_, X, Y) in [(b1, o1_, BTc, ATc), (b1, o1_ + 128, ATc, BTc), (b2, o2_, KTc, ATc), (b3, o3_, BTc, RTc), (b3, o3_ + 128, KTc, RTc)]:
                    for hh in range(2):
                        ps_ = hs[hh]
                        ins = e.matmul(self.psA[ps_, bank, o_ + hh * 64:o_ + (hh + 1) * 64], lhsT=X[ps_, cs], rhs=Y[ps_, cs],
                                       start=True, stop=True, tile_position=(hh * 64, hh * 64))
                return ins

            P.op("pe", mmats, rd=[dk[1], dk[2], dk[3], dk[0]], wr=[k1, k3])
            P.op("dve", lambda e: e.tensor_tensor(out=NN, in0=self.psA[:, b1, o1_:o1_ + 256], in1=maskA[:, 0:256], op=ALU.mult), rd=[k1, "maskA"], wr=[kNN])
            P.op("dve", lambda e: e.tensor_tensor(out=MAK, in0=self.psA[:, b2, o2_:o2_ + 128], in1=maskA[:, 256:384], op=ALU.mult), rd=[k2, "maskA"], wr=[kMAK])
            P.op("dve", lambda e: e.tensor_tensor(out=MBt, in0=self.psA[:, b3, o3_:o3_ + 256], in1=maskB, op=ALU.mult), rd=[k3, "maskB"], wr=[kMB])
            P.op("dve", lambda e: e.tensor_tensor(out=Q, in0=NN[:, 0:128], in1=self.ident[:], op=ALU.add), rd=[kNN, "ident"], wr=[kQ])
            yield
            cur, curk = NN, kNN
            for lvl in range(5):
                bL, oL, kL = next_half()
                pn = PPs[lvl % 2]
                kpn = f"PP{p}_{lvl % 2}"

                def sqm(e, cur=cur, bL=bL, oL=oL):
                    e.matmul(self.psA[:, bL, oL:oL + 128], lhsT=cur[:, 128:256], rhs=cur[:, 0:128], start=True, stop=True)
                    return e.matmul(self.psA[:, bL, oL + 128:oL + 256], lhsT=cur[:, 0:128], rhs=cur[:, 128:256], start=True, stop=True)

                P.op("pe", sqm, rd=[curk], wr=[kL])
                P.op("act", lambda e, pn=pn, bL=bL, oL=oL: e.activation(out=pn, in_=self.psA[:, bL, oL:oL + 256], func=AF.Copy), rd=[kL], wr=[kpn])
                yield
                bQ, oQ, kQb = next_half()
                P.op("pe", lambda e, pn=pn, bQ=bQ, oQ=oQ: e.matmul(self.psA[:, bQ, oQ:oQ + 128], lhsT=pn[:, 128:256], rhs=Q, start=True, stop=True),
                     rd=[kpn, kQ], wr=[kQb])
                P.op("dve", lambda e, bQ=bQ, oQ=oQ: e.tensor_tensor(out=Q, in0=Q, in1=self.psA[:, bQ, oQ:oQ + 128], op=ALU.add), rd=[kQb, kQ], wr=[kQ])
                yield
                cur, curk = pn, kpn
            tb_, to_, kt_ = next_tslot()

            def trs(e):
                ins = None
                for o_, X in [(0, BHc), (64, KHc)]:
                    for hh in range(2):
                        ps_ = hs[hh]
                        ins = e.transpose(out=self.psT[ps_, tb_, to_ + o_:to_ + o_ + 64], in_=X[ps_, cs], identity=self.ident[ps_, ps_],
                                          tile_position=(hh * 64, hh * 64))
                return ins

            P.op("pe", trs, rd=[dk[4], dk[5], "ident"], wr=[kt_])
            P.op("act", lambda e: e.activation(out=BK, in_=self.psT[:, tb_, to_:to_ + 128], func=AF.Copy), rd=[kt_], wr=[kBK])
            yield
            bU, oU, kUb = next_half()

            def mmU(e):
                for hh in range(2):
                    ps_ = hs[hh]
                    e.matmul(self.psA[ps_, bU, oU:oU + 64], lhsT=ATc[ps_, cs], rhs=STbp[ps_, :], start=True, stop=False, tile_position=(hh * 64, hh * 64))
                return e.matmul(self.psA[:, bU, oU:oU + 64], lhsT=MAK, rhs=V2c, start=False, stop=True)

            P.op("pe", mmU, rd=[dk[1], kSTb, "STball", kMAK, vk], wr=[kUb])
            P.op("act", lambda e: e.activation(out=Up, in_=self.psA[:, bU, oU:oU + 64], func=AF.Copy), rd=[kUb], wr=[kU])
            yield
            bS, oS, kSb = next_half()
            P.op("pe", lambda e: e.matmul(self.psA[:, bS, oS:oS + 64], lhsT=Q, rhs=Up, start=True, stop=True), rd=[kQ, kU], wr=[kSb])
            P.op("act", lambda e: e.activation(out=SAp, in_=self.psA[:, bS, oS:oS + 64], func=AF.Copy), rd=[kSb], wr=[kSA])
            yield
            bY, oY, kYb = next_half()

            def mmY(e):
                ins = None
                for hh in range(2):
                    ps_ = hs[hh]
                    e.matmul(self.psA[ps_, bY, oY:oY + 64], lhsT=STbp[ps_, :], rhs=RTc[ps_, cs], start=True, stop=False, tile_position=(hh * 64, hh * 64))
                for hh in range(2):
                    ps_ = hs[hh]
                    e.matmul(self.psA[ps_, bY, oY:oY + 64], lhsT=V2c[ps_, :], rhs=MBt[ps_, 128 + hh * 64:128 + (hh + 1) * 64], start=False, stop=False,
                             tile_position=(hh * 64, hh * 64))
                for hh in range(2):
                    ps_ = hs[hh]
                    ins = e.matmul(self.psA[ps_, bY, oY:oY + 64], lhsT=SAp[ps_, :], rhs=MBt[ps_, hh * 64:(hh + 1) * 64], start=False, stop=True,
                                   tile_position=(hh * 64, hh * 64))
                return ins

            P.op("pe", mmY, rd=[kSTb, "STball", dk[0], vk, kMB, kSA], wr=[kYb])
            P.op("act", lambda e: e.activation(out=Yst[bufi][:, p, cs], in_=self.psA[:, bY, oY:oY + 64], func=AF.Copy), rd=[kYb], wr=[f"Yst{bufi}"])
            yield
            bN, oN, kNb = next_half()

            def mmN(e):
                ins = None
                for hh in range(2):
                    ps_ = hs[hh]
                    e.matmul(self.psA[ps_, bN, oN:oN + 64], lhsT=BK[ps_, 0:64], rhs=SAp[ps_, :], start=True, stop=False, tile_position=(hh * 64, hh * 64))
                for hh in range(2):
                    ps_ = hs[hh]
                    ins = e.matmul(self.psA[ps_, bN, oN:oN + 64], lhsT=BK[ps_, 64:128], rhs=V2c[ps_, :], start=False, stop=True,
                                   tile_position=(hh * 64, hh * 64))
                return ins

            P.op("pe", mmN, rd=[kBK, kSA, vk], wr=[kNb])
            P.op("dve", lambda e: e.scalar_tensor_tensor(out=STp, in0=STp, scalar=WCt[:, p, ch:ch + 1], in1=self.psA[:, bN, oN:oN + 64],
                                                         op0=ALU.mult, op1=ALU.add), rd=[kST, "STall", "WCt", kNb], wr=[kST])
            P.op("act", lambda e: e.activation(out=STbp, in_=STp, func=AF.Copy), rd=[kST, "STball"], wr=[kSTb])
            yield

        load_block(0)
        for ch in range(NCH):
            bi = ch // 2
            if ch % 2 == 0 and bi + 1 < nblk:
                load_block(bi + 1)
            gens = [chunk_gen(p, ch) for p in range(KC)]
            while gens:
                alive = []
                for g in gens:
                    try:
                        next(g)
                        alive.append(g)
                    except StopIteration:
                        pass
                gens = alive
            if ch % 2 == 1:
                bufi = bi % 2
                P.op("sp", lambda e, bi=bi, bufi=bufi: e.dma_start(out=d_y[:, bi * 128:(bi + 1) * 128].rearrange("(q p) t -> p q t", p=128), in_=Yst[bufi]),
                     rd=[f"Yst{bufi}"], dma=f"Yst{bufi}")
        P.barrier()

        def do_epi(p):
            r0 = p * 128
            ld = lambda dst, src, key: P.op("sp", lambda e: e.dma_start(out=dst, in_=src[r0:r0 + 128, :]), wr=[key], dma=key)
            tt_ = lambda out, a, b_, op, rd, wr: P.op("dve", lambda e: e.tensor_tensor(out=out, in0=a, in1=b_, op=op), rd=rd, wr=wr)
            ld(X3, d_y, "X3")
            ld(eN, d_bo, "eN")
            ld(X4, ggd, "X4")
            YT_ = X3
            P.op("act", lambda e: e.activation(out=S1, in_=YT_, func=AF.Copy), rd=["X3"], wr=["S1"])
            P.op("act", lambda e: e.activation(out=RT, in_=YT_, func=AF.Square), rd=["X3"], wr=["RT"])
            mu, var, tG = X1, X2, X5
            for t2 in range(c.NTT):
                sl_ = slice(t2 * TS, (t2 + 1) * TS)
                b = self.next_ps()
                P.op("pe", lambda e, b=b, sl_=sl_: e.matmul(self.psA[:, b, 0:TS], lhsT=bones, rhs=S1[:, sl_], start=True, stop=True), rd=["S1", "bones"], wr=[f"ps{b}"])
                P.op("act", lambda e, b=b, sl_=sl_: e.activation(out=mu[:, sl_], in_=self.psA[:, b, 0:TS], func=AF.Copy, scale=1.0 / 64), rd=[f"ps{b}", "X1"], wr=["X1"])
                b2 = self.next_ps()
                P.op("pe", lambda e, b2=b2, sl_=sl_: e.matmul(self.psA[:, b2, 0:TS], lhsT=bones, rhs=RT[:, sl_], start=True, stop=True), rd=["RT", "bones"], wr=[f"ps{b2}"])
                P.op("dve", lambda e, sl_=sl_: e.tensor_tensor(out=tG[:, sl_], in0=mu[:, sl_], in1=mu[:, sl_], op=ALU.mult), rd=["X1", "X5"], wr=["X5"])
                P.op("dve", lambda e, b2=b2, sl_=sl_: e.scalar_tensor_tensor(out=var[:, sl_], in0=self.psA[:, b2, 0:TS], scalar=1.0 / 64, in1=tG[:, sl_],
                                                                             op0=ALU.mult, op1=ALU.subtract), rd=[f"ps{b2}", "X5", "X2"], wr=["X2"])
            P.op("act", lambda e: e.activation(out=var, in_=var, func=AF.Sqrt, bias=self.consts[:, 2:3], scale=1.0), rd=["X2", "consts"], wr=["X2"])
            P.op("dve", lambda e: e.reciprocal(out=var, in_=var), rd=["X2"], wr=["X2"])
            tt_(YT_, YT_, mu, ALU.subtract, ["X3", "X1"], ["X3"])
            tt_(YT_, YT_, var, ALU.mult, ["X3", "X2"], ["X3"])
            P.op("dve", lambda e: e.tensor_scalar(out=YT_, in0=YT_, scalar1=vcol("a_lnx_g", p), scalar2=vcol("a_lnx_b", p), op0=ALU.mult, op1=ALU.add),
                 rd=["X3", "cols"], wr=["X3"])
            tt_(YT_, YT_, eN, ALU.add, ["X3", "eN"], ["X3"])
            tt_(zT[:, p, :], YT_, X4, ALU.mult, ["X3", "X4"], ["zT"])

        for p in range(KC):
            do_epi(p)
        P.barrier()
        nbw = 512 if D >= 512 else D
        self.gemm_tm(lambda k, a, b_: zT[:, k, a:b_], ["zT"], KC, 128, W["a_w_o"], D, nbw, self.resid_epilogue(x_ap, nbw))
        P.barrier()

    def final_norm(self, x_ap, g_ap, out_ap):
        c, P = self.c, self.P
        P.op("sp", lambda e: e.dma_start(out=self.grep[:], in_=g_ap.partition_broadcast(128)), wr=["grep"], dma="grep")
        for tb in range(c.NTB):
            s = tb % 2
            xt = self.xt[s]
            ss = self.small[:, s:s + 1]
            rs = self.small[:, 2 + s:3 + s]
            P.op("sp", lambda e, xt=xt, tb=tb: e.dma_start(out=xt[:], in_=x_ap[tb * 128:(tb + 1) * 128, :]),
                 wr=[f"xt{s}"], dma=f"xt{s}")
            P.op("act", lambda e, xt=xt, ss=ss, s=s: e.activation(out=self.hn[s][:], in_=xt[:], func=AF.Square, accum_out=ss),
                 rd=[f"xt{s}"], wr=[f"hn{s}", f"ss{s}"])
            P.op("act", lambda e, ss=ss, rs=rs: e.activation(out=rs, in_=ss, func=AF.Sqrt, scale=1.0 / c.D, bias=self.consts[:, 0:1]),
                 rd=[f"ss{s}", "consts"], wr=[f"rs{s}"])
            P.op("dve", lambda e, rs=rs: e.reciprocal(out=rs, in_=rs), rd=[f"rs{s}"], wr=[f"rsd{s}", f"rs{s}"])
            P.op("dve", lambda e, xt=xt, rs=rs: e.scalar_tensor_tensor(
                out=xt[:], in0=xt[:], scalar=rs, in1=self.grep[:], op0=ALU.mult, op1=ALU.mult),
                rd=[f"rsd{s}", f"rs{s}", "grep"], wr=[f"xt{s}"])
            P.op("sp", lambda e, xt=xt, tb=tb: e.dma_start(out=out_ap[tb * 128:(tb + 1) * 128, :], in_=xt[:]),
                 rd=[f"xt{s}"], dma=f"xt{s}")

    def finish(self):
        P, nc = self.P, self.nc
        P.barrier()
        with nc.Block() as block:
            @block.sync
            def _(e):
                for f in P.streams["sp"]:
                    f(e)

            @block.scalar
            def _(e):
                for f in P.streams["act"]:
                    f(e)

            @block.vector
            def _(e):
                for f in P.streams["dve"]:
                    f(e)

            @block.gpsimd
            def _(e):
                for f in P.streams["pool"]:
                    f(e)

            @block.tensor
            def _(e):
                for f in P.streams["pe"]:
                    f(e)
        self.es.close()
        return nc


def build_program(cfg, layers=("a", "f0", "b", "f1", "final")):
    B = Builder(cfg)
    c = cfg
    x_in = B.din("x", [c.T, c.D])
    f_norm_g = B.din("f_norm_g", [2, c.D])
    f_w_gu = B.din("f_w_gu", [2, c.D, 2 * c.DFF])
    f_w_d = B.din("f_w_d", [2, c.DFF, c.D])
    final_g = B.din("final_g", [1, c.D])
    W = {}
    for nm, shp in [("b_norm_g", [1, c.D]), ("b_w_in", [c.D, c.WIN]), ("b_b_f", [1, c.H]), ("b_qn_g", [1, 64]), ("b_kn_g", [1, 64]),
                    ("b_on_g", [1, c.D]), ("b_w_o", [c.D, c.D])]:
        if "b" in layers:
            W[nm] = B.din(nm, shp)
    for nm, shp in [("a_norm_g", [1, c.D]), ("a_mix", [6, c.D]), ("a_w_rkv", [3, c.D, c.D]), ("a_w0", [1, c.D]), ("a_w1", [c.D, c.LW]),
                    ("a_w2", [c.LW, c.D]), ("a_a0", [1, c.D]), ("a_a1", [c.D, c.LA]), ("a_a2", [c.LA, c.D]), ("a_g1", [c.D, c.LG]),
                    ("a_g2", [c.LG, c.D]), ("a_k_k", [1, c.D]), ("a_k_a", [1, c.D]), ("a_r_k", [1, c.D]), ("a_lnx_g", [1, c.D]),
                    ("a_lnx_b", [1, c.D]), ("a_w_o", [c.D, c.D])]:
        if "a" in layers:
            W[nm] = B.din(nm, shp)
    out = B.nc.dram_tensor("out", [c.T, c.D], F32, kind="ExternalOutput").ap()
    xres = B.dscr("xres", [c.T, c.D], F32)
    mid = B.dscr("mid", [c.DFF, c.T], BF16)
    B.setup_common()
    P = B.P
    P.op("sp", lambda e: e.dma_start(out=xres, in_=x_in), dma="xcopy")
    P.barrier()
    for L in layers:
        if L == "f0" or L == "f1":
            l = int(L[1])
            B.swiglu(xres, f_norm_g[l:l + 1, :], f_w_gu[l], f_w_d[l], mid)
        elif L == "a":
            B.rwkv(xres, W)
        elif L == "b":
            B.fox(xres, W)
        elif L == "final":
            B.final_norm(xres, final_g, out)
    return B.finish()


def make_in_map(cfg, inputs, core, layers=("a", "f0", "b", "f1", "final")):
    m = {"x": np.ascontiguousarray(inputs["x"][core]),
         "f_norm_g": np.ascontiguousarray(inputs["f_norm_g"]),
         "f_w_gu": np.ascontiguousarray(inputs["f_w_gu"]),
         "f_w_d": np.ascontiguousarray(inputs["f_w_d"]),
         "final_g": np.ascontiguousarray(inputs["final_g"]).reshape(1, -1)}
    if "a" in layers:
        for nm in ["a_norm_g", "a_w0", "a_a0", "a_k_k", "a_k_a", "a_r_k", "a_lnx_g", "a_lnx_b"]:
            m[nm] = np.ascontiguousarray(inputs[nm]).reshape(1, -1)
        for nm in ["a_mix", "a_w_rkv", "a_w1", "a_w2", "a_a1", "a_a2", "a_g1", "a_g2", "a_w_o"]:
            m[nm] = np.ascontiguousarray(inputs[nm][0])
    if "b" in layers:
        for nm in ["b_norm_g", "b_b_f", "b_qn_g", "b_kn_g", "b_on_g"]:
            m[nm] = np.ascontiguousarray(inputs[nm]).reshape(1, -1)
        m["b_w_in"] = np.ascontiguousarray(inputs["b_w_in"][0])
        m["b_w_o"] = np.ascontiguousarray(inputs["b_w_o"][0])
    return m


_CACHE = {}


def kernel(**inputs):
    cfg = Cfg()
    layers = ("a", "f0", "b", "f1", "final")
    if "nc" not in _CACHE:
        _CACHE["nc"] = build_program(cfg, layers)
    nc = _CACHE["nc"]
    inputs = {k: np.asarray(v) for k, v in inputs.items()}
    in_maps = [make_in_map(cfg, inputs, core, layers) for core in range(8)]
    res = run_bass_kernel_spmd(nc, in_maps, core_ids=list(range(8)))
    out = np.stack([np.asarray(r["out"]) for r in res.results], axis=0)
    return out.astype(np.float32)
```

```python
import contextlib
import numpy as np
import concourse.bass as bass
import concourse.mybir as mybir
from concourse.bass_utils import run_bass_kernel_spmd

F32 = mybir.dt.float32
BF16 = mybir.dt.bfloat16
AF = mybir.ActivationFunctionType
ALU = mybir.AluOpType
AX = mybir.AxisListType

RMS_EPS = 1e-6
DEBUG = False
GN_EPS = 64e-5


class Cfg:
    def __init__(self, T=2048, D=2048, DFF=5632, LW=96, LA=96, LG=256):
        self.T, self.D, self.DFF, self.LW, self.LA, self.LG = T, D, DFF, LW, LA, LG
        self.H = D // 64
        self.KC = D // 128
        self.FC = DFF // 128
        self.TS = min(512, T)
        self.NTT = T // self.TS
        self.NTB = T // 128
        self.NCH = T // 64
        self.WIN = 4 * D + 3 * self.H


ENGS = ["sp", "act", "dve", "pool", "pe"]


class Prog:
    def __init__(self, nc, es):
        self.nc, self.es = nc, es
        self.streams = {e: [] for e in ENGS}
        self.sems, self.semval = {}, {}
        self.seen = {e: {} for e in ENGS}
        self.last_w, self.readers = {}, {}
        self.n_ops = 0

    def _sem(self, name):
        if name not in self.sems:
            self.sems[name] = self.es.enter_context(self.nc.semaphore("s_" + name.replace(":", "_")))
            self.semval[name] = 0
        return self.sems[name]

    def op(self, eng, fn, rd=(), wr=(), dma=None):
        deps = {}

        def add(d):
            if d is not None:
                deps[d[0]] = max(deps.get(d[0], 0), d[1])

        for k in rd:
            add(self.last_w.get(k))
        for k in wr:
            add(self.last_w.get(k))
            for s, v in self.readers.get(k, {}).items():
                add((s, v))
        own = "eng:" + eng
        waits = []
        for s, v in deps.items():
            if s == own and eng == "pe":
                continue
            if self.seen[eng].get(s, 0) < v:
                self.seen[eng][s] = v
                waits.append((self._sem(s), v))
        sname = ("dma:" + dma) if dma else own
        inc = 16 if dma else 1
        sem = self._sem(sname)
        self.semval[sname] += inc
        nv = self.semval[sname]

        def emit(e, waits=waits, fn=fn, sem=sem, inc=inc):
            for (s, v) in waits:
                e.wait_ge(s, v)
            ins = fn(e)
            ins.then_inc(sem, inc)

        self.streams[eng].append(emit)
        for k in wr:
            self.last_w[k] = (sname, nv)
            self.readers[k] = {}
        for k in rd:
            r = self.readers.setdefault(k, {})
            r[sname] = max(r.get(sname, 0), nv)
        self.n_ops += 1

    def barrier(self):
        for eng in ENGS:
            waits = []
            for s, v in self.semval.items():
                if v > 0 and self.seen[eng].get(s, 0) < v and s != "eng:" + eng:
                    self.seen[eng][s] = v
                    waits.append((self.sems[s], v))

            def emit(e, waits=waits):
                for (s, v) in waits:
                    e.wait_ge(s, v)

            self.streams[eng].append(emit)
        self.last_w, self.readers = {}, {}


class Builder:
    def __init__(self, cfg):
        self.c = cfg
        self.nc = bass.Bass("TRN2", target_bir_lowering=False)
        self.es = contextlib.ExitStack()
        self.P = Prog(self.nc, self.es)
        self.dram = {}
        self._uid = 0
        self.ps_ctr = 0
        self.wb_ctr = 0
        self.stg_ctr = 0

    def din(self, name, shape):
        self.dram[name] = self.nc.dram_tensor(name, list(shape), F32, kind="ExternalInput").ap()
        return self.dram[name]

    def dscr(self, name, shape, dt):
        self.dram[name] = self.nc.dram_tensor(name, list(shape), dt, kind=("ExternalOutput" if DEBUG else "Internal")).ap()
        return self.dram[name]

    def sb(self, name, shape, dt):
        return self.es.enter_context(self.nc.sbuf_tensor(name, list(shape), dt))

    def psum(self, name, shape, dt):
        return self.es.enter_context(self.nc.psum_tensor(name, list(shape), dt))

    def uid(self, p):
        self._uid += 1
        return f"{p}{self._uid}"

    def setup_common(self):
        c = self.c
        P = self.P
        self.psA = self.psum("psA", [128, 6, 512], F32)
        self.psT = self.psum("psT", [128, 2, 1024], BF16)
        self.ident = self.sb("ident", [128, 128], BF16)
        self.identf = self.sb("identf", [128, 128], F32)
        self.consts = self.sb("consts", [128, 8], F32)
        self.small = self.sb("small", [128, 64], F32)
        self.junk = self.sb("junk", [128, 128], F32)
        self.junk2 = self.sb("junk2", [128, 64], F32)
        nc = self.nc

        def mk_ident(e):
            return e.affine_select(out=self.identf[:], in_=self.identf[:], pattern=[[-1, 128]],
                                   compare_op=ALU.not_equal, fill=1.0, base=0, channel_multiplier=1)

        P.op("pool", lambda e: e.memset(self.identf[:], 0.0), wr=["identf"])
        P.op("pool", mk_ident, rd=["identf"], wr=["identf"])
        P.op("dve", lambda e: e.tensor_copy(out=self.ident[:], in_=self.identf[:]), rd=["identf"], wr=["ident"])

        def mk_consts(e):
            e.memset(self.consts[:, 0:1], RMS_EPS)
            e.memset(self.consts[:, 1:2], 1.0)
            e.memset(self.consts[:, 2:3], GN_EPS)
            e.memset(self.consts[:, 4:5], -0.5)
            return e.memset(self.consts[:, 3:4], 0.0)

        P.op("pool", mk_consts, wr=["consts"])
        hsz = c.KC * (c.T + 2)
        self.TH = c.T // (2 if c.T >= 1024 else 1)
        r1 = max(hsz + c.KC * c.T, c.FC * self.TH, 17 * c.T, 16 * c.T + c.KC * c.T)
        self.R1 = self.sb("R1", [128, r1], BF16)
        self.actA = self.R1[:, 0:hsz].rearrange("p (c t) -> p c t", t=c.T + 2)
        self.actB = self.R1[:, hsz:hsz + c.KC * c.T]
        self.WQ = c.KC * 128
        self.NWQ = 12
        r2 = max(self.NWQ * self.WQ, 2 * c.FC * 256, 8 * c.D, 10 * c.T + 1408, 2 * (4 * c.T + c.NTB * 132) + 4 * c.TS + 384, 7 * c.T + 140, 9 * c.D + 8 * c.H)
        self.R2 = self.sb("R2", [128, r2], BF16)
        self.NWQ = r2 // self.WQ
        self.wq_ptr = 0
        self.xt = [self.R2[:, i * 2 * c.D:(i + 1) * 2 * c.D].bitcast(F32) for i in range(2)]
        self.grep = self.R2[:, 4 * c.D:6 * c.D].bitcast(F32)
        self.hn = [self.R2[:, (6 + i) * c.D:(7 + i) * c.D] for i in range(2)]
        self.NSTG = 3
        self.stg = [self.sb(f"stg{i}", [128, 512], F32) for i in range(self.NSTG)]
        P.op("pool", lambda e: e.memset(self.actA[:, :, 0:2], 0.0), wr=["actA"])

    def next_ps(self):
        b = self.ps_ctr % 6
        self.ps_ctr += 1
        return b

    def next_wb(self):
        b = self.wb_ctr % self.NWB
        self.wb_ctr += 1
        return b

    def next_stg(self):
        b = self.stg_ctr % self.NSTG
        self.stg_ctr += 1
        return b

    def norm_transpose(self, x_ap, g_ap):
        c, P = self.c, self.P
        P.op("sp", lambda e: e.dma_start(out=self.grep[:], in_=g_ap.partition_broadcast(128)),
             wr=["grep"], dma="grep")
        for tb in range(c.NTB):
            s = tb % 2
            xt, hn = self.xt[s], self.hn[s]
            ss = self.small[:, s:s + 1]
            rs = self.small[:, 2 + s:3 + s]
            P.op("sp", lambda e, xt=xt, tb=tb: e.dma_start(out=xt[:], in_=x_ap[tb * 128:(tb + 1) * 128, :]),
                 wr=[f"xt{s}"], dma=f"xt{s}")
            P.op("act", lambda e, xt=xt, hn=hn, ss=ss: e.activation(out=hn[:], in_=xt[:], func=AF.Square, accum_out=ss),
                 rd=[f"xt{s}"], wr=[f"hn{s}", f"ss{s}"])
            P.op("act", lambda e, ss=ss, rs=rs: e.activation(out=rs, in_=ss, func=AF.Sqrt, scale=1.0 / c.D,
                                                                 bias=self.consts[:, 0:1]),
                 rd=[f"ss{s}", "consts"], wr=[f"rs{s}"])
            P.op("dve", lambda e, rs=rs: e.reciprocal(out=rs, in_=rs), rd=[f"rs{s}"], wr=[f"rsd{s}", f"rs{s}"])
            P.op("dve", lambda e, xt=xt, hn=hn, rs=rs: e.scalar_tensor_tensor(
                out=hn[:], in0=xt[:], scalar=rs, in1=self.grep[:], op0=ALU.mult, op1=ALU.mult),
                rd=[f"xt{s}", f"rsd{s}", f"rs{s}", "grep"], wr=[f"hn{s}"])
            for c0 in range(0, c.KC, 8):
                nch = min(8, c.KC - c0)
                tbk = (tb * ((c.KC + 7) // 8) + c0 // 8) % 2

                def tr(e, hn=hn, c0=c0, nch=nch, tbk=tbk):
                    ins = None
                    for i in range(nch):
                        ins = e.transpose(out=self.psT[:, tbk, i * 128:(i + 1) * 128],
                                          in_=hn[:, (c0 + i) * 128:(c0 + i + 1) * 128], identity=self.ident[:])
                    return ins

                P.op("pe", tr, rd=[f"hn{s}", "ident"], wr=[f"psT{tbk}"])
                eng = "act" if (c0 // 8) % 2 == 0 else "dve"

                def ev(e, c0=c0, nch=nch, tbk=tbk, tb=tb, eng=eng):
                    src = self.psT[:, tbk, 0:nch * 128].rearrange("p (c t) -> p c t", t=128)
                    dst = self.actA[:, c0:c0 + nch, 2 + tb * 128:2 + (tb + 1) * 128]
                    if eng == "act":
                        return e.activation(out=dst, in_=src, func=AF.Copy)
                    return e.tensor_copy(out=dst, in_=src)

                P.op(eng, ev, rd=[f"psT{tbk}"], wr=["actA"])

    def hT(self, k, t0, t1):
        return self.actA[:, k, 2 + t0:2 + t1]

    def load_w(self, w_ap, kcn, kp, col0, ncols):
        nq = (kcn * ncols + self.WQ - 1) // self.WQ
        ext = getattr(self, "wext", None)
        next_ = (ext.shape[1] // self.WQ) if ext is not None else 0
        if self.wq_ptr < self.NWQ and self.wq_ptr + nq > self.NWQ:
            self.wq_ptr = self.NWQ if nq <= next_ else 0
        if self.wq_ptr >= self.NWQ and self.wq_ptr + nq > self.NWQ + next_:
            self.wq_ptr = 0
        q0 = self.wq_ptr
        self.wq_ptr += nq
        keys = [f"wq{q0 + i}" for i in range(nq)]
        if q0 < self.NWQ:
            view = self.R2[0:kp, q0 * self.WQ:q0 * self.WQ + kcn * ncols].rearrange("p (c n) -> p c n", n=ncols)
        else:
            qe = q0 - self.NWQ
            view = ext[0:kp, qe * self.WQ:qe * self.WQ + kcn * ncols].rearrange("p (c n) -> p c n", n=ncols)
        src = w_ap[:, col0:col0 + ncols].rearrange("(c p) n -> p c n", p=kp)
        self.P.op("pool", lambda e: e.dma_start(out=view, in_=src), wr=keys, dma=f"wq{q0}")
        return keys, view

    def gemm_fm(self, xfn, xkeys, kcn, kp, groups, epilogue, t0=0, t1=None):
        c, P = self.c, self.P
        t1 = c.T if t1 is None else t1
        for gi, grp in enumerate(groups):
            wts = [self.load_w(w_ap, kcn, kp, col0, ncols) + (ncols,) for (w_ap, col0, ncols) in grp]
            for tt in range((t1 - t0) // c.TS):
                lo, hi = t0 + tt * c.TS, t0 + (tt + 1) * c.TS
                banks = []
                for (s, view, ncols) in wts:
                    b = self.next_ps()
                    banks.append((b, ncols))

                    def mm(e, view=view, ncols=ncols, b=b, lo=lo, hi=hi):
                        ins = None
                        for k in range(kcn):
                            ins = e.matmul(self.psA[0:ncols, b, 0:hi - lo], lhsT=view[:, k, :], rhs=xfn(k, lo, hi),
                                           start=(k == 0), stop=(k == kcn - 1))
                        return ins

                    P.op("pe", mm, rd=s + xkeys, wr=[f"ps{b}"])
                epilogue(gi, tt, lo, hi, banks)

    def gemm_tm(self, xfn, xkeys, kcn, kp, w_ap, ncols_total, nbw, epilogue, t0=0, t1=None):
        c, P = self.c, self.P
        t1 = c.T if t1 is None else t1
        for nb in range(ncols_total // nbw):
            s, view = self.load_w(w_ap, kcn, kp, nb * nbw, nbw)
            for tb in range((t1 - t0) // 128):
                lo = t0 + tb * 128
                b = self.next_ps()

                def mm(e, view=view, b=b, lo=lo):
                    ins = None
                    for k in range(kcn):
                        ins = e.matmul(self.psA[:, b, 0:nbw], lhsT=xfn(k, lo, lo + 128), rhs=view[:, k, :],
                                       start=(k == 0), stop=(k == kcn - 1))
                    return ins

                P.op("pe", mm, rd=s + xkeys, wr=[f"ps{b}"])
                epilogue(nb, lo, b)

    def resid_epilogue(self, x_ap, nbw):
        P = self.P
        cnt = [0]

        def ep(nb, lo, b):
            s = self.next_stg()
            stg = self.stg[s]
            P.op("sp", lambda e: e.dma_start(out=stg[:, 0:nbw], in_=x_ap[lo:lo + 128, nb * nbw:(nb + 1) * nbw]),
                 wr=[f"stg{s}"], dma=f"stg{s}")
            eng = "dve"
            P.op(eng, lambda e: e.tensor_tensor(out=stg[:, 0:nbw], in0=stg[:, 0:nbw], in1=self.psA[:, b, 0:nbw], op=ALU.add),
                 rd=[f"ps{b}", f"stg{s}"], wr=[f"stg{s}"])
            P.op("sp", lambda e: e.dma_start(out=x_ap[lo:lo + 128, nb * nbw:(nb + 1) * nbw], in_=stg[:, 0:nbw]),
                 rd=[f"stg{s}"], dma=f"stg{s}")
            cnt[0] += 1

        return ep

    def swiglu(self, x_ap, g_ap, wgu_ap, wd_ap, mid_ap):
        c, P = self.c, self.P
        self.norm_transpose(x_ap, g_ap)
        P.barrier()
        groups = [[(wgu_ap, j * 128, 128), (wgu_ap, c.DFF + j * 128, 128)] for j in range(c.FC)]
        sg = [self.sb(self.uid("sg"), [128, c.TS], F32) for _ in range(2)]
        mo = [self.sb(self.uid("mo"), [128, c.TS], BF16) for _ in range(2)]
        it = [0]

        def ep(gi, tt, lo, hi, banks):
            s = it[0] % 2
            it[0] += 1
            (bg, _), (bu, _) = banks
            P.op("act", lambda e: e.activation(out=sg[s][:], in_=self.psA[:, bg, 0:c.TS], func=AF.Silu),
                 rd=[f"ps{bg}"], wr=[f"sg{s}"])
            P.op("dve", lambda e: e.tensor_tensor(out=mo[s][:], in0=sg[s][:], in1=self.psA[:, bu, 0:c.TS], op=ALU.mult),
                 rd=[f"sg{s}", f"ps{bu}"], wr=[f"mo{s}"])
            P.op("sp", lambda e: e.dma_start(out=mid_ap[gi * 128:(gi + 1) * 128, lo:hi], in_=mo[s][:]),
                 rd=[f"mo{s}"], dma=f"mo{s}")

        self.gemm_fm(self.hT, ["actA"], c.KC, 128, groups, ep)
        P.barrier()
        TH = self.TH
        nh = c.T // TH
        nbw = 256
        self.wq_ptr = 0
        free0 = c.FC * TH
        nfree = (self.R1.shape[1] - free0) // self.WQ
        self.wext = self.R1[:, free0:free0 + nfree * self.WQ] if nfree > 0 else None
        for h in range(nh):
            midv = self.R1[:, 0:c.FC * TH].rearrange("p (c t) -> p c t", t=TH)
            for cc in range(c.FC):
                P.op("sp", lambda e, cc=cc, h=h: e.dma_start(out=midv[:, cc, :], in_=mid_ap[cc * 128:(cc + 1) * 128, h * TH:(h + 1) * TH]),
                     wr=["actB"], dma=f"actB{cc % 4}")
            xfn = lambda k, a, b_, h=h: midv[:, k, a - h * TH:b_ - h * TH]
            self.gemm_tm(xfn, ["actB"], c.FC, 128, wd_ap, c.D, nbw, self.resid_epilogue(x_ap, nbw), t0=h * TH, t1=(h + 1) * TH)
            P.barrier()
        self.wext = None
        self.wq_ptr = 0

    def store_fm(self, dst_ap, row_of_group):
        P = self.P
        it = [0]

        def ep(gi, tt, lo, hi, banks):
            for bi, (b, ncols) in enumerate(banks):
                s = self.next_stg()
                stg = self.stg[s]
                eng = "act" if it[0] % 2 == 0 else "dve"
                it[0] += 1
                n = hi - lo
                if eng == "act":
                    P.op("act", lambda e, b=b, ncols=ncols, stg=stg, n=n: e.activation(out=stg[0:ncols, 0:n], in_=self.psA[0:ncols, b, 0:n], func=AF.Copy),
                         rd=[f"ps{b}"], wr=[f"stg{s}"])
                else:
                    P.op("dve", lambda e, b=b, ncols=ncols, stg=stg, n=n: e.tensor_copy(out=stg[0:ncols, 0:n], in_=self.psA[0:ncols, b, 0:n]),
                         rd=[f"ps{b}"], wr=[f"stg{s}"])
                r0 = row_of_group(gi, bi)
                P.op("sp", lambda e, r0=r0, ncols=ncols, stg=stg, n=n, lo=lo, hi=hi: e.dma_start(out=dst_ap[r0:r0 + ncols, lo:hi], in_=stg[0:ncols, 0:n]),
                     rd=[f"stg{s}"], dma=f"stg{s}")

        return ep

    def store_tm(self, dst_ap, nbw, col0=0):
        P = self.P
        it = [0]

        def ep(nb, lo, b):
            s = self.next_stg()
            stg = self.stg[s]
            eng = "act" if it[0] % 2 == 0 else "dve"
            it[0] += 1
            if eng == "act":
                P.op("act", lambda e: e.activation(out=stg[:, 0:nbw], in_=self.psA[:, b, 0:nbw], func=AF.Copy), rd=[f"ps{b}"], wr=[f"stg{s}"])
            else:
                P.op("dve", lambda e: e.tensor_copy(out=stg[:, 0:nbw], in_=self.psA[:, b, 0:nbw]), rd=[f"ps{b}"], wr=[f"stg{s}"])
            P.op("sp", lambda e: e.dma_start(out=dst_ap[lo:lo + 128, col0 + nb * nbw:col0 + (nb + 1) * nbw], in_=stg[:, 0:nbw]),
                 rd=[f"stg{s}"], dma=f"stg{s}")

        return ep

    def r2view(self, off, n, dt=BF16, parts=128, arena=None):
        w = n * (2 if dt == F32 else 1)
        arena = self.R2 if arena is None else arena
        v = arena[0:parts, off:off + w]
        if dt == F32:
            v = v.bitcast(F32)
        return v, off + w

    def transpose_to_fm(self, src_fn, dst3):
        c, P = self.c, self.P
        for tb in range(c.NTB):
            for c0 in range(0, c.KC, 8):
                nch = min(8, c.KC - c0)
                tbk = (tb * ((c.KC + 7) // 8) + c0 // 8) % 2

                def tr(e, tb=tb, c0=c0, nch=nch, tbk=tbk):
                    ins = None
                    src = src_fn(tb)
                    for i in range(nch):
                        ins = e.transpose(out=self.psT[:, tbk, i * 128:(i + 1) * 128],
                                          in_=src[:, (c0 + i) * 128:(c0 + i + 1) * 128], identity=self.ident[:])
                    return ins

                P.op("pe", tr, rd=["ztm", "ident"], wr=[f"psT{tbk}"])
                eng = "act" if (tb + c0 // 8) % 2 == 0 else "dve"

                def ev(e, c0=c0, nch=nch, tbk=tbk, tb=tb, eng=eng):
                    src = self.psT[:, tbk, 0:nch * 128].rearrange("p (c t) -> p c t", t=128)
                    dst = dst3[:, c0:c0 + nch, tb * 128:(tb + 1) * 128]
                    if eng == "act":
                        return e.activation(out=dst, in_=src, func=AF.Copy)
                    return e.tensor_copy(out=dst, in_=src)

                P.op(eng, ev, rd=[f"psT{tbk}"], wr=["zT"])

    def fox(self, x_ap, W):
        c, P = self.c, self.P
        D, T, H, KC, TS = c.D, c.T, c.H, c.KC, c.TS
        w_in = W["b_w_in"]
        qk = self.dscr("b_qk", [2 * D, T], F32)
        fa = self.dscr("b_fa", [3 * H, T], F32)
        vg = self.dscr("b_vg", [T, 2 * D], F32)
        fat = self.dscr("b_fat", [T, 3 * H], F32)
        qhat = self.dscr("b_qhat", [2 * D, T], BF16)
        qaug = self.dscr("b_qaug", [H, 4, T], BF16)
        kaug = self.dscr("b_kaug", [H, 4, T], BF16)
        akd = self.dscr("b_akd", [H, T], BF16)
        vpr = self.dscr("b_vpr", [T, D], BF16)
        self.norm_transpose(x_ap, W["b_norm_g"])
        P.barrier()
        groups = [[(w_in, j * 128, 128)] for j in range(2 * KC)]
        self.gemm_fm(self.hT, ["actA"], KC, 128, groups, self.store_fm(qk, lambda gi, bi: gi * 128))
        self.gemm_fm(self.hT, ["actA"], KC, 128, [[(w_in, 4 * D, 3 * H)]], self.store_fm(fa, lambda gi, bi: 0))
        self.gemm_tm(self.hT, ["actA"], KC, 128, w_in[:, 2 * D:4 * D], 2 * D, 512 if D >= 512 else 2 * D,
                     self.store_tm(vg, 512 if D >= 512 else 2 * D))
        self.gemm_tm(self.hT, ["actA"], KC, 128, w_in[:, 4 * D:4 * D + 3 * H], 3 * H, 3 * H, self.store_tm(fat, 3 * H))
        P.barrier()
        off = 0
        ft, off = self.r2view(off, T, F32, H, self.R1)
        f2, off = self.r2view(off, T, F32, H, self.R1)
        ct, off = self.r2view(off, T, F32, H, self.R1)
        qa, off = self.r2view(off, 4 * T, BF16, H, self.R1)
        ka, off = self.r2view(off, 4 * T, BF16, H, self.R1)
        akt, off = self.r2view(off, T, F32, H, self.R1)
        akb, off = self.r2view(off, T, BF16, H, self.R1)
        qa = qa.rearrange("p (r t) -> p r t", t=T)
        ka = ka.rearrange("p (r t) -> p r t", t=T)
        nb = self.small[0:H, 8:9]
        P.op("sp", lambda e: e.dma_start(out=ft, in_=fa[0:H, :]), wr=["ft"], dma="ft")
        P.op("sp", lambda e: e.dma_start(out=akt, in_=fa[H:2 * H, :]), wr=["akt"], dma="akt")
        P.op("sp", lambda e: e.dma_start(out=nb, in_=W["b_b_f"].rearrange("o h -> h o")), wr=["nb"], dma="nb")
        P.op("dve", lambda e: e.tensor_scalar(out=nb, in0=nb, scalar1=-1.0, scalar2=None, op0=ALU.mult), rd=["nb"], wr=["nb"])
        P.barrier()
        P.op("act", lambda e: e.activation(out=f2, in_=ft, func=AF.Exp, scale=-1.0, bias=nb), rd=["ft", "nb"], wr=["f2"])
        P.op("act", lambda e: e.activation(out=f2, in_=f2, func=AF.Ln, scale=1.0, bias=self.consts[0:H, 1:2]), rd=["consts"], wr=["f2"])
        P.op("dve", lambda e: e.tensor_scalar(out=f2, in0=f2, scalar1=-0.5, scalar2=None, op0=ALU.mult), rd=["f2"], wr=["f2"])
        P.op("dve", lambda e: e.tensor_tensor_scan(out=ct, data0=f2, data1=f2, initial=0.0, op0=ALU.add, op1=ALU.add), rd=["f2"], wr=["ct"])
        P.op("dve", lambda e: e.tensor_copy(out=qa[:, 0, :], in_=ct), rd=["ct"], wr=["qa"])
        P.op("dve", lambda e: e.tensor_tensor(out=qa[:, 1, :], in0=ct, in1=qa[:, 0, :], op=ALU.subtract), rd=["ct", "qa"], wr=["qa"])
        P.op("pool", lambda e: e.memset(qa[:, 2:4, :], 1.0), wr=["qa"])
        P.op("pool", lambda e: e.memset(ka[:, 0:2, :], 1.0), wr=["ka"])
        P.op("dve", lambda e: e.tensor_scalar(out=ka[:, 2:4, :], in0=qa[:, 0:2, :], scalar1=-1.0, scalar2=None, op0=ALU.mult), rd=["qa"], wr=["ka"])
        P.op("act", lambda e: e.activation(out=akb, in_=akt, func=AF.Sigmoid), rd=["akt"], wr=["akb"])
        P.op("sp", lambda e: e.dma_start(out=qaug, in_=qa), rd=["qa"], dma="qa")
        P.op("sp", lambda e: e.dma_start(out=kaug, in_=ka), rd=["ka"], dma="ka")
        P.op("sp", lambda e: e.dma_start(out=akd, in_=akb), rd=["akb"], dma="akb")
        P.barrier()
        off = 0
        kt, off = self.r2view(off, T + 2, F32)
        tmpf, off = self.r2view(off, T, F32)
        akx, off = self.r2view(off, T, BF16)
        sq, off = self.r2view(off, T, BF16)
        outb, off = self.r2view(off, T, BF16)
        bones, off = self.r2view(off, 128, BF16)
        gq = self.small[:, 10:11]
        gk = self.small[:, 11:12]

        P.op("dve", lambda e: e.memset(bones, 0.0), wr=["bones"])
        P.op("dve", lambda e: e.memset(bones[0:64, 0:64], 1.0), wr=["bones"])
        P.op("dve", lambda e: e.memset(bones[64:128, 64:128], 1.0), wr=["bones"])
        P.op("pool", lambda e: e.memset(kt[:, 0:2], 0.0), wr=["kt"])
        for hh in range(2):
            P.op("sp", lambda e, hh=hh: e.dma_start(out=gq[hh * 64:(hh + 1) * 64, :], in_=W["b_qn_g"].rearrange("o n -> n o")), wr=["gq"], dma="gq")
            P.op("sp", lambda e, hh=hh: e.dma_start(out=gk[hh * 64:(hh + 1) * 64, :], in_=W["b_kn_g"].rearrange("o n -> n o")), wr=["gk"], dma="gk")
        P.op("dve", lambda e: e.tensor_scalar(out=gq, in0=gq, scalar1=0.125, scalar2=None, op0=ALU.mult), rd=["gq"], wr=["gq"])
        P.barrier()
        for which in range(2):
            for p in range(KC):
                row0 = which * D + p * 128
                P.op("sp", lambda e, row0=row0: e.dma_start(out=kt[:, 2:T + 2], in_=qk[row0:row0 + 128, :]), wr=["kt"], dma="kt")
                if which == 1:
                    for hh in range(2):
                        P.op("sp", lambda e, hh=hh, p=p: e.dma_start(out=akx[hh * 64:(hh + 1) * 64, :], in_=akd[2 * p + hh:2 * p + hh + 1, :].partition_broadcast(64)),
                             wr=["akx"], dma="akx")
                    P.op("pool", lambda e: e.tensor_tensor(out=tmpf, in0=kt[:, 1:T + 1], in1=kt[:, 2:T + 2], op=ALU.subtract), rd=["kt"], wr=["tmpf"])
                    P.op("pool", lambda e: e.tensor_tensor(out=tmpf, in0=tmpf, in1=akx, op=ALU.mult), rd=["akx", "tmpf"], wr=["tmpf"])
                    P.op("dve", lambda e: e.tensor_tensor(out=kt[:, 2:T + 2], in0=kt[:, 2:T + 2], in1=tmpf, op=ALU.add), rd=["tmpf", "kt"], wr=["kt"])
                P.op("act", lambda e: e.activation(out=sq, in_=kt[:, 2:T + 2], func=AF.Square), rd=["kt"], wr=["sq"])
                for tt in range(c.NTT):
                    b = self.next_ps()
                    P.op("pe", lambda e, b=b, tt=tt: e.matmul(self.psA[:, b, 0:TS], lhsT=bones, rhs=sq[:, tt * TS:(tt + 1) * TS], start=True, stop=True),
                         rd=["sq", "bones"], wr=[f"ps{b}"])
                    P.op("act", lambda e, b=b, tt=tt: e.activation(out=tmpf[:, tt * TS:(tt + 1) * TS], in_=self.psA[:, b, 0:TS], func=AF.Sqrt,
                                                                 scale=1.0 / 64, bias=self.consts[:, 0:1]),
                         rd=[f"ps{b}", "consts", "tmpf"], wr=["tmpf"])
                P.op("dve", lambda e: e.reciprocal(out=tmpf, in_=tmpf), rd=["tmpf"], wr=["tmpf"])
                gcol = gq if which == 0 else gk
                P.op("dve", lambda e, gcol=gcol: e.scalar_tensor_tensor(out=outb, in0=kt[:, 2:T + 2], scalar=gcol, in1=tmpf, op0=ALU.mult, op1=ALU.mult),
                     rd=["kt", "tmpf", "gq", "gk"], wr=["outb"])
                P.op("sp", lambda e, row0=row0: e.dma_start(out=qhat[row0:row0 + 128, :], in_=outb), rd=["outb"], dma="outb")
        P.barrier()
        off = 0
        vt, off = self.r2view(off, D, F32)
        vp, off = self.r2view(off, D, F32)
        gt, off = self.r2view(off, D, F32)
        ong, off = self.r2view(off, D, F32)
        vo, off = self.r2view(off, D, BF16)
        al, off = self.r2view(off, 3 * H, F32)
        av, off = self.r2view(off, H, F32)
        G3 = self.actB.rearrange("p (b d) -> p b d", d=D)
        ztm = self.R1[:, 0:c.NTB * D].rearrange("p (b d) -> p b d", d=D)
        P.op("sp", lambda e: e.dma_start(out=ong, in_=W["b_on_g"].partition_broadcast(128)), wr=["ong"], dma="ong")
        for tb in range(c.NTB):
            r0 = tb * 128
            P.op("sp", lambda e, r0=r0: e.dma_start(out=vt, in_=vg[r0:r0 + 128, 0:D]), wr=["vt"], dma="vt")
            P.op("sp", lambda e, r0=r0: e.dma_start(out=gt, in_=vg[r0:r0 + 128, D:2 * D]), wr=["gt"], dma="gt")
            P.op("sp", lambda e, r0=r0: e.dma_start(out=al, in_=fat[r0:r0 + 128, :]), wr=["al"], dma="al")
            if tb == 0:
                P.op("pool", lambda e: e.memset(vp[0:1, :], 0.0), wr=["vp"])
                P.op("sp", lambda e: e.dma_start(out=vp[1:128, :], in_=vg[0:127, 0:D]), wr=["vp"], dma="vp")
            else:
                P.op("sp", lambda e, r0=r0: e.dma_start(out=vp, in_=vg[r0 - 1:r0 + 127, 0:D]), wr=["vp"], dma="vp")
            P.op("act", lambda e: e.activation(out=av, in_=al[:, 2 * H:3 * H], func=AF.Sigmoid), rd=["al"], wr=["av"])
            P.op("pool", lambda e: e.tensor_tensor(out=vp, in0=vp, in1=vt, op=ALU.subtract), rd=["vp", "vt"], wr=["vp"])
            P.op("dve", lambda e: e.tensor_tensor(out=vp.rearrange("p (h n) -> p h n", n=64), in0=vp.rearrange("p (h n) -> p h n", n=64),
                                                  in1=av.unsqueeze(2).broadcast_to([128, H, 64]), op=ALU.mult), rd=["vp", "av"], wr=["vp"])
            P.op("pool", lambda e: e.tensor_tensor(out=vo, in0=vp, in1=vt, op=ALU.add), rd=["vp", "vt"], wr=["vo"])
            P.op("sp", lambda e, r0=r0: e.dma_start(out=vpr[r0:r0 + 128, :], in_=vo), rd=["vo"], dma="vo")
            P.op("act", lambda e: e.activation(out=gt, in_=gt, func=AF.Sigmoid), rd=["gt"], wr=["gt"])
            P.op("pool", lambda e, tb=tb: e.tensor_tensor(out=G3[:, tb, :], in0=gt, in1=ong, op=ALU.mult), rd=["gt", "ong"], wr=["G"])
        P.barrier()
        off = 0
        QA, KA, VV = [], [], []
        for i in range(2):
            qs, ks = [], []
            for hh in range(2):
                v_, off = self.r2view(off, T, BF16)
                qs.append(v_)
                v_, off = self.r2view(off, T, BF16)
                ks.append(v_)
            QA.append(qs)
            KA.append(ks)
            v_, off = self.r2view(off, c.NTB * 2 * 66, BF16)
            VV.append(v_.rearrange("p (b h n) -> p b h n", h=2, n=66))
        PT = []
        for i in range(4):
            v_, off = self.r2view(off, TS, BF16)
            PT.append(v_)
        tri, off = self.r2view(off, 128, BF16)
        trif, off = self.r2view(off, 128, F32)

        P.op("pool", lambda e: e.memset(trif, 1.0), wr=["trif"])

        def mk_tri(e):
            return e.affine_select(out=trif, in_=trif, pattern=[[1, 128]], compare_op=ALU.is_ge, fill=0.0, base=0, channel_multiplier=-1)

        P.op("pool", mk_tri, rd=["trif"], wr=["trif"])
        P.op("dve", lambda e: e.tensor_copy(out=tri, in_=trif), rd=["trif"], wr=["tri"])
        for i in range(2):
            P.op("pool", lambda e, i=i: e.memset(VV[i], 1.0), wr=[f"VV{i}"])
        nq = T // TS
        nsub = TS // 128
        NPT = len(PT)

        def loads(p):
            i = p % 2
            for hh in range(2):
                h = 2 * p + hh
                P.op("sp", lambda e, i=i, hh=hh, h=h: e.dma_start(out=QA[i][hh][0:64, :], in_=qhat[h * 64:(h + 1) * 64, :]), wr=[f"QA{i}{hh}"], dma=f"QA{i}{hh}")
                P.op("sp", lambda e, i=i, hh=hh, h=h: e.dma_start(out=QA[i][hh][64:68, :], in_=qaug[h]), wr=[f"QA{i}{hh}"], dma=f"QA{i}{hh}")
                P.op("sp", lambda e, i=i, hh=hh, h=h: e.dma_start(out=KA[i][hh][0:64, :], in_=qhat[D + h * 64:D + (h + 1) * 64, :]), wr=[f"KA{i}{hh}"], dma=f"KA{i}{hh}")
                P.op("sp", lambda e, i=i, hh=hh, h=h: e.dma_start(out=KA[i][hh][64:68, :], in_=kaug[h]), wr=[f"KA{i}{hh}"], dma=f"KA{i}{hh}")
                P.op("sp", lambda e, i=i, hh=hh, h=h: e.dma_start(out=VV[i][:, :, hh, 0:64], in_=vpr[:, h * 64:(h + 1) * 64].rearrange("(b s) n -> s b n", s=128)),
                     wr=[f"VV{i}"], dma=f"VV{i}{hh}")

        def epilogue(h, I, ob):
            for u in range(nsub):
                tb = I * nsub + u
                oc = self.psA[:, ob, u * 66:u * 66 + 64]
                den = self.psA[:, ob, u * 66 + 64:u * 66 + 65]
                k0 = 16 + (tb % 4) * 4
                rc = self.small[:, k0:k0 + 1]
                ssq = self.small[:, k0 + 1:k0 + 2]
                rst = self.small[:, k0 + 2:k0 + 3]
                kk = f"ep{tb % 4}"
                osb = self.junk[:, (tb % 2) * 64:(tb % 2) * 64 + 64]
                ko = f"osb{tb % 2}"
                P.op("dve", lambda e, rc=rc, den=den: e.reciprocal(out=rc, in_=den), rd=[f"ps{ob}"], wr=[kk + "rc"])
                P.op("dve", lambda e, oc=oc, rc=rc, osb=osb: e.tensor_scalar(out=osb, in0=oc, scalar1=rc, scalar2=None, op0=ALU.mult),
                     rd=[f"ps{ob}", kk + "rc"], wr=[ko])
                P.op("dve", lambda e, osb=osb: e.tensor_tensor(out=self.junk2[:, 0:64], in0=osb, in1=osb, op=ALU.mult), rd=[ko], wr=["junk2"])
                P.op("dve", lambda e, ssq=ssq: e.tensor_scalar(out=self.junk2[:, 0:64], in0=self.junk2[:, 0:64], scalar1=1.0 / 64, scalar2=None,
                                                              op0=ALU.mult, op1=ALU.add, accum_out=ssq), rd=["junk2"], wr=[kk + "ss", "junk2"])
                P.op("pool", lambda e, ssq=ssq, rst=rst: e.tensor_scalar(out=rst, in0=ssq, scalar1=1.0, scalar2=RMS_EPS, op0=ALU.mult, op1=ALU.add),
                     rd=[kk + "ss"], wr=[kk + "rst"])
                P.op("pool", lambda e, rst=rst: e.tensor_tensor(out=rst, in0=rst, in1=self.consts[:, 4:5], op=ALU.pow), rd=[kk + "rst", "consts"], wr=[kk + "rst2"])
                P.op("dve", lambda e, osb=osb, rst=rst, tb=tb, h=h: e.scalar_tensor_tensor(
                    out=ztm[:, tb, h * 64:(h + 1) * 64], in0=osb, scalar=rst, in1=G3[:, tb, h * 64:(h + 1) * 64], op0=ALU.mult, op1=ALU.mult),
                    rd=[ko, kk + "rst2", "G"], wr=["ztm"])

        gctr = [0, 0]
        obank = [0]
        LOOK = 2
        loads(0)
        for p in range(KC):
            i = p % 2
            if p + 1 < KC:
                loads(p + 1)
            steps = []
            for hh in range(2):
                for I in range(nq):
                    ob = 4 + (obank[0] % 2)
                    obank[0] += 1
                    nJ = (I + 1) * nsub
                    for J in range(nJ):
                        steps.append((hh, I, J, nJ, ob))

            def emit_S(st, i=i):
                hh, I, J, nJ, ob = st
                q_hi = (I + 1) * TS
                t_lo = max(I * TS, J * 128)
                N = q_hi - t_lo
                sbk = gctr[0] % 4
                gctr[0] += 1
                pti = gctr[1] % NPT
                gctr[1] += 1
                pt = PT[pti]
                P.op("pe", lambda e: e.matmul(self.psA[:, sbk, 0:N], lhsT=KA[i][hh][0:68, J * 128:(J + 1) * 128], rhs=QA[i][hh][0:68, t_lo:q_hi],
                                              start=True, stop=True), rd=[f"QA{i}{hh}", f"KA{i}{hh}"], wr=[f"ps{sbk}"])
                P.op("act", lambda e: e.activation(out=pt[:, 0:N], in_=self.psA[:, sbk, 0:N], func=AF.Exp), rd=[f"ps{sbk}"], wr=[f"PT{pti}"])
                if J * 128 >= I * TS:
                    P.op("dve", lambda e: e.tensor_tensor(out=pt[:, 0:128], in0=pt[:, 0:128], in1=tri, op=ALU.mult), rd=[f"PT{pti}", "tri"], wr=[f"PT{pti}"])
                return (pt, pti, t_lo, q_hi)

            def emit_PV(st, info, i=i, p=p):
                hh, I, J, nJ, ob = st
                pt, pti, t_lo, q_hi = info

                def pv(e):
                    ins = None
                    for u0 in range(t_lo, q_hi, 128):
                        u = (u0 - I * TS) // 128
                        ins = e.matmul(self.psA[:, ob, u * 66:u * 66 + 65], lhsT=pt[:, u0 - t_lo:u0 - t_lo + 128], rhs=VV[i][:, J, hh, 0:65],
                                       start=(J == 0 and u == 0), stop=(J == nJ - 1 and u == nsub - 1))
                    return ins

                P.op("pe", pv, rd=[f"PT{pti}", f"VV{i}"], wr=[f"ps{ob}"])
                if J == nJ - 1:
                    epilogue(2 * p + hh, I, ob)

            infos = {}
            for idx in range(len(steps) + LOOK):
                if idx < len(steps):
                    infos[idx] = emit_S(steps[idx])
                if idx - LOOK >= 0:
                    emit_PV(steps[idx - LOOK], infos.pop(idx - LOOK))
        P.barrier()
        if DEBUG:
            dz = self.dscr("dbg_ztm", [128, c.NTB, D], BF16)
            dg = self.dscr("dbg_G", [128, c.NTB, D], BF16)
            P.op("sp", lambda e: e.dma_start(out=dz, in_=ztm), dma="dbg")
            P.op("sp", lambda e: e.dma_start(out=dg, in_=G3), dma="dbg")
            P.barrier()
        zT = self.actB.rearrange("p (c t) -> p c t", t=T)
        self.transpose_to_fm(lambda tb: ztm[:, tb, :], zT)
        P.barrier()
        nbw = 512 if D >= 512 else D
        self.gemm_tm(lambda k, a, b_: zT[:, k, a:b_], ["zT"], KC, 128, W["b_w_o"], D, nbw, self.resid_epilogue(x_ap, nbw))
        P.barrier()

    def rwkv(self, x_ap, W):
        c, P = self.c, self.P
        D, T, H, KC, TS, NCH = c.D, c.T, c.H, c.KC, c.TS, c.NCH
        rr = self.dscr("a_rr", [D, T], F32)
        kkd = self.dscr("a_kk", [D, T], F32)
        vvd = self.dscr("a_vv", [D, T], F32)
        vtm = self.dscr("a_vtm", [T, D], F32)
        lwd = self.dscr("a_lw", [D, T], F32)
        aad = self.dscr("a_aa", [D, T], F32)
        ggd = self.dscr("a_gg", [D, T], F32)
        dd = {nm_: self.dscr("a_d" + nm_, [D, T], BF16) for nm_ in ["rt", "at", "bt", "kt", "bh", "kh"]}
        d_bo = self.dscr("a_dbo", [D, T], F32)
        d_wc = self.dscr("a_dwc", [D, NCH], F32)
        d_y = self.dscr("a_dy", [D, T], F32)
        rowsA = self.sb("rowsA", [128, 128], F32)
        rowsB = self.sb("rowsB", [128, 128], F32)
        cols = self.sb("cols", [128, 256], F32)
        lora = self.sb("lora", [128, 2, T], BF16)
        wcs = self.sb("wcs", [128, NCH], F32)
        P.op("dve", lambda e: e.memset(rowsA[:], 0.0), wr=["rowsA"])
        P.op("dve", lambda e: e.memset(rowsB[:], 0.0), wr=["rowsB"])
        P.op("sp", lambda e: e.dma_start(out=rowsA[0:6 * KC, :], in_=W["a_mix"].rearrange("s (c p) -> (s c) p", p=128)), wr=["rowsA"], dma="rowsA")
        vecs = ["a_w0", "a_a0", "a_k_k", "a_k_a", "a_r_k", "a_lnx_g", "a_lnx_b"]
        for i, nm in enumerate(vecs):
            P.op("sp", lambda e, i=i, nm=nm: e.dma_start(out=rowsB[i * KC:(i + 1) * KC, :], in_=W[nm].rearrange("o (c p) -> (o c) p", p=128)),
                 wr=["rowsB"], dma="rowsB")
        b0 = self.next_ps()
        P.op("pe", lambda e: e.transpose(out=self.psA[:, b0, 0:128], in_=rowsA[:], identity=self.identf[:]), rd=["rowsA", "identf"], wr=[f"ps{b0}"])
        P.op("dve", lambda e: e.tensor_copy(out=cols[:, 0:128], in_=self.psA[:, b0, 0:128]), rd=[f"ps{b0}"], wr=["cols"])
        b1 = self.next_ps()
        P.op("pe", lambda e: e.transpose(out=self.psA[:, b1, 0:128], in_=rowsB[:], identity=self.identf[:]), rd=["rowsB", "identf"], wr=[f"ps{b1}"])
        P.op("dve", lambda e: e.tensor_copy(out=cols[:, 128:256], in_=self.psA[:, b1, 0:128]), rd=[f"ps{b1}"], wr=["cols"])
        mixc = lambda s_, k: cols[:, s_ * KC + k:s_ * KC + k + 1]
        vcol = lambda nm, k: cols[:, 128 + vecs.index(nm) * KC + k:128 + vecs.index(nm) * KC + k + 1]
        omk0 = 128 + 7 * KC
        P.op("dve", lambda e: e.tensor_scalar(out=cols[:, omk0:omk0 + KC], in0=cols[:, 128 + 3 * KC:128 + 4 * KC], scalar1=-1.0, scalar2=1.0,
                                              op0=ALU.mult, op1=ALU.add), rd=["cols"], wr=["cols"])
        self.norm_transpose(x_ap, W["a_norm_g"])
        P.barrier()
        xm = self.actB.rearrange("p (c t) -> p c t", t=T)
        xmf = lambda k, a, b_: xm[:, k, a:b_]

        def mix(s_):
            for k in range(KC):
                P.op("dve", lambda e, k=k: e.tensor_tensor(out=xm[:, k, :], in0=self.actA[:, k, 1:T + 1], in1=self.actA[:, k, 2:T + 2], op=ALU.subtract),
                     rd=["actA"], wr=["xm"])
                P.op("dve", lambda e, k=k: e.scalar_tensor_tensor(out=xm[:, k, :], in0=xm[:, k, :], scalar=mixc(s_, k), in1=self.actA[:, k, 2:T + 2],
                                                                  op0=ALU.mult, op1=ALU.add), rd=["actA", "xm", "cols"], wr=["xm"])

        fullg = lambda w_ap, n: [[(w_ap, j * 128, min(128, n - j * 128))] for j in range((n + 127) // 128)]
        for s_, dst in [(0, rr), (1, kkd), (2, vvd)]:
            mix(s_)
            self.gemm_fm(xmf, ["xm"], KC, 128, fullg(W["a_w_rkv"][s_], D), self.store_fm(dst, lambda gi, bi: gi * 128))
            if s_ == 2:
                nbw = 512 if D >= 512 else D
                self.gemm_tm(xmf, ["xm"], KC, 128, W["a_w_rkv"][2], D, nbw, self.store_tm(vtm, nbw))
            P.barrier()

        def lora_ep(func):
            def ep(gi, tt, lo, hi, banks):
                (b, ncols), = banks
                P.op("act", lambda e: e.activation(out=lora[0:ncols, gi, lo:hi], in_=self.psA[0:ncols, b, 0:hi - lo], func=func),
                     rd=[f"ps{b}"], wr=["lora"])
            return ep

        def out_ep(dst, func, bias_nm, post_scale):
            def ep(gi, tt, lo, hi, banks):
                (b, ncols), = banks
                s = self.next_stg()
                stg = self.stg[s]
                n = hi - lo
                if func is None:
                    P.op("act", lambda e: e.activation(out=stg[:, 0:n], in_=self.psA[:, b, 0:n], func=AF.Copy), rd=[f"ps{b}"], wr=[f"stg{s}"])
                else:
                    P.op("act", lambda e: e.activation(out=stg[:, 0:n], in_=self.psA[:, b, 0:n], func=func, bias=vcol(bias_nm, gi), scale=1.0),
                         rd=[f"ps{b}", "cols"], wr=[f"stg{s}"])
                if post_scale is not None:
                    P.op("dve", lambda e: e.tensor_scalar(out=stg[:, 0:n], in0=stg[:, 0:n], scalar1=post_scale, scalar2=None, op0=ALU.mult),
                         rd=[f"stg{s}"], wr=[f"stg{s}"])
                P.op("sp", lambda e: e.dma_start(out=dst[gi * 128:(gi + 1) * 128, lo:hi], in_=stg[:, 0:n]), rd=[f"stg{s}"], dma=f"stg{s}")
            return ep

        for s_, w1n, w2n, L, f1, dst, f2, bnm, psc in [
                (3, "a_w1", "a_w2", c.LW, AF.Tanh, lwd, AF.Sigmoid, "a_w0", -float(np.exp(-0.5))),
                (4, "a_a1", "a_a2", c.LA, AF.Copy, aad, AF.Sigmoid, "a_a0", None),
                (5, "a_g1", "a_g2", c.LG, AF.Sigmoid, ggd, None, None, None)]:
            mix(s_)
            self.gemm_fm(xmf, ["xm"], KC, 128, fullg(W[w1n], L), lora_ep(f1))
            kcn2 = (L + 127) // 128
            kp2 = L if L < 128 else 128
            self.gemm_fm(lambda k, a, b_, kp2=kp2: lora[0:kp2, k, a:b_], ["lora"], kcn2, kp2, fullg(W[w2n], D), out_ep(dst, f2, bnm, psc))
            P.barrier()

        off = 0
        maskA, off = self.r2view(off, 384, F32)
        maskB, off = self.r2view(off, 256, F32)
        bones, off = self.r2view(off, 128, BF16)
        fA = []
        for i in range(5):
            v_, off = self.r2view(off, T, F32)
            fA.append(v_)
        assert off <= self.R2.shape[1], (off, self.R2.shape)
        X1, X2, X3, X4, X5 = fA
        o1 = 0
        eN, o1 = self.r2view(o1, T, F32, 128, self.R1)
        cmask, o1 = self.r2view(o1, T, F32, 128, self.R1)
        BO, o1 = self.r2view(o1, T, F32, 128, self.R1)
        YT, o1 = self.r2view(o1, T, F32, 128, self.R1)
        RT, o1 = self.r2view(o1, T, BF16, 128, self.R1)
        AT, o1 = self.r2view(o1, T, BF16, 128, self.R1)
        BT, o1 = self.r2view(o1, T, BF16, 128, self.R1)
        KT, o1 = self.r2view(o1, T, BF16, 128, self.R1)
        BH, o1 = self.r2view(o1, T, BF16, 128, self.R1)
        KH, o1 = self.r2view(o1, T, BF16, 128, self.R1)
        S1, o1 = self.r2view(o1, T, BF16, 128, self.R1)
        V2p, o1 = self.r2view(o1, T, BF16, 128, self.R1)
        zoff = max(o1, c.KC * (c.T + 2))
        V2p = V2p.rearrange("p (c i) -> p c i", i=64)
        zT = self.R1[:, zoff:zoff + KC * T].rearrange("p (c t) -> p c t", t=T)
        P.op("dve", lambda e: e.memset(bones, 0.0), wr=["bones"])
        P.op("dve", lambda e: e.memset(bones[0:64, 0:64], 1.0), wr=["bones"])
        P.op("dve", lambda e: e.memset(bones[64:128, 64:128], 1.0), wr=["bones"])
        P.op("dve", lambda e: e.memset(cmask, 1.0), wr=["cmask"])
        P.op("dve", lambda e: e.memset(cmask.rearrange("p (c t) -> p c t", t=64)[:, :, 0:1], 0.0), wr=["cmask"])
        P.op("pool", lambda e: e.memset(maskA, 1.0), wr=["maskA"])
        P.op("pool", lambda e: e.memset(maskB, 1.0), wr=["maskB"])
        for (mk, o_, cm, base, pat) in [(maskA, 0, -1, -1, 1), (maskA, 128, 1, -1, -1), (maskA, 256, -1, -1, 1), (maskB, 0, -1, 0, 1), (maskB, 128, -1, 0, 1)]:
            P.op("pool", lambda e, mk=mk, o_=o_, cm=cm, base=base, pat=pat: e.affine_select(
                out=mk[:, o_:o_ + 128], in_=mk[:, o_:o_ + 128], pattern=[[pat, 128]], compare_op=ALU.is_ge, fill=0.0, base=base, channel_multiplier=cm),
                rd=["maskA", "maskB"], wr=["maskA", "maskB"])
            P.op("pool", lambda e, mk=mk, o_=o_: e.memset(mk[0:64, o_ + 64:o_ + 128], 0.0), rd=["maskA", "maskB"], wr=["maskA", "maskB"])
            P.op("pool", lambda e, mk=mk, o_=o_: e.memset(mk[64:128, o_:o_ + 64], 0.0), rd=["maskA", "maskB"], wr=["maskA", "maskB"])
        P.op("dve", lambda e: e.memset(self.psA[:], 0.0), wr=[f"ps{i}" for i in range(6)])
        P.barrier()
        itc = [0]

        def do_pair(p):
            r0 = p * 128
            ld = lambda dst, src, key: P.op("sp", lambda e: e.dma_start(out=dst, in_=src[r0:r0 + 128, :]), wr=[key], dma=key)
            tt_ = lambda out, a, b_, op, rd, wr, eng="dve": P.op(eng, lambda e: e.tensor_tensor(out=out, in0=a, in1=b_, op=op), rd=rd, wr=wr)
            v3 = lambda a_: a_.rearrange("p (c t) -> p c t", t=64)
            ld(X1, lwd, "X1")
            P.op("dve", lambda e: e.tensor_tensor_scan(out=X2, data0=cmask, data1=X1, initial=0.0, op0=ALU.mult, op1=ALU.add), rd=["cmask", "X1"], wr=["X2"])
            tt_(X1, X2, X1, ALU.subtract, ["X2", "X1"], ["X1"], "pool")
            P.op("act", lambda e: e.activation(out=X3, in_=X2, func=AF.Exp), rd=["X2"], wr=["X3"])
            P.op("act", lambda e: e.activation(out=eN, in_=X2, func=AF.Exp, scale=-1.0), rd=["X2"], wr=["eN"])
            P.op("act", lambda e: e.activation(out=X1, in_=X1, func=AF.Exp), rd=["X1"], wr=["X1"])
            ld(X2, kkd, "X2")
            P.op("dve", lambda e: e.tensor_scalar(out=X4, in0=X2, scalar1=vcol("a_k_k", p), scalar2=None, op0=ALU.mult), rd=["X2", "cols"], wr=["X4"])
            P.op("act", lambda e: e.activation(out=S1, in_=X4, func=AF.Square), rd=["X4"], wr=["S1"])
            for t2 in range(c.NTT):
                b = self.next_ps()
                P.op("pe", lambda e, b=b, t2=t2: e.matmul(self.psA[:, b, 0:TS], lhsT=bones, rhs=S1[:, t2 * TS:(t2 + 1) * TS], start=True, stop=True),
                     rd=["S1", "bones"], wr=[f"ps{b}"])
                P.op("act", lambda e, b=b, t2=t2: e.activation(out=X5[:, t2 * TS:(t2 + 1) * TS], in_=self.psA[:, b, 0:TS], func=AF.Sqrt),
                     rd=[f"ps{b}", "X5"], wr=["X5"])
            P.op("dve", lambda e: e.tensor_scalar(out=X5, in0=X5, scalar1=1e-12, scalar2=None, op0=ALU.max), rd=["X5"], wr=["X5"])
            P.op("dve", lambda e: e.reciprocal(out=X5, in_=X5), rd=["X5"], wr=["X5"])
            tt_(X4, X4, X5, ALU.mult, ["X4", "X5"], ["X4"])
            P.op("dve", lambda e: e.scalar_tensor_tensor(out=AT, in0=X4, scalar=-1.0, in1=X1, op0=ALU.mult, op1=ALU.mult), rd=["X4", "X1"], wr=["AT"])
            ld(X1, aad, "X1")
            P.op("dve", lambda e: e.tensor_scalar(out=X5, in0=X1, scalar1=vcol("a_k_a", p), scalar2=cols[:, omk0 + p:omk0 + p + 1], op0=ALU.mult, op1=ALU.add),
                 rd=["X1", "cols"], wr=["X5"])
            tt_(X2, X2, X5, ALU.mult, ["X2", "X5"], ["X2"], "pool")
            tt_(X1, X4, X1, ALU.mult, ["X4", "X1"], ["X1"], "pool")
            WCb = X3.rearrange("p (c t) -> p c t", t=64)[:, :, 63:64].broadcast_to([128, NCH, 64])
            tt_(X5, X1, eN, ALU.mult, ["X1", "eN"], ["X5"])
            P.op("act", lambda e: e.activation(out=BT, in_=X5, func=AF.Copy), rd=["X5"], wr=["BT"])
            tt_(v3(BH), v3(X5), WCb, ALU.mult, ["X5", "X3"], ["BH"])
            tt_(X5, X2, eN, ALU.mult, ["X2", "eN", "BT", "BH"], ["X5"])
            P.op("act", lambda e: e.activation(out=KT, in_=X5, func=AF.Copy), rd=["X5"], wr=["KT"])
            tt_(v3(KH), v3(X5), WCb, ALU.mult, ["X5", "X3"], ["KH"])
            ld(X1, rr, "X1")
            tt_(RT, X1, X3, ALU.mult, ["X1", "X3"], ["RT"], "pool")
            P.op("dve", lambda e: e.scalar_tensor_tensor(out=S1, in0=X1, scalar=vcol("a_r_k", p), in1=X2, op0=ALU.mult, op1=ALU.mult),
                 rd=["X1", "X2", "cols", "S1"], wr=["S1"])
            ld(X4, vvd, "X4")
            for t2 in range(c.NTT):
                b = self.next_ps()
                P.op("pe", lambda e, b=b, t2=t2: e.matmul(self.psA[:, b, 0:TS], lhsT=bones, rhs=S1[:, t2 * TS:(t2 + 1) * TS], start=True, stop=True),
                     rd=["S1", "bones"], wr=[f"ps{b}"])
                P.op("dve", lambda e, b=b, t2=t2: e.tensor_tensor(out=BO[:, t2 * TS:(t2 + 1) * TS], in0=self.psA[:, b, 0:TS], in1=X4[:, t2 * TS:(t2 + 1) * TS], op=ALU.mult),
                     rd=[f"ps{b}", "X4", "BO"], wr=["BO"])
            eP = X3
            for nm_, til in [("rt", RT), ("at", AT), ("bt", BT), ("kt", KT), ("bh", BH), ("kh", KH)]:
                P.op("sp", lambda e, nm_=nm_, til=til: e.dma_start(out=dd[nm_][r0:r0 + 128, :], in_=til), rd=[nm_.upper()], dma="st_" + nm_)
            P.op("sp", lambda e: e.dma_start(out=d_bo[r0:r0 + 128, :], in_=BO), rd=["BO"], dma="st_bo")
            P.op("dve", lambda e: e.tensor_copy(out=wcs[:], in_=X3.rearrange("p (c t) -> p c t", t=64)[:, :, 63]), rd=["X3"], wr=["wcs"])
            P.op("sp", lambda e: e.dma_start(out=d_wc[r0:r0 + 128, :], in_=wcs[:]), rd=["wcs"], dma="st_wc")

        for p in range(KC):
            do_pair(p)
        P.barrier()

        NS = 1408
        o1 = 0
        D6 = []
        for bufi in range(2):
            lst = []
            for k6 in range(6):
                v_, o1 = self.r2view(o1, KC * 128, BF16, 128, self.R1)
                lst.append(v_.rearrange("p (q t) -> p q t", t=128))
            D6.append(lst)
        V2b = []
        for bufi in range(2):
            v_, o1 = self.r2view(o1, KC * 128, BF16, 128, self.R1)
            V2b.append(v_.rearrange("p (q c i) -> p q c i", c=2, i=64))
        Yst = []
        for bufi in range(2):
            v_, o1 = self.r2view(o1, KC * 128, F32, 128, self.R1)
            Yst.append(v_.rearrange("p (q t) -> p q t", t=128))
        WCt, o1 = self.r2view(o1, KC * NCH, F32, 128, self.R1)
        WCt = WCt.rearrange("p (q c) -> p q c", c=NCH)
        STt, o1 = self.r2view(o1, KC * 64, F32, 128, self.R1)
        STbt, o1 = self.r2view(o1, KC * 64, BF16, 128, self.R1)
        Ust, o1 = self.r2view(o1, KC * 64, BF16, 128, self.R1)
        SAt, o1 = self.r2view(o1, KC * 64, BF16, 128, self.R1)
        v4 = lambda a_: a_.rearrange("p (q i) -> p q i", i=64)
        STt, STbt, Ust, SAt = v4(STt), v4(STbt), v4(Ust), v4(SAt)
        assert o1 <= self.R1.shape[1], (o1, self.R1.shape)
        sets_off = 384 * 2 + 256 * 2 + 128
        assert sets_off + KC * NS <= self.R2.shape[1], (sets_off + KC * NS, self.R2.shape)

        def pset(p):
            o_ = sets_off + p * NS
            g = lambda a_, n: self.R2[:, o_ + a_:o_ + a_ + n]
            return dict(NN=g(0, 256), MAK=g(256, 128), MB=g(384, 256), PP=[g(640, 256), g(896, 256)], Q=g(1152, 128), BK=g(1280, 128))

        P.op("sp", lambda e: e.dma_start(out=WCt, in_=d_wc.rearrange("(q p) c -> p q c", p=128)), wr=["WCt"], dma="WCt")
        P.op("dve", lambda e: e.memset(STt, 0.0), wr=["STall"])
        P.op("dve", lambda e: e.memset(STbt, 0.0), wr=["STball"])
        P.barrier()
        hctr = [0]

        def next_half():
            b_ = self.next_ps()
            return b_, 0, f"ps{b_}"

        tctr = [0]

        def next_tslot():
            t_ = tctr[0] % 2
            tctr[0] += 1
            return t_, 0, f"psT{t_}"

        nblk = T // 128
        d6n = ["rt", "at", "bt", "kt", "bh", "kh"]

        def load_block(bi):
            bufi = bi % 2
            for k6 in range(6):
                P.op("sp", lambda e, k6=k6: e.dma_start(out=D6[bufi][k6], in_=dd[d6n[k6]][:, bi * 128:(bi + 1) * 128].rearrange("(q p) t -> p q t", p=128)),
                     wr=[f"D6_{bufi}"], dma=f"D6_{bufi}")
            for hh in range(2):
                for cl in range(2):
                    src = vtm[bi * 128 + cl * 64:bi * 128 + (cl + 1) * 64, :].rearrange("s (q h i) -> h s q i", h=2, i=64)[hh]
                    P.op("pool", lambda e, hh=hh, cl=cl, src=src: e.dma_start(out=V2b[bufi][hh * 64:(hh + 1) * 64, :, cl, :], in_=src),
                         wr=[f"V2_{bufi}"], dma=f"V2_{bufi}")

        def chunk_gen(p, ch):
            bi, cl = ch // 2, ch % 2
            bufi = bi % 2
            cs = slice(cl * 64, (cl + 1) * 64)
            RTc, ATc, BTc, KTc, BHc, KHc = [D6[bufi][k6][:, p, :] for k6 in range(6)]
            dk = [f"D6_{bufi}"] * 6
            V2c = V2b[bufi][:, p, cl, :]
            vk = f"V2_{bufi}"
            S_ = pset(p)
            NN, MAK, MBt, PPs, Q, BK = S_["NN"], S_["MAK"], S_["MB"], S_["PP"], S_["Q"], S_["BK"]
            kNN, kMAK, kMB, kQ, kBK = f"NN{p}", f"MAK{p}", f"MB{p}", f"Q{p}", f"BK{p}"
            STp, STbp, Up, SAp = STt[:, p, :], STbt[:, p, :], Ust[:, p, :], SAt[:, p, :]
            kST, kSTb, kU, kSA = f"ST{p}", f"STb{p}", f"U{p}", f"SA{p}"
            hs = [slice(0, 64), slice(64, 128)]
            b1, o1_, k1 = next_half()
            b2, o2_, k2 = b1, 256, k1
            b3, o3_, k3 = next_half()

            def mmats(e):
                ins = None
                for (bank, o_, X, Y) in [(b1, o1_, BTc, ATc), (b1, o1_ + 128, ATc, BTc), (b2, o2_, KTc, ATc), (b3, o3_, BTc, RTc), (b3, o3_ + 128, KTc, RTc)]:
                    for hh in range(2):
                        ps_ = hs[hh]
                        ins = e.matmul(self.psA[ps_, bank, o_ + hh * 64:o_ + (hh + 1) * 64], lhsT=X[ps_, cs], rhs=Y[ps_, cs],
                                       start=True, stop=True, tile_position=(hh * 64, hh * 64))
                return ins

            P.op("pe", mmats, rd=[dk[1], dk[2], dk[3], dk[0]], wr=[k1, k3])
            P.op("dve", lambda e: e.tensor_tensor(out=NN, in0=self.psA[:, b1, o1_:o1_ + 256], in1=maskA[:, 0:256], op=ALU.mult), rd=[k1, "maskA"], wr=[kNN])
            P.op("dve", lambda e: e.tensor_tensor(out=MAK, in0=self.psA[:, b2, o2_:o2_ + 128], in1=maskA[:, 256:384], op=ALU.mult), rd=[k2, "maskA"], wr=[kMAK])
            P.op("dve", lambda e: e.tensor_tensor(out=MBt, in0=self.psA[:, b3, o3_:o3_ + 256], in1=maskB, op=ALU.mult), rd=[k3, "maskB"], wr=[kMB])
            P.op("dve", lambda e: e.tensor_tensor(out=Q, in0=NN[:, 0:128], in1=self.ident[:], op=ALU.add), rd=[kNN, "ident"], wr=[kQ])
            yield
            cur, curk = NN, kNN
            for lvl in range(5):
                bL, oL, kL = next_half()
                pn = PPs[lvl % 2]
                kpn = f"PP{p}_{lvl % 2}"

                def sqm(e, cur=cur, bL=bL, oL=oL):
                    e.matmul(self.psA[:, bL, oL:oL + 128], lhsT=cur[:, 128:256], rhs=cur[:, 0:128], start=True, stop=True)
                    return e.matmul(self.psA[:, bL, oL + 128:oL + 256], lhsT=cur[:, 0:128], rhs=cur[:, 128:256], start=True, stop=True)

                P.op("pe", sqm, rd=[curk], wr=[kL])
                P.op("act", lambda e, pn=pn, bL=bL, oL=oL: e.activation(out=pn, in_=self.psA[:, bL, oL:oL + 256], func=AF.Copy), rd=[kL], wr=[kpn])
                yield
                bQ, oQ, kQb = next_half()
                P.op("pe", lambda e, pn=pn, bQ=bQ, oQ=oQ: e.matmul(self.psA[:, bQ, oQ:oQ + 128], lhsT=pn[:, 128:256], rhs=Q, start=True, stop=True),
                     rd=[kpn, kQ], wr=[kQb])
                P.op("dve", lambda e, bQ=bQ, oQ=oQ: e.tensor_tensor(out=Q, in0=Q, in1=self.psA[:, bQ, oQ:oQ + 128], op=ALU.add), rd=[kQb, kQ], wr=[kQ])
                yield
                cur, curk = pn, kpn
            tb_, to_, kt_ = next_tslot()

            def trs(e):
                ins = None
                for o_, X in [(0, BHc), (64, KHc)]:
                    for hh in range(2):
                        ps_ = hs[hh]
                        ins = e.transpose(out=self.psT[ps_, tb_, to_ + o_:to_ + o_ + 64], in_=X[ps_, cs], identity=self.ident[ps_, ps_],
                                          tile_position=(hh * 64, hh * 64))
                return ins

            P.op("pe", trs, rd=[dk[4], dk[5], "ident"], wr=[kt_])
            P.op("act", lambda e: e.activation(out=BK, in_=self.psT[:, tb_, to_:to_ + 128], func=AF.Copy), rd=[kt_], wr=[kBK])
            yield
            bU, oU, kUb = next_half()

            def mmU(e):
                for hh in range(2):
                    ps_ = hs[hh]
                    e.matmul(self.psA[ps_, bU, oU:oU + 64], lhsT=ATc[ps_, cs], rhs=STbp[ps_, :], start=True, stop=False, tile_position=(hh * 64, hh * 64))
                return e.matmul(self.psA[:, bU, oU:oU + 64], lhsT=MAK, rhs=V2c, start=False, stop=True)

            P.op("pe", mmU, rd=[dk[1], kSTb, "STball", kMAK, vk], wr=[kUb])
            P.op("act", lambda e: e.activation(out=Up, in_=self.psA[:, bU, oU:oU + 64], func=AF.Copy), rd=[kUb], wr=[kU])
            yield
            bS, oS, kSb = next_half()
            P.op("pe", lambda e: e.matmul(self.psA[:, bS, oS:oS + 64], lhsT=Q, rhs=Up, start=True, stop=True), rd=[kQ, kU], wr=[kSb])
            P.op("act", lambda e: e.activation(out=SAp, in_=self.psA[:, bS, oS:oS + 64], func=AF.Copy), rd=[kSb], wr=[kSA])
            yield
            bY, oY, kYb = next_half()

            def mmY(e):
                ins = None
                for hh in range(2):
                    ps_ = hs[hh]
                    e.matmul(self.psA[ps_, bY, oY:oY + 64], lhsT=STbp[ps_, :], rhs=RTc[ps_, cs], start=True, stop=False, tile_position=(hh * 64, hh * 64))
                for hh in range(2):
                    ps_ = hs[hh]
                    e.matmul(self.psA[ps_, bY, oY:oY + 64], lhsT=V2c[ps_, :], rhs=MBt[ps_, 128 + hh * 64:128 + (hh + 1) * 64], start=False, stop=False,
                             tile_position=(hh * 64, hh * 64))
                for hh in range(2):
                    ps_ = hs[hh]
                    ins = e.matmul(self.psA[ps_, bY, oY:oY + 64], lhsT=SAp[ps_, :], rhs=MBt[ps_, hh * 64:(hh + 1) * 64], start=False, stop=True,
                                   tile_position=(hh * 64, hh * 64))
                return ins

            P.op("pe", mmY, rd=[kSTb, "STball", dk[0], vk, kMB, kSA], wr=[kYb])
            P.op("act", lambda e: e.activation(out=Yst[bufi][:, p, cs], in_=self.psA[:, bY, oY:oY + 64], func=AF.Copy), rd=[kYb], wr=[f"Yst{bufi}"])
            yield
            bN, oN, kNb = next_half()

            def mmN(e):
                ins = None
                for hh in range(2):
                    ps_ = hs[hh]
                    e.matmul(self.psA[ps_, bN, oN:oN + 64], lhsT=BK[ps_, 0:64], rhs=SAp[ps_, :], start=True, stop=False, tile_position=(hh * 64, hh * 64))
                for hh in range(2):
                    ps_ = hs[hh]
                    ins = e.matmul(self.psA[ps_, bN, oN:oN + 64], lhsT=BK[ps_, 64:128], rhs=V2c[ps_, :], start=False, stop=True,
                                   tile_position=(hh * 64, hh * 64))
                return ins

            P.op("pe", mmN, rd=[kBK, kSA, vk], wr=[kNb])
            P.op("dve", lambda e: e.scalar_tensor_tensor(out=STp, in0=STp, scalar=WCt[:, p, ch:ch + 1], in1=self.psA[:, bN, oN:oN + 64],
                                                         op0=ALU.mult, op1=ALU.add), rd=[kST, "STall", "WCt", kNb], wr=[kST])
            P.op("act", lambda e: e.activation(out=STbp, in_=STp, func=AF.Copy), rd=[kST, "STball"], wr=[kSTb])
            yield

        load_block(0)
        for ch in range(NCH):
            bi = ch // 2
            if ch % 2 == 0 and bi + 1 < nblk:
                load_block(bi + 1)
            gens = [chunk_gen(p, ch) for p in range(KC)]
            while gens:
                alive = []
                for g in gens:
                    try:
                        next(g)
                        alive.append(g)
                    except StopIteration:
                        pass
                gens = alive
            if ch % 2 == 1:
                bufi = bi % 2
                P.op("sp", lambda e, bi=bi, bufi=bufi: e.dma_start(out=d_y[:, bi * 128:(bi + 1) * 128].rearrange("(q p) t -> p q t", p=128), in_=Yst[bufi]),
                     rd=[f"Yst{bufi}"], dma=f"Yst{bufi}")
        P.barrier()

        def do_epi(p):
            r0 = p * 128
            ld = lambda dst, src, key: P.op("sp", lambda e: e.dma_start(out=dst, in_=src[r0:r0 + 128, :]), wr=[key], dma=key)
            tt_ = lambda out, a, b_, op, rd, wr, eng="dve": P.op(eng, lambda e: e.tensor_tensor(out=out, in0=a, in1=b_, op=op), rd=rd, wr=wr)
            ld(X3, d_y, "X3")
            ld(eN, d_bo, "eN")
            ld(X4, ggd, "X4")
            YT_ = X3
            P.op("act", lambda e: e.activation(out=S1, in_=YT_, func=AF.Copy), rd=["X3"], wr=["S1"])
            P.op("act", lambda e: e.activation(out=RT, in_=YT_, func=AF.Square), rd=["X3"], wr=["RT"])
            mu, var, tG = X1, X2, X5
            for t2 in range(c.NTT):
                sl_ = slice(t2 * TS, (t2 + 1) * TS)
                b = self.next_ps()
                P.op("pe", lambda e, b=b, sl_=sl_: e.matmul(self.psA[:, b, 0:TS], lhsT=bones, rhs=S1[:, sl_], start=True, stop=True), rd=["S1", "bones"], wr=[f"ps{b}"])
                P.op("act", lambda e, b=b, sl_=sl_: e.activation(out=mu[:, sl_], in_=self.psA[:, b, 0:TS], func=AF.Copy, scale=1.0 / 64), rd=[f"ps{b}", "X1"], wr=["X1"])
                b2 = self.next_ps()
                P.op("pe", lambda e, b2=b2, sl_=sl_: e.matmul(self.psA[:, b2, 0:TS], lhsT=bones, rhs=RT[:, sl_], start=True, stop=True), rd=["RT", "bones"], wr=[f"ps{b2}"])
                P.op("dve", lambda e, sl_=sl_: e.tensor_tensor(out=tG[:, sl_], in0=mu[:, sl_], in1=mu[:, sl_], op=ALU.mult), rd=["X1", "X5"], wr=["X5"])
                P.op("dve", lambda e, b2=b2, sl_=sl_: e.scalar_tensor_tensor(out=var[:, sl_], in0=self.psA[:, b2, 0:TS], scalar=1.0 / 64, in1=tG[:, sl_],
                                                                             op0=ALU.mult, op1=ALU.subtract), rd=[f"ps{b2}", "X5", "X2"], wr=["X2"])
            P.op("act", lambda e: e.activation(out=var, in_=var, func=AF.Sqrt, bias=self.consts[:, 2:3], scale=1.0), rd=["X2", "consts"], wr=["X2"])
            P.op("dve", lambda e: e.reciprocal(out=var, in_=var), rd=["X2"], wr=["X2"])
            tt_(YT_, YT_, mu, ALU.subtract, ["X3", "X1"], ["X3"])
            tt_(YT_, YT_, var, ALU.mult, ["X3", "X2"], ["X3"])
            P.op("dve", lambda e: e.tensor_scalar(out=YT_, in0=YT_, scalar1=vcol("a_lnx_g", p), scalar2=vcol("a_lnx_b", p), op0=ALU.mult, op1=ALU.add),
                 rd=["X3", "cols"], wr=["X3"])
            tt_(YT_, YT_, eN, ALU.add, ["X3", "eN"], ["X3"], "pool")
            tt_(zT[:, p, :], YT_, X4, ALU.mult, ["X3", "X4"], ["zT"], "pool")

        for p in range(KC):
            do_epi(p)
        P.barrier()
        nbw = 512 if D >= 512 else D
        self.gemm_tm(lambda k, a, b_: zT[:, k, a:b_], ["zT"], KC, 128, W["a_w_o"], D, nbw, self.resid_epilogue(x_ap, nbw))
        P.barrier()

    def final_norm(self, x_ap, g_ap, out_ap):
        c, P = self.c, self.P
        P.op("sp", lambda e: e.dma_start(out=self.grep[:], in_=g_ap.partition_broadcast(128)), wr=["grep"], dma="grep")
        for tb in range(c.NTB):
            s = tb % 2
            xt = self.xt[s]
            ss = self.small[:, s:s + 1]
            rs = self.small[:, 2 + s:3 + s]
            P.op("sp", lambda e, xt=xt, tb=tb: e.dma_start(out=xt[:], in_=x_ap[tb * 128:(tb + 1) * 128, :]),
                 wr=[f"xt{s}"], dma=f"xt{s}")
            P.op("act", lambda e, xt=xt, ss=ss, s=s: e.activation(out=self.hn[s][:], in_=xt[:], func=AF.Square, accum_out=ss),
                 rd=[f"xt{s}"], wr=[f"hn{s}", f"ss{s}"])
            P.op("act", lambda e, ss=ss, rs=rs: e.activation(out=rs, in_=ss, func=AF.Sqrt, scale=1.0 / c.D, bias=self.consts[:, 0:1]),
                 rd=[f"ss{s}", "consts"], wr=[f"rs{s}"])
            P.op("dve", lambda e, rs=rs: e.reciprocal(out=rs, in_=rs), rd=[f"rs{s}"], wr=[f"rsd{s}", f"rs{s}"])
            P.op("dve", lambda e, xt=xt, rs=rs: e.scalar_tensor_tensor(
                out=xt[:], in0=xt[:], scalar=rs, in1=self.grep[:], op0=ALU.mult, op1=ALU.mult),
                rd=[f"rsd{s}", f"rs{s}", "grep"], wr=[f"xt{s}"])
            P.op("sp", lambda e, xt=xt, tb=tb: e.dma_start(out=out_ap[tb * 128:(tb + 1) * 128, :], in_=xt[:]),
                 rd=[f"xt{s}"], dma=f"xt{s}")

    def finish(self):
        P, nc = self.P, self.nc
        P.barrier()
        with nc.Block() as block:
            @block.sync
            def _(e):
                for f in P.streams["sp"]:
                    f(e)

            @block.scalar
            def _(e):
                for f in P.streams["act"]:
                    f(e)

            @block.vector
            def _(e):
                for f in P.streams["dve"]:
                    f(e)

            @block.gpsimd
            def _(e):
                for f in P.streams["pool"]:
                    f(e)

            @block.tensor
            def _(e):
                for f in P.streams["pe"]:
                    f(e)
        self.es.close()
        return nc


def build_program(cfg, layers=("a", "f0", "b", "f1", "final")):
    B = Builder(cfg)
    c = cfg
    x_in = B.din("x", [c.T, c.D])
    f_norm_g = B.din("f_norm_g", [2, c.D])
    f_w_gu = B.din("f_w_gu", [2, c.D, 2 * c.DFF])
    f_w_d = B.din("f_w_d", [2, c.DFF, c.D])
    final_g = B.din("final_g", [1, c.D])
    W = {}
    for nm, shp in [("b_norm_g", [1, c.D]), ("b_w_in", [c.D, c.WIN]), ("b_b_f", [1, c.H]), ("b_qn_g", [1, 64]), ("b_kn_g", [1, 64]),
                    ("b_on_g", [1, c.D]), ("b_w_o", [c.D, c.D])]:
        if "b" in layers:
            W[nm] = B.din(nm, shp)
    for nm, shp in [("a_norm_g", [1, c.D]), ("a_mix", [6, c.D]), ("a_w_rkv", [3, c.D, c.D]), ("a_w0", [1, c.D]), ("a_w1", [c.D, c.LW]),
                    ("a_w2", [c.LW, c.D]), ("a_a0", [1, c.D]), ("a_a1", [c.D, c.LA]), ("a_a2", [c.LA, c.D]), ("a_g1", [c.D, c.LG]),
                    ("a_g2", [c.LG, c.D]), ("a_k_k", [1, c.D]), ("a_k_a", [1, c.D]), ("a_r_k", [1, c.D]), ("a_lnx_g", [1, c.D]),
                    ("a_lnx_b", [1, c.D]), ("a_w_o", [c.D, c.D])]:
        if "a" in layers:
            W[nm] = B.din(nm, shp)
    out = B.nc.dram_tensor("out", [c.T, c.D], F32, kind="ExternalOutput").ap()
    xres = B.dscr("xres", [c.T, c.D], F32)
    mid = B.dscr("mid", [c.DFF, c.T], BF16)
    B.setup_common()
    P = B.P
    P.op("sp", lambda e: e.dma_start(out=xres, in_=x_in), dma="xcopy")
    P.barrier()
    for L in layers:
        if L == "f0" or L == "f1":
            l = int(L[1])
            B.swiglu(xres, f_norm_g[l:l + 1, :], f_w_gu[l], f_w_d[l], mid)
        elif L == "a":
            B.rwkv(xres, W)
        elif L == "b":
            B.fox(xres, W)
        elif L == "final":
            B.final_norm(xres, final_g, out)
    return B.finish()


def make_in_map(cfg, inputs, core, layers=("a", "f0", "b", "f1", "final")):
    m = {"x": np.ascontiguousarray(inputs["x"][core]),
         "f_norm_g": np.ascontiguousarray(inputs["f_norm_g"]),
         "f_w_gu": np.ascontiguousarray(inputs["f_w_gu"]),
         "f_w_d": np.ascontiguousarray(inputs["f_w_d"]),
         "final_g": np.ascontiguousarray(inputs["final_g"]).reshape(1, -1)}
    if "a" in layers:
        for nm in ["a_norm_g", "a_w0", "a_a0", "a_k_k", "a_k_a", "a_r_k", "a_lnx_g", "a_lnx_b"]:
            m[nm] = np.ascontiguousarray(inputs[nm]).reshape(1, -1)
        for nm in ["a_mix", "a_w_rkv", "a_w1", "a_w2", "a_a1", "a_a2", "a_g1", "a_g2", "a_w_o"]:
            m[nm] = np.ascontiguousarray(inputs[nm][0])
    if "b" in layers:
        for nm in ["b_norm_g", "b_b_f", "b_qn_g", "b_kn_g", "b_on_g"]:
            m[nm] = np.ascontiguousarray(inputs[nm]).reshape(1, -1)
        m["b_w_in"] = np.ascontiguousarray(inputs["b_w_in"][0])
        m["b_w_o"] = np.ascontiguousarray(inputs["b_w_o"][0])
    return m


_CACHE = {}


def kernel(**inputs):
    cfg = Cfg()
    layers = ("a", "f0", "b", "f1", "final")
    if "nc" not in _CACHE:
        _CACHE["nc"] = build_program(cfg, layers)
    nc = _CACHE["nc"]
    inputs = {k: np.asarray(v) for k, v in inputs.items()}
    in_maps = [make_in_map(cfg, inputs, core, layers) for core in range(8)]
    res = run_bass_kernel_spmd(nc, in_maps, core_ids=list(range(8)))
    out = np.stack([np.asarray(r["out"]) for r in res.results], axis=0)
    return out.astype(np.float32)
```

```python
import contextlib
import numpy as np
import concourse.bass as bass
import concourse.mybir as mybir
from concourse.bass_utils import run_bass_kernel_spmd

F32 = mybir.dt.float32
BF16 = mybir.dt.bfloat16
AF = mybir.ActivationFunctionType
ALU = mybir.AluOpType
AX = mybir.AxisListType

RMS_EPS = 1e-6
DEBUG = False
GN_EPS = 64e-5


class Cfg:
    def __init__(self, T=2048, D=2048, DFF=5632, LW=96, LA=96, LG=256):
        self.T, self.D, self.DFF, self.LW, self.LA, self.LG = T, D, DFF, LW, LA, LG
        self.H = D // 64
        self.KC = D // 128
        self.FC = DFF // 128
        self.TS = min(512, T)
        self.NTT = T // self.TS
        self.NTB = T // 128
        self.NCH = T // 64
        self.WIN = 4 * D + 3 * self.H


ENGS = ["sp", "act", "dve", "pool", "pe"]


class Prog:
    def __init__(self, nc, es):
        self.nc, self.es = nc, es
        self.streams = {e: [] for e in ENGS}
        self.sems, self.semval = {}, {}
        self.seen = {e: {} for e in ENGS}
        self.last_w, self.readers = {}, {}
        self.n_ops = 0

    def _sem(self, name):
        if name not in self.sems:
            self.sems[name] = self.es.enter_context(self.nc.semaphore("s_" + name.replace(":", "_")))
            self.semval[name] = 0
        return self.sems[name]

    def op(self, eng, fn, rd=(), wr=(), dma=None):
        deps = {}

        def add(d):
            if d is not None:
                deps[d[0]] = max(deps.get(d[0], 0), d[1])

        for k in rd:
            add(self.last_w.get(k))
        for k in wr:
            add(self.last_w.get(k))
            for s, v in self.readers.get(k, {}).items():
                add((s, v))
        own = "eng:" + eng
        waits = []
        for s, v in deps.items():
            if s == own and eng == "pe":
                continue
            if self.seen[eng].get(s, 0) < v:
                self.seen[eng][s] = v
                waits.append((self._sem(s), v))
        sname = ("dma:" + dma) if dma else own
        inc = 16 if dma else 1
        sem = self._sem(sname)
        self.semval[sname] += inc
        nv = self.semval[sname]

        def emit(e, waits=waits, fn=fn, sem=sem, inc=inc):
            for (s, v) in waits:
                e.wait_ge(s, v)
            ins = fn(e)
            ins.then_inc(sem, inc)

        self.streams[eng].append(emit)
        for k in wr:
            self.last_w[k] = (sname, nv)
            self.readers[k] = {}
        for k in rd:
            r = self.readers.setdefault(k, {})
            r[sname] = max(r.get(sname, 0), nv)
        self.n_ops += 1

    def barrier(self):
        for eng in ENGS:
            waits = []
            for s, v in self.semval.items():
                if v > 0 and self.seen[eng].get(s, 0) < v and s != "eng:" + eng:
                    self.seen[eng][s] = v
                    waits.append((self.sems[s], v))

            def emit(e, waits=waits):
                for (s, v) in waits:
                    e.wait_ge(s, v)

            self.streams[eng].append(emit)
        self.last_w, self.readers = {}, {}


class Builder:
    def __init__(self, cfg):
        self.c = cfg
        self.nc = bass.Bass("TRN2", target_bir_lowering=False)
        self.es = contextlib.ExitStack()
        self.P = Prog(self.nc, self.es)
        self.dram = {}
        self._uid = 0
        self.ps_ctr = 0
        self.wb_ctr = 0
        self.stg_ctr = 0

    def din(self, name, shape):
        self.dram[name] = self.nc.dram_tensor(name, list(shape), F32, kind="ExternalInput").ap()
        return self.dram[name]

    def dscr(self, name, shape, dt):
        self.dram[name] = self.nc.dram_tensor(name, list(shape), dt, kind=("ExternalOutput" if DEBUG else "Internal")).ap()
        return self.dram[name]

    def sb(self, name, shape, dt):
        return self.es.enter_context(self.nc.sbuf_tensor(name, list(shape), dt))

    def psum(self, name, shape, dt):
        return self.es.enter_context(self.nc.psum_tensor(name, list(shape), dt))

    def uid(self, p):
        self._uid += 1
        return f"{p}{self._uid}"

    def setup_common(self):
        c = self.c
        P = self.P
        self.psA = self.psum("psA", [128, 6, 512], F32)
        self.psT = self.psum("psT", [128, 2, 1024], BF16)
        self.ident = self.sb("ident", [128, 128], BF16)
        self.identf = self.sb("identf", [128, 128], F32)
        self.consts = self.sb("consts", [128, 8], F32)
        self.small = self.sb("small", [128, 64], F32)
        self.junk = self.sb("junk", [128, 128], F32)
        self.junk2 = self.sb("junk2", [128, 64], F32)
        self.osb = self.sb("osb", [128, 2, 256], F32)
        self.osq = self.sb("osq", [128, 256], F32)
        nc = self.nc

        def mk_ident(e):
            return e.affine_select(out=self.identf[:], in_=self.identf[:], pattern=[[-1, 128]],
                                   compare_op=ALU.not_equal, fill=1.0, base=0, channel_multiplier=1)

        P.op("pool", lambda e: e.memset(self.identf[:], 0.0), wr=["identf"])
        P.op("pool", mk_ident, rd=["identf"], wr=["identf"])
        P.op("dve", lambda e: e.tensor_copy(out=self.ident[:], in_=self.identf[:]), rd=["identf"], wr=["ident"])

        def mk_consts(e):
            e.memset(self.consts[:, 0:1], RMS_EPS)
            e.memset(self.consts[:, 1:2], 1.0)
            e.memset(self.consts[:, 2:3], GN_EPS)
            e.memset(self.consts[:, 4:8], -0.5)
            return e.memset(self.consts[:, 3:4], 0.0)

        P.op("pool", mk_consts, wr=["consts"])
        hsz = c.KC * (c.T + 2)
        self.TH = c.T // (2 if c.T >= 1024 else 1)
        r1 = max(hsz + c.KC * c.T, c.FC * self.TH, 17 * c.T, 16 * c.T + c.KC * c.T)
        self.R1 = self.sb("R1", [128, r1], BF16)
        self.actA = self.R1[:, 0:hsz].rearrange("p (c t) -> p c t", t=c.T + 2)
        self.actB = self.R1[:, hsz:hsz + c.KC * c.T]
        self.WQ = c.KC * 128
        self.NWQ = 12
        r2 = max(self.NWQ * self.WQ, 2 * c.FC * 256, 8 * c.D, 10 * c.T + 1408, 2 * (4 * c.T + c.NTB * 132) + 4 * c.TS + 384, 7 * c.T + 140, 9 * c.D + 8 * c.H)
        self.R2 = self.sb("R2", [128, r2], BF16)
        self.NWQ = r2 // self.WQ
        self.wq_ptr = 0
        self.xt = [self.R2[:, i * 2 * c.D:(i + 1) * 2 * c.D].bitcast(F32) for i in range(2)]
        self.grep = self.R2[:, 4 * c.D:6 * c.D].bitcast(F32)
        self.hn = [self.R2[:, (6 + i) * c.D:(7 + i) * c.D] for i in range(2)]
        self.NSTG = 3
        self.stg = [self.sb(f"stg{i}", [128, 512], F32) for i in range(self.NSTG)]
        P.op("pool", lambda e: e.memset(self.actA[:, :, 0:2], 0.0), wr=["actA"])

    def next_ps(self):
        b = self.ps_ctr % 6
        self.ps_ctr += 1
        return b

    def next_wb(self):
        b = self.wb_ctr % self.NWB
        self.wb_ctr += 1
        return b

    def next_stg(self):
        b = self.stg_ctr % self.NSTG
        self.stg_ctr += 1
        return b

    def norm_transpose(self, x_ap, g_ap):
        c, P = self.c, self.P
        P.op("sp", lambda e: e.dma_start(out=self.grep[:], in_=g_ap.partition_broadcast(128)),
             wr=["grep"], dma="grep")
        for tb in range(c.NTB):
            s = tb % 2
            xt, hn = self.xt[s], self.hn[s]
            ss = self.small[:, s:s + 1]
            rs = self.small[:, 2 + s:3 + s]
            P.op("sp", lambda e, xt=xt, tb=tb: e.dma_start(out=xt[:], in_=x_ap[tb * 128:(tb + 1) * 128, :]),
                 wr=[f"xt{s}"], dma=f"xt{s}")
            P.op("act", lambda e, xt=xt, hn=hn, ss=ss: e.activation(out=hn[:], in_=xt[:], func=AF.Square, accum_out=ss),
                 rd=[f"xt{s}"], wr=[f"hn{s}", f"ss{s}"])
            P.op("act", lambda e, ss=ss, rs=rs: e.activation(out=rs, in_=ss, func=AF.Sqrt, scale=1.0 / c.D,
                                                                 bias=self.consts[:, 0:1]),
                 rd=[f"ss{s}", "consts"], wr=[f"rs{s}"])
            P.op("dve", lambda e, rs=rs: e.reciprocal(out=rs, in_=rs), rd=[f"rs{s}"], wr=[f"rsd{s}", f"rs{s}"])
            P.op("dve", lambda e, xt=xt, hn=hn, rs=rs: e.scalar_tensor_tensor(
                out=hn[:], in0=xt[:], scalar=rs, in1=self.grep[:], op0=ALU.mult, op1=ALU.mult),
                rd=[f"xt{s}", f"rsd{s}", f"rs{s}", "grep"], wr=[f"hn{s}"])
            for c0 in range(0, c.KC, 8):
                nch = min(8, c.KC - c0)
                tbk = (tb * ((c.KC + 7) // 8) + c0 // 8) % 2

                def tr(e, hn=hn, c0=c0, nch=nch, tbk=tbk):
                    ins = None
                    for i in range(nch):
                        ins = e.transpose(out=self.psT[:, tbk, i * 128:(i + 1) * 128],
                                          in_=hn[:, (c0 + i) * 128:(c0 + i + 1) * 128], identity=self.ident[:])
                    return ins

                P.op("pe", tr, rd=[f"hn{s}", "ident"], wr=[f"psT{tbk}"])
                eng = "act" if (c0 // 8) % 2 == 0 else "dve"

                def ev(e, c0=c0, nch=nch, tbk=tbk, tb=tb, eng=eng):
                    src = self.psT[:, tbk, 0:nch * 128].rearrange("p (c t) -> p c t", t=128)
                    dst = self.actA[:, c0:c0 + nch, 2 + tb * 128:2 + (tb + 1) * 128]
                    if eng == "act":
                        return e.activation(out=dst, in_=src, func=AF.Copy)
                    return e.tensor_copy(out=dst, in_=src)

                P.op(eng, ev, rd=[f"psT{tbk}"], wr=["actA"])

    def hT(self, k, t0, t1):
        return self.actA[:, k, 2 + t0:2 + t1]

    def load_w(self, w_ap, kcn, kp, col0, ncols):
        nq = (kcn * ncols + self.WQ - 1) // self.WQ
        ext = getattr(self, "wext", None)
        next_ = (ext.shape[1] // self.WQ) if ext is not None else 0
        if self.wq_ptr < self.NWQ and self.wq_ptr + nq > self.NWQ:
            self.wq_ptr = self.NWQ if nq <= next_ else 0
        if self.wq_ptr >= self.NWQ and self.wq_ptr + nq > self.NWQ + next_:
            self.wq_ptr = 0
        q0 = self.wq_ptr
        self.wq_ptr += nq
        keys = [f"wq{q0 + i}" for i in range(nq)]
        if q0 < self.NWQ:
            view = self.R2[0:kp, q0 * self.WQ:q0 * self.WQ + kcn * ncols].rearrange("p (c n) -> p c n", n=ncols)
        else:
            qe = q0 - self.NWQ
            view = ext[0:kp, qe * self.WQ:qe * self.WQ + kcn * ncols].rearrange("p (c n) -> p c n", n=ncols)
        src = w_ap[:, col0:col0 + ncols].rearrange("(c p) n -> p c n", p=kp)
        self.P.op("pool", lambda e: e.dma_start(out=view, in_=src), wr=keys, dma=f"wq{q0}")
        return keys, view

    def gemm_fm(self, xfn, xkeys, kcn, kp, groups, epilogue, t0=0, t1=None):
        c, P = self.c, self.P
        t1 = c.T if t1 is None else t1
        for gi, grp in enumerate(groups):
            wts = [self.load_w(w_ap, kcn, kp, col0, ncols) + (ncols,) for (w_ap, col0, ncols) in grp]
            for tt in range((t1 - t0) // c.TS):
                lo, hi = t0 + tt * c.TS, t0 + (tt + 1) * c.TS
                banks = []
                for (s, view, ncols) in wts:
                    b = self.next_ps()
                    banks.append((b, ncols))

                    def mm(e, view=view, ncols=ncols, b=b, lo=lo, hi=hi):
                        ins = None
                        for k in range(kcn):
                            ins = e.matmul(self.psA[0:ncols, b, 0:hi - lo], lhsT=view[:, k, :], rhs=xfn(k, lo, hi),
                                           start=(k == 0), stop=(k == kcn - 1))
                        return ins

                    P.op("pe", mm, rd=s + xkeys, wr=[f"ps{b}"])
                epilogue(gi, tt, lo, hi, banks)

    def gemm_tm(self, xfn, xkeys, kcn, kp, w_ap, ncols_total, nbw, epilogue, t0=0, t1=None):
        c, P = self.c, self.P
        t1 = c.T if t1 is None else t1
        for nb in range(ncols_total // nbw):
            s, view = self.load_w(w_ap, kcn, kp, nb * nbw, nbw)
            for tb in range((t1 - t0) // 128):
                lo = t0 + tb * 128
                b = self.next_ps()

                def mm(e, view=view, b=b, lo=lo):
                    ins = None
                    for k in range(kcn):
                        ins = e.matmul(self.psA[:, b, 0:nbw], lhsT=xfn(k, lo, lo + 128), rhs=view[:, k, :],
                                       start=(k == 0), stop=(k == kcn - 1))
                    return ins

                P.op("pe", mm, rd=s + xkeys, wr=[f"ps{b}"])
                epilogue(nb, lo, b)

    def resid_epilogue(self, x_ap, nbw):
        P = self.P
        cnt = [0]

        def ep(nb, lo, b):
            s = self.next_stg()
            stg = self.stg[s]
            P.op("sp", lambda e: e.dma_start(out=stg[:, 0:nbw], in_=x_ap[lo:lo + 128, nb * nbw:(nb + 1) * nbw]),
                 wr=[f"stg{s}"], dma=f"stg{s}")
            eng = "dve"
            P.op(eng, lambda e: e.tensor_tensor(out=stg[:, 0:nbw], in0=stg[:, 0:nbw], in1=self.psA[:, b, 0:nbw], op=ALU.add),
                 rd=[f"ps{b}", f"stg{s}"], wr=[f"stg{s}"])
            P.op("sp", lambda e: e.dma_start(out=x_ap[lo:lo + 128, nb * nbw:(nb + 1) * nbw], in_=stg[:, 0:nbw]),
                 rd=[f"stg{s}"], dma=f"stg{s}")
            cnt[0] += 1

        return ep

    def swiglu(self, x_ap, g_ap, wgu_ap, wd_ap, mid_ap):
        c, P = self.c, self.P
        self.norm_transpose(x_ap, g_ap)
        P.barrier()
        groups = [[(wgu_ap, j * 128, 128), (wgu_ap, c.DFF + j * 128, 128)] for j in range(c.FC)]
        if not hasattr(self, "_sg"):
            self._sg = [self.sb(self.uid("sg"), [128, c.TS], F32) for _ in range(2)]
            self._mo = [self.sb(self.uid("mo"), [128, c.TS], BF16) for _ in range(2)]
        sg, mo = self._sg, self._mo
        it = [0]

        def ep(gi, tt, lo, hi, banks):
            s = it[0] % 2
            it[0] += 1
            (bg, _), (bu, _) = banks
            P.op("act", lambda e: e.activation(out=sg[s][:], in_=self.psA[:, bg, 0:c.TS], func=AF.Silu),
                 rd=[f"ps{bg}"], wr=[f"sg{s}"])
            P.op("dve", lambda e: e.tensor_tensor(out=mo[s][:], in0=sg[s][:], in1=self.psA[:, bu, 0:c.TS], op=ALU.mult),
                 rd=[f"sg{s}", f"ps{bu}"], wr=[f"mo{s}"])
            P.op("sp", lambda e: e.dma_start(out=mid_ap[gi * 128:(gi + 1) * 128, lo:hi], in_=mo[s][:]),
                 rd=[f"mo{s}"], dma=f"mo{s}")

        self.gemm_fm(self.hT, ["actA"], c.KC, 128, groups, ep)
        P.barrier()
        nkh = c.T // self.TH
        FCH = c.FC // nkh
        nbw = 256
        self.wq_ptr = 0
        free0 = FCH * c.T
        nfree = (self.R1.shape[1] - free0) // self.WQ
        self.wext = self.R1[:, free0:free0 + nfree * self.WQ] if nfree > 0 else None
        for kh in range(nkh):
            midv = self.R1[:, 0:FCH * c.T].rearrange("p (c t) -> p c t", t=c.T)
            for cc in range(FCH):
                P.op("sp", lambda e, cc=cc, kh=kh: e.dma_start(out=midv[:, cc, :], in_=mid_ap[(kh * FCH + cc) * 128:(kh * FCH + cc + 1) * 128, :]),
                     wr=["actB"], dma=f"actB{cc % 4}")
            xfn = lambda k, a, b_: midv[:, k, a:b_]
            self.gemm_tm(xfn, ["actB"], FCH, 128, wd_ap[kh * FCH * 128:(kh + 1) * FCH * 128, :], c.D, nbw, self.resid_epilogue(x_ap, nbw))
            P.barrier()
        self.wext = None
        self.wq_ptr = 0

    def store_fm(self, dst_ap, row_of_group):
        P = self.P
        it = [0]

        def ep(gi, tt, lo, hi, banks):
            for bi, (b, ncols) in enumerate(banks):
                s = self.next_stg()
                stg = self.stg[s]
                eng = "act" if it[0] % 2 == 0 else "dve"
                it[0] += 1
                n = hi - lo
                if eng == "act":
                    P.op("act", lambda e, b=b, ncols=ncols, stg=stg, n=n: e.activation(out=stg[0:ncols, 0:n], in_=self.psA[0:ncols, b, 0:n], func=AF.Copy),
                         rd=[f"ps{b}"], wr=[f"stg{s}"])
                else:
                    P.op("dve", lambda e, b=b, ncols=ncols, stg=stg, n=n: e.tensor_copy(out=stg[0:ncols, 0:n], in_=self.psA[0:ncols, b, 0:n]),
                         rd=[f"ps{b}"], wr=[f"stg{s}"])
                r0 = row_of_group(gi, bi)
                P.op("sp", lambda e, r0=r0, ncols=ncols, stg=stg, n=n, lo=lo, hi=hi: e.dma_start(out=dst_ap[r0:r0 + ncols, lo:hi], in_=stg[0:ncols, 0:n]),
                     rd=[f"stg{s}"], dma=f"stg{s}")

        return ep

    def store_tm(self, dst_ap, nbw, col0=0):
        P = self.P
        it = [0]

        def ep(nb, lo, b):
            s = self.next_stg()
            stg = self.stg[s]
            eng = "act" if it[0] % 2 == 0 else "dve"
            it[0] += 1
            if eng == "act":
                P.op("act", lambda e: e.activation(out=stg[:, 0:nbw], in_=self.psA[:, b, 0:nbw], func=AF.Copy), rd=[f"ps{b}"], wr=[f"stg{s}"])
            else:
                P.op("dve", lambda e: e.tensor_copy(out=stg[:, 0:nbw], in_=self.psA[:, b, 0:nbw]), rd=[f"ps{b}"], wr=[f"stg{s}"])
            P.op("sp", lambda e: e.dma_start(out=dst_ap[lo:lo + 128, col0 + nb * nbw:col0 + (nb + 1) * nbw], in_=stg[:, 0:nbw]),
                 rd=[f"stg{s}"], dma=f"stg{s}")

        return ep

    def r2view(self, off, n, dt=BF16, parts=128, arena=None):
        w = n * (2 if dt == F32 else 1)
        arena = self.R2 if arena is None else arena
        v = arena[0:parts, off:off + w]
        if dt == F32:
            v = v.bitcast(F32)
        return v, off + w

    def transpose_to_fm(self, src_fn, dst3):
        c, P = self.c, self.P
        for tb in range(c.NTB):
            for c0 in range(0, c.KC, 8):
                nch = min(8, c.KC - c0)
                tbk = (tb * ((c.KC + 7) // 8) + c0 // 8) % 2

                def tr(e, tb=tb, c0=c0, nch=nch, tbk=tbk):
                    ins = None
                    src = src_fn(tb)
                    for i in range(nch):
                        ins = e.transpose(out=self.psT[:, tbk, i * 128:(i + 1) * 128],
                                          in_=src[:, (c0 + i) * 128:(c0 + i + 1) * 128], identity=self.ident[:])
                    return ins

                P.op("pe", tr, rd=["ztm", "ident"], wr=[f"psT{tbk}"])
                eng = "act" if (tb + c0 // 8) % 2 == 0 else "dve"

                def ev(e, c0=c0, nch=nch, tbk=tbk, tb=tb, eng=eng):
                    src = self.psT[:, tbk, 0:nch * 128].rearrange("p (c t) -> p c t", t=128)
                    dst = dst3[:, c0:c0 + nch, tb * 128:(tb + 1) * 128]
                    if eng == "act":
                        return e.activation(out=dst, in_=src, func=AF.Copy)
                    return e.tensor_copy(out=dst, in_=src)

                P.op(eng, ev, rd=[f"psT{tbk}"], wr=["zT"])

    def fox(self, x_ap, W):
        c, P = self.c, self.P
        D, T, H, KC, TS = c.D, c.T, c.H, c.KC, c.TS
        w_in = W["b_w_in"]
        qk = self.dscr("b_qk", [2 * D, T], F32)
        fa = self.dscr("b_fa", [3 * H, T], F32)
        vg = self.dscr("b_vg", [T, 2 * D], F32)
        fat = self.dscr("b_fat", [T, 3 * H], F32)
        qhat = self.dscr("b_qhat", [2 * D, T], BF16)
        qaug = self.dscr("b_qaug", [H, 4, T], BF16)
        kaug = self.dscr("b_kaug", [H, 4, T], BF16)
        akd = self.dscr("b_akd", [H, T], BF16)
        vpr = self.dscr("b_vpr", [T, D], BF16)
        self.norm_transpose(x_ap, W["b_norm_g"])
        P.barrier()
        groups = [[(w_in, j * 128, 128)] for j in range(2 * KC)]
        self.gemm_fm(self.hT, ["actA"], KC, 128, groups, self.store_fm(qk, lambda gi, bi: gi * 128))
        self.gemm_fm(self.hT, ["actA"], KC, 128, [[(w_in, 4 * D, 3 * H)]], self.store_fm(fa, lambda gi, bi: 0))
        self.gemm_tm(self.hT, ["actA"], KC, 128, w_in[:, 2 * D:4 * D], 2 * D, 512 if D >= 512 else 2 * D,
                     self.store_tm(vg, 512 if D >= 512 else 2 * D))
        self.gemm_tm(self.hT, ["actA"], KC, 128, w_in[:, 4 * D:4 * D + 3 * H], 3 * H, 3 * H, self.store_tm(fat, 3 * H))
        P.barrier()
        off = 0
        ft, off = self.r2view(off, T, F32, H, self.R1)
        f2, off = self.r2view(off, T, F32, H, self.R1)
        ct, off = self.r2view(off, T, F32, H, self.R1)
        qa, off = self.r2view(off, 4 * T, BF16, H, self.R1)
        ka, off = self.r2view(off, 4 * T, BF16, H, self.R1)
        akt, off = self.r2view(off, T, F32, H, self.R1)
        akb, off = self.r2view(off, T, BF16, H, self.R1)
        qa = qa.rearrange("p (r t) -> p r t", t=T)
        ka = ka.rearrange("p (r t) -> p r t", t=T)
        nb = self.small[0:H, 8:9]
        P.op("sp", lambda e: e.dma_start(out=ft, in_=fa[0:H, :]), wr=["ft"], dma="ft")
        P.op("sp", lambda e: e.dma_start(out=akt, in_=fa[H:2 * H, :]), wr=["akt"], dma="akt")
        P.op("sp", lambda e: e.dma_start(out=nb, in_=W["b_b_f"].rearrange("o h -> h o")), wr=["nb"], dma="nb")
        P.op("dve", lambda e: e.tensor_scalar(out=nb, in0=nb, scalar1=-1.0, scalar2=None, op0=ALU.mult), rd=["nb"], wr=["nb"])
        P.barrier()
        P.op("act", lambda e: e.activation(out=f2, in_=ft, func=AF.Exp, scale=-1.0, bias=nb), rd=["ft", "nb"], wr=["f2"])
        P.op("act", lambda e: e.activation(out=f2, in_=f2, func=AF.Ln, scale=1.0, bias=self.consts[0:H, 1:2]), rd=["consts"], wr=["f2"])
        P.op("dve", lambda e: e.tensor_scalar(out=f2, in0=f2, scalar1=-0.5, scalar2=None, op0=ALU.mult), rd=["f2"], wr=["f2"])
        P.op("dve", lambda e: e.tensor_tensor_scan(out=ct, data0=f2, data1=f2, initial=0.0, op0=ALU.add, op1=ALU.add), rd=["f2"], wr=["ct"])
        P.op("dve", lambda e: e.tensor_copy(out=qa[:, 0, :], in_=ct), rd=["ct"], wr=["qa"])
        P.op("dve", lambda e: e.tensor_tensor(out=qa[:, 1, :], in0=ct, in1=qa[:, 0, :], op=ALU.subtract), rd=["ct", "qa"], wr=["qa"])
        P.op("pool", lambda e: e.memset(qa[:, 2:4, :], 1.0), wr=["qa"])
        P.op("pool", lambda e: e.memset(ka[:, 0:2, :], 1.0), wr=["ka"])
        P.op("dve", lambda e: e.tensor_scalar(out=ka[:, 2:4, :], in0=qa[:, 0:2, :], scalar1=-1.0, scalar2=None, op0=ALU.mult), rd=["qa"], wr=["ka"])
        P.op("act", lambda e: e.activation(out=akb, in_=akt, func=AF.Sigmoid), rd=["akt"], wr=["akb"])
        P.op("sp", lambda e: e.dma_start(out=qaug, in_=qa), rd=["qa"], dma="qa")
        P.op("sp", lambda e: e.dma_start(out=kaug, in_=ka), rd=["ka"], dma="ka")
        P.op("sp", lambda e: e.dma_start(out=akd, in_=akb), rd=["akb"], dma="akb")
        P.barrier()
        off = 0
        kt, off = self.r2view(off, T + 2, F32)
        tmpf, off = self.r2view(off, T, F32)
        akx, off = self.r2view(off, T, BF16)
        sq, off = self.r2view(off, T, BF16)
        outb, off = self.r2view(off, T, BF16)
        bones, off = self.r2view(off, 128, BF16)
        gq = self.small[:, 10:11]
        gk = self.small[:, 11:12]

        P.op("dve", lambda e: e.memset(bones, 0.0), wr=["bones"])
        P.op("dve", lambda e: e.memset(bones[0:64, 0:64], 1.0), wr=["bones"])
        P.op("dve", lambda e: e.memset(bones[64:128, 64:128], 1.0), wr=["bones"])
        P.op("pool", lambda e: e.memset(kt[:, 0:2], 0.0), wr=["kt"])
        for hh in range(2):
            P.op("sp", lambda e, hh=hh: e.dma_start(out=gq[hh * 64:(hh + 1) * 64, :], in_=W["b_qn_g"].rearrange("o n -> n o")), wr=["gq"], dma="gq")
            P.op("sp", lambda e, hh=hh: e.dma_start(out=gk[hh * 64:(hh + 1) * 64, :], in_=W["b_kn_g"].rearrange("o n -> n o")), wr=["gk"], dma="gk")
        P.op("dve", lambda e: e.tensor_scalar(out=gq, in0=gq, scalar1=0.125, scalar2=None, op0=ALU.mult), rd=["gq"], wr=["gq"])
        P.barrier()
        for which in range(2):
            for p in range(KC):
                row0 = which * D + p * 128
                P.op("sp", lambda e, row0=row0: e.dma_start(out=kt[:, 2:T + 2], in_=qk[row0:row0 + 128, :]), wr=["kt"], dma="kt")
                if which == 1:
                    for hh in range(2):
                        P.op("sp", lambda e, hh=hh, p=p: e.dma_start(out=akx[hh * 64:(hh + 1) * 64, :], in_=akd[2 * p + hh:2 * p + hh + 1, :].partition_broadcast(64)),
                             wr=["akx"], dma="akx")
                    P.op("dve", lambda e: e.tensor_tensor(out=tmpf, in0=kt[:, 1:T + 1], in1=kt[:, 2:T + 2], op=ALU.subtract), rd=["kt"], wr=["tmpf"])
                    P.op("dve", lambda e: e.tensor_tensor(out=tmpf, in0=tmpf, in1=akx, op=ALU.mult), rd=["akx", "tmpf"], wr=["tmpf"])
                    P.op("dve", lambda e: e.tensor_tensor(out=kt[:, 2:T + 2], in0=kt[:, 2:T + 2], in1=tmpf, op=ALU.add), rd=["tmpf", "kt"], wr=["kt"])
                P.op("act", lambda e: e.activation(out=sq, in_=kt[:, 2:T + 2], func=AF.Square), rd=["kt"], wr=["sq"])
                for tt in range(c.NTT):
                    b = self.next_ps()
                    P.op("pe", lambda e, b=b, tt=tt: e.matmul(self.psA[:, b, 0:TS], lhsT=bones, rhs=sq[:, tt * TS:(tt + 1) * TS], start=True, stop=True),
                         rd=["sq", "bones"], wr=[f"ps{b}"])
                    P.op("act", lambda e, b=b, tt=tt: e.activation(out=tmpf[:, tt * TS:(tt + 1) * TS], in_=self.psA[:, b, 0:TS], func=AF.Sqrt,
                                                                 scale=1.0 / 64, bias=self.consts[:, 0:1]),
                         rd=[f"ps{b}", "consts", "tmpf"], wr=["tmpf"])
                P.op("dve", lambda e: e.reciprocal(out=tmpf, in_=tmpf), rd=["tmpf"], wr=["tmpf"])
                gcol = gq if which == 0 else gk
                P.op("dve", lambda e, gcol=gcol: e.scalar_tensor_tensor(out=outb, in0=kt[:, 2:T + 2], scalar=gcol, in1=tmpf, op0=ALU.mult, op1=ALU.mult),
                     rd=["kt", "tmpf", "gq", "gk"], wr=["outb"])
                P.op("sp", lambda e, row0=row0: e.dma_start(out=qhat[row0:row0 + 128, :], in_=outb), rd=["outb"], dma="outb")
        P.barrier()
        off = 0
        vt, off = self.r2view(off, D, F32)
        vp, off = self.r2view(off, D, F32)
        gt, off = self.r2view(off, D, F32)
        ong, off = self.r2view(off, D, F32)
        vo, off = self.r2view(off, D, BF16)
        al, off = self.r2view(off, 3 * H, F32)
        av, off = self.r2view(off, H, F32)
        G3 = self.actB.rearrange("p (b d) -> p b d", d=D)
        ztm = self.R1[:, 0:c.NTB * D].rearrange("p (b d) -> p b d", d=D)
        P.op("sp", lambda e: e.dma_start(out=ong, in_=W["b_on_g"].partition_broadcast(128)), wr=["ong"], dma="ong")
        for tb in range(c.NTB):
            r0 = tb * 128
            P.op("sp", lambda e, r0=r0: e.dma_start(out=vt, in_=vg[r0:r0 + 128, 0:D]), wr=["vt"], dma="vt")
            P.op("sp", lambda e, r0=r0: e.dma_start(out=gt, in_=vg[r0:r0 + 128, D:2 * D]), wr=["gt"], dma="gt")
            P.op("sp", lambda e, r0=r0: e.dma_start(out=al, in_=fat[r0:r0 + 128, :]), wr=["al"], dma="al")
            if tb == 0:
                P.op("pool", lambda e: e.memset(vp[0:1, :], 0.0), wr=["vp"])
                P.op("sp", lambda e: e.dma_start(out=vp[1:128, :], in_=vg[0:127, 0:D]), wr=["vp"], dma="vp")
            else:
                P.op("sp", lambda e, r0=r0: e.dma_start(out=vp, in_=vg[r0 - 1:r0 + 127, 0:D]), wr=["vp"], dma="vp")
            P.op("act", lambda e: e.activation(out=av, in_=al[:, 2 * H:3 * H], func=AF.Sigmoid), rd=["al"], wr=["av"])
            P.op("dve", lambda e: e.tensor_tensor(out=vp, in0=vp, in1=vt, op=ALU.subtract), rd=["vp", "vt"], wr=["vp"])
            P.op("dve", lambda e: e.tensor_tensor(out=vp.rearrange("p (h n) -> p h n", n=64), in0=vp.rearrange("p (h n) -> p h n", n=64),
                                                  in1=av.unsqueeze(2).broadcast_to([128, H, 64]), op=ALU.mult), rd=["vp", "av"], wr=["vp"])
            P.op("dve", lambda e: e.tensor_tensor(out=vo, in0=vp, in1=vt, op=ALU.add), rd=["vp", "vt"], wr=["vo"])
            P.op("sp", lambda e, r0=r0: e.dma_start(out=vpr[r0:r0 + 128, :], in_=vo), rd=["vo"], dma="vo")
            P.op("act", lambda e: e.activation(out=gt, in_=gt, func=AF.Sigmoid), rd=["gt"], wr=["gt"])
            P.op("pool", lambda e, tb=tb: e.tensor_tensor(out=G3[:, tb, :], in0=gt, in1=ong, op=ALU.mult), rd=["gt", "ong"], wr=["G"])
        P.barrier()
        off = 0
        QA, KA, VV = [], [], []
        for i in range(2):
            qs, ks = [], []
            for hh in range(2):
                v_, off = self.r2view(off, T, BF16)
                qs.append(v_)
                v_, off = self.r2view(off, T, BF16)
                ks.append(v_)
            QA.append(qs)
            KA.append(ks)
            v_, off = self.r2view(off, c.NTB * 2 * 66, BF16)
            VV.append(v_.rearrange("p (b h n) -> p b h n", h=2, n=66))
        PT = []
        for i in range(4):
            v_, off = self.r2view(off, TS, BF16)
            PT.append(v_)
        tri, off = self.r2view(off, 128, BF16)
        trif, off = self.r2view(off, 128, F32)

        P.op("pool", lambda e: e.memset(trif, 1.0), wr=["trif"])

        def mk_tri(e):
            return e.affine_select(out=trif, in_=trif, pattern=[[1, 128]], compare_op=ALU.is_ge, fill=0.0, base=0, channel_multiplier=-1)

        P.op("pool", mk_tri, rd=["trif"], wr=["trif"])
        P.op("dve", lambda e: e.tensor_copy(out=tri, in_=trif), rd=["trif"], wr=["tri"])
        for i in range(2):
            P.op("pool", lambda e, i=i: e.memset(VV[i], 1.0), wr=[f"VV{i}"])
        nq = T // TS
        nsub = TS // 128
        NPT = len(PT)

        def loads(p):
            i = p % 2
            for hh in range(2):
                h = 2 * p + hh
                P.op("sp", lambda e, i=i, hh=hh, h=h: e.dma_start(out=QA[i][hh][0:64, :], in_=qhat[h * 64:(h + 1) * 64, :]), wr=[f"QA{i}{hh}"], dma=f"QA{i}{hh}")
                P.op("sp", lambda e, i=i, hh=hh, h=h: e.dma_start(out=QA[i][hh][64:68, :], in_=qaug[h]), wr=[f"QA{i}{hh}"], dma=f"QA{i}{hh}")
                P.op("sp", lambda e, i=i, hh=hh, h=h: e.dma_start(out=KA[i][hh][0:64, :], in_=qhat[D + h * 64:D + (h + 1) * 64, :]), wr=[f"KA{i}{hh}"], dma=f"KA{i}{hh}")
                P.op("sp", lambda e, i=i, hh=hh, h=h: e.dma_start(out=KA[i][hh][64:68, :], in_=kaug[h]), wr=[f"KA{i}{hh}"], dma=f"KA{i}{hh}")
                P.op("sp", lambda e, i=i, hh=hh, h=h: e.dma_start(out=VV[i][:, :, hh, 0:64], in_=vpr[:, h * 64:(h + 1) * 64].rearrange("(b s) n -> s b n", s=128)),
                     wr=[f"VV{i}"], dma=f"VV{i}{hh}")

        epc = [0]

        def epilogue(h, I, ob):
            par = epc[0] % 2
            epc[0] += 1
            psv = self.psA[:, ob, 0:nsub * 66].rearrange("p (u n) -> p u n", n=66)
            oc = psv[:, :, 0:64]
            k0 = 16 + par * 16
            rc4 = self.small[:, k0:k0 + nsub]
            ssq4 = self.small[:, k0 + 4:k0 + 4 + nsub]
            rst4 = self.small[:, k0 + 8:k0 + 8 + nsub]
            osb = self.osb[:, par, 0:nsub * 64]
            osb3 = osb.rearrange("p (u n) -> p u n", n=64)
            sq = self.osq[:, 0:nsub * 64]
            kk = f"ep{par}"
            tb0 = I * nsub
            bc = lambda v_: v_.unsqueeze(2).broadcast_to([128, nsub, 64])
            P.op("dve", lambda e: e.reciprocal(out=rc4, in_=psv[:, :, 64]), rd=[f"ps{ob}"], wr=[kk + "rc"])
            P.op("dve", lambda e: e.tensor_tensor(out=osb3, in0=oc, in1=bc(rc4), op=ALU.mult), rd=[f"ps{ob}", kk + "rc"], wr=[kk + "osb"])
            P.op("dve", lambda e: e.tensor_tensor(out=sq, in0=osb, in1=osb, op=ALU.mult), rd=[kk + "osb"], wr=["osq"])
            P.op("dve", lambda e: e.tensor_reduce(out=ssq4, in_=sq.rearrange("p (u n) -> p u n", n=64), axis=AX.X, op=ALU.add), rd=["osq"], wr=[kk + "ss"])
            P.op("pool", lambda e: e.tensor_scalar(out=rst4, in0=ssq4, scalar1=1.0 / 64, scalar2=RMS_EPS, op0=ALU.mult, op1=ALU.add), rd=[kk + "ss"], wr=[kk + "rst"])
            P.op("pool", lambda e: e.tensor_tensor(out=rst4, in0=rst4, in1=self.consts[:, 4:4 + nsub], op=ALU.pow), rd=[kk + "rst", "consts"], wr=[kk + "rst2"])
            P.op("dve", lambda e: e.tensor_tensor(out=osb3, in0=osb3, in1=bc(rst4), op=ALU.mult), rd=[kk + "osb", kk + "rst2"], wr=[kk + "osb"])
            P.op("dve", lambda e: e.tensor_tensor(out=ztm[:, tb0:tb0 + nsub, h * 64:(h + 1) * 64], in0=osb3, in1=G3[:, tb0:tb0 + nsub, h * 64:(h + 1) * 64], op=ALU.mult),
                 rd=[kk + "osb", "G"], wr=["ztm"])

        gctr = [0, 0]
        obank = [0]
        LOOK = 2
        loads(0)
        for p in range(KC):
            i = p % 2
            if p + 1 < KC:
                loads(p + 1)
            steps = []
            for hh in range(2):
                for I in range(nq):
                    ob = 4 + (obank[0] % 2)
                    obank[0] += 1
                    nJ = (I + 1) * nsub
                    for J in range(nJ):
                        steps.append((hh, I, J, nJ, ob))

            def emit_S(st, i=i):
                hh, I, J, nJ, ob = st
                q_hi = (I + 1) * TS
                t_lo = max(I * TS, J * 128)
                N = q_hi - t_lo
                sbk = gctr[0] % 4
                gctr[0] += 1
                pti = gctr[1] % NPT
                gctr[1] += 1
                pt = PT[pti]
                P.op("pe", lambda e: e.matmul(self.psA[:, sbk, 0:N], lhsT=KA[i][hh][0:68, J * 128:(J + 1) * 128], rhs=QA[i][hh][0:68, t_lo:q_hi],
                                              start=True, stop=True), rd=[f"QA{i}{hh}", f"KA{i}{hh}"], wr=[f"ps{sbk}"])
                P.op("act", lambda e: e.activation(out=pt[:, 0:N], in_=self.psA[:, sbk, 0:N], func=AF.Exp), rd=[f"ps{sbk}"], wr=[f"PT{pti}"])
                if J * 128 >= I * TS:
                    P.op("dve", lambda e: e.tensor_tensor(out=pt[:, 0:128], in0=pt[:, 0:128], in1=tri, op=ALU.mult), rd=[f"PT{pti}", "tri"], wr=[f"PT{pti}"])
                return (pt, pti, t_lo, q_hi)

            def emit_PV(st, info, i=i, p=p):
                hh, I, J, nJ, ob = st
                pt, pti, t_lo, q_hi = info

                def pv(e):
                    ins = None
                    for u0 in range(t_lo, q_hi, 128):
                        u = (u0 - I * TS) // 128
                        ins = e.matmul(self.psA[:, ob, u * 66:u * 66 + 65], lhsT=pt[:, u0 - t_lo:u0 - t_lo + 128], rhs=VV[i][:, J, hh, 0:65],
                                       start=(J == 0 and u == 0), stop=(J == nJ - 1 and u == nsub - 1))
                    return ins

                P.op("pe", pv, rd=[f"PT{pti}", f"VV{i}"], wr=[f"ps{ob}"])
                if J == nJ - 1:
                    epilogue(2 * p + hh, I, ob)

            infos = {}
            for idx in range(len(steps) + LOOK):
                if idx < len(steps):
                    infos[idx] = emit_S(steps[idx])
                if idx - LOOK >= 0:
                    emit_PV(steps[idx - LOOK], infos.pop(idx - LOOK))
        P.barrier()
        if DEBUG:
            dz = self.dscr("dbg_ztm", [128, c.NTB, D], BF16)
            dg = self.dscr("dbg_G", [128, c.NTB, D], BF16)
            P.op("sp", lambda e: e.dma_start(out=dz, in_=ztm), dma="dbg")
            P.op("sp", lambda e: e.dma_start(out=dg, in_=G3), dma="dbg")
            P.barrier()
        zT = self.actB.rearrange("p (c t) -> p c t", t=T)
        self.transpose_to_fm(lambda tb: ztm[:, tb, :], zT)
        P.barrier()
        nbw = 512 if D >= 512 else D
        self.gemm_tm(lambda k, a, b_: zT[:, k, a:b_], ["zT"], KC, 128, W["b_w_o"], D, nbw, self.resid_epilogue(x_ap, nbw))
        P.barrier()

    def rwkv(self, x_ap, W):
        c, P = self.c, self.P
        D, T, H, KC, TS, NCH = c.D, c.T, c.H, c.KC, c.TS, c.NCH
        rr = self.dscr("a_rr", [D, T], F32)
        kkd = self.dscr("a_kk", [D, T], F32)
        vvd = self.dscr("a_vv", [D, T], F32)
        vtm = self.dscr("a_vtm", [T, D], F32)
        lwd = self.dscr("a_lw", [D, T], F32)
        aad = self.dscr("a_aa", [D, T], F32)
        ggd = self.dscr("a_gg", [D, T], F32)
        dd = {nm_: self.dscr("a_d" + nm_, [D, T], BF16) for nm_ in ["rt", "at", "bt", "kt", "bh", "kh"]}
        d_bo = self.dscr("a_dbo", [D, T], F32)
        d_wc = self.dscr("a_dwc", [D, NCH], F32)
        d_y = self.dscr("a_dy", [D, T], F32)
        rowsA = self.sb("rowsA", [128, 128], F32)
        rowsB = self.sb("rowsB", [128, 128], F32)
        cols = self.sb("cols", [128, 256], F32)
        lora = self.sb("lora", [128, 2, T], BF16)
        wcs = self.sb("wcs", [128, NCH], F32)
        P.op("dve", lambda e: e.memset(rowsA[:], 0.0), wr=["rowsA"])
        P.op("dve", lambda e: e.memset(rowsB[:], 0.0), wr=["rowsB"])
        P.op("sp", lambda e: e.dma_start(out=rowsA[0:6 * KC, :], in_=W["a_mix"].rearrange("s (c p) -> (s c) p", p=128)), wr=["rowsA"], dma="rowsA")
        vecs = ["a_w0", "a_a0", "a_k_k", "a_k_a", "a_r_k", "a_lnx_g", "a_lnx_b"]
        for i, nm in enumerate(vecs):
            P.op("sp", lambda e, i=i, nm=nm: e.dma_start(out=rowsB[i * KC:(i + 1) * KC, :], in_=W[nm].rearrange("o (c p) -> (o c) p", p=128)),
                 wr=["rowsB"], dma="rowsB")
        b0 = self.next_ps()
        P.op("pe", lambda e: e.transpose(out=self.psA[:, b0, 0:128], in_=rowsA[:], identity=self.identf[:]), rd=["rowsA", "identf"], wr=[f"ps{b0}"])
        P.op("dve", lambda e: e.tensor_copy(out=cols[:, 0:128], in_=self.psA[:, b0, 0:128]), rd=[f"ps{b0}"], wr=["cols"])
        b1 = self.next_ps()
        P.op("pe", lambda e: e.transpose(out=self.psA[:, b1, 0:128], in_=rowsB[:], identity=self.identf[:]), rd=["rowsB", "identf"], wr=[f"ps{b1}"])
        P.op("dve", lambda e: e.tensor_copy(out=cols[:, 128:256], in_=self.psA[:, b1, 0:128]), rd=[f"ps{b1}"], wr=["cols"])
        mixc = lambda s_, k: cols[:, s_ * KC + k:s_ * KC + k + 1]
        vcol = lambda nm, k: cols[:, 128 + vecs.index(nm) * KC + k:128 + vecs.index(nm) * KC + k + 1]
        omk0 = 128 + 7 * KC
        P.op("dve", lambda e: e.tensor_scalar(out=cols[:, omk0:omk0 + KC], in0=cols[:, 128 + 3 * KC:128 + 4 * KC], scalar1=-1.0, scalar2=1.0,
                                              op0=ALU.mult, op1=ALU.add), rd=["cols"], wr=["cols"])
        self.norm_transpose(x_ap, W["a_norm_g"])
        P.barrier()
        xm = self.actB.rearrange("p (c t) -> p c t", t=T)
        xmf = lambda k, a, b_: xm[:, k, a:b_]

        def mix(s_):
            for k in range(KC):
                P.op("pool" if k % 2 == 0 else "dve", lambda e, k=k: e.tensor_tensor(out=xm[:, k, :], in0=self.actA[:, k, 1:T + 1], in1=self.actA[:, k, 2:T + 2], op=ALU.subtract),
                     rd=["actA"], wr=[f"xm{k}"])
                P.op("dve", lambda e, k=k: e.scalar_tensor_tensor(out=xm[:, k, :], in0=xm[:, k, :], scalar=mixc(s_, k), in1=self.actA[:, k, 2:T + 2],
                                                                  op0=ALU.mult, op1=ALU.add), rd=["actA", f"xm{k}", "cols"], wr=["xm", f"xm{k}"])

        fullg = lambda w_ap, n: [[(w_ap, j * 128, min(128, n - j * 128))] for j in range((n + 127) // 128)]
        for s_, dst in [(0, rr), (1, kkd), (2, vvd)]:
            mix(s_)
            self.gemm_fm(xmf, ["xm"], KC, 128, fullg(W["a_w_rkv"][s_], D), self.store_fm(dst, lambda gi, bi: gi * 128))
            if s_ == 2:
                nbw = 512 if D >= 512 else D
                self.gemm_tm(xmf, ["xm"], KC, 128, W["a_w_rkv"][2], D, nbw, self.store_tm(vtm, nbw))
            P.barrier()

        def lora_ep(func):
            def ep(gi, tt, lo, hi, banks):
                (b, ncols), = banks
                P.op("act", lambda e: e.activation(out=lora[0:ncols, gi, lo:hi], in_=self.psA[0:ncols, b, 0:hi - lo], func=func),
                     rd=[f"ps{b}"], wr=["lora"])
            return ep

        def out_ep(dst, func, bias_nm, post_scale):
            def ep(gi, tt, lo, hi, banks):
                (b, ncols), = banks
                s = self.next_stg()
                stg = self.stg[s]
                n = hi - lo
                if func is None:
                    P.op("act", lambda e: e.activation(out=stg[:, 0:n], in_=self.psA[:, b, 0:n], func=AF.Copy), rd=[f"ps{b}"], wr=[f"stg{s}"])
                else:
                    P.op("act", lambda e: e.activation(out=stg[:, 0:n], in_=self.psA[:, b, 0:n], func=func, bias=vcol(bias_nm, gi), scale=1.0),
                         rd=[f"ps{b}", "cols"], wr=[f"stg{s}"])
                if post_scale is not None:
                    P.op("dve", lambda e: e.tensor_scalar(out=stg[:, 0:n], in0=stg[:, 0:n], scalar1=post_scale, scalar2=None, op0=ALU.mult),
                         rd=[f"stg{s}"], wr=[f"stg{s}"])
                P.op("sp", lambda e: e.dma_start(out=dst[gi * 128:(gi + 1) * 128, lo:hi], in_=stg[:, 0:n]), rd=[f"stg{s}"], dma=f"stg{s}")
            return ep

        for s_, w1n, w2n, L, f1, dst, f2, bnm, psc in [
                (3, "a_w1", "a_w2", c.LW, AF.Tanh, lwd, AF.Sigmoid, "a_w0", -float(np.exp(-0.5))),
                (4, "a_a1", "a_a2", c.LA, AF.Copy, aad, AF.Sigmoid, "a_a0", None),
                (5, "a_g1", "a_g2", c.LG, AF.Sigmoid, ggd, None, None, None)]:
            mix(s_)
            self.gemm_fm(xmf, ["xm"], KC, 128, fullg(W[w1n], L), lora_ep(f1))
            kcn2 = (L + 127) // 128
            kp2 = L if L < 128 else 128
            self.gemm_fm(lambda k, a, b_, kp2=kp2: lora[0:kp2, k, a:b_], ["lora"], kcn2, kp2, fullg(W[w2n], D), out_ep(dst, f2, bnm, psc))
            P.barrier()

        off = 0
        maskA, off = self.r2view(off, 384, F32)
        maskB, off = self.r2view(off, 256, F32)
        bones, off = self.r2view(off, 128, BF16)
        fA = []
        for i in range(5):
            v_, off = self.r2view(off, T, F32)
            fA.append(v_)
        assert off <= self.R2.shape[1], (off, self.R2.shape)
        X1, X2, X3, X4, X5 = fA
        o1 = 0
        eN, o1 = self.r2view(o1, T, F32, 128, self.R1)
        cmask, o1 = self.r2view(o1, T, F32, 128, self.R1)
        BO, o1 = self.r2view(o1, T, F32, 128, self.R1)
        YT, o1 = self.r2view(o1, T, F32, 128, self.R1)
        RT, o1 = self.r2view(o1, T, BF16, 128, self.R1)
        AT, o1 = self.r2view(o1, T, BF16, 128, self.R1)
        BT, o1 = self.r2view(o1, T, BF16, 128, self.R1)
        KT, o1 = self.r2view(o1, T, BF16, 128, self.R1)
        BH, o1 = self.r2view(o1, T, BF16, 128, self.R1)
        KH, o1 = self.r2view(o1, T, BF16, 128, self.R1)
        S1, o1 = self.r2view(o1, T, BF16, 128, self.R1)
        V2p, o1 = self.r2view(o1, T, BF16, 128, self.R1)
        zoff = max(o1, c.KC * (c.T + 2))
        V2p = V2p.rearrange("p (c i) -> p c i", i=64)
        zT = self.R1[:, zoff:zoff + KC * T].rearrange("p (c t) -> p c t", t=T)
        P.op("dve", lambda e: e.memset(bones, 0.0), wr=["bones"])
        P.op("dve", lambda e: e.memset(bones[0:64, 0:64], 1.0), wr=["bones"])
        P.op("dve", lambda e: e.memset(bones[64:128, 64:128], 1.0), wr=["bones"])
        P.op("dve", lambda e: e.memset(cmask, 1.0), wr=["cmask"])
        P.op("dve", lambda e: e.memset(cmask.rearrange("p (c t) -> p c t", t=64)[:, :, 0:1], 0.0), wr=["cmask"])
        P.op("pool", lambda e: e.memset(maskA, 1.0), wr=["maskA"])
        P.op("pool", lambda e: e.memset(maskB, 1.0), wr=["maskB"])
        for (mk, o_, cm, base, pat) in [(maskA, 0, -1, -1, 1), (maskA, 128, 1, -1, -1), (maskA, 256, -1, -1, 1), (maskB, 0, -1, 0, 1), (maskB, 128, -1, 0, 1)]:
            P.op("pool", lambda e, mk=mk, o_=o_, cm=cm, base=base, pat=pat: e.affine_select(
                out=mk[:, o_:o_ + 128], in_=mk[:, o_:o_ + 128], pattern=[[pat, 128]], compare_op=ALU.is_ge, fill=0.0, base=base, channel_multiplier=cm),
                rd=["maskA", "maskB"], wr=["maskA", "maskB"])
            P.op("pool", lambda e, mk=mk, o_=o_: e.memset(mk[0:64, o_ + 64:o_ + 128], 0.0), rd=["maskA", "maskB"], wr=["maskA", "maskB"])
            P.op("pool", lambda e, mk=mk, o_=o_: e.memset(mk[64:128, o_:o_ + 64], 0.0), rd=["maskA", "maskB"], wr=["maskA", "maskB"])
        P.op("dve", lambda e: e.memset(self.psA[:], 0.0), wr=[f"ps{i}" for i in range(6)])
        P.barrier()
        itc = [0]

        def do_pair(p):
            r0 = p * 128
            ld = lambda dst, src, key: P.op("sp", lambda e: e.dma_start(out=dst, in_=src[r0:r0 + 128, :]), wr=[key], dma=key)
            tt_ = lambda out, a, b_, op, rd, wr, eng="dve": P.op(eng, lambda e: e.tensor_tensor(out=out, in0=a, in1=b_, op=op), rd=rd, wr=wr)
            v3 = lambda a_: a_.rearrange("p (c t) -> p c t", t=64)
            ld(X1, lwd, "X1")
            P.op("dve", lambda e: e.tensor_tensor_scan(out=X2, data0=cmask, data1=X1, initial=0.0, op0=ALU.mult, op1=ALU.add), rd=["cmask", "X1"], wr=["X2"])
            tt_(X1, X2, X1, ALU.subtract, ["X2", "X1"], ["X1"], "pool")
            P.op("act", lambda e: e.activation(out=X3, in_=X2, func=AF.Exp), rd=["X2"], wr=["X3"])
            P.op("act", lambda e: e.activation(out=eN, in_=X2, func=AF.Exp, scale=-1.0), rd=["X2"], wr=["eN"])
            P.op("act", lambda e: e.activation(out=X1, in_=X1, func=AF.Exp), rd=["X1"], wr=["X1"])
            ld(X2, kkd, "X2")
            P.op("dve", lambda e: e.tensor_scalar(out=X4, in0=X2, scalar1=vcol("a_k_k", p), scalar2=None, op0=ALU.mult), rd=["X2", "cols"], wr=["X4"])
            P.op("act", lambda e: e.activation(out=S1, in_=X4, func=AF.Square), rd=["X4"], wr=["S1"])
            for t2 in range(c.NTT):
                b = self.next_ps()
                P.op("pe", lambda e, b=b, t2=t2: e.matmul(self.psA[:, b, 0:TS], lhsT=bones, rhs=S1[:, t2 * TS:(t2 + 1) * TS], start=True, stop=True),
                     rd=["S1", "bones"], wr=[f"ps{b}"])
                P.op("act", lambda e, b=b, t2=t2: e.activation(out=X5[:, t2 * TS:(t2 + 1) * TS], in_=self.psA[:, b, 0:TS], func=AF.Sqrt),
                     rd=[f"ps{b}", "X5"], wr=["X5"])
            P.op("dve", lambda e: e.tensor_scalar(out=X5, in0=X5, scalar1=1e-12, scalar2=None, op0=ALU.max), rd=["X5"], wr=["X5"])
            P.op("dve", lambda e: e.reciprocal(out=X5, in_=X5), rd=["X5"], wr=["X5"])
            tt_(X4, X4, X5, ALU.mult, ["X4", "X5"], ["X4"])
            P.op("dve", lambda e: e.scalar_tensor_tensor(out=AT, in0=X4, scalar=-1.0, in1=X1, op0=ALU.mult, op1=ALU.mult), rd=["X4", "X1"], wr=["AT"])
            ld(X1, aad, "X1")
            P.op("dve", lambda e: e.tensor_scalar(out=X5, in0=X1, scalar1=vcol("a_k_a", p), scalar2=cols[:, omk0 + p:omk0 + p + 1], op0=ALU.mult, op1=ALU.add),
                 rd=["X1", "cols"], wr=["X5"])
            tt_(X2, X2, X5, ALU.mult, ["X2", "X5"], ["X2"], "pool")
            tt_(X1, X4, X1, ALU.mult, ["X4", "X1"], ["X1"], "pool")
            WCb = X3.rearrange("p (c t) -> p c t", t=64)[:, :, 63:64].broadcast_to([128, NCH, 64])
            tt_(X5, X1, eN, ALU.mult, ["X1", "eN"], ["X5"])
            P.op("act", lambda e: e.activation(out=BT, in_=X5, func=AF.Copy), rd=["X5"], wr=["BT"])
            tt_(v3(BH), v3(X5), WCb, ALU.mult, ["X5", "X3"], ["BH"])
            tt_(X5, X2, eN, ALU.mult, ["X2", "eN", "BT", "BH"], ["X5"])
            P.op("act", lambda e: e.activation(out=KT, in_=X5, func=AF.Copy), rd=["X5"], wr=["KT"])
            tt_(v3(KH), v3(X5), WCb, ALU.mult, ["X5", "X3"], ["KH"])
            ld(X1, rr, "X1")
            tt_(RT, X1, X3, ALU.mult, ["X1", "X3"], ["RT"], "pool")
            P.op("dve", lambda e: e.scalar_tensor_tensor(out=S1, in0=X1, scalar=vcol("a_r_k", p), in1=X2, op0=ALU.mult, op1=ALU.mult),
                 rd=["X1", "X2", "cols", "S1"], wr=["S1"])
            ld(X4, vvd, "X4")
            for t2 in range(c.NTT):
                b = self.next_ps()
                P.op("pe", lambda e, b=b, t2=t2: e.matmul(self.psA[:, b, 0:TS], lhsT=bones, rhs=S1[:, t2 * TS:(t2 + 1) * TS], start=True, stop=True),
                     rd=["S1", "bones"], wr=[f"ps{b}"])
                P.op("dve", lambda e, b=b, t2=t2: e.tensor_tensor(out=BO[:, t2 * TS:(t2 + 1) * TS], in0=self.psA[:, b, 0:TS], in1=X4[:, t2 * TS:(t2 + 1) * TS], op=ALU.mult),
                     rd=[f"ps{b}", "X4", "BO"], wr=["BO"])
            eP = X3
            for nm_, til in [("rt", RT), ("at", AT), ("bt", BT), ("kt", KT), ("bh", BH), ("kh", KH)]:
                P.op("sp", lambda e, nm_=nm_, til=til: e.dma_start(out=dd[nm_][r0:r0 + 128, :], in_=til), rd=[nm_.upper()], dma="st_" + nm_)
            P.op("sp", lambda e: e.dma_start(out=d_bo[r0:r0 + 128, :], in_=BO), rd=["BO"], dma="st_bo")
            P.op("dve", lambda e: e.tensor_copy(out=wcs[:], in_=X3.rearrange("p (c t) -> p c t", t=64)[:, :, 63]), rd=["X3"], wr=["wcs"])
            P.op("sp", lambda e: e.dma_start(out=d_wc[r0:r0 + 128, :], in_=wcs[:]), rd=["wcs"], dma="st_wc")

        for p in range(KC):
            do_pair(p)
        P.barrier()

        NS = 1408
        o1 = 0
        D6 = []
        for bufi in range(2):
            lst = []
            for k6 in range(6):
                v_, o1 = self.r2view(o1, KC * 128, BF16, 128, self.R1)
                lst.append(v_.rearrange("p (q t) -> p q t", t=128))
            D6.append(lst)
        V2b = []
        for bufi in range(2):
            v_, o1 = self.r2view(o1, KC * 128, BF16, 128, self.R1)
            V2b.append(v_.rearrange("p (q c i) -> p q c i", c=2, i=64))
        Yst = []
        for bufi in range(2):
            v_, o1 = self.r2view(o1, KC * 128, F32, 128, self.R1)
            Yst.append(v_.rearrange("p (q t) -> p q t", t=128))
        WCt, o1 = self.r2view(o1, KC * NCH, F32, 128, self.R1)
        WCt = WCt.rearrange("p (q c) -> p q c", c=NCH)
        STt, o1 = self.r2view(o1, KC * 64, F32, 128, self.R1)
        STbt, o1 = self.r2view(o1, KC * 64, BF16, 128, self.R1)
        Ust, o1 = self.r2view(o1, KC * 64, BF16, 128, self.R1)
        SAt, o1 = self.r2view(o1, KC * 64, BF16, 128, self.R1)
        v4 = lambda a_: a_.rearrange("p (q i) -> p q i", i=64)
        STt, STbt, Ust, SAt = v4(STt), v4(STbt), v4(Ust), v4(SAt)
        assert o1 <= self.R1.shape[1], (o1, self.R1.shape)
        sets_off = 384 * 2 + 256 * 2 + 128
        assert sets_off + KC * NS <= self.R2.shape[1], (sets_off + KC * NS, self.R2.shape)

        def pset(p):
            o_ = sets_off + p * NS
            g = lambda a_, n: self.R2[:, o_ + a_:o_ + a_ + n]
            return dict(NN=g(0, 256), MAK=g(256, 128), MB=g(384, 256), PP=[g(640, 256), g(896, 256)], Q=g(1152, 128), BK=g(1280, 128))

        P.op("sp", lambda e: e.dma_start(out=WCt, in_=d_wc.rearrange("(q p) c -> p q c", p=128)), wr=["WCt"], dma="WCt")
        P.op("dve", lambda e: e.memset(STt, 0.0), wr=["STall"])
        P.op("dve", lambda e: e.memset(STbt, 0.0), wr=["STball"])
        P.barrier()
        hctr = [0]

        def next_half():
            b_ = self.next_ps()
            return b_, 0, f"ps{b_}"

        tctr = [0]

        def next_tslot():
            t_ = tctr[0] % 2
            tctr[0] += 1
            return t_, 0, f"psT{t_}"

        nblk = T // 128
        d6n = ["rt", "at", "bt", "kt", "bh", "kh"]

        def load_block(bi):
            bufi = bi % 2
            for k6 in range(6):
                P.op("sp", lambda e, k6=k6: e.dma_start(out=D6[bufi][k6], in_=dd[d6n[k6]][:, bi * 128:(bi + 1) * 128].rearrange("(q p) t -> p q t", p=128)),
                     wr=[f"D6_{bufi}"], dma=f"D6_{bufi}")
            for hh in range(2):
                for cl in range(2):
                    src = vtm[bi * 128 + cl * 64:bi * 128 + (cl + 1) * 64, :].rearrange("s (q h i) -> h s q i", h=2, i=64)[hh]
                    P.op("pool", lambda e, hh=hh, cl=cl, src=src: e.dma_start(out=V2b[bufi][hh * 64:(hh + 1) * 64, :, cl, :], in_=src),
                         wr=[f"V2_{bufi}"], dma=f"V2_{bufi}")

        def chunk_gen(p, ch):
            bi, cl = ch // 2, ch % 2
            bufi = bi % 2
            cs = slice(cl * 64, (cl + 1) * 64)
            RTc, ATc, BTc, KTc, BHc, KHc = [D6[bufi][k6][:, p, :] for k6 in range(6)]
            dk = [f"D6_{bufi}"] * 6
            V2c = V2b[bufi][:, p, cl, :]
            vk = f"V2_{bufi}"
            S_ = pset(p)
            NN, MAK, MBt, PPs, Q, BK = S_["NN"], S_["MAK"], S_["MB"], S_["PP"], S_["Q"], S_["BK"]
            kNN, kMAK, kMB, kQ, kBK = f"NN{p}", f"MAK{p}", f"MB{p}", f"Q{p}", f"BK{p}"
            STp, STbp, Up, SAp = STt[:, p, :], STbt[:, p, :], Ust[:, p, :], SAt[:, p, :]
            kST, kSTb, kU, kSA = f"ST{p}", f"STb{p}", f"U{p}", f"SA{p}"
            hs = [slice(0, 64), slice(64, 128)]
            b1, o1_, k1 = next_half()
            b2, o2_, k2 = b1, 256, k1
            b3, o3_, k3 = next_half()

            def mmats(e):
                ins = None
                for (bank, o_, X, Y) in [(b1, o1_, BTc, ATc), (b1, o1_ + 128, ATc, BTc), (b2, o2_, KTc, ATc), (b3, o3_, BTc, RTc), (b3, o3_ + 128, KTc, RTc)]:
                    for hh in range(2):
                        ps_ = hs[hh]
                        ins = e.matmul(self.psA[ps_, bank, o_ + hh * 64:o_ + (hh + 1) * 64], lhsT=X[ps_, cs], rhs=Y[ps_, cs],
                                       start=True, stop=True, tile_position=(hh * 64, hh * 64))
                return ins

            P.op("pe", mmats, rd=[dk[1], dk[2], dk[3], dk[0]], wr=[k1, k3])
            P.op("dve", lambda e: e.tensor_tensor(out=NN, in0=self.psA[:, b1, o1_:o1_ + 256], in1=maskA[:, 0:256], op=ALU.mult), rd=[k1, "maskA"], wr=[kNN])
            P.op("dve", lambda e: e.tensor_tensor(out=MAK, in0=self.psA[:, b2, o2_:o2_ + 128], in1=maskA[:, 256:384], op=ALU.mult), rd=[k2, "maskA"], wr=[kMAK])
            P.op("dve", lambda e: e.tensor_tensor(out=MBt, in0=self.psA[:, b3, o3_:o3_ + 256], in1=maskB, op=ALU.mult), rd=[k3, "maskB"], wr=[kMB])
            P.op("dve", lambda e: e.tensor_tensor(out=Q, in0=NN[:, 0:128], in1=self.ident[:], op=ALU.add), rd=[kNN, "ident"], wr=[kQ])
            yield
            cur, curk = NN, kNN
            for lvl in range(5):
                bL, oL, kL = next_half()
                pn = PPs[lvl % 2]
                kpn = f"PP{p}_{lvl % 2}"

                def sqm(e, cur=cur, bL=bL, oL=oL):
                    e.matmul(self.psA[:, bL, oL:oL + 128], lhsT=cur[:, 128:256], rhs=cur[:, 0:128], start=True, stop=True)
                    return e.matmul(self.psA[:, bL, oL + 128:oL + 256], lhsT=cur[:, 0:128], rhs=cur[:, 128:256], start=True, stop=True)

                P.op("pe", sqm, rd=[curk], wr=[kL])
                P.op("act", lambda e, pn=pn, bL=bL, oL=oL: e.activation(out=pn, in_=self.psA[:, bL, oL:oL + 256], func=AF.Copy), rd=[kL], wr=[kpn])
                yield
                bQ, oQ, kQb = next_half()
                P.op("pe", lambda e, pn=pn, bQ=bQ, oQ=oQ: e.matmul(self.psA[:, bQ, oQ:oQ + 128], lhsT=pn[:, 128:256], rhs=Q, start=True, stop=True),
                     rd=[kpn, kQ], wr=[kQb])
                P.op("dve", lambda e, bQ=bQ, oQ=oQ: e.tensor_tensor(out=Q, in0=Q, in1=self.psA[:, bQ, oQ:oQ + 128], op=ALU.add), rd=[kQb, kQ], wr=[kQ])
                yield
                cur, curk = pn, kpn
            tb_, to_, kt_ = next_tslot()

            def trs(e):
                ins = None
                for o_, X in [(0, BHc), (64, KHc)]:
                    for hh in range(2):
                        ps_ = hs[hh]
                        ins = e.transpose(out=self.psT[ps_, tb_, to_ + o_:to_ + o_ + 64], in_=X[ps_, cs], identity=self.ident[ps_, ps_],
                                          tile_position=(hh * 64, hh * 64))
                return ins

            P.op("pe", trs, rd=[dk[4], dk[5], "ident"], wr=[kt_])
            P.op("act", lambda e: e.activation(out=BK, in_=self.psT[:, tb_, to_:to_ + 128], func=AF.Copy), rd=[kt_], wr=[kBK])
            yield
            bU, oU, kUb = next_half()

            def mmU(e):
                for hh in range(2):
                    ps_ = hs[hh]
                    e.matmul(self.psA[ps_, bU, oU:oU + 64], lhsT=ATc[ps_, cs], rhs=STbp[ps_, :], start=True, stop=False, tile_position=(hh * 64, hh * 64))
                return e.matmul(self.psA[:, bU, oU:oU + 64], lhsT=MAK, rhs=V2c, start=False, stop=True)

            P.op("pe", mmU, rd=[dk[1], kSTb, "STball", kMAK, vk], wr=[kUb])
            P.op("act", lambda e: e.activation(out=Up, in_=self.psA[:, bU, oU:oU + 64], func=AF.Copy), rd=[kUb], wr=[kU])
            yield
            bS, oS, kSb = next_half()
            P.op("pe", lambda e: e.matmul(self.psA[:, bS, oS:oS + 64], lhsT=Q, rhs=Up, start=True, stop=True), rd=[kQ, kU], wr=[kSb])
            P.op("act", lambda e: e.activation(out=SAp, in_=self.psA[:, bS, oS:oS + 64], func=AF.Copy), rd=[kSb], wr=[kSA])
            yield
            bY, oY, kYb = next_half()

            def mmY(e):
                ins = None
                for hh in range(2):
                    ps_ = hs[hh]
                    e.matmul(self.psA[ps_, bY, oY:oY + 64], lhsT=STbp[ps_, :], rhs=RTc[ps_, cs], start=True, stop=False, tile_position=(hh * 64, hh * 64))
                for hh in range(2):
                    ps_ = hs[hh]
                    e.matmul(self.psA[ps_, bY, oY:oY + 64], lhsT=V2c[ps_, :], rhs=MBt[ps_, 128 + hh * 64:128 + (hh + 1) * 64], start=False, stop=False,
                             tile_position=(hh * 64, hh * 64))
                for hh in range(2):
                    ps_ = hs[hh]
                    ins = e.matmul(self.psA[ps_, bY, oY:oY + 64], lhsT=SAp[ps_, :], rhs=MBt[ps_, hh * 64:(hh + 1) * 64], start=False, stop=True,
                                   tile_position=(hh * 64, hh * 64))
                return ins

            P.op("pe", mmY, rd=[kSTb, "STball", dk[0], vk, kMB, kSA], wr=[kYb])
            P.op("act", lambda e: e.activation(out=Yst[bufi][:, p, cs], in_=self.psA[:, bY, oY:oY + 64], func=AF.Copy), rd=[kYb], wr=[f"Yst{bufi}"])
            yield
            bN, oN, kNb = next_half()

            def mmN(e):
                ins = None
                for hh in range(2):
                    ps_ = hs[hh]
                    e.matmul(self.psA[ps_, bN, oN:oN + 64], lhsT=BK[ps_, 0:64], rhs=SAp[ps_, :], start=True, stop=False, tile_position=(hh * 64, hh * 64))
                for hh in range(2):
                    ps_ = hs[hh]
                    ins = e.matmul(self.psA[ps_, bN, oN:oN + 64], lhsT=BK[ps_, 64:128], rhs=V2c[ps_, :], start=False, stop=True,
                                   tile_position=(hh * 64, hh * 64))
                return ins

            P.op("pe", mmN, rd=[kBK, kSA, vk], wr=[kNb])
            P.op("dve", lambda e: e.scalar_tensor_tensor(out=STp, in0=STp, scalar=WCt[:, p, ch:ch + 1], in1=self.psA[:, bN, oN:oN + 64],
                                                         op0=ALU.mult, op1=ALU.add), rd=[kST, "STall", "WCt", kNb], wr=[kST])
            P.op("act", lambda e: e.activation(out=STbp, in_=STp, func=AF.Copy), rd=[kST, "STball"], wr=[kSTb])
            yield

        load_block(0)
        for ch in range(NCH):
            bi = ch // 2
            if ch % 2 == 0 and bi + 1 < nblk:
                load_block(bi + 1)
            gens = [chunk_gen(p, ch) for p in range(KC)]
            while gens:
                alive = []
                for g in gens:
                    try:
                        next(g)
                        alive.append(g)
                    except StopIteration:
                        pass
                gens = alive
            if ch % 2 == 1:
                bufi = bi % 2
                P.op("sp", lambda e, bi=bi, bufi=bufi: e.dma_start(out=d_y[:, bi * 128:(bi + 1) * 128].rearrange("(q p) t -> p q t", p=128), in_=Yst[bufi]),
                     rd=[f"Yst{bufi}"], dma=f"Yst{bufi}")
        P.barrier()

        def do_epi(p):
            r0 = p * 128
            ld = lambda dst, src, key: P.op("sp", lambda e: e.dma_start(out=dst, in_=src[r0:r0 + 128, :]), wr=[key], dma=key)
            tt_ = lambda out, a, b_, op, rd, wr, eng="dve": P.op(eng, lambda e: e.tensor_tensor(out=out, in0=a, in1=b_, op=op), rd=rd, wr=wr)
            ld(X3, d_y, "X3")
            ld(eN, d_bo, "eN")
            ld(X4, ggd, "X4")
            YT_ = X3
            P.op("act", lambda e: e.activation(out=S1, in_=YT_, func=AF.Copy), rd=["X3"], wr=["S1"])
            P.op("act", lambda e: e.activation(out=RT, in_=YT_, func=AF.Square), rd=["X3"], wr=["RT"])
            mu, var, tG = X1, X2, X5
            for t2 in range(c.NTT):
                sl_ = slice(t2 * TS, (t2 + 1) * TS)
                b = self.next_ps()
                P.op("pe", lambda e, b=b, sl_=sl_: e.matmul(self.psA[:, b, 0:TS], lhsT=bones, rhs=S1[:, sl_], start=True, stop=True), rd=["S1", "bones"], wr=[f"ps{b}"])
                P.op("act", lambda e, b=b, sl_=sl_: e.activation(out=mu[:, sl_], in_=self.psA[:, b, 0:TS], func=AF.Copy, scale=1.0 / 64), rd=[f"ps{b}", "X1"], wr=["X1"])
                b2 = self.next_ps()
                P.op("pe", lambda e, b2=b2, sl_=sl_: e.matmul(self.psA[:, b2, 0:TS], lhsT=bones, rhs=RT[:, sl_], start=True, stop=True), rd=["RT", "bones"], wr=[f"ps{b2}"])
                P.op("dve", lambda e, sl_=sl_: e.tensor_tensor(out=tG[:, sl_], in0=mu[:, sl_], in1=mu[:, sl_], op=ALU.mult), rd=["X1", "X5"], wr=["X5"])
                P.op("dve", lambda e, b2=b2, sl_=sl_: e.scalar_tensor_tensor(out=var[:, sl_], in0=self.psA[:, b2, 0:TS], scalar=1.0 / 64, in1=tG[:, sl_],
                                                                             op0=ALU.mult, op1=ALU.subtract), rd=[f"ps{b2}", "X5", "X2"], wr=["X2"])
            P.op("act", lambda e: e.activation(out=var, in_=var, func=AF.Sqrt, bias=self.consts[:, 2:3], scale=1.0), rd=["X2", "consts"], wr=["X2"])
            P.op("dve", lambda e: e.reciprocal(out=var, in_=var), rd=["X2"], wr=["X2"])
            tt_(YT_, YT_, mu, ALU.subtract, ["X3", "X1"], ["X3"])
            tt_(YT_, YT_, var, ALU.mult, ["X3", "X2"], ["X3"])
            P.op("dve", lambda e: e.tensor_scalar(out=YT_, in0=YT_, scalar1=vcol("a_lnx_g", p), scalar2=vcol("a_lnx_b", p), op0=ALU.mult, op1=ALU.add),
                 rd=["X3", "cols"], wr=["X3"])
            tt_(YT_, YT_, eN, ALU.add, ["X3", "eN"], ["X3"], "pool")
            tt_(zT[:, p, :], YT_, X4, ALU.mult, ["X3", "X4"], ["zT"], "pool")

        for p in range(KC):
            do_epi(p)
        P.barrier()
        nbw = 512 if D >= 512 else D
        self.gemm_tm(lambda k, a, b_: zT[:, k, a:b_], ["zT"], KC, 128, W["a_w_o"], D, nbw, self.resid_epilogue(x_ap, nbw))
        P.barrier()

    def final_norm(self, x_ap, g_ap, out_ap):
        c, P = self.c, self.P
        P.op("sp", lambda e: e.dma_start(out=self.grep[:], in_=g_ap.partition_broadcast(128)), wr=["grep"], dma="grep")
        for tb in range(c.NTB):
            s = tb % 2
            xt = self.xt[s]
            ss = self.small[:, s:s + 1]
            rs = self.small[:, 2 + s:3 + s]
            P.op("sp", lambda e, xt=xt, tb=tb: e.dma_start(out=xt[:], in_=x_ap[tb * 128:(tb + 1) * 128, :]),
                 wr=[f"xt{s}"], dma=f"xt{s}")
            P.op("act", lambda e, xt=xt, ss=ss, s=s: e.activation(out=self.hn[s][:], in_=xt[:], func=AF.Square, accum_out=ss),
                 rd=[f"xt{s}"], wr=[f"hn{s}", f"ss{s}"])
            P.op("act", lambda e, ss=ss, rs=rs: e.activation(out=rs, in_=ss, func=AF.Sqrt, scale=1.0 / c.D, bias=self.consts[:, 0:1]),
                 rd=[f"ss{s}", "consts"], wr=[f"rs{s}"])
            P.op("dve", lambda e, rs=rs: e.reciprocal(out=rs, in_=rs), rd=[f"rs{s}"], wr=[f"rsd{s}", f"rs{s}"])
            P.op("dve", lambda e, xt=xt, rs=rs: e.scalar_tensor_tensor(
                out=xt[:], in0=xt[:], scalar=rs, in1=self.grep[:], op0=ALU.mult, op1=ALU.mult),
                rd=[f"rsd{s}", f"rs{s}", "grep"], wr=[f"xt{s}"])
            P.op("sp", lambda e, xt=xt, tb=tb: e.dma_start(out=out_ap[tb * 128:(tb + 1) * 128, :], in_=xt[:]),
                 rd=[f"xt{s}"], dma=f"xt{s}")

    def finish(self):
        P, nc = self.P, self.nc
        P.barrier()
        with nc.Block() as block:
            @block.sync
            def _(e):
                for f in P.streams["sp"]:
                    f(e)

            @block.scalar
            def _(e):
                for f in P.streams["act"]:
                    f(e)

            @block.vector
            def _(e):
                for f in P.streams["dve"]:
                    f(e)

            @block.gpsimd
            def _(e):
                for f in P.streams["pool"]:
                    f(e)

            @block.tensor
            def _(e):
                for f in P.streams["pe"]:
                    f(e)
        self.es.close()
        return nc


def build_program(cfg, layers=("a", "f0", "b", "f1", "final")):
    B = Builder(cfg)
    c = cfg
    x_in = B.din("x", [c.T, c.D])
    f_norm_g = B.din("f_norm_g", [2, c.D])
    f_w_gu = B.din("f_w_gu", [2, c.D, 2 * c.DFF])
    f_w_d = B.din("f_w_d", [2, c.DFF, c.D])
    final_g = B.din("final_g", [1, c.D])
    W = {}
    for nm, shp in [("b_norm_g", [1, c.D]), ("b_w_in", [c.D, c.WIN]), ("b_b_f", [1, c.H]), ("b_qn_g", [1, 64]), ("b_kn_g", [1, 64]),
                    ("b_on_g", [1, c.D]), ("b_w_o", [c.D, c.D])]:
        if "b" in layers:
            W[nm] = B.din(nm, shp)
    for nm, shp in [("a_norm_g", [1, c.D]), ("a_mix", [6, c.D]), ("a_w_rkv", [3, c.D, c.D]), ("a_w0", [1, c.D]), ("a_w1", [c.D, c.LW]),
                    ("a_w2", [c.LW, c.D]), ("a_a0", [1, c.D]), ("a_a1", [c.D, c.LA]), ("a_a2", [c.LA, c.D]), ("a_g1", [c.D, c.LG]),
                    ("a_g2", [c.LG, c.D]), ("a_k_k", [1, c.D]), ("a_k_a", [1, c.D]), ("a_r_k", [1, c.D]), ("a_lnx_g", [1, c.D]),
                    ("a_lnx_b", [1, c.D]), ("a_w_o", [c.D, c.D])]:
        if "a" in layers:
            W[nm] = B.din(nm, shp)
    out = B.nc.dram_tensor("out", [c.T, c.D], F32, kind="ExternalOutput").ap()
    xres = B.dscr("xres", [c.T, c.D], F32)
    mid = B.dscr("mid", [c.DFF, c.T], BF16)
    B.setup_common()
    P = B.P
    P.op("sp", lambda e: e.dma_start(out=xres, in_=x_in), dma="xcopy")
    P.barrier()
    for L in layers:
        if L == "f0" or L == "f1":
            l = int(L[1])
            B.swiglu(xres, f_norm_g[l:l + 1, :], f_w_gu[l], f_w_d[l], mid)
        elif L == "a":
            B.rwkv(xres, W)
        elif L == "b":
            B.fox(xres, W)
        elif L == "final":
            B.final_norm(xres, final_g, out)
    return B.finish()


def make_in_map(cfg, inputs, core, layers=("a", "f0", "b", "f1", "final")):
    m = {"x": np.ascontiguousarray(inputs["x"][core]),
         "f_norm_g": np.ascontiguousarray(inputs["f_norm_g"]),
         "f_w_gu": np.ascontiguousarray(inputs["f_w_gu"]),
         "f_w_d": np.ascontiguousarray(inputs["f_w_d"]),
         "final_g": np.ascontiguousarray(inputs["final_g"]).reshape(1, -1)}
    if "a" in layers:
        for nm in ["a_norm_g", "a_w0", "a_a0", "a_k_k", "a_k_a", "a_r_k", "a_lnx_g", "a_lnx_b"]:
            m[nm] = np.ascontiguousarray(inputs[nm]).reshape(1, -1)
        for nm in ["a_mix", "a_w_rkv", "a_w1", "a_w2", "a_a1", "a_a2", "a_g1", "a_g2", "a_w_o"]:
            m[nm] = np.ascontiguousarray(inputs[nm][0])
    if "b" in layers:
        for nm in ["b_norm_g", "b_b_f", "b_qn_g", "b_kn_g", "b_on_g"]:
            m[nm] = np.ascontiguousarray(inputs[nm]).reshape(1, -1)
        m["b_w_in"] = np.ascontiguousarray(inputs["b_w_in"][0])
        m["b_w_o"] = np.ascontiguousarray(inputs["b_w_o"][0])
    return m


_CACHE = {}


def kernel(**inputs):
    cfg = Cfg()
    layers = ("a", "f0", "b", "f1", "final")
    if "nc" not in _CACHE:
        _CACHE["nc"] = build_program(cfg, layers)
    nc = _CACHE["nc"]
    inputs = {k: np.asarray(v) for k, v in inputs.items()}
    in_maps = [make_in_map(cfg, inputs, core, layers) for core in range(8)]
    res = run_bass_kernel_spmd(nc, in_maps, core_ids=list(range(8)))
    out = np.stack([np.asarray(r["out"]) for r in res.results], axis=0)
    return out.astype(np.float32)
```

```python
import contextlib
import numpy as np
import concourse.bass as bass
import concourse.mybir as mybir
from concourse.bass_utils import run_bass_kernel_spmd

F32 = mybir.dt.float32
BF16 = mybir.dt.bfloat16
AF = mybir.ActivationFunctionType
ALU = mybir.AluOpType
AX = mybir.AxisListType

RMS_EPS = 1e-6
DEBUG = False
GN_EPS = 64e-5


class Cfg:
    def __init__(self, T=2048, D=2048, DFF=5632, LW=96, LA=96, LG=256):
        self.T, self.D, self.DFF, self.LW, self.LA, self.LG = T, D, DFF, LW, LA, LG
        self.H = D // 64
        self.KC = D // 128
        self.FC = DFF // 128
        self.TS = min(512, T)
        self.NTT = T // self.TS
        self.NTB = T // 128
        self.NCH = T // 64
        self.WIN = 4 * D + 3 * self.H


ENGS = ["sp", "act", "dve", "pool", "pe"]


class Prog:
    def __init__(self, nc, es):
        self.nc, self.es = nc, es
        self.streams = {e: [] for e in ENGS}
        self.sems, self.semval = {}, {}
        self.seen = {e: {} for e in ENGS}
        self.last_w, self.readers = {}, {}
        self.n_ops = 0

    def _sem(self, name):
        if name not in self.sems:
            self.sems[name] = self.es.enter_context(self.nc.semaphore("s_" + name.replace(":", "_")))
            self.semval[name] = 0
        return self.sems[name]

    def op(self, eng, fn, rd=(), wr=(), dma=None):
        deps = {}

        def add(d):
            if d is not None:
                deps[d[0]] = max(deps.get(d[0], 0), d[1])

        for k in rd:
            add(self.last_w.get(k))
        for k in wr:
            add(self.last_w.get(k))
            for s, v in self.readers.get(k, {}).items():
                add((s, v))
        own = "eng:" + eng
        waits = []
        for s, v in deps.items():
            if s == own and eng == "pe":
                continue
            if self.seen[eng].get(s, 0) < v:
                self.seen[eng][s] = v
                waits.append((self._sem(s), v))
        sname = ("dma:" + dma) if dma else own
        inc = 16 if dma else 1
        sem = self._sem(sname)
        self.semval[sname] += inc
        nv = self.semval[sname]

        def emit(e, waits=waits, fn=fn, sem=sem, inc=inc):
            for (s, v) in waits:
                e.wait_ge(s, v)
            ins = fn(e)
            ins.then_inc(sem, inc)

        self.streams[eng].append(emit)
        for k in wr:
            self.last_w[k] = (sname, nv)
            self.readers[k] = {}
        for k in rd:
            r = self.readers.setdefault(k, {})
            r[sname] = max(r.get(sname, 0), nv)
        self.n_ops += 1

    def barrier(self):
        for eng in ENGS:
            waits = []
            for s, v in self.semval.items():
                if v > 0 and self.seen[eng].get(s, 0) < v and s != "eng:" + eng:
                    self.seen[eng][s] = v
                    waits.append((self.sems[s], v))

            def emit(e, waits=waits):
                for (s, v) in waits:
                    e.wait_ge(s, v)

            self.streams[eng].append(emit)
        self.last_w, self.readers = {}, {}


class Builder:
    def __init__(self, cfg):
        self.c = cfg
        self.nc = bass.Bass("TRN2", target_bir_lowering=False)
        self.es = contextlib.ExitStack()
        self.P = Prog(self.nc, self.es)
        self.dram = {}
        self._uid = 0
        self.ps_ctr = 0
        self.wb_ctr = 0
        self.stg_ctr = 0

    def din(self, name, shape):
        self.dram[name] = self.nc.dram_tensor(name, list(shape), F32, kind="ExternalInput").ap()
        return self.dram[name]

    def dscr(self, name, shape, dt):
        self.dram[name] = self.nc.dram_tensor(name, list(shape), dt, kind=("ExternalOutput" if DEBUG else "Internal")).ap()
        return self.dram[name]

    def sb(self, name, shape, dt):
        return self.es.enter_context(self.nc.sbuf_tensor(name, list(shape), dt))

    def psum(self, name, shape, dt):
        return self.es.enter_context(self.nc.psum_tensor(name, list(shape), dt))

    def uid(self, p):
        self._uid += 1
        return f"{p}{self._uid}"

    def setup_common(self):
        c = self.c
        P = self.P
        self.psA = self.psum("psA", [128, 6, 512], F32)
        self.psT = self.psum("psT", [128, 2, 1024], BF16)
        self.ident = self.sb("ident", [128, 128], BF16)
        self.identf = self.sb("identf", [128, 128], F32)
        self.consts = self.sb("consts", [128, 8], F32)
        self.small = self.sb("small", [128, 64], F32)
        self.junk = self.sb("junk", [128, 128], F32)
        self.junk2 = self.sb("junk2", [128, 64], F32)
        self.osb = self.sb("osb", [128, 2, 256], F32)
        self.osq = self.sb("osq", [128, 256], F32)
        nc = self.nc

        def mk_ident(e):
            return e.affine_select(out=self.identf[:], in_=self.identf[:], pattern=[[-1, 128]],
                                   compare_op=ALU.not_equal, fill=1.0, base=0, channel_multiplier=1)

        P.op("pool", lambda e: e.memset(self.identf[:], 0.0), wr=["identf"])
        P.op("pool", mk_ident, rd=["identf"], wr=["identf"])
        P.op("dve", lambda e: e.tensor_copy(out=self.ident[:], in_=self.identf[:]), rd=["identf"], wr=["ident"])

        def mk_consts(e):
            e.memset(self.consts[:, 0:1], RMS_EPS)
            e.memset(self.consts[:, 1:2], 1.0)
            e.memset(self.consts[:, 2:3], GN_EPS)
            e.memset(self.consts[:, 4:8], -0.5)
            return e.memset(self.consts[:, 3:4], 0.0)

        P.op("pool", mk_consts, wr=["consts"])
        hsz = c.KC * (c.T + 2)
        self.TH = c.T // (2 if c.T >= 1024 else 1)
        r1 = max(hsz + c.KC * c.T, c.FC * self.TH, 17 * c.T, 16 * c.T + c.KC * c.T, 26 * c.T)
        self.R1 = self.sb("R1", [128, r1], BF16)
        self.actA = self.R1[:, 0:hsz].rearrange("p (c t) -> p c t", t=c.T + 2)
        self.actB = self.R1[:, hsz:hsz + c.KC * c.T]
        self.WQ = c.KC * 128
        self.NWQ = 12
        r2 = max(self.NWQ * self.WQ, 2 * c.FC * 256, 8 * c.D, 10 * c.T + 1408, 2 * (4 * c.T + c.NTB * 132) + 4 * c.TS + 384, 7 * c.T + 140, 9 * c.D + 8 * c.H)
        self.R2 = self.sb("R2", [128, r2], BF16)
        self.NWQ = r2 // self.WQ
        self.wq_ptr = 0
        self.xt = [self.R2[:, i * 2 * c.D:(i + 1) * 2 * c.D].bitcast(F32) for i in range(2)]
        self.grep = self.R2[:, 4 * c.D:6 * c.D].bitcast(F32)
        self.hn = [self.R2[:, (6 + i) * c.D:(7 + i) * c.D] for i in range(2)]
        self.NSTG = 5
        self.stg = [self.sb(f"stg{i}", [128, 512], F32) for i in range(self.NSTG)]
        P.op("pool", lambda e: e.memset(self.actA[:, :, 0:2], 0.0), wr=["actA"])

    def next_ps(self):
        b = self.ps_ctr % 6
        self.ps_ctr += 1
        return b

    def next_wb(self):
        b = self.wb_ctr % self.NWB
        self.wb_ctr += 1
        return b

    def next_stg(self):
        b = self.stg_ctr % self.NSTG
        self.stg_ctr += 1
        return b

    def norm_transpose(self, x_ap, g_ap):
        c, P = self.c, self.P
        P.op("sp", lambda e: e.dma_start(out=self.grep[:], in_=g_ap.partition_broadcast(128)),
             wr=["grep"], dma="grep")
        for tb in range(c.NTB):
            s = tb % 2
            xt, hn = self.xt[s], self.hn[s]
            ss = self.small[:, s:s + 1]
            rs = self.small[:, 2 + s:3 + s]
            P.op("sp", lambda e, xt=xt, tb=tb: e.dma_start(out=xt[:], in_=x_ap[tb * 128:(tb + 1) * 128, :]),
                 wr=[f"xt{s}"], dma=f"xt{s}")
            P.op("act", lambda e, xt=xt, hn=hn, ss=ss: e.activation(out=hn[:], in_=xt[:], func=AF.Square, accum_out=ss),
                 rd=[f"xt{s}"], wr=[f"hn{s}", f"ss{s}"])
            P.op("act", lambda e, ss=ss, rs=rs: e.activation(out=rs, in_=ss, func=AF.Sqrt, scale=1.0 / c.D,
                                                                 bias=self.consts[:, 0:1]),
                 rd=[f"ss{s}", "consts"], wr=[f"rs{s}"])
            P.op("dve", lambda e, rs=rs: e.reciprocal(out=rs, in_=rs), rd=[f"rs{s}"], wr=[f"rsd{s}", f"rs{s}"])
            P.op("dve", lambda e, xt=xt, hn=hn, rs=rs: e.scalar_tensor_tensor(
                out=hn[:], in0=xt[:], scalar=rs, in1=self.grep[:], op0=ALU.mult, op1=ALU.mult),
                rd=[f"xt{s}", f"rsd{s}", f"rs{s}", "grep"], wr=[f"hn{s}"])
            for c0 in range(0, c.KC, 8):
                nch = min(8, c.KC - c0)
                tbk = (tb * ((c.KC + 7) // 8) + c0 // 8) % 2

                def tr(e, hn=hn, c0=c0, nch=nch, tbk=tbk):
                    ins = None
                    for i in range(nch):
                        ins = e.transpose(out=self.psT[:, tbk, i * 128:(i + 1) * 128],
                                          in_=hn[:, (c0 + i) * 128:(c0 + i + 1) * 128], identity=self.ident[:])
                    return ins

                P.op("pe", tr, rd=[f"hn{s}", "ident"], wr=[f"psT{tbk}"])
                eng = "act" if (c0 // 8) % 2 == 0 else "dve"

                def ev(e, c0=c0, nch=nch, tbk=tbk, tb=tb, eng=eng):
                    src = self.psT[:, tbk, 0:nch * 128].rearrange("p (c t) -> p c t", t=128)
                    dst = self.actA[:, c0:c0 + nch, 2 + tb * 128:2 + (tb + 1) * 128]
                    if eng == "act":
                        return e.activation(out=dst, in_=src, func=AF.Copy)
                    return e.tensor_copy(out=dst, in_=src)

                P.op(eng, ev, rd=[f"psT{tbk}"], wr=["actA"])

    def hT(self, k, t0, t1):
        return self.actA[:, k, 2 + t0:2 + t1]

    def load_w(self, w_ap, kcn, kp, col0, ncols):
        nq = (kcn * ncols + self.WQ - 1) // self.WQ
        ext = getattr(self, "wext", None)
        next_ = (ext.shape[1] // self.WQ) if ext is not None else 0
        if self.wq_ptr < self.NWQ and self.wq_ptr + nq > self.NWQ:
            self.wq_ptr = self.NWQ if nq <= next_ else 0
        if self.wq_ptr >= self.NWQ and self.wq_ptr + nq > self.NWQ + next_:
            self.wq_ptr = 0
        q0 = self.wq_ptr
        self.wq_ptr += nq
        keys = [f"wq{q0 + i}" for i in range(nq)]
        if q0 < self.NWQ:
            view = self.R2[0:kp, q0 * self.WQ:q0 * self.WQ + kcn * ncols].rearrange("p (c n) -> p c n", n=ncols)
        else:
            qe = q0 - self.NWQ
            view = ext[0:kp, qe * self.WQ:qe * self.WQ + kcn * ncols].rearrange("p (c n) -> p c n", n=ncols)
        src = w_ap[:, col0:col0 + ncols].rearrange("(c p) n -> p c n", p=kp)
        self.P.op("pool", lambda e: e.dma_start(out=view, in_=src), wr=keys, dma=f"wq{q0}")
        return keys, view

    def gemm_fm(self, xfn, xkeys, kcn, kp, groups, epilogue, t0=0, t1=None):
        c, P = self.c, self.P
        t1 = c.T if t1 is None else t1
        for gi, grp in enumerate(groups):
            wts = [self.load_w(w_ap, kcn, kp, col0, ncols) + (ncols,) for (w_ap, col0, ncols) in grp]
            for tt in range((t1 - t0) // c.TS):
                lo, hi = t0 + tt * c.TS, t0 + (tt + 1) * c.TS
                banks = []
                for (s, view, ncols) in wts:
                    b = self.next_ps()
                    banks.append((b, ncols))

                    def mm(e, view=view, ncols=ncols, b=b, lo=lo, hi=hi):
                        ins = None
                        for k in range(kcn):
                            ins = e.matmul(self.psA[0:ncols, b, 0:hi - lo], lhsT=view[:, k, :], rhs=xfn(k, lo, hi),
                                           start=(k == 0), stop=(k == kcn - 1))
                        return ins

                    P.op("pe", mm, rd=s + xkeys, wr=[f"ps{b}"])
                epilogue(gi, tt, lo, hi, banks)

    def gemm_tm(self, xfn, xkeys, kcn, kp, w_ap, ncols_total, nbw, epilogue, t0=0, t1=None):
        c, P = self.c, self.P
        t1 = c.T if t1 is None else t1
        for nb in range(ncols_total // nbw):
            s, view = self.load_w(w_ap, kcn, kp, nb * nbw, nbw)
            for tb in range((t1 - t0) // 128):
                lo = t0 + tb * 128
                b = self.next_ps()

                def mm(e, view=view, b=b, lo=lo):
                    ins = None
                    for k in range(kcn):
                        ins = e.matmul(self.psA[:, b, 0:nbw], lhsT=xfn(k, lo, lo + 128), rhs=view[:, k, :],
                                       start=(k == 0), stop=(k == kcn - 1))
                    return ins

                P.op("pe", mm, rd=s + xkeys, wr=[f"ps{b}"])
                epilogue(nb, lo, b)

    def resid_epilogue(self, x_ap, nbw):
        P = self.P
        cnt = [0]

        def ep(nb, lo, b):
            s = self.next_stg()
            stg = self.stg[s]
            P.op("sp", lambda e: e.dma_start(out=stg[:, 0:nbw], in_=x_ap[lo:lo + 128, nb * nbw:(nb + 1) * nbw]),
                 wr=[f"stg{s}"], dma=f"stg{s}")
            eng = "dve"
            P.op(eng, lambda e: e.tensor_tensor(out=stg[:, 0:nbw], in0=stg[:, 0:nbw], in1=self.psA[:, b, 0:nbw], op=ALU.add),
                 rd=[f"ps{b}", f"stg{s}"], wr=[f"stg{s}"])
            P.op("act", lambda e: e.dma_start(out=x_ap[lo:lo + 128, nb * nbw:(nb + 1) * nbw], in_=stg[:, 0:nbw]),
                 rd=[f"stg{s}"], dma=f"stg{s}")
            cnt[0] += 1

        return ep

    def swiglu(self, x_ap, g_ap, wgu_ap, wd_ap, mid_ap):
        c, P = self.c, self.P
        self.norm_transpose(x_ap, g_ap)
        P.barrier()
        groups = [[(wgu_ap, j * 128, 128), (wgu_ap, c.DFF + j * 128, 128)] for j in range(c.FC)]
        if not hasattr(self, "_sg"):
            self._sg = [self.sb(self.uid("sg"), [128, c.TS], F32) for _ in range(2)]
            self._mo = [self.sb(self.uid("mo"), [128, c.TS], BF16) for _ in range(2)]
        sg, mo = self._sg, self._mo
        it = [0]

        def ep(gi, tt, lo, hi, banks):
            s = it[0] % 2
            it[0] += 1
            (bg, _), (bu, _) = banks
            P.op("act", lambda e: e.activation(out=sg[s][:], in_=self.psA[:, bg, 0:c.TS], func=AF.Silu),
                 rd=[f"ps{bg}"], wr=[f"sg{s}"])
            P.op("dve", lambda e: e.tensor_tensor(out=mo[s][:], in0=sg[s][:], in1=self.psA[:, bu, 0:c.TS], op=ALU.mult),
                 rd=[f"sg{s}", f"ps{bu}"], wr=[f"mo{s}"])
            P.op("sp", lambda e: e.dma_start(out=mid_ap[gi * 128:(gi + 1) * 128, lo:hi], in_=mo[s][:]),
                 rd=[f"mo{s}"], dma=f"mo{s}")

        self.gemm_fm(self.hT, ["actA"], c.KC, 128, groups, ep)
        P.barrier()
        nkh = c.T // self.TH
        FCH = c.FC // nkh
        nbw = 256
        self.wq_ptr = 0
        free0 = FCH * c.T
        nfree = (self.R1.shape[1] - free0) // self.WQ
        self.wext = self.R1[:, free0:free0 + nfree * self.WQ] if nfree > 0 else None
        for kh in range(nkh):
            midv = self.R1[:, 0:FCH * c.T].rearrange("p (c t) -> p c t", t=c.T)
            for cc in range(FCH):
                P.op("sp", lambda e, cc=cc, kh=kh: e.dma_start(out=midv[:, cc, :], in_=mid_ap[(kh * FCH + cc) * 128:(kh * FCH + cc + 1) * 128, :]),
                     wr=["actB"], dma=f"actB{cc % 4}")
            xfn = lambda k, a, b_: midv[:, k, a:b_]
            self.gemm_tm(xfn, ["actB"], FCH, 128, wd_ap[kh * FCH * 128:(kh + 1) * FCH * 128, :], c.D, nbw, self.resid_epilogue(x_ap, nbw))
            P.barrier()
        self.wext = None
        self.wq_ptr = 0

    def store_fm(self, dst_ap, row_of_group):
        P = self.P
        it = [0]

        def ep(gi, tt, lo, hi, banks):
            for bi, (b, ncols) in enumerate(banks):
                s = self.next_stg()
                stg = self.stg[s]
                eng = "act" if it[0] % 2 == 0 else "dve"
                it[0] += 1
                n = hi - lo
                if eng == "act":
                    P.op("act", lambda e, b=b, ncols=ncols, stg=stg, n=n: e.activation(out=stg[0:ncols, 0:n], in_=self.psA[0:ncols, b, 0:n], func=AF.Copy),
                         rd=[f"ps{b}"], wr=[f"stg{s}"])
                else:
                    P.op("dve", lambda e, b=b, ncols=ncols, stg=stg, n=n: e.tensor_copy(out=stg[0:ncols, 0:n], in_=self.psA[0:ncols, b, 0:n]),
                         rd=[f"ps{b}"], wr=[f"stg{s}"])
                r0 = row_of_group(gi, bi)
                P.op("sp", lambda e, r0=r0, ncols=ncols, stg=stg, n=n, lo=lo, hi=hi: e.dma_start(out=dst_ap[r0:r0 + ncols, lo:hi], in_=stg[0:ncols, 0:n]),
                     rd=[f"stg{s}"], dma=f"stg{s}")

        return ep

    def store_tm(self, dst_ap, nbw, col0=0):
        P = self.P
        it = [0]

        def ep(nb, lo, b):
            s = self.next_stg()
            stg = self.stg[s]
            eng = "act" if it[0] % 2 == 0 else "dve"
            it[0] += 1
            if eng == "act":
                P.op("act", lambda e: e.activation(out=stg[:, 0:nbw], in_=self.psA[:, b, 0:nbw], func=AF.Copy), rd=[f"ps{b}"], wr=[f"stg{s}"])
            else:
                P.op("dve", lambda e: e.tensor_copy(out=stg[:, 0:nbw], in_=self.psA[:, b, 0:nbw]), rd=[f"ps{b}"], wr=[f"stg{s}"])
            P.op("sp", lambda e: e.dma_start(out=dst_ap[lo:lo + 128, col0 + nb * nbw:col0 + (nb + 1) * nbw], in_=stg[:, 0:nbw]),
                 rd=[f"stg{s}"], dma=f"stg{s}")

        return ep

    def r2view(self, off, n, dt=BF16, parts=128, arena=None):
        w = n * (2 if dt == F32 else 1)
        arena = self.R2 if arena is None else arena
        v = arena[0:parts, off:off + w]
        if dt == F32:
            v = v.bitcast(F32)
        return v, off + w

    def transpose_to_fm(self, src_fn, dst3):
        c, P = self.c, self.P
        for tb in range(c.NTB):
            for c0 in range(0, c.KC, 8):
                nch = min(8, c.KC - c0)
                tbk = (tb * ((c.KC + 7) // 8) + c0 // 8) % 2

                def tr(e, tb=tb, c0=c0, nch=nch, tbk=tbk):
                    ins = None
                    src = src_fn(tb)
                    for i in range(nch):
                        ins = e.transpose(out=self.psT[:, tbk, i * 128:(i + 1) * 128],
                                          in_=src[:, (c0 + i) * 128:(c0 + i + 1) * 128], identity=self.ident[:])
                    return ins

                P.op("pe", tr, rd=["ztm", "ident"], wr=[f"psT{tbk}"])
                eng = "act" if (tb + c0 // 8) % 2 == 0 else "dve"

                def ev(e, c0=c0, nch=nch, tbk=tbk, tb=tb, eng=eng):
                    src = self.psT[:, tbk, 0:nch * 128].rearrange("p (c t) -> p c t", t=128)
                    dst = dst3[:, c0:c0 + nch, tb * 128:(tb + 1) * 128]
                    if eng == "act":
                        return e.activation(out=dst, in_=src, func=AF.Copy)
                    return e.tensor_copy(out=dst, in_=src)

                P.op(eng, ev, rd=[f"psT{tbk}"], wr=["zT"])

    def fox(self, x_ap, W):
        c, P = self.c, self.P
        D, T, H, KC, TS = c.D, c.T, c.H, c.KC, c.TS
        w_in = W["b_w_in"]
        qk = self.dscr("b_qk", [2 * D, T], F32)
        fa = self.dscr("b_fa", [3 * H, T], F32)
        vg = self.dscr("b_vg", [T, 2 * D], F32)
        fat = self.dscr("b_fat", [T, 3 * H], F32)
        qhat = self.dscr("b_qhat", [2 * D, T], BF16)
        qaug = self.dscr("b_qaug", [H, 4, T], BF16)
        kaug = self.dscr("b_kaug", [H, 4, T], BF16)
        akd = self.dscr("b_akd", [H, T], BF16)
        vpr = self.dscr("b_vpr", [T, D], BF16)
        self.norm_transpose(x_ap, W["b_norm_g"])
        P.barrier()
        groups = [[(w_in, j * 128, 128)] for j in range(2 * KC)]
        self.gemm_fm(self.hT, ["actA"], KC, 128, groups, self.store_fm(qk, lambda gi, bi: gi * 128))
        self.gemm_fm(self.hT, ["actA"], KC, 128, [[(w_in, 4 * D, 3 * H)]], self.store_fm(fa, lambda gi, bi: 0))
        self.gemm_tm(self.hT, ["actA"], KC, 128, w_in[:, 2 * D:4 * D], 2 * D, 512 if D >= 512 else 2 * D,
                     self.store_tm(vg, 512 if D >= 512 else 2 * D))
        self.gemm_tm(self.hT, ["actA"], KC, 128, w_in[:, 4 * D:4 * D + 3 * H], 3 * H, 3 * H, self.store_tm(fat, 3 * H))
        P.barrier()
        off = 0
        ft, off = self.r2view(off, T, F32, H, self.R1)
        f2, off = self.r2view(off, T, F32, H, self.R1)
        ct, off = self.r2view(off, T, F32, H, self.R1)
        qa, off = self.r2view(off, 4 * T, BF16, H, self.R1)
        ka, off = self.r2view(off, 4 * T, BF16, H, self.R1)
        akt, off = self.r2view(off, T, F32, H, self.R1)
        akb, off = self.r2view(off, T, BF16, H, self.R1)
        qa = qa.rearrange("p (r t) -> p r t", t=T)
        ka = ka.rearrange("p (r t) -> p r t", t=T)
        nb = self.small[0:H, 8:9]
        P.op("sp", lambda e: e.dma_start(out=ft, in_=fa[0:H, :]), wr=["ft"], dma="ft")
        P.op("sp", lambda e: e.dma_start(out=akt, in_=fa[H:2 * H, :]), wr=["akt"], dma="akt")
        P.op("sp", lambda e: e.dma_start(out=nb, in_=W["b_b_f"].rearrange("o h -> h o")), wr=["nb"], dma="nb")
        P.op("dve", lambda e: e.tensor_scalar(out=nb, in0=nb, scalar1=-1.0, scalar2=None, op0=ALU.mult), rd=["nb"], wr=["nb"])
        P.barrier()
        P.op("act", lambda e: e.activation(out=f2, in_=ft, func=AF.Exp, scale=-1.0, bias=nb), rd=["ft", "nb"], wr=["f2"])
        P.op("act", lambda e: e.activation(out=f2, in_=f2, func=AF.Ln, scale=1.0, bias=self.consts[0:H, 1:2]), rd=["consts"], wr=["f2"])
        P.op("dve", lambda e: e.tensor_scalar(out=f2, in0=f2, scalar1=-0.5, scalar2=None, op0=ALU.mult), rd=["f2"], wr=["f2"])
        P.op("dve", lambda e: e.tensor_tensor_scan(out=ct, data0=f2, data1=f2, initial=0.0, op0=ALU.add, op1=ALU.add), rd=["f2"], wr=["ct"])
        P.op("dve", lambda e: e.tensor_copy(out=qa[:, 0, :], in_=ct), rd=["ct"], wr=["qa"])
        P.op("dve", lambda e: e.tensor_tensor(out=qa[:, 1, :], in0=ct, in1=qa[:, 0, :], op=ALU.subtract), rd=["ct", "qa"], wr=["qa"])
        P.op("pool", lambda e: e.memset(qa[:, 2:4, :], 1.0), wr=["qa"])
        P.op("pool", lambda e: e.memset(ka[:, 0:2, :], 1.0), wr=["ka"])
        P.op("dve", lambda e: e.tensor_scalar(out=ka[:, 2:4, :], in0=qa[:, 0:2, :], scalar1=-1.0, scalar2=None, op0=ALU.mult), rd=["qa"], wr=["ka"])
        P.op("act", lambda e: e.activation(out=akb, in_=akt, func=AF.Sigmoid), rd=["akt"], wr=["akb"])
        P.op("sp", lambda e: e.dma_start(out=qaug, in_=qa), rd=["qa"], dma="qa")
        P.op("sp", lambda e: e.dma_start(out=kaug, in_=ka), rd=["ka"], dma="ka")
        P.op("sp", lambda e: e.dma_start(out=akd, in_=akb), rd=["akb"], dma="akb")
        P.barrier()
        sets4 = []
        off = 0
        bones, off = self.r2view(off, 128, BF16)
        for si, arena in enumerate([self.R2, self.R1]):
            o_ = off if si == 0 else 0
            kt_, o_ = self.r2view(o_, T + 2, F32, 128, arena)
            tm_, o_ = self.r2view(o_, T, F32, 128, arena)
            ak_, o_ = self.r2view(o_, T, BF16, 128, arena)
            sq_, o_ = self.r2view(o_, T, BF16, 128, arena)
            ob_, o_ = self.r2view(o_, T, BF16, 128, arena)
            sets4.append((kt_, tm_, ak_, sq_, ob_))
        gq = self.small[:, 10:11]
        gk = self.small[:, 11:12]
        P.op("dve", lambda e: e.memset(bones, 0.0), wr=["bones"])
        P.op("dve", lambda e: e.memset(bones[0:64, 0:64], 1.0), wr=["bones"])
        P.op("dve", lambda e: e.memset(bones[64:128, 64:128], 1.0), wr=["bones"])
        for si in range(2):
            P.op("pool", lambda e, si=si: e.memset(sets4[si][0][:, 0:2], 0.0), wr=[f"kt{si}"])
        for hh in range(2):
            P.op("sp", lambda e, hh=hh: e.dma_start(out=gq[hh * 64:(hh + 1) * 64, :], in_=W["b_qn_g"].rearrange("o n -> n o")), wr=["gq"], dma="gq")
            P.op("sp", lambda e, hh=hh: e.dma_start(out=gk[hh * 64:(hh + 1) * 64, :], in_=W["b_kn_g"].rearrange("o n -> n o")), wr=["gk"], dma="gk")
        P.op("dve", lambda e: e.tensor_scalar(out=gq, in0=gq, scalar1=0.125, scalar2=None, op0=ALU.mult), rd=["gq"], wr=["gq"])
        P.barrier()

        def b4_chunk(which, p, si):
            kt, tmpf, akx, sq, outb = sets4[si]
            kkt, ktm, kak, ksq, kob = f"kt{si}", f"tmpf{si}", f"akx{si}", f"sq{si}", f"outb{si}"
            row0 = which * D + p * 128
            P.op("sp", lambda e: e.dma_start(out=kt[:, 2:T + 2], in_=qk[row0:row0 + 128, :]), wr=[kkt], dma=kkt)
            if which == 1:
                for hh in range(2):
                    P.op("sp", lambda e, hh=hh: e.dma_start(out=akx[hh * 64:(hh + 1) * 64, :], in_=akd[2 * p + hh:2 * p + hh + 1, :].partition_broadcast(64)),
                         wr=[kak], dma=kak)
                P.op("dve", lambda e: e.tensor_tensor(out=tmpf, in0=kt[:, 1:T + 1], in1=kt[:, 2:T + 2], op=ALU.subtract), rd=[kkt], wr=[ktm])
                P.op("dve", lambda e: e.tensor_tensor(out=tmpf, in0=tmpf, in1=akx, op=ALU.mult), rd=[kak, ktm], wr=[ktm])
                P.op("dve", lambda e: e.tensor_tensor(out=kt[:, 2:T + 2], in0=kt[:, 2:T + 2], in1=tmpf, op=ALU.add), rd=[ktm, kkt], wr=[kkt])
            P.op("act", lambda e: e.activation(out=sq, in_=kt[:, 2:T + 2], func=AF.Square), rd=[kkt], wr=[ksq])
            for tt in range(c.NTT):
                b = self.next_ps()
                P.op("pe", lambda e, b=b, tt=tt: e.matmul(self.psA[:, b, 0:TS], lhsT=bones, rhs=sq[:, tt * TS:(tt + 1) * TS], start=True, stop=True),
                     rd=[ksq, "bones"], wr=[f"ps{b}"])
                P.op("act", lambda e, b=b, tt=tt: e.activation(out=tmpf[:, tt * TS:(tt + 1) * TS], in_=self.psA[:, b, 0:TS], func=AF.Sqrt,
                                                             scale=1.0 / 64, bias=self.consts[:, 0:1]),
                     rd=[f"ps{b}", "consts", ktm], wr=[ktm])
            P.op("dve", lambda e: e.reciprocal(out=tmpf, in_=tmpf), rd=[ktm], wr=[ktm])
            gcol = gq if which == 0 else gk
            P.op("dve", lambda e: e.scalar_tensor_tensor(out=outb, in0=kt[:, 2:T + 2], scalar=gcol, in1=tmpf, op0=ALU.mult, op1=ALU.mult),
                 rd=[kkt, ktm, "gq", "gk"], wr=[kob])
            P.op("sp", lambda e: e.dma_start(out=qhat[row0:row0 + 128, :], in_=outb), rd=[kob], dma=kob)

        cidx = 0
        for which in range(2):
            for p in range(KC):
                b4_chunk(which, p, cidx % 2)
                cidx += 1
        P.barrier()
        off = 0
        ong, off = self.r2view(off, D, F32)
        sets5 = []
        need5 = 7 * D + 8 * H
        r1off5 = 0 if need5 <= c.KC * (T + 2) else c.KC * (T + 2) + c.KC * T
        for si, arena in enumerate([self.R2, self.R1]):
            o_ = off if si == 0 else r1off5
            vt_, o_ = self.r2view(o_, D, F32, 128, arena)
            vp_, o_ = self.r2view(o_, D, F32, 128, arena)
            gt_, o_ = self.r2view(o_, D, F32, 128, arena)
            vo_, o_ = self.r2view(o_, D, BF16, 128, arena)
            al_, o_ = self.r2view(o_, 3 * H, F32, 128, arena)
            av_, o_ = self.r2view(o_, H, F32, 128, arena)
            sets5.append((vt_, vp_, gt_, vo_, al_, av_))
        G3 = self.actB.rearrange("p (b d) -> p b d", d=D)
        ztm = self.R1[:, 0:c.NTB * D].rearrange("p (b d) -> p b d", d=D)
        P.op("sp", lambda e: e.dma_start(out=ong, in_=W["b_on_g"].partition_broadcast(128)), wr=["ong"], dma="ong")

        def b5_block(tb, si):
            vt, vp, gt, vo, al, av = sets5[si]
            kvt, kvp, kgt, kvo, kal, kav = [f"{n_}{si}" for n_ in ["vt", "vp", "gt", "vo", "al", "av"]]
            r0 = tb * 128
            P.op("sp", lambda e: e.dma_start(out=vt, in_=vg[r0:r0 + 128, 0:D]), wr=[kvt], dma=kvt)
            P.op("sp", lambda e: e.dma_start(out=gt, in_=vg[r0:r0 + 128, D:2 * D]), wr=[kgt], dma=kgt)
            P.op("sp", lambda e: e.dma_start(out=al, in_=fat[r0:r0 + 128, :]), wr=[kal], dma=kal)
            if tb == 0:
                P.op("pool", lambda e: e.memset(vp[0:1, :], 0.0), wr=[kvp])
                P.op("sp", lambda e: e.dma_start(out=vp[1:128, :], in_=vg[0:127, 0:D]), wr=[kvp], dma=kvp)
            else:
                P.op("sp", lambda e: e.dma_start(out=vp, in_=vg[r0 - 1:r0 + 127, 0:D]), wr=[kvp], dma=kvp)
            P.op("act", lambda e: e.activation(out=av, in_=al[:, 2 * H:3 * H], func=AF.Sigmoid), rd=[kal], wr=[kav])
            P.op("dve", lambda e: e.tensor_tensor(out=vp, in0=vp, in1=vt, op=ALU.subtract), rd=[kvp, kvt], wr=[kvp])
            P.op("dve", lambda e: e.tensor_tensor(out=vp.rearrange("p (h n) -> p h n", n=64), in0=vp.rearrange("p (h n) -> p h n", n=64),
                                                  in1=av.unsqueeze(2).broadcast_to([128, H, 64]), op=ALU.mult), rd=[kvp, kav], wr=[kvp])
            P.op("dve", lambda e: e.tensor_tensor(out=vo, in0=vp, in1=vt, op=ALU.add), rd=[kvp, kvt], wr=[kvo])
            P.op("sp", lambda e: e.dma_start(out=vpr[r0:r0 + 128, :], in_=vo), rd=[kvo], dma=kvo)
            P.op("act", lambda e: e.activation(out=gt, in_=gt, func=AF.Sigmoid), rd=[kgt], wr=[kgt])
            P.op("pool", lambda e: e.tensor_tensor(out=G3[:, tb, :], in0=gt, in1=ong, op=ALU.mult), rd=[kgt, "ong"], wr=["G"])

        for tb in range(c.NTB):
            b5_block(tb, tb % 2)
        P.barrier()
        off = 0
        QA, KA, VV = [], [], []
        for i in range(2):
            qs, ks = [], []
            for hh in range(2):
                v_, off = self.r2view(off, T, BF16)
                qs.append(v_)
                v_, off = self.r2view(off, T, BF16)
                ks.append(v_)
            QA.append(qs)
            KA.append(ks)
            v_, off = self.r2view(off, c.NTB * 2 * 66, BF16)
            VV.append(v_.rearrange("p (b h n) -> p b h n", h=2, n=66))
        PT = []
        for i in range(4):
            v_, off = self.r2view(off, TS, BF16)
            PT.append(v_)
        tri, off = self.r2view(off, 128, BF16)
        trif, off = self.r2view(off, 128, F32)

        P.op("pool", lambda e: e.memset(trif, 1.0), wr=["trif"])

        def mk_tri(e):
            return e.affine_select(out=trif, in_=trif, pattern=[[1, 128]], compare_op=ALU.is_ge, fill=0.0, base=0, channel_multiplier=-1)

        P.op("pool", mk_tri, rd=["trif"], wr=["trif"])
        P.op("dve", lambda e: e.tensor_copy(out=tri, in_=trif), rd=["trif"], wr=["tri"])
        for i in range(2):
            P.op("pool", lambda e, i=i: e.memset(VV[i], 1.0), wr=[f"VV{i}"])
        nq = T // TS
        nsub = TS // 128
        NPT = len(PT)

        def loads(p):
            i = p % 2
            for hh in range(2):
                h = 2 * p + hh
                P.op("sp", lambda e, i=i, hh=hh, h=h: e.dma_start(out=QA[i][hh][0:64, :], in_=qhat[h * 64:(h + 1) * 64, :]), wr=[f"QA{i}{hh}"], dma=f"QA{i}{hh}")
                P.op("sp", lambda e, i=i, hh=hh, h=h: e.dma_start(out=QA[i][hh][64:68, :], in_=qaug[h]), wr=[f"QA{i}{hh}"], dma=f"QA{i}{hh}")
                P.op("sp", lambda e, i=i, hh=hh, h=h: e.dma_start(out=KA[i][hh][0:64, :], in_=qhat[D + h * 64:D + (h + 1) * 64, :]), wr=[f"KA{i}{hh}"], dma=f"KA{i}{hh}")
                P.op("sp", lambda e, i=i, hh=hh, h=h: e.dma_start(out=KA[i][hh][64:68, :], in_=kaug[h]), wr=[f"KA{i}{hh}"], dma=f"KA{i}{hh}")
                P.op("sp", lambda e, i=i, hh=hh, h=h: e.dma_start(out=VV[i][:, :, hh, 0:64], in_=vpr[:, h * 64:(h + 1) * 64].rearrange("(b s) n -> s b n", s=128)),
                     wr=[f"VV{i}"], dma=f"VV{i}{hh}")

        epc = [0]

        def epilogue(h, I, ob):
            par = epc[0] % 2
            epc[0] += 1
            psv = self.psA[:, ob, 0:nsub * 66].rearrange("p (u n) -> p u n", n=66)
            oc = psv[:, :, 0:64]
            k0 = 16 + par * 16
            rc4 = self.small[:, k0:k0 + nsub]
            ssq4 = self.small[:, k0 + 4:k0 + 4 + nsub]
            rst4 = self.small[:, k0 + 8:k0 + 8 + nsub]
            osb = self.osb[:, par, 0:nsub * 64]
            osb3 = osb.rearrange("p (u n) -> p u n", n=64)
            sq = self.osq[:, 0:nsub * 64]
            kk = f"ep{par}"
            tb0 = I * nsub
            bc = lambda v_: v_.unsqueeze(2).broadcast_to([128, nsub, 64])
            P.op("dve", lambda e: e.reciprocal(out=rc4, in_=psv[:, :, 64]), rd=[f"ps{ob}"], wr=[kk + "rc"])
            P.op("dve", lambda e: e.tensor_tensor(out=osb3, in0=oc, in1=bc(rc4), op=ALU.mult), rd=[f"ps{ob}", kk + "rc"], wr=[kk + "osb"])
            P.op("dve", lambda e: e.tensor_tensor(out=sq, in0=osb, in1=osb, op=ALU.mult), rd=[kk + "osb"], wr=["osq"])
            P.op("dve", lambda e: e.tensor_reduce(out=ssq4, in_=sq.rearrange("p (u n) -> p u n", n=64), axis=AX.X, op=ALU.add), rd=["osq"], wr=[kk + "ss"])
            P.op("pool", lambda e: e.tensor_scalar(out=rst4, in0=ssq4, scalar1=1.0 / 64, scalar2=RMS_EPS, op0=ALU.mult, op1=ALU.add), rd=[kk + "ss"], wr=[kk + "rst"])
            P.op("pool", lambda e: e.tensor_tensor(out=rst4, in0=rst4, in1=self.consts[:, 4:4 + nsub], op=ALU.pow), rd=[kk + "rst", "consts"], wr=[kk + "rst2"])
            P.op("dve", lambda e: e.tensor_tensor(out=osb3, in0=osb3, in1=bc(rst4), op=ALU.mult), rd=[kk + "osb", kk + "rst2"], wr=[kk + "osb"])
            P.op("dve", lambda e: e.tensor_tensor(out=ztm[:, tb0:tb0 + nsub, h * 64:(h + 1) * 64], in0=osb3, in1=G3[:, tb0:tb0 + nsub, h * 64:(h + 1) * 64], op=ALU.mult),
                 rd=[kk + "osb", "G"], wr=["ztm"])

        gctr = [0, 0]
        obank = [0]
        LOOK = 2
        loads(0)
        for p in range(KC):
            i = p % 2
            if p + 1 < KC:
                loads(p + 1)
            steps = []
            for hh in range(2):
                for I in range(nq):
                    ob = 4 + (obank[0] % 2)
                    obank[0] += 1
                    nJ = (I + 1) * nsub
                    for J in range(nJ):
                        steps.append((hh, I, J, nJ, ob))

            def emit_S(st, i=i):
                hh, I, J, nJ, ob = st
                q_hi = (I + 1) * TS
                t_lo = max(I * TS, J * 128)
                N = q_hi - t_lo
                sbk = gctr[0] % 4
                gctr[0] += 1
                pti = gctr[1] % NPT
                gctr[1] += 1
                pt = PT[pti]
                P.op("pe", lambda e: e.matmul(self.psA[:, sbk, 0:N], lhsT=KA[i][hh][0:68, J * 128:(J + 1) * 128], rhs=QA[i][hh][0:68, t_lo:q_hi],
                                              start=True, stop=True), rd=[f"QA{i}{hh}", f"KA{i}{hh}"], wr=[f"ps{sbk}"])
                P.op("act", lambda e: e.activation(out=pt[:, 0:N], in_=self.psA[:, sbk, 0:N], func=AF.Exp), rd=[f"ps{sbk}"], wr=[f"PT{pti}"])
                if J * 128 >= I * TS:
                    P.op("dve", lambda e: e.tensor_tensor(out=pt[:, 0:128], in0=pt[:, 0:128], in1=tri, op=ALU.mult), rd=[f"PT{pti}", "tri"], wr=[f"PT{pti}"])
                return (pt, pti, t_lo, q_hi)

            def emit_PV(st, info, i=i, p=p):
                hh, I, J, nJ, ob = st
                pt, pti, t_lo, q_hi = info

                def pv(e):
                    ins = None
                    for u0 in range(t_lo, q_hi, 128):
                        u = (u0 - I * TS) // 128
                        ins = e.matmul(self.psA[:, ob, u * 66:u * 66 + 65], lhsT=pt[:, u0 - t_lo:u0 - t_lo + 128], rhs=VV[i][:, J, hh, 0:65],
                                       start=(J == 0 and u == 0), stop=(J == nJ - 1 and u == nsub - 1))
                    return ins

                P.op("pe", pv, rd=[f"PT{pti}", f"VV{i}"], wr=[f"ps{ob}"])
                if J == nJ - 1:
                    epilogue(2 * p + hh, I, ob)

            infos = {}
            for idx in range(len(steps) + LOOK):
                if idx < len(steps):
                    infos[idx] = emit_S(steps[idx])
                if idx - LOOK >= 0:
                    emit_PV(steps[idx - LOOK], infos.pop(idx - LOOK))
        P.barrier()
        if DEBUG:
            dz = self.dscr("dbg_ztm", [128, c.NTB, D], BF16)
            dg = self.dscr("dbg_G", [128, c.NTB, D], BF16)
            P.op("sp", lambda e: e.dma_start(out=dz, in_=ztm), dma="dbg")
            P.op("sp", lambda e: e.dma_start(out=dg, in_=G3), dma="dbg")
            P.barrier()
        zT = self.actB.rearrange("p (c t) -> p c t", t=T)
        self.transpose_to_fm(lambda tb: ztm[:, tb, :], zT)
        P.barrier()
        nbw = 512 if D >= 512 else D
        self.gemm_tm(lambda k, a, b_: zT[:, k, a:b_], ["zT"], KC, 128, W["b_w_o"], D, nbw, self.resid_epilogue(x_ap, nbw))
        P.barrier()

    def rwkv(self, x_ap, W):
        c, P = self.c, self.P
        D, T, H, KC, TS, NCH = c.D, c.T, c.H, c.KC, c.TS, c.NCH
        rr = self.dscr("a_rr", [D, T], F32)
        kkd = self.dscr("a_kk", [D, T], F32)
        vvd = self.dscr("a_vv", [D, T], F32)
        vtm = self.dscr("a_vtm", [T, D], F32)
        lwd = self.dscr("a_lw", [D, T], F32)
        aad = self.dscr("a_aa", [D, T], F32)
        ggd = self.dscr("a_gg", [D, T], F32)
        dd = {nm_: self.dscr("a_d" + nm_, [D, T], BF16) for nm_ in ["rt", "at", "bt", "kt", "bh", "kh"]}
        d_bo = self.dscr("a_dbo", [D, T], F32)
        d_wc = self.dscr("a_dwc", [D, NCH], F32)
        d_y = self.dscr("a_dy", [D, T], F32)
        rowsA = self.sb("rowsA", [128, 128], F32)
        rowsB = self.sb("rowsB", [128, 128], F32)
        cols = self.sb("cols", [128, 256], F32)
        lora = self.sb("lora", [128, 2, T], BF16)
        wcs = self.sb("wcs", [128, NCH], F32)
        P.op("dve", lambda e: e.memset(rowsA[:], 0.0), wr=["rowsA"])
        P.op("dve", lambda e: e.memset(rowsB[:], 0.0), wr=["rowsB"])
        P.op("sp", lambda e: e.dma_start(out=rowsA[0:6 * KC, :], in_=W["a_mix"].rearrange("s (c p) -> (s c) p", p=128)), wr=["rowsA"], dma="rowsA")
        vecs = ["a_w0", "a_a0", "a_k_k", "a_k_a", "a_r_k", "a_lnx_g", "a_lnx_b"]
        for i, nm in enumerate(vecs):
            P.op("sp", lambda e, i=i, nm=nm: e.dma_start(out=rowsB[i * KC:(i + 1) * KC, :], in_=W[nm].rearrange("o (c p) -> (o c) p", p=128)),
                 wr=["rowsB"], dma="rowsB")
        b0 = self.next_ps()
        P.op("pe", lambda e: e.transpose(out=self.psA[:, b0, 0:128], in_=rowsA[:], identity=self.identf[:]), rd=["rowsA", "identf"], wr=[f"ps{b0}"])
        P.op("dve", lambda e: e.tensor_copy(out=cols[:, 0:128], in_=self.psA[:, b0, 0:128]), rd=[f"ps{b0}"], wr=["cols"])
        b1 = self.next_ps()
        P.op("pe", lambda e: e.transpose(out=self.psA[:, b1, 0:128], in_=rowsB[:], identity=self.identf[:]), rd=["rowsB", "identf"], wr=[f"ps{b1}"])
        P.op("dve", lambda e: e.tensor_copy(out=cols[:, 128:256], in_=self.psA[:, b1, 0:128]), rd=[f"ps{b1}"], wr=["cols"])
        mixc = lambda s_, k: cols[:, s_ * KC + k:s_ * KC + k + 1]
        vcol = lambda nm, k: cols[:, 128 + vecs.index(nm) * KC + k:128 + vecs.index(nm) * KC + k + 1]
        omk0 = 128 + 7 * KC
        P.op("dve", lambda e: e.tensor_scalar(out=cols[:, omk0:omk0 + KC], in0=cols[:, 128 + 3 * KC:128 + 4 * KC], scalar1=-1.0, scalar2=1.0,
                                              op0=ALU.mult, op1=ALU.add), rd=["cols"], wr=["cols"])
        self.norm_transpose(x_ap, W["a_norm_g"])
        P.barrier()
        xm = self.actB.rearrange("p (c t) -> p c t", t=T)
        xmf = lambda k, a, b_: xm[:, k, a:b_]

        def mix(s_):
            for k in range(KC):
                P.op("pool" if k % 2 == 0 else "dve", lambda e, k=k: e.tensor_tensor(out=xm[:, k, :], in0=self.actA[:, k, 1:T + 1], in1=self.actA[:, k, 2:T + 2], op=ALU.subtract),
                     rd=["actA"], wr=[f"xm{k}"])
                P.op("dve", lambda e, k=k: e.scalar_tensor_tensor(out=xm[:, k, :], in0=xm[:, k, :], scalar=mixc(s_, k), in1=self.actA[:, k, 2:T + 2],
                                                                  op0=ALU.mult, op1=ALU.add), rd=["actA", f"xm{k}", "cols"], wr=["xm", f"xm{k}"])

        fullg = lambda w_ap, n: [[(w_ap, j * 128, min(128, n - j * 128))] for j in range((n + 127) // 128)]
        for s_, dst in [(0, rr), (1, kkd), (2, vvd)]:
            mix(s_)
            self.gemm_fm(xmf, ["xm"], KC, 128, fullg(W["a_w_rkv"][s_], D), self.store_fm(dst, lambda gi, bi: gi * 128))
            if s_ == 2:
                nbw = 512 if D >= 512 else D
                self.gemm_tm(xmf, ["xm"], KC, 128, W["a_w_rkv"][2], D, nbw, self.store_tm(vtm, nbw))
            P.barrier()

        def lora_ep(func):
            def ep(gi, tt, lo, hi, banks):
                (b, ncols), = banks
                P.op("act", lambda e: e.activation(out=lora[0:ncols, gi, lo:hi], in_=self.psA[0:ncols, b, 0:hi - lo], func=func),
                     rd=[f"ps{b}"], wr=["lora"])
            return ep

        def out_ep(dst, func, bias_nm, post_scale):
            def ep(gi, tt, lo, hi, banks):
                (b, ncols), = banks
                s = self.next_stg()
                stg = self.stg[s]
                n = hi - lo
                if func is None:
                    P.op("act", lambda e: e.activation(out=stg[:, 0:n], in_=self.psA[:, b, 0:n], func=AF.Copy), rd=[f"ps{b}"], wr=[f"stg{s}"])
                else:
                    P.op("act", lambda e: e.activation(out=stg[:, 0:n], in_=self.psA[:, b, 0:n], func=func, bias=vcol(bias_nm, gi), scale=1.0),
                         rd=[f"ps{b}", "cols"], wr=[f"stg{s}"])
                if post_scale is not None:
                    P.op("dve", lambda e: e.tensor_scalar(out=stg[:, 0:n], in0=stg[:, 0:n], scalar1=post_scale, scalar2=None, op0=ALU.mult),
                         rd=[f"stg{s}"], wr=[f"stg{s}"])
                P.op("sp", lambda e: e.dma_start(out=dst[gi * 128:(gi + 1) * 128, lo:hi], in_=stg[:, 0:n]), rd=[f"stg{s}"], dma=f"stg{s}")
            return ep

        for s_, w1n, w2n, L, f1, dst, f2, bnm, psc in [
                (3, "a_w1", "a_w2", c.LW, AF.Tanh, lwd, AF.Sigmoid, "a_w0", -float(np.exp(-0.5))),
                (4, "a_a1", "a_a2", c.LA, AF.Copy, aad, AF.Sigmoid, "a_a0", None),
                (5, "a_g1", "a_g2", c.LG, AF.Sigmoid, ggd, None, None, None)]:
            mix(s_)
            self.gemm_fm(xmf, ["xm"], KC, 128, fullg(W[w1n], L), lora_ep(f1))
            kcn2 = (L + 127) // 128
            kp2 = L if L < 128 else 128
            self.gemm_fm(lambda k, a, b_, kp2=kp2: lora[0:kp2, k, a:b_], ["lora"], kcn2, kp2, fullg(W[w2n], D), out_ep(dst, f2, bnm, psc))
            P.barrier()

        off = 0
        maskA, off = self.r2view(off, 384, F32)
        maskB, off = self.r2view(off, 256, F32)
        bones, off = self.r2view(off, 128, BF16)
        fA = []
        for i in range(5):
            v_, off = self.r2view(off, T, F32)
            fA.append(v_)
        assert off <= self.R2.shape[1], (off, self.R2.shape)
        X1, X2, X3, X4, X5 = fA
        fB = []
        ob_ = 16 * T
        for i in range(5):
            v_, ob_ = self.r2view(ob_, T, F32, 128, self.R1)
            fB.append(v_)
        assert ob_ <= self.R1.shape[1], (ob_, self.R1.shape)
        Xsets = [fA, fB]
        o1 = 0
        eN, o1 = self.r2view(o1, T, F32, 128, self.R1)
        cmask, o1 = self.r2view(o1, T, F32, 128, self.R1)
        BO, o1 = self.r2view(o1, T, F32, 128, self.R1)
        YT, o1 = self.r2view(o1, T, F32, 128, self.R1)
        RT, o1 = self.r2view(o1, T, BF16, 128, self.R1)
        AT, o1 = self.r2view(o1, T, BF16, 128, self.R1)
        BT, o1 = self.r2view(o1, T, BF16, 128, self.R1)
        KT, o1 = self.r2view(o1, T, BF16, 128, self.R1)
        BH, o1 = self.r2view(o1, T, BF16, 128, self.R1)
        KH, o1 = self.r2view(o1, T, BF16, 128, self.R1)
        S1, o1 = self.r2view(o1, T, BF16, 128, self.R1)
        V2p, o1 = self.r2view(o1, T, BF16, 128, self.R1)
        zoff = max(o1, c.KC * (c.T + 2))
        V2p = V2p.rearrange("p (c i) -> p c i", i=64)
        zT = self.R1[:, zoff:zoff + KC * T].rearrange("p (c t) -> p c t", t=T)
        P.op("dve", lambda e: e.memset(bones, 0.0), wr=["bones"])
        P.op("dve", lambda e: e.memset(bones[0:64, 0:64], 1.0), wr=["bones"])
        P.op("dve", lambda e: e.memset(bones[64:128, 64:128], 1.0), wr=["bones"])
        P.op("dve", lambda e: e.memset(cmask, 1.0), wr=["cmask"])
        P.op("dve", lambda e: e.memset(cmask.rearrange("p (c t) -> p c t", t=64)[:, :, 0:1], 0.0), wr=["cmask"])
        P.op("pool", lambda e: e.memset(maskA, 1.0), wr=["maskA"])
        P.op("pool", lambda e: e.memset(maskB, 1.0), wr=["maskB"])
        for (mk, o_, cm, base, pat) in [(maskA, 0, -1, -1, 1), (maskA, 128, 1, -1, -1), (maskA, 256, -1, -1, 1), (maskB, 0, -1, 0, 1), (maskB, 128, -1, 0, 1)]:
            P.op("pool", lambda e, mk=mk, o_=o_, cm=cm, base=base, pat=pat: e.affine_select(
                out=mk[:, o_:o_ + 128], in_=mk[:, o_:o_ + 128], pattern=[[pat, 128]], compare_op=ALU.is_ge, fill=0.0, base=base, channel_multiplier=cm),
                rd=["maskA", "maskB"], wr=["maskA", "maskB"])
            P.op("pool", lambda e, mk=mk, o_=o_: e.memset(mk[0:64, o_ + 64:o_ + 128], 0.0), rd=["maskA", "maskB"], wr=["maskA", "maskB"])
            P.op("pool", lambda e, mk=mk, o_=o_: e.memset(mk[64:128, o_:o_ + 64], 0.0), rd=["maskA", "maskB"], wr=["maskA", "maskB"])
        P.op("dve", lambda e: e.memset(self.psA[:], 0.0), wr=[f"ps{i}" for i in range(6)])
        P.barrier()
        itc = [0]

        def do_pair(p):
            r0 = p * 128
            si = p % 2
            X1, X2, X3, X4, X5 = Xsets[si]
            kx1, kx2, kx3, kx4, kx5 = [f"X{i}_{si}" for i in range(1, 6)]
            ld = lambda dst, src, key: P.op("sp", lambda e: e.dma_start(out=dst, in_=src[r0:r0 + 128, :]), wr=[key], dma=key)
            tt_ = lambda out, a, b_, op, rd, wr, eng="dve": P.op(eng, lambda e: e.tensor_tensor(out=out, in0=a, in1=b_, op=op), rd=rd, wr=wr)
            v3 = lambda a_: a_.rearrange("p (c t) -> p c t", t=64)
            ld(X1, lwd, kx1)
            P.op("dve", lambda e: e.tensor_tensor_scan(out=X2, data0=cmask, data1=X1, initial=0.0, op0=ALU.mult, op1=ALU.add), rd=["cmask", kx1], wr=[kx2])
            tt_(X1, X2, X1, ALU.subtract, [kx2, kx1], [kx1], "pool")
            P.op("act", lambda e: e.activation(out=X3, in_=X2, func=AF.Exp), rd=[kx2], wr=[kx3])
            P.op("act", lambda e: e.activation(out=eN, in_=X2, func=AF.Exp, scale=-1.0), rd=[kx2], wr=["eN"])
            P.op("act", lambda e: e.activation(out=X1, in_=X1, func=AF.Exp), rd=[kx1], wr=[kx1])
            ld(X2, kkd, kx2)
            P.op("dve", lambda e: e.tensor_scalar(out=X4, in0=X2, scalar1=vcol("a_k_k", p), scalar2=None, op0=ALU.mult), rd=[kx2, "cols"], wr=[kx4])
            P.op("act", lambda e: e.activation(out=S1, in_=X4, func=AF.Square), rd=[kx4], wr=["S1"])
            for t2 in range(c.NTT):
                b = self.next_ps()
                P.op("pe", lambda e, b=b, t2=t2: e.matmul(self.psA[:, b, 0:TS], lhsT=bones, rhs=S1[:, t2 * TS:(t2 + 1) * TS], start=True, stop=True),
                     rd=["S1", "bones"], wr=[f"ps{b}"])
                P.op("act", lambda e, b=b, t2=t2: e.activation(out=X5[:, t2 * TS:(t2 + 1) * TS], in_=self.psA[:, b, 0:TS], func=AF.Sqrt),
                     rd=[f"ps{b}", kx5], wr=[kx5])
            P.op("dve", lambda e: e.tensor_scalar(out=X5, in0=X5, scalar1=1e-12, scalar2=None, op0=ALU.max), rd=[kx5], wr=[kx5])
            P.op("dve", lambda e: e.reciprocal(out=X5, in_=X5), rd=[kx5], wr=[kx5])
            tt_(X4, X4, X5, ALU.mult, [kx4, kx5], [kx4])
            P.op("dve", lambda e: e.scalar_tensor_tensor(out=AT, in0=X4, scalar=-1.0, in1=X1, op0=ALU.mult, op1=ALU.mult), rd=[kx4, kx1], wr=["AT"])
            ld(X1, aad, kx1)
            P.op("dve", lambda e: e.tensor_scalar(out=X5, in0=X1, scalar1=vcol("a_k_a", p), scalar2=cols[:, omk0 + p:omk0 + p + 1], op0=ALU.mult, op1=ALU.add),
                 rd=[kx1, "cols"], wr=[kx5])
            tt_(X2, X2, X5, ALU.mult, [kx2, kx5], [kx2], "pool")
            tt_(X1, X4, X1, ALU.mult, [kx4, kx1], [kx1], "pool")
            WCb = X3.rearrange("p (c t) -> p c t", t=64)[:, :, 63:64].broadcast_to([128, NCH, 64])
            tt_(X5, X1, eN, ALU.mult, [kx1, "eN"], [kx5])
            P.op("act", lambda e: e.activation(out=BT, in_=X5, func=AF.Copy), rd=[kx5], wr=["BT"])
            tt_(v3(BH), v3(X5), WCb, ALU.mult, [kx5, kx3], ["BH"])
            tt_(X5, X2, eN, ALU.mult, [kx2, "eN", "BT", "BH"], [kx5])
            P.op("act", lambda e: e.activation(out=KT, in_=X5, func=AF.Copy), rd=[kx5], wr=["KT"])
            tt_(v3(KH), v3(X5), WCb, ALU.mult, [kx5, kx3], ["KH"])
            ld(X1, rr, kx1)
            tt_(RT, X1, X3, ALU.mult, [kx1, kx3], ["RT"], "pool")
            P.op("dve", lambda e: e.scalar_tensor_tensor(out=S1, in0=X1, scalar=vcol("a_r_k", p), in1=X2, op0=ALU.mult, op1=ALU.mult),
                 rd=[kx1, kx2, "cols", "S1"], wr=["S1"])
            ld(X4, vvd, kx4)
            for t2 in range(c.NTT):
                b = self.next_ps()
                P.op("pe", lambda e, b=b, t2=t2: e.matmul(self.psA[:, b, 0:TS], lhsT=bones, rhs=S1[:, t2 * TS:(t2 + 1) * TS], start=True, stop=True),
                     rd=["S1", "bones"], wr=[f"ps{b}"])
                P.op("dve", lambda e, b=b, t2=t2: e.tensor_tensor(out=BO[:, t2 * TS:(t2 + 1) * TS], in0=self.psA[:, b, 0:TS], in1=X4[:, t2 * TS:(t2 + 1) * TS], op=ALU.mult),
                     rd=[f"ps{b}", kx4, "BO"], wr=["BO"])
            eP = X3
            for nm_, til in [("rt", RT), ("at", AT), ("bt", BT), ("kt", KT), ("bh", BH), ("kh", KH)]:
                P.op("sp", lambda e, nm_=nm_, til=til: e.dma_start(out=dd[nm_][r0:r0 + 128, :], in_=til), rd=[nm_.upper()], dma="st_" + nm_)
            P.op("sp", lambda e: e.dma_start(out=d_bo[r0:r0 + 128, :], in_=BO), rd=["BO"], dma="st_bo")
            P.op("dve", lambda e: e.tensor_copy(out=wcs[:], in_=X3.rearrange("p (c t) -> p c t", t=64)[:, :, 63]), rd=[kx3], wr=["wcs"])
            P.op("sp", lambda e: e.dma_start(out=d_wc[r0:r0 + 128, :], in_=wcs[:]), rd=["wcs"], dma="st_wc")

        for p in range(KC):
            do_pair(p)
        P.barrier()

        NS = 1408
        o1 = 0
        D6 = []
        for bufi in range(2):
            lst = []
            for k6 in range(6):
                v_, o1 = self.r2view(o1, KC * 128, BF16, 128, self.R1)
                lst.append(v_.rearrange("p (q t) -> p q t", t=128))
            D6.append(lst)
        V2b = []
        for bufi in range(2):
            v_, o1 = self.r2view(o1, KC * 128, BF16, 128, self.R1)
            V2b.append(v_.rearrange("p (q c i) -> p q c i", c=2, i=64))
        Yst = []
        for bufi in range(2):
            v_, o1 = self.r2view(o1, KC * 128, F32, 128, self.R1)
            Yst.append(v_.rearrange("p (q t) -> p q t", t=128))
        WCt, o1 = self.r2view(o1, KC * NCH, F32, 128, self.R1)
        WCt = WCt.rearrange("p (q c) -> p q c", c=NCH)
        STt, o1 = self.r2view(o1, KC * 64, F32, 128, self.R1)
        STbt, o1 = self.r2view(o1, KC * 64, BF16, 128, self.R1)
        Ust, o1 = self.r2view(o1, KC * 64, BF16, 128, self.R1)
        SAt, o1 = self.r2view(o1, KC * 64, BF16, 128, self.R1)
        v4 = lambda a_: a_.rearrange("p (q i) -> p q i", i=64)
        STt, STbt, Ust, SAt = v4(STt), v4(STbt), v4(Ust), v4(SAt)
        assert o1 <= self.R1.shape[1], (o1, self.R1.shape)
        sets_off = 384 * 2 + 256 * 2 + 128
        assert sets_off + KC * NS <= self.R2.shape[1], (sets_off + KC * NS, self.R2.shape)

        def pset(p):
            o_ = sets_off + p * NS
            g = lambda a_, n: self.R2[:, o_ + a_:o_ + a_ + n]
            return dict(NN=g(0, 256), MAK=g(256, 128), MB=g(384, 256), PP=[g(640, 256), g(896, 256)], Q=g(1152, 128), BK=g(1280, 128))

        P.op("sp", lambda e: e.dma_start(out=WCt, in_=d_wc.rearrange("(q p) c -> p q c", p=128)), wr=["WCt"], dma="WCt")
        P.op("dve", lambda e: e.memset(STt, 0.0), wr=["STall"])
        P.op("dve", lambda e: e.memset(STbt, 0.0), wr=["STball"])
        P.barrier()
        hctr = [0]

        def next_half():
            b_ = self.next_ps()
            return b_, 0, f"ps{b_}"

        tctr = [0]

        def next_tslot():
            t_ = tctr[0] % 2
            tctr[0] += 1
            return t_, 0, f"psT{t_}"

        nblk = T // 128
        d6n = ["rt", "at", "bt", "kt", "bh", "kh"]

        def load_block(bi):
            bufi = bi % 2
            for k6 in range(6):
                P.op("sp", lambda e, k6=k6: e.dma_start(out=D6[bufi][k6], in_=dd[d6n[k6]][:, bi * 128:(bi + 1) * 128].rearrange("(q p) t -> p q t", p=128)),
                     wr=[f"D6_{bufi}"], dma=f"D6_{bufi}")
            for hh in range(2):
                for cl in range(2):
                    src = vtm[bi * 128 + cl * 64:bi * 128 + (cl + 1) * 64, :].rearrange("s (q h i) -> h s q i", h=2, i=64)[hh]
                    P.op("pool", lambda e, hh=hh, cl=cl, src=src: e.dma_start(out=V2b[bufi][hh * 64:(hh + 1) * 64, :, cl, :], in_=src),
                         wr=[f"V2_{bufi}"], dma=f"V2_{bufi}")

        def chunk_gen(p, ch):
            bi, cl = ch // 2, ch % 2
            bufi = bi % 2
            cs = slice(cl * 64, (cl + 1) * 64)
            RTc, ATc, BTc, KTc, BHc, KHc = [D6[bufi][k6][:, p, :] for k6 in range(6)]
            dk = [f"D6_{bufi}"] * 6
            V2c = V2b[bufi][:, p, cl, :]
            vk = f"V2_{bufi}"
            S_ = pset(p)
            NN, MAK, MBt, PPs, Q, BK = S_["NN"], S_["MAK"], S_["MB"], S_["PP"], S_["Q"], S_["BK"]
            kNN, kMAK, kMB, kQ, kBK = f"NN{p}", f"MAK{p}", f"MB{p}", f"Q{p}", f"BK{p}"
            STp, STbp, Up, SAp = STt[:, p, :], STbt[:, p, :], Ust[:, p, :], SAt[:, p, :]
            kST, kSTb, kU, kSA = f"ST{p}", f"STb{p}", f"U{p}", f"SA{p}"
            hs = [slice(0, 64), slice(64, 128)]
            b1, o1_, k1 = next_half()
            b2, o2_, k2 = b1, 256, k1
            b3, o3_, k3 = next_half()

            def mmats(e):
                ins = None
                for (bank, o_, X, Y) in [(b1, o1_, BTc, ATc), (b1, o1_ + 128, ATc, BTc), (b2, o2_, KTc, ATc), (b3, o3_, BTc, RTc), (b3, o3_ + 128, KTc, RTc)]:
                    for hh in range(2):
                        ps_ = hs[hh]
                        ins = e.matmul(self.psA[ps_, bank, o_ + hh * 64:o_ + (hh + 1) * 64], lhsT=X[ps_, cs], rhs=Y[ps_, cs],
                                       start=True, stop=True, tile_position=(hh * 64, hh * 64))
                return ins

            P.op("pe", mmats, rd=[dk[1], dk[2], dk[3], dk[0]], wr=[k1, k3])
            P.op("dve", lambda e: e.tensor_tensor(out=NN, in0=self.psA[:, b1, o1_:o1_ + 256], in1=maskA[:, 0:256], op=ALU.mult), rd=[k1, "maskA"], wr=[kNN])
            P.op("dve", lambda e: e.tensor_tensor(out=MAK, in0=self.psA[:, b2, o2_:o2_ + 128], in1=maskA[:, 256:384], op=ALU.mult), rd=[k2, "maskA"], wr=[kMAK])
            P.op("dve", lambda e: e.tensor_tensor(out=MBt, in0=self.psA[:, b3, o3_:o3_ + 256], in1=maskB, op=ALU.mult), rd=[k3, "maskB"], wr=[kMB])
            P.op("dve", lambda e: e.tensor_tensor(out=Q, in0=NN[:, 0:128], in1=self.ident[:], op=ALU.add), rd=[kNN, "ident"], wr=[kQ])
            yield
            cur, curk = NN, kNN
            for lvl in range(5):
                bL, oL, kL = next_half()
                pn = PPs[lvl % 2]
                kpn = f"PP{p}_{lvl % 2}"

                def sqm(e, cur=cur, bL=bL, oL=oL):
                    e.matmul(self.psA[:, bL, oL:oL + 128], lhsT=cur[:, 128:256], rhs=cur[:, 0:128], start=True, stop=True)
                    return e.matmul(self.psA[:, bL, oL + 128:oL + 256], lhsT=cur[:, 0:128], rhs=cur[:, 128:256], start=True, stop=True)

                P.op("pe", sqm, rd=[curk], wr=[kL])
                P.op("act", lambda e, pn=pn, bL=bL, oL=oL: e.activation(out=pn, in_=self.psA[:, bL, oL:oL + 256], func=AF.Copy), rd=[kL], wr=[kpn])
                yield
                bQ, oQ, kQb = next_half()
                P.op("pe", lambda e, pn=pn, bQ=bQ, oQ=oQ: e.matmul(self.psA[:, bQ, oQ:oQ + 128], lhsT=pn[:, 128:256], rhs=Q, start=True, stop=True),
                     rd=[kpn, kQ], wr=[kQb])
                P.op("dve", lambda e, bQ=bQ, oQ=oQ: e.tensor_tensor(out=Q, in0=Q, in1=self.psA[:, bQ, oQ:oQ + 128], op=ALU.add), rd=[kQb, kQ], wr=[kQ])
                yield
                cur, curk = pn, kpn
            tb_, to_, kt_ = next_tslot()

            def trs(e):
                ins = None
                for o_, X in [(0, BHc), (64, KHc)]:
                    for hh in range(2):
                        ps_ = hs[hh]
                        ins = e.transpose(out=self.psT[ps_, tb_, to_ + o_:to_ + o_ + 64], in_=X[ps_, cs], identity=self.ident[ps_, ps_],
                                          tile_position=(hh * 64, hh * 64))
                return ins

            P.op("pe", trs, rd=[dk[4], dk[5], "ident"], wr=[kt_])
            P.op("act", lambda e: e.activation(out=BK, in_=self.psT[:, tb_, to_:to_ + 128], func=AF.Copy), rd=[kt_], wr=[kBK])
            yield
            bU, oU, kUb = next_half()

            def mmU(e):
                for hh in range(2):
                    ps_ = hs[hh]
                    e.matmul(self.psA[ps_, bU, oU:oU + 64], lhsT=ATc[ps_, cs], rhs=STbp[ps_, :], start=True, stop=False, tile_position=(hh * 64, hh * 64))
                return e.matmul(self.psA[:, bU, oU:oU + 64], lhsT=MAK, rhs=V2c, start=False, stop=True)

            P.op("pe", mmU, rd=[dk[1], kSTb, "STball", kMAK, vk], wr=[kUb])
            P.op("act", lambda e: e.activation(out=Up, in_=self.psA[:, bU, oU:oU + 64], func=AF.Copy), rd=[kUb], wr=[kU])
            yield
            bS, oS, kSb = next_half()
            P.op("pe", lambda e: e.matmul(self.psA[:, bS, oS:oS + 64], lhsT=Q, rhs=Up, start=True, stop=True), rd=[kQ, kU], wr=[kSb])
            P.op("act", lambda e: e.activation(out=SAp, in_=self.psA[:, bS, oS:oS + 64], func=AF.Copy), rd=[kSb], wr=[kSA])
            yield
            bY, oY, kYb = next_half()

            def mmY(e):
                ins = None
                for hh in range(2):
                    ps_ = hs[hh]
                    e.matmul(self.psA[ps_, bY, oY:oY + 64], lhsT=STbp[ps_, :], rhs=RTc[ps_, cs], start=True, stop=False, tile_position=(hh * 64, hh * 64))
                for hh in range(2):
                    ps_ = hs[hh]
                    e.matmul(self.psA[ps_, bY, oY:oY + 64], lhsT=V2c[ps_, :], rhs=MBt[ps_, 128 + hh * 64:128 + (hh + 1) * 64], start=False, stop=False,
                             tile_position=(hh * 64, hh * 64))
                for hh in range(2):
                    ps_ = hs[hh]
                    ins = e.matmul(self.psA[ps_, bY, oY:oY + 64], lhsT=SAp[ps_, :], rhs=MBt[ps_, hh * 64:(hh + 1) * 64], start=False, stop=True,
                                   tile_position=(hh * 64, hh * 64))
                return ins

            P.op("pe", mmY, rd=[kSTb, "STball", dk[0], vk, kMB, kSA], wr=[kYb])
            P.op("act", lambda e: e.activation(out=Yst[bufi][:, p, cs], in_=self.psA[:, bY, oY:oY + 64], func=AF.Copy), rd=[kYb], wr=[f"Yst{bufi}"])
            yield
            bN, oN, kNb = next_half()

            def mmN(e):
                ins = None
                for hh in range(2):
                    ps_ = hs[hh]
                    e.matmul(self.psA[ps_, bN, oN:oN + 64], lhsT=BK[ps_, 0:64], rhs=SAp[ps_, :], start=True, stop=False, tile_position=(hh * 64, hh * 64))
                for hh in range(2):
                    ps_ = hs[hh]
                    ins = e.matmul(self.psA[ps_, bN, oN:oN + 64], lhsT=BK[ps_, 64:128], rhs=V2c[ps_, :], start=False, stop=True,
                                   tile_position=(hh * 64, hh * 64))
                return ins

            P.op("pe", mmN, rd=[kBK, kSA, vk], wr=[kNb])
            P.op("dve", lambda e: e.scalar_tensor_tensor(out=STp, in0=STp, scalar=WCt[:, p, ch:ch + 1], in1=self.psA[:, bN, oN:oN + 64],
                                                         op0=ALU.mult, op1=ALU.add), rd=[kST, "STall", "WCt", kNb], wr=[kST])
            P.op("act", lambda e: e.activation(out=STbp, in_=STp, func=AF.Copy), rd=[kST, "STball"], wr=[kSTb])
            yield

        load_block(0)
        for ch in range(NCH):
            bi = ch // 2
            if ch % 2 == 0 and bi + 1 < nblk:
                load_block(bi + 1)
            gens = [chunk_gen(p, ch) for p in range(KC)]
            while gens:
                alive = []
                for g in gens:
                    try:
                        next(g)
                        alive.append(g)
                    except StopIteration:
                        pass
                gens = alive
            if ch % 2 == 1:
                bufi = bi % 2
                P.op("sp", lambda e, bi=bi, bufi=bufi: e.dma_start(out=d_y[:, bi * 128:(bi + 1) * 128].rearrange("(q p) t -> p q t", p=128), in_=Yst[bufi]),
                     rd=[f"Yst{bufi}"], dma=f"Yst{bufi}")
        P.barrier()

        def do_epi(p):
            r0 = p * 128
            ld = lambda dst, src, key: P.op("sp", lambda e: e.dma_start(out=dst, in_=src[r0:r0 + 128, :]), wr=[key], dma=key)
            tt_ = lambda out, a, b_, op, rd, wr, eng="dve": P.op(eng, lambda e: e.tensor_tensor(out=out, in0=a, in1=b_, op=op), rd=rd, wr=wr)
            ld(X3, d_y, "X3")
            ld(eN, d_bo, "eN")
            ld(X4, ggd, "X4")
            YT_ = X3
            P.op("act", lambda e: e.activation(out=S1, in_=YT_, func=AF.Copy), rd=["X3"], wr=["S1"])
            P.op("act", lambda e: e.activation(out=RT, in_=YT_, func=AF.Square), rd=["X3"], wr=["RT"])
            mu, var, tG = X1, X2, X5
            for t2 in range(c.NTT):
                sl_ = slice(t2 * TS, (t2 + 1) * TS)
                b = self.next_ps()
                P.op("pe", lambda e, b=b, sl_=sl_: e.matmul(self.psA[:, b, 0:TS], lhsT=bones, rhs=S1[:, sl_], start=True, stop=True), rd=["S1", "bones"], wr=[f"ps{b}"])
                P.op("act", lambda e, b=b, sl_=sl_: e.activation(out=mu[:, sl_], in_=self.psA[:, b, 0:TS], func=AF.Copy, scale=1.0 / 64), rd=[f"ps{b}", "X1"], wr=["X1"])
                b2 = self.next_ps()
                P.op("pe", lambda e, b2=b2, sl_=sl_: e.matmul(self.psA[:, b2, 0:TS], lhsT=bones, rhs=RT[:, sl_], start=True, stop=True), rd=["RT", "bones"], wr=[f"ps{b2}"])
                P.op("dve", lambda e, sl_=sl_: e.tensor_tensor(out=tG[:, sl_], in0=mu[:, sl_], in1=mu[:, sl_], op=ALU.mult), rd=["X1", "X5"], wr=["X5"])
                P.op("dve", lambda e, b2=b2, sl_=sl_: e.scalar_tensor_tensor(out=var[:, sl_], in0=self.psA[:, b2, 0:TS], scalar=1.0 / 64, in1=tG[:, sl_],
                                                                             op0=ALU.mult, op1=ALU.subtract), rd=[f"ps{b2}", "X5", "X2"], wr=["X2"])
            P.op("act", lambda e: e.activation(out=var, in_=var, func=AF.Sqrt, bias=self.consts[:, 2:3], scale=1.0), rd=["X2", "consts"], wr=["X2"])
            P.op("dve", lambda e: e.reciprocal(out=var, in_=var), rd=["X2"], wr=["X2"])
            tt_(YT_, YT_, mu, ALU.subtract, ["X3", "X1"], ["X3"])
            tt_(YT_, YT_, var, ALU.mult, ["X3", "X2"], ["X3"])
            P.op("dve", lambda e: e.tensor_scalar(out=YT_, in0=YT_, scalar1=vcol("a_lnx_g", p), scalar2=vcol("a_lnx_b", p), op0=ALU.mult, op1=ALU.add),
                 rd=["X3", "cols"], wr=["X3"])
            tt_(YT_, YT_, eN, ALU.add, ["X3", "eN"], ["X3"], "pool")
            tt_(zT[:, p, :], YT_, X4, ALU.mult, ["X3", "X4"], ["zT"], "pool")

        for p in range(KC):
            do_epi(p)
        P.barrier()
        nbw = 512 if D >= 512 else D
        self.gemm_tm(lambda k, a, b_: zT[:, k, a:b_], ["zT"], KC, 128, W["a_w_o"], D, nbw, self.resid_epilogue(x_ap, nbw))
        P.barrier()

    def final_norm(self, x_ap, g_ap, out_ap):
        c, P = self.c, self.P
        P.op("sp", lambda e: e.dma_start(out=self.grep[:], in_=g_ap.partition_broadcast(128)), wr=["grep"], dma="grep")
        for tb in range(c.NTB):
            s = tb % 2
            xt = self.xt[s]
            ss = self.small[:, s:s + 1]
            rs = self.small[:, 2 + s:3 + s]
            P.op("sp", lambda e, xt=xt, tb=tb: e.dma_start(out=xt[:], in_=x_ap[tb * 128:(tb + 1) * 128, :]),
                 wr=[f"xt{s}"], dma=f"xt{s}")
            P.op("act", lambda e, xt=xt, ss=ss, s=s: e.activation(out=self.hn[s][:], in_=xt[:], func=AF.Square, accum_out=ss),
                 rd=[f"xt{s}"], wr=[f"hn{s}", f"ss{s}"])
            P.op("act", lambda e, ss=ss, rs=rs: e.activation(out=rs, in_=ss, func=AF.Sqrt, scale=1.0 / c.D, bias=self.consts[:, 0:1]),
                 rd=[f"ss{s}", "consts"], wr=[f"rs{s}"])
            P.op("dve", lambda e, rs=rs: e.reciprocal(out=rs, in_=rs), rd=[f"rs{s}"], wr=[f"rsd{s}", f"rs{s}"])
            P.op("dve", lambda e, xt=xt, rs=rs: e.scalar_tensor_tensor(
                out=xt[:], in0=xt[:], scalar=rs, in1=self.grep[:], op0=ALU.mult, op1=ALU.mult),
                rd=[f"rsd{s}", f"rs{s}", "grep"], wr=[f"xt{s}"])
            P.op("sp", lambda e, xt=xt, tb=tb: e.dma_start(out=out_ap[tb * 128:(tb + 1) * 128, :], in_=xt[:]),
                 rd=[f"xt{s}"], dma=f"xt{s}")

    def finish(self):
        P, nc = self.P, self.nc
        P.barrier()
        with nc.Block() as block:
            @block.sync
            def _(e):
                for f in P.streams["sp"]:
                    f(e)

            @block.scalar
            def _(e):
                for f in P.streams["act"]:
                    f(e)

            @block.vector
            def _(e):
                for f in P.streams["dve"]:
                    f(e)

            @block.gpsimd
            def _(e):
                for f in P.streams["pool"]:
                    f(e)

            @block.tensor
            def _(e):
                for f in P.streams["pe"]:
                    f(e)
        self.es.close()
        return nc


def build_program(cfg, layers=("a", "f0", "b", "f1", "final")):
    B = Builder(cfg)
    c = cfg
    x_in = B.din("x", [c.T, c.D])
    f_norm_g = B.din("f_norm_g", [2, c.D])
    f_w_gu = B.din("f_w_gu", [2, c.D, 2 * c.DFF])
    f_w_d = B.din("f_w_d", [2, c.DFF, c.D])
    final_g = B.din("final_g", [1, c.D])
    W = {}
    for nm, shp in [("b_norm_g", [1, c.D]), ("b_w_in", [c.D, c.WIN]), ("b_b_f", [1, c.H]), ("b_qn_g", [1, 64]), ("b_kn_g", [1, 64]),
                    ("b_on_g", [1, c.D]), ("b_w_o", [c.D, c.D])]:
        if "b" in layers:
            W[nm] = B.din(nm, shp)
    for nm, shp in [("a_norm_g", [1, c.D]), ("a_mix", [6, c.D]), ("a_w_rkv", [3, c.D, c.D]), ("a_w0", [1, c.D]), ("a_w1", [c.D, c.LW]),
                    ("a_w2", [c.LW, c.D]), ("a_a0", [1, c.D]), ("a_a1", [c.D, c.LA]), ("a_a2", [c.LA, c.D]), ("a_g1", [c.D, c.LG]),
                    ("a_g2", [c.LG, c.D]), ("a_k_k", [1, c.D]), ("a_k_a", [1, c.D]), ("a_r_k", [1, c.D]), ("a_lnx_g", [1, c.D]),
                    ("a_lnx_b", [1, c.D]), ("a_w_o", [c.D, c.D])]:
        if "a" in layers:
            W[nm] = B.din(nm, shp)
    out = B.nc.dram_tensor("out", [c.T, c.D], F32, kind="ExternalOutput").ap()
    xres = B.dscr("xres", [c.T, c.D], F32)
    mid = B.dscr("mid", [c.DFF, c.T], BF16)
    B.setup_common()
    P = B.P
    P.op("sp", lambda e: e.dma_start(out=xres, in_=x_in), dma="xcopy")
    P.barrier()
    for L in layers:
        if L == "f0" or L == "f1":
            l = int(L[1])
            B.swiglu(xres, f_norm_g[l:l + 1, :], f_w_gu[l], f_w_d[l], mid)
        elif L == "a":
            B.rwkv(xres, W)
        elif L == "b":
            B.fox(xres, W)
        elif L == "final":
            B.final_norm(xres, final_g, out)
    return B.finish()


def make_in_map(cfg, inputs, core, layers=("a", "f0", "b", "f1", "final")):
    m = {"x": np.ascontiguousarray(inputs["x"][core]),
         "f_norm_g": np.ascontiguousarray(inputs["f_norm_g"]),
         "f_w_gu": np.ascontiguousarray(inputs["f_w_gu"]),
         "f_w_d": np.ascontiguousarray(inputs["f_w_d"]),
         "final_g": np.ascontiguousarray(inputs["final_g"]).reshape(1, -1)}
    if "a" in layers:
        for nm in ["a_norm_g", "a_w0", "a_a0", "a_k_k", "a_k_a", "a_r_k", "a_lnx_g", "a_lnx_b"]:
            m[nm] = np.ascontiguousarray(inputs[nm]).reshape(1, -1)
        for nm in ["a_mix", "a_w_rkv", "a_w1", "a_w2", "a_a1", "a_a2", "a_g1", "a_g2", "a_w_o"]:
            m[nm] = np.ascontiguousarray(inputs[nm][0])
    if "b" in layers:
        for nm in ["b_norm_g", "b_b_f", "b_qn_g", "b_kn_g", "b_on_g"]:
            m[nm] = np.ascontiguousarray(inputs[nm]).reshape(1, -1)
        m["b_w_in"] = np.ascontiguousarray(inputs["b_w_in"][0])
        m["b_w_o"] = np.ascontiguousarray(inputs["b_w_o"][0])
    return m


_CACHE = {}


def kernel(**inputs):
    cfg = Cfg()
    layers = ("a", "f0", "b", "f1", "final")
    if "nc" not in _CACHE:
        _CACHE["nc"] = build_program(cfg, layers)
    nc = _CACHE["nc"]
    inputs = {k: np.asarray(v) for k, v in inputs.items()}
    in_maps = [make_in_map(cfg, inputs, core, layers) for core in range(8)]
    res = run_bass_kernel_spmd(nc, in_maps, core_ids=list(range(8)))
    out = np.stack([np.asarray(r["out"]) for r in res.results], axis=0)
    return out.astype(np.float32)
```

```python
import contextlib
import numpy as np
import concourse.bass as bass
import concourse.mybir as mybir
from concourse.bass_utils import run_bass_kernel_spmd

F32 = mybir.dt.float32
BF16 = mybir.dt.bfloat16
AF = mybir.ActivationFunctionType
ALU = mybir.AluOpType
AX = mybir.AxisListType

RMS_EPS = 1e-6
DEBUG = False
GN_EPS = 64e-5


class Cfg:
    def __init__(self, T=2048, D=2048, DFF=5632, LW=96, LA=96, LG=256):
        self.T, self.D, self.DFF, self.LW, self.LA, self.LG = T, D, DFF, LW, LA, LG
        self.H = D // 64
        self.KC = D // 128
        self.FC = DFF // 128
        self.TS = min(512, T)
        self.NTT = T // self.TS
        self.NTB = T // 128
        self.NCH = T // 64
        self.WIN = 4 * D + 3 * self.H


ENGS = ["sp", "act", "dve", "pool", "pe"]


class Prog:
    def __init__(self, nc, es):
        self.nc, self.es = nc, es
        self.streams = {e: [] for e in ENGS}
        self.sems, self.semval = {}, {}
        self.seen = {e: {} for e in ENGS}
        self.last_w, self.readers = {}, {}
        self.n_ops = 0

    def _sem(self, name):
        if name not in self.sems:
            self.sems[name] = self.es.enter_context(self.nc.semaphore("s_" + name.replace(":", "_")))
            self.semval[name] = 0
        return self.sems[name]

    def op(self, eng, fn, rd=(), wr=(), dma=None):
        deps = {}

        def add(d):
            if d is not None:
                deps[d[0]] = max(deps.get(d[0], 0), d[1])

        for k in rd:
            add(self.last_w.get(k))
        for k in wr:
            add(self.last_w.get(k))
            for s, v in self.readers.get(k, {}).items():
                add((s, v))
        own = "eng:" + eng
        waits = []
        for s, v in deps.items():
            if s == own and eng == "pe":
                continue
            if self.seen[eng].get(s, 0) < v:
                self.seen[eng][s] = v
                waits.append((self._sem(s), v))
        sname = ("dma:" + dma) if dma else own
        inc = 16 if dma else 1
        sem = self._sem(sname)
        self.semval[sname] += inc
        nv = self.semval[sname]

        def emit(e, waits=waits, fn=fn, sem=sem, inc=inc):
            for (s, v) in waits:
                e.wait_ge(s, v)
            ins = fn(e)
            ins.then_inc(sem, inc)

        self.streams[eng].append(emit)
        for k in wr:
            self.last_w[k] = (sname, nv)
            self.readers[k] = {}
        for k in rd:
            r = self.readers.setdefault(k, {})
            r[sname] = max(r.get(sname, 0), nv)
        self.n_ops += 1

    def barrier(self):
        for eng in ENGS:
            waits = []
            for s, v in self.semval.items():
                if v > 0 and self.seen[eng].get(s, 0) < v and s != "eng:" + eng:
                    self.seen[eng][s] = v
                    waits.append((self.sems[s], v))

            def emit(e, waits=waits):
                for (s, v) in waits:
                    e.wait_ge(s, v)

            self.streams[eng].append(emit)
        self.last_w, self.readers = {}, {}


class Builder:
    def __init__(self, cfg):
        self.c = cfg
        self.nc = bass.Bass("TRN2", target_bir_lowering=False)
        self.es = contextlib.ExitStack()
        self.P = Prog(self.nc, self.es)
        self.dram = {}
        self._uid = 0
        self.ps_ctr = 0
        self.wb_ctr = 0
        self.stg_ctr = 0

    def din(self, name, shape):
        self.dram[name] = self.nc.dram_tensor(name, list(shape), F32, kind="ExternalInput").ap()
        return self.dram[name]

    def dscr(self, name, shape, dt):
        self.dram[name] = self.nc.dram_tensor(name, list(shape), dt, kind=("ExternalOutput" if DEBUG else "Internal")).ap()
        return self.dram[name]

    def sb(self, name, shape, dt):
        return self.es.enter_context(self.nc.sbuf_tensor(name, list(shape), dt))

    def psum(self, name, shape, dt):
        return self.es.enter_context(self.nc.psum_tensor(name, list(shape), dt))

    def uid(self, p):
        self._uid += 1
        return f"{p}{self._uid}"

    def setup_common(self):
        c = self.c
        P = self.P
        self.psA = self.psum("psA", [128, 6, 512], F32)
        self.psT = self.psum("psT", [128, 2, 1024], BF16)
        self.ident = self.sb("ident", [128, 128], BF16)
        self.identf = self.sb("identf", [128, 128], F32)
        self.consts = self.sb("consts", [128, 8], F32)
        self.small = self.sb("small", [128, 64], F32)
        self.junk = self.sb("junk", [128, 128], F32)
        self.junk2 = self.sb("junk2", [128, 64], F32)
        self.osb = self.sb("osb", [128, 2, 256], F32)
        self.osq = self.sb("osq", [128, 256], F32)
        nc = self.nc

        def mk_ident(e):
            return e.affine_select(out=self.identf[:], in_=self.identf[:], pattern=[[-1, 128]],
                                   compare_op=ALU.not_equal, fill=1.0, base=0, channel_multiplier=1)

        P.op("pool", lambda e: e.memset(self.identf[:], 0.0), wr=["identf"])
        P.op("pool", mk_ident, rd=["identf"], wr=["identf"])
        P.op("dve", lambda e: e.tensor_copy(out=self.ident[:], in_=self.identf[:]), rd=["identf"], wr=["ident"])

        def mk_consts(e):
            e.memset(self.consts[:, 0:1], RMS_EPS)
            e.memset(self.consts[:, 1:2], 1.0)
            e.memset(self.consts[:, 2:3], GN_EPS)
            e.memset(self.consts[:, 4:8], -0.5)
            return e.memset(self.consts[:, 3:4], 0.0)

        P.op("pool", mk_consts, wr=["consts"])
        hsz = c.KC * (c.T + 2)
        self.TH = c.T // (2 if c.T >= 1024 else 1)
        r1 = max(hsz + c.KC * c.T, c.FC * self.TH, 17 * c.T, 16 * c.T + c.KC * c.T, 29 * c.T)
        self.R1 = self.sb("R1", [128, r1], BF16)
        self.actA = self.R1[:, 0:hsz].rearrange("p (c t) -> p c t", t=c.T + 2)
        self.actB = self.R1[:, hsz:hsz + c.KC * c.T]
        self.WQ = c.KC * 128
        self.NWQ = 12
        r2 = max(self.NWQ * self.WQ, 2 * c.FC * 256, 8 * c.D, 10 * c.T + 1408, 2 * (4 * c.T + c.NTB * 132) + 4 * c.TS + 384, 7 * c.T + 140, 9 * c.D + 8 * c.H)
        self.R2 = self.sb("R2", [128, r2], BF16)
        self.NWQ = r2 // self.WQ
        self.wq_ptr = 0
        self.xt = [self.R2[:, i * 2 * c.D:(i + 1) * 2 * c.D].bitcast(F32) for i in range(2)]
        self.grep = self.R2[:, 4 * c.D:6 * c.D].bitcast(F32)
        self.hn = [self.R2[:, (6 + i) * c.D:(7 + i) * c.D] for i in range(2)]
        self.NSTG = 5
        self.stg = [self.sb(f"stg{i}", [128, 512], F32) for i in range(self.NSTG)]
        P.op("pool", lambda e: e.memset(self.actA[:, :, 0:2], 0.0), wr=["actA"])

    def next_ps(self):
        b = self.ps_ctr % 6
        self.ps_ctr += 1
        return b

    def next_wb(self):
        b = self.wb_ctr % self.NWB
        self.wb_ctr += 1
        return b

    def next_stg(self):
        b = self.stg_ctr % self.NSTG
        self.stg_ctr += 1
        return b

    def norm_transpose(self, x_ap, g_ap):
        c, P = self.c, self.P
        P.op("sp", lambda e: e.dma_start(out=self.grep[:], in_=g_ap.partition_broadcast(128)),
             wr=["grep"], dma="grep")
        for tb in range(c.NTB):
            s = tb % 2
            xt, hn = self.xt[s], self.hn[s]
            ss = self.small[:, s:s + 1]
            rs = self.small[:, 2 + s:3 + s]
            P.op("sp", lambda e, xt=xt, tb=tb: e.dma_start(out=xt[:], in_=x_ap[tb * 128:(tb + 1) * 128, :]),
                 wr=[f"xt{s}"], dma=f"xt{s}")
            P.op("act", lambda e, xt=xt, hn=hn, ss=ss: e.activation(out=hn[:], in_=xt[:], func=AF.Square, accum_out=ss),
                 rd=[f"xt{s}"], wr=[f"hn{s}", f"ss{s}"])
            P.op("act", lambda e, ss=ss, rs=rs: e.activation(out=rs, in_=ss, func=AF.Sqrt, scale=1.0 / c.D,
                                                                 bias=self.consts[:, 0:1]),
                 rd=[f"ss{s}", "consts"], wr=[f"rs{s}"])
            P.op("dve", lambda e, rs=rs: e.reciprocal(out=rs, in_=rs), rd=[f"rs{s}"], wr=[f"rsd{s}", f"rs{s}"])
            P.op("dve", lambda e, xt=xt, hn=hn, rs=rs: e.scalar_tensor_tensor(
                out=hn[:], in0=xt[:], scalar=rs, in1=self.grep[:], op0=ALU.mult, op1=ALU.mult),
                rd=[f"xt{s}", f"rsd{s}", f"rs{s}", "grep"], wr=[f"hn{s}"])
            for c0 in range(0, c.KC, 8):
                nch = min(8, c.KC - c0)
                tbk = (tb * ((c.KC + 7) // 8) + c0 // 8) % 2

                def tr(e, hn=hn, c0=c0, nch=nch, tbk=tbk):
                    ins = None
                    for i in range(nch):
                        ins = e.transpose(out=self.psT[:, tbk, i * 128:(i + 1) * 128],
                                          in_=hn[:, (c0 + i) * 128:(c0 + i + 1) * 128], identity=self.ident[:])
                    return ins

                P.op("pe", tr, rd=[f"hn{s}", "ident"], wr=[f"psT{tbk}"])
                eng = "act" if (c0 // 8) % 2 == 0 else "dve"

                def ev(e, c0=c0, nch=nch, tbk=tbk, tb=tb, eng=eng):
                    src = self.psT[:, tbk, 0:nch * 128].rearrange("p (c t) -> p c t", t=128)
                    dst = self.actA[:, c0:c0 + nch, 2 + tb * 128:2 + (tb + 1) * 128]
                    if eng == "act":
                        return e.activation(out=dst, in_=src, func=AF.Copy)
                    return e.tensor_copy(out=dst, in_=src)

                P.op(eng, ev, rd=[f"psT{tbk}"], wr=["actA"])

    def hT(self, k, t0, t1):
        return self.actA[:, k, 2 + t0:2 + t1]

    def load_w(self, w_ap, kcn, kp, col0, ncols):
        nq = (kcn * ncols + self.WQ - 1) // self.WQ
        ext = getattr(self, "wext", None)
        next_ = (ext.shape[1] // self.WQ) if ext is not None else 0
        if self.wq_ptr < self.NWQ and self.wq_ptr + nq > self.NWQ:
            self.wq_ptr = self.NWQ if nq <= next_ else 0
        if self.wq_ptr >= self.NWQ and self.wq_ptr + nq > self.NWQ + next_:
            self.wq_ptr = 0
        q0 = self.wq_ptr
        self.wq_ptr += nq
        keys = [f"wq{q0 + i}" for i in range(nq)]
        if q0 < self.NWQ:
            view = self.R2[0:kp, q0 * self.WQ:q0 * self.WQ + kcn * ncols].rearrange("p (c n) -> p c n", n=ncols)
        else:
            qe = q0 - self.NWQ
            view = ext[0:kp, qe * self.WQ:qe * self.WQ + kcn * ncols].rearrange("p (c n) -> p c n", n=ncols)
        src = w_ap[:, col0:col0 + ncols].rearrange("(c p) n -> p c n", p=kp)
        self.P.op("pool", lambda e: e.dma_start(out=view, in_=src), wr=keys, dma=f"wq{q0}")
        return keys, view

    def gemm_fm(self, xfn, xkeys, kcn, kp, groups, epilogue, t0=0, t1=None):
        c, P = self.c, self.P
        t1 = c.T if t1 is None else t1
        for gi, grp in enumerate(groups):
            wts = [self.load_w(w_ap, kcn, kp, col0, ncols) + (ncols,) for (w_ap, col0, ncols) in grp]
            for tt in range((t1 - t0) // c.TS):
                lo, hi = t0 + tt * c.TS, t0 + (tt + 1) * c.TS
                banks = []
                for (s, view, ncols) in wts:
                    b = self.next_ps()
                    banks.append((b, ncols))

                    def mm(e, view=view, ncols=ncols, b=b, lo=lo, hi=hi):
                        ins = None
                        for k in range(kcn):
                            ins = e.matmul(self.psA[0:ncols, b, 0:hi - lo], lhsT=view[:, k, :], rhs=xfn(k, lo, hi),
                                           start=(k == 0), stop=(k == kcn - 1))
                        return ins

                    P.op("pe", mm, rd=s + xkeys, wr=[f"ps{b}"])
                epilogue(gi, tt, lo, hi, banks)

    def gemm_tm(self, xfn, xkeys, kcn, kp, w_ap, ncols_total, nbw, epilogue, t0=0, t1=None):
        c, P = self.c, self.P
        t1 = c.T if t1 is None else t1
        for nb in range(ncols_total // nbw):
            s, view = self.load_w(w_ap, kcn, kp, nb * nbw, nbw)
            for tb in range((t1 - t0) // 128):
                lo = t0 + tb * 128
                b = self.next_ps()

                def mm(e, view=view, b=b, lo=lo):
                    ins = None
                    for k in range(kcn):
                        ins = e.matmul(self.psA[:, b, 0:nbw], lhsT=xfn(k, lo, lo + 128), rhs=view[:, k, :],
                                       start=(k == 0), stop=(k == kcn - 1))
                    return ins

                P.op("pe", mm, rd=s + xkeys, wr=[f"ps{b}"])
                epilogue(nb, lo, b)

    def resid_epilogue(self, x_ap, nbw):
        P = self.P
        cnt = [0]

        def ep(nb, lo, b):
            s = self.next_stg()
            stg = self.stg[s]
            P.op("sp", lambda e: e.dma_start(out=stg[:, 0:nbw], in_=x_ap[lo:lo + 128, nb * nbw:(nb + 1) * nbw]),
                 wr=[f"stg{s}"], dma=f"stg{s}")
            eng = "dve"
            P.op(eng, lambda e: e.tensor_tensor(out=stg[:, 0:nbw], in0=stg[:, 0:nbw], in1=self.psA[:, b, 0:nbw], op=ALU.add),
                 rd=[f"ps{b}", f"stg{s}"], wr=[f"stg{s}"])
            P.op("act", lambda e: e.dma_start(out=x_ap[lo:lo + 128, nb * nbw:(nb + 1) * nbw], in_=stg[:, 0:nbw]),
                 rd=[f"stg{s}"], dma=f"stg{s}")
            cnt[0] += 1

        return ep

    def swiglu(self, x_ap, g_ap, wgu_ap, wd_ap, mid_ap):
        c, P = self.c, self.P
        self.norm_transpose(x_ap, g_ap)
        P.barrier()
        groups = [[(wgu_ap, j * 128, 128), (wgu_ap, c.DFF + j * 128, 128)] for j in range(c.FC)]
        if not hasattr(self, "_sg"):
            self._sg = [self.sb(self.uid("sg"), [128, c.TS], F32) for _ in range(2)]
            self._mo = [self.sb(self.uid("mo"), [128, c.TS], BF16) for _ in range(2)]
        sg, mo = self._sg, self._mo
        it = [0]

        def ep(gi, tt, lo, hi, banks):
            s = it[0] % 2
            it[0] += 1
            (bg, _), (bu, _) = banks
            P.op("act", lambda e: e.activation(out=sg[s][:], in_=self.psA[:, bg, 0:c.TS], func=AF.Silu),
                 rd=[f"ps{bg}"], wr=[f"sg{s}"])
            P.op("dve", lambda e: e.tensor_tensor(out=mo[s][:], in0=sg[s][:], in1=self.psA[:, bu, 0:c.TS], op=ALU.mult),
                 rd=[f"sg{s}", f"ps{bu}"], wr=[f"mo{s}"])
            P.op("sp", lambda e: e.dma_start(out=mid_ap[gi * 128:(gi + 1) * 128, lo:hi], in_=mo[s][:]),
                 rd=[f"mo{s}"], dma=f"mo{s}")

        self.gemm_fm(self.hT, ["actA"], c.KC, 128, groups, ep)
        P.barrier()
        nkh = c.T // self.TH
        FCH = c.FC // nkh
        nbw = 256
        self.wq_ptr = 0
        free0 = FCH * c.T
        nfree = (self.R1.shape[1] - free0) // self.WQ
        self.wext = self.R1[:, free0:free0 + nfree * self.WQ] if nfree > 0 else None
        for kh in range(nkh):
            midv = self.R1[:, 0:FCH * c.T].rearrange("p (c t) -> p c t", t=c.T)
            for cc in range(FCH):
                P.op("sp", lambda e, cc=cc, kh=kh: e.dma_start(out=midv[:, cc, :], in_=mid_ap[(kh * FCH + cc) * 128:(kh * FCH + cc + 1) * 128, :]),
                     wr=["actB"], dma=f"actB{cc % 4}")
            xfn = lambda k, a, b_: midv[:, k, a:b_]
            self.gemm_tm(xfn, ["actB"], FCH, 128, wd_ap[kh * FCH * 128:(kh + 1) * FCH * 128, :], c.D, nbw, self.resid_epilogue(x_ap, nbw))
            P.barrier()
        self.wext = None
        self.wq_ptr = 0

    def store_fm(self, dst_ap, row_of_group):
        P = self.P
        it = [0]

        def ep(gi, tt, lo, hi, banks):
            for bi, (b, ncols) in enumerate(banks):
                s = self.next_stg()
                stg = self.stg[s]
                eng = "act" if it[0] % 2 == 0 else "dve"
                it[0] += 1
                n = hi - lo
                if eng == "act":
                    P.op("act", lambda e, b=b, ncols=ncols, stg=stg, n=n: e.activation(out=stg[0:ncols, 0:n], in_=self.psA[0:ncols, b, 0:n], func=AF.Copy),
                         rd=[f"ps{b}"], wr=[f"stg{s}"])
                else:
                    P.op("dve", lambda e, b=b, ncols=ncols, stg=stg, n=n: e.tensor_copy(out=stg[0:ncols, 0:n], in_=self.psA[0:ncols, b, 0:n]),
                         rd=[f"ps{b}"], wr=[f"stg{s}"])
                r0 = row_of_group(gi, bi)
                P.op("sp", lambda e, r0=r0, ncols=ncols, stg=stg, n=n, lo=lo, hi=hi: e.dma_start(out=dst_ap[r0:r0 + ncols, lo:hi], in_=stg[0:ncols, 0:n]),
                     rd=[f"stg{s}"], dma=f"stg{s}")

        return ep

    def store_tm(self, dst_ap, nbw, col0=0):
        P = self.P
        it = [0]

        def ep(nb, lo, b):
            s = self.next_stg()
            stg = self.stg[s]
            eng = "act" if it[0] % 2 == 0 else "dve"
            it[0] += 1
            if eng == "act":
                P.op("act", lambda e: e.activation(out=stg[:, 0:nbw], in_=self.psA[:, b, 0:nbw], func=AF.Copy), rd=[f"ps{b}"], wr=[f"stg{s}"])
            else:
                P.op("dve", lambda e: e.tensor_copy(out=stg[:, 0:nbw], in_=self.psA[:, b, 0:nbw]), rd=[f"ps{b}"], wr=[f"stg{s}"])
            P.op("sp", lambda e: e.dma_start(out=dst_ap[lo:lo + 128, col0 + nb * nbw:col0 + (nb + 1) * nbw], in_=stg[:, 0:nbw]),
                 rd=[f"stg{s}"], dma=f"stg{s}")

        return ep

    def r2view(self, off, n, dt=BF16, parts=128, arena=None):
        w = n * (2 if dt == F32 else 1)
        arena = self.R2 if arena is None else arena
        v = arena[0:parts, off:off + w]
        if dt == F32:
            v = v.bitcast(F32)
        return v, off + w

    def transpose_to_fm(self, src_fn, dst3):
        c, P = self.c, self.P
        for tb in range(c.NTB):
            for c0 in range(0, c.KC, 8):
                nch = min(8, c.KC - c0)
                tbk = (tb * ((c.KC + 7) // 8) + c0 // 8) % 2

                def tr(e, tb=tb, c0=c0, nch=nch, tbk=tbk):
                    ins = None
                    src = src_fn(tb)
                    for i in range(nch):
                        ins = e.transpose(out=self.psT[:, tbk, i * 128:(i + 1) * 128],
                                          in_=src[:, (c0 + i) * 128:(c0 + i + 1) * 128], identity=self.ident[:])
                    return ins

                P.op("pe", tr, rd=["ztm", "ident"], wr=[f"psT{tbk}"])
                eng = "act" if (tb + c0 // 8) % 2 == 0 else "dve"

                def ev(e, c0=c0, nch=nch, tbk=tbk, tb=tb, eng=eng):
                    src = self.psT[:, tbk, 0:nch * 128].rearrange("p (c t) -> p c t", t=128)
                    dst = dst3[:, c0:c0 + nch, tb * 128:(tb + 1) * 128]
                    if eng == "act":
                        return e.activation(out=dst, in_=src, func=AF.Copy)
                    return e.tensor_copy(out=dst, in_=src)

                P.op(eng, ev, rd=[f"psT{tbk}"], wr=["zT"])

    def fox(self, x_ap, W):
        c, P = self.c, self.P
        D, T, H, KC, TS = c.D, c.T, c.H, c.KC, c.TS
        w_in = W["b_w_in"]
        qk = self.dscr("b_qk", [2 * D, T], F32)
        fa = self.dscr("b_fa", [3 * H, T], F32)
        vg = self.dscr("b_vg", [T, 2 * D], F32)
        fat = self.dscr("b_fat", [T, 3 * H], F32)
        qhat = self.dscr("b_qhat", [2 * D, T], BF16)
        qaug = self.dscr("b_qaug", [H, 4, T], BF16)
        kaug = self.dscr("b_kaug", [H, 4, T], BF16)
        akd = self.dscr("b_akd", [H, T], BF16)
        vpr = self.dscr("b_vpr", [T, D], BF16)
        self.norm_transpose(x_ap, W["b_norm_g"])
        P.barrier()
        groups = [[(w_in, j * 128, 128)] for j in range(2 * KC)]
        self.gemm_fm(self.hT, ["actA"], KC, 128, groups, self.store_fm(qk, lambda gi, bi: gi * 128))
        self.gemm_fm(self.hT, ["actA"], KC, 128, [[(w_in, 4 * D, 3 * H)]], self.store_fm(fa, lambda gi, bi: 0))
        self.gemm_tm(self.hT, ["actA"], KC, 128, w_in[:, 2 * D:4 * D], 2 * D, 512 if D >= 512 else 2 * D,
                     self.store_tm(vg, 512 if D >= 512 else 2 * D))
        self.gemm_tm(self.hT, ["actA"], KC, 128, w_in[:, 4 * D:4 * D + 3 * H], 3 * H, 3 * H, self.store_tm(fat, 3 * H))
        P.barrier()
        off = 0
        ft, off = self.r2view(off, T, F32, H, self.R1)
        f2, off = self.r2view(off, T, F32, H, self.R1)
        ct, off = self.r2view(off, T, F32, H, self.R1)
        qa, off = self.r2view(off, 4 * T, BF16, H, self.R1)
        ka, off = self.r2view(off, 4 * T, BF16, H, self.R1)
        akt, off = self.r2view(off, T, F32, H, self.R1)
        akb, off = self.r2view(off, T, BF16, H, self.R1)
        qa = qa.rearrange("p (r t) -> p r t", t=T)
        ka = ka.rearrange("p (r t) -> p r t", t=T)
        nb = self.small[0:H, 8:9]
        P.op("sp", lambda e: e.dma_start(out=ft, in_=fa[0:H, :]), wr=["ft"], dma="ft")
        P.op("sp", lambda e: e.dma_start(out=akt, in_=fa[H:2 * H, :]), wr=["akt"], dma="akt")
        P.op("sp", lambda e: e.dma_start(out=nb, in_=W["b_b_f"].rearrange("o h -> h o")), wr=["nb"], dma="nb")
        P.op("dve", lambda e: e.tensor_scalar(out=nb, in0=nb, scalar1=-1.0, scalar2=None, op0=ALU.mult), rd=["nb"], wr=["nb"])
        P.barrier()
        P.op("act", lambda e: e.activation(out=f2, in_=ft, func=AF.Exp, scale=-1.0, bias=nb), rd=["ft", "nb"], wr=["f2"])
        P.op("act", lambda e: e.activation(out=f2, in_=f2, func=AF.Ln, scale=1.0, bias=self.consts[0:H, 1:2]), rd=["consts"], wr=["f2"])
        P.op("dve", lambda e: e.tensor_scalar(out=f2, in0=f2, scalar1=-0.5, scalar2=None, op0=ALU.mult), rd=["f2"], wr=["f2"])
        P.op("dve", lambda e: e.tensor_tensor_scan(out=ct, data0=f2, data1=f2, initial=0.0, op0=ALU.add, op1=ALU.add), rd=["f2"], wr=["ct"])
        P.op("dve", lambda e: e.tensor_copy(out=qa[:, 0, :], in_=ct), rd=["ct"], wr=["qa"])
        P.op("dve", lambda e: e.tensor_tensor(out=qa[:, 1, :], in0=ct, in1=qa[:, 0, :], op=ALU.subtract), rd=["ct", "qa"], wr=["qa"])
        P.op("pool", lambda e: e.memset(qa[:, 2:4, :], 1.0), wr=["qa"])
        P.op("pool", lambda e: e.memset(ka[:, 0:2, :], 1.0), wr=["ka"])
        P.op("dve", lambda e: e.tensor_scalar(out=ka[:, 2:4, :], in0=qa[:, 0:2, :], scalar1=-1.0, scalar2=None, op0=ALU.mult), rd=["qa"], wr=["ka"])
        P.op("act", lambda e: e.activation(out=akb, in_=akt, func=AF.Sigmoid), rd=["akt"], wr=["akb"])
        P.op("sp", lambda e: e.dma_start(out=qaug, in_=qa), rd=["qa"], dma="qa")
        P.op("sp", lambda e: e.dma_start(out=kaug, in_=ka), rd=["ka"], dma="ka")
        P.op("sp", lambda e: e.dma_start(out=akd, in_=akb), rd=["akb"], dma="akb")
        P.barrier()
        sets4 = []
        off = 0
        bones, off = self.r2view(off, 128, BF16)
        for si, arena in enumerate([self.R2, self.R1]):
            o_ = off if si == 0 else 0
            kt_, o_ = self.r2view(o_, T + 2, F32, 128, arena)
            tm_, o_ = self.r2view(o_, T, F32, 128, arena)
            ak_, o_ = self.r2view(o_, T, BF16, 128, arena)
            sq_, o_ = self.r2view(o_, T, BF16, 128, arena)
            ob_, o_ = self.r2view(o_, T, BF16, 128, arena)
            sets4.append((kt_, tm_, ak_, sq_, ob_))
        gq = self.small[:, 10:11]
        gk = self.small[:, 11:12]
        P.op("dve", lambda e: e.memset(bones, 0.0), wr=["bones"])
        P.op("dve", lambda e: e.memset(bones[0:64, 0:64], 1.0), wr=["bones"])
        P.op("dve", lambda e: e.memset(bones[64:128, 64:128], 1.0), wr=["bones"])
        for si in range(2):
            P.op("pool", lambda e, si=si: e.memset(sets4[si][0][:, 0:2], 0.0), wr=[f"kt{si}"])
        for hh in range(2):
            P.op("sp", lambda e, hh=hh: e.dma_start(out=gq[hh * 64:(hh + 1) * 64, :], in_=W["b_qn_g"].rearrange("o n -> n o")), wr=["gq"], dma="gq")
            P.op("sp", lambda e, hh=hh: e.dma_start(out=gk[hh * 64:(hh + 1) * 64, :], in_=W["b_kn_g"].rearrange("o n -> n o")), wr=["gk"], dma="gk")
        P.op("dve", lambda e: e.tensor_scalar(out=gq, in0=gq, scalar1=0.125, scalar2=None, op0=ALU.mult), rd=["gq"], wr=["gq"])
        P.barrier()

        def b4_chunk(which, p, si):
            kt, tmpf, akx, sq, outb = sets4[si]
            kkt, ktm, kak, ksq, kob = f"kt{si}", f"tmpf{si}", f"akx{si}", f"sq{si}", f"outb{si}"
            row0 = which * D + p * 128
            P.op("sp", lambda e: e.dma_start(out=kt[:, 2:T + 2], in_=qk[row0:row0 + 128, :]), wr=[kkt], dma=kkt)
            if which == 1:
                for hh in range(2):
                    P.op("sp", lambda e, hh=hh: e.dma_start(out=akx[hh * 64:(hh + 1) * 64, :], in_=akd[2 * p + hh:2 * p + hh + 1, :].partition_broadcast(64)),
                         wr=[kak], dma=kak)
                P.op("dve", lambda e: e.tensor_tensor(out=tmpf, in0=kt[:, 1:T + 1], in1=kt[:, 2:T + 2], op=ALU.subtract), rd=[kkt], wr=[ktm])
                P.op("dve", lambda e: e.tensor_tensor(out=tmpf, in0=tmpf, in1=akx, op=ALU.mult), rd=[kak, ktm], wr=[ktm])
                P.op("dve", lambda e: e.tensor_tensor(out=kt[:, 2:T + 2], in0=kt[:, 2:T + 2], in1=tmpf, op=ALU.add), rd=[ktm, kkt], wr=[kkt])
            P.op("act", lambda e: e.activation(out=sq, in_=kt[:, 2:T + 2], func=AF.Square), rd=[kkt], wr=[ksq])
            for tt in range(c.NTT):
                b = self.next_ps()
                P.op("pe", lambda e, b=b, tt=tt: e.matmul(self.psA[:, b, 0:TS], lhsT=bones, rhs=sq[:, tt * TS:(tt + 1) * TS], start=True, stop=True),
                     rd=[ksq, "bones"], wr=[f"ps{b}"])
                P.op("act", lambda e, b=b, tt=tt: e.activation(out=tmpf[:, tt * TS:(tt + 1) * TS], in_=self.psA[:, b, 0:TS], func=AF.Sqrt,
                                                             scale=1.0 / 64, bias=self.consts[:, 0:1]),
                     rd=[f"ps{b}", "consts", ktm], wr=[ktm])
            P.op("dve", lambda e: e.reciprocal(out=tmpf, in_=tmpf), rd=[ktm], wr=[ktm])
            gcol = gq if which == 0 else gk
            P.op("dve", lambda e: e.scalar_tensor_tensor(out=outb, in0=kt[:, 2:T + 2], scalar=gcol, in1=tmpf, op0=ALU.mult, op1=ALU.mult),
                 rd=[kkt, ktm, "gq", "gk"], wr=[kob])
            P.op("pool", lambda e: e.dma_start(out=qhat[row0:row0 + 128, :], in_=outb), rd=[kob], dma=kob)

        cidx = 0
        for which in range(2):
            for p in range(KC):
                b4_chunk(which, p, cidx % 2)
                cidx += 1
        P.barrier()
        off = 0
        ong, off = self.r2view(off, D, F32)
        sets5 = []
        need5 = 7 * D + 8 * H
        r1off5 = 0 if need5 <= c.KC * (T + 2) else c.KC * (T + 2) + c.KC * T
        for si, arena in enumerate([self.R2, self.R1]):
            o_ = off if si == 0 else r1off5
            vt_, o_ = self.r2view(o_, D, F32, 128, arena)
            vp_, o_ = self.r2view(o_, D, F32, 128, arena)
            gt_, o_ = self.r2view(o_, D, F32, 128, arena)
            vo_, o_ = self.r2view(o_, D, BF16, 128, arena)
            al_, o_ = self.r2view(o_, 3 * H, F32, 128, arena)
            av_, o_ = self.r2view(o_, H, F32, 128, arena)
            sets5.append((vt_, vp_, gt_, vo_, al_, av_))
        G3 = self.actB.rearrange("p (b d) -> p b d", d=D)
        ztm = self.R1[:, 0:c.NTB * D].rearrange("p (b d) -> p b d", d=D)
        P.op("sp", lambda e: e.dma_start(out=ong, in_=W["b_on_g"].partition_broadcast(128)), wr=["ong"], dma="ong")

        def b5_block(tb, si):
            vt, vp, gt, vo, al, av = sets5[si]
            kvt, kvp, kgt, kvo, kal, kav = [f"{n_}{si}" for n_ in ["vt", "vp", "gt", "vo", "al", "av"]]
            r0 = tb * 128
            P.op("sp", lambda e: e.dma_start(out=vt, in_=vg[r0:r0 + 128, 0:D]), wr=[kvt], dma=kvt)
            P.op("sp", lambda e: e.dma_start(out=gt, in_=vg[r0:r0 + 128, D:2 * D]), wr=[kgt], dma=kgt)
            P.op("sp", lambda e: e.dma_start(out=al, in_=fat[r0:r0 + 128, :]), wr=[kal], dma=kal)
            if tb == 0:
                P.op("pool", lambda e: e.memset(vp[0:1, :], 0.0), wr=[kvp])
                P.op("sp", lambda e: e.dma_start(out=vp[1:128, :], in_=vg[0:127, 0:D]), wr=[kvp], dma=kvp)
            else:
                P.op("sp", lambda e: e.dma_start(out=vp, in_=vg[r0 - 1:r0 + 127, 0:D]), wr=[kvp], dma=kvp)
            P.op("act", lambda e: e.activation(out=av, in_=al[:, 2 * H:3 * H], func=AF.Sigmoid), rd=[kal], wr=[kav])
            P.op("dve", lambda e: e.tensor_tensor(out=vp, in0=vp, in1=vt, op=ALU.subtract), rd=[kvp, kvt], wr=[kvp])
            P.op("dve", lambda e: e.tensor_tensor(out=vp.rearrange("p (h n) -> p h n", n=64), in0=vp.rearrange("p (h n) -> p h n", n=64),
                                                  in1=av.unsqueeze(2).broadcast_to([128, H, 64]), op=ALU.mult), rd=[kvp, kav], wr=[kvp])
            P.op("dve", lambda e: e.tensor_tensor(out=vo, in0=vp, in1=vt, op=ALU.add), rd=[kvp, kvt], wr=[kvo])
            P.op("pool", lambda e: e.dma_start(out=vpr[r0:r0 + 128, :], in_=vo), rd=[kvo], dma=kvo)
            P.op("act", lambda e: e.activation(out=gt, in_=gt, func=AF.Sigmoid), rd=[kgt], wr=[kgt])
            P.op("pool", lambda e: e.tensor_tensor(out=G3[:, tb, :], in0=gt, in1=ong, op=ALU.mult), rd=[kgt, "ong"], wr=["G"])

        for tb in range(c.NTB):
            b5_block(tb, tb % 2)
        P.barrier()
        off = 0
        QA, KA, VV = [], [], []
        for i in range(2):
            qs, ks = [], []
            for hh in range(2):
                v_, off = self.r2view(off, T, BF16)
                qs.append(v_)
                v_, off = self.r2view(off, T, BF16)
                ks.append(v_)
            QA.append(qs)
            KA.append(ks)
            v_, off = self.r2view(off, c.NTB * 2 * 66, BF16)
            VV.append(v_.rearrange("p (b h n) -> p b h n", h=2, n=66))
        PT = []
        for i in range(4):
            v_, off = self.r2view(off, TS, BF16)
            PT.append(v_)
        tri, off = self.r2view(off, 128, BF16)
        trif, off = self.r2view(off, 128, F32)

        P.op("pool", lambda e: e.memset(trif, 1.0), wr=["trif"])

        def mk_tri(e):
            return e.affine_select(out=trif, in_=trif, pattern=[[1, 128]], compare_op=ALU.is_ge, fill=0.0, base=0, channel_multiplier=-1)

        P.op("pool", mk_tri, rd=["trif"], wr=["trif"])
        P.op("dve", lambda e: e.tensor_copy(out=tri, in_=trif), rd=["trif"], wr=["tri"])
        for i in range(2):
            P.op("pool", lambda e, i=i: e.memset(VV[i], 1.0), wr=[f"VV{i}"])
        nq = T // TS
        nsub = TS // 128
        NPT = len(PT)

        def loads(p):
            i = p % 2
            for hh in range(2):
                h = 2 * p + hh
                P.op("sp", lambda e, i=i, hh=hh, h=h: e.dma_start(out=QA[i][hh][0:64, :], in_=qhat[h * 64:(h + 1) * 64, :]), wr=[f"QA{i}{hh}"], dma=f"QA{i}{hh}")
                P.op("sp", lambda e, i=i, hh=hh, h=h: e.dma_start(out=QA[i][hh][64:68, :], in_=qaug[h]), wr=[f"QA{i}{hh}"], dma=f"QA{i}{hh}")
                P.op("sp", lambda e, i=i, hh=hh, h=h: e.dma_start(out=KA[i][hh][0:64, :], in_=qhat[D + h * 64:D + (h + 1) * 64, :]), wr=[f"KA{i}{hh}"], dma=f"KA{i}{hh}")
                P.op("sp", lambda e, i=i, hh=hh, h=h: e.dma_start(out=KA[i][hh][64:68, :], in_=kaug[h]), wr=[f"KA{i}{hh}"], dma=f"KA{i}{hh}")
                P.op("sp", lambda e, i=i, hh=hh, h=h: e.dma_start(out=VV[i][:, :, hh, 0:64], in_=vpr[:, h * 64:(h + 1) * 64].rearrange("(b s) n -> s b n", s=128)),
                     wr=[f"VV{i}"], dma=f"VV{i}{hh}")

        epc = [0]

        def epilogue(h, I, ob):
            par = epc[0] % 2
            epc[0] += 1
            psv = self.psA[:, ob, 0:nsub * 66].rearrange("p (u n) -> p u n", n=66)
            oc = psv[:, :, 0:64]
            k0 = 16 + par * 16
            rc4 = self.small[:, k0:k0 + nsub]
            ssq4 = self.small[:, k0 + 4:k0 + 4 + nsub]
            rst4 = self.small[:, k0 + 8:k0 + 8 + nsub]
            osb = self.osb[:, par, 0:nsub * 64]
            osb3 = osb.rearrange("p (u n) -> p u n", n=64)
            sq = self.osq[:, 0:nsub * 64]
            kk = f"ep{par}"
            tb0 = I * nsub
            bc = lambda v_: v_.unsqueeze(2).broadcast_to([128, nsub, 64])
            P.op("dve", lambda e: e.reciprocal(out=rc4, in_=psv[:, :, 64]), rd=[f"ps{ob}"], wr=[kk + "rc"])
            P.op("dve", lambda e: e.tensor_tensor(out=osb3, in0=oc, in1=bc(rc4), op=ALU.mult), rd=[f"ps{ob}", kk + "rc"], wr=[kk + "osb"])
            P.op("dve", lambda e: e.tensor_tensor(out=sq, in0=osb, in1=osb, op=ALU.mult), rd=[kk + "osb"], wr=["osq"])
            P.op("dve", lambda e: e.tensor_reduce(out=ssq4, in_=sq.rearrange("p (u n) -> p u n", n=64), axis=AX.X, op=ALU.add), rd=["osq"], wr=[kk + "ss"])
            P.op("pool", lambda e: e.tensor_scalar(out=rst4, in0=ssq4, scalar1=1.0 / 64, scalar2=RMS_EPS, op0=ALU.mult, op1=ALU.add), rd=[kk + "ss"], wr=[kk + "rst"])
            P.op("pool", lambda e: e.tensor_tensor(out=rst4, in0=rst4, in1=self.consts[:, 4:4 + nsub], op=ALU.pow), rd=[kk + "rst", "consts"], wr=[kk + "rst2"])
            P.op("dve", lambda e: e.tensor_tensor(out=osb3, in0=osb3, in1=bc(rst4), op=ALU.mult), rd=[kk + "osb", kk + "rst2"], wr=[kk + "osb"])
            P.op("dve", lambda e: e.tensor_tensor(out=ztm[:, tb0:tb0 + nsub, h * 64:(h + 1) * 64], in0=osb3, in1=G3[:, tb0:tb0 + nsub, h * 64:(h + 1) * 64], op=ALU.mult),
                 rd=[kk + "osb", "G"], wr=["ztm"])

        gctr = [0, 0]
        obank = [0]
        LOOK = 2
        loads(0)
        for p in range(KC):
            i = p % 2
            if p + 1 < KC:
                loads(p + 1)
            steps = []
            for hh in range(2):
                for I in range(nq):
                    ob = 4 + (obank[0] % 2)
                    obank[0] += 1
                    nJ = (I + 1) * nsub
                    for J in range(nJ):
                        steps.append((hh, I, J, nJ, ob))

            def emit_S(st, i=i):
                hh, I, J, nJ, ob = st
                q_hi = (I + 1) * TS
                t_lo = max(I * TS, J * 128)
                N = q_hi - t_lo
                sbk = gctr[0] % 4
                gctr[0] += 1
                pti = gctr[1] % NPT
                gctr[1] += 1
                pt = PT[pti]
                P.op("pe", lambda e: e.matmul(self.psA[:, sbk, 0:N], lhsT=KA[i][hh][0:68, J * 128:(J + 1) * 128], rhs=QA[i][hh][0:68, t_lo:q_hi],
                                              start=True, stop=True), rd=[f"QA{i}{hh}", f"KA{i}{hh}"], wr=[f"ps{sbk}"])
                P.op("act", lambda e: e.activation(out=pt[:, 0:N], in_=self.psA[:, sbk, 0:N], func=AF.Exp), rd=[f"ps{sbk}"], wr=[f"PT{pti}"])
                if J * 128 >= I * TS:
                    P.op("dve", lambda e: e.tensor_tensor(out=pt[:, 0:128], in0=pt[:, 0:128], in1=tri, op=ALU.mult), rd=[f"PT{pti}", "tri"], wr=[f"PT{pti}"])
                return (pt, pti, t_lo, q_hi)

            def emit_PV(st, info, i=i, p=p):
                hh, I, J, nJ, ob = st
                pt, pti, t_lo, q_hi = info

                def pv(e):
                    ins = None
                    for u0 in range(t_lo, q_hi, 128):
                        u = (u0 - I * TS) // 128
                        ins = e.matmul(self.psA[:, ob, u * 66:u * 66 + 65], lhsT=pt[:, u0 - t_lo:u0 - t_lo + 128], rhs=VV[i][:, J, hh, 0:65],
                                       start=(J == 0 and u == 0), stop=(J == nJ - 1 and u == nsub - 1))
                    return ins

                P.op("pe", pv, rd=[f"PT{pti}", f"VV{i}"], wr=[f"ps{ob}"])
                if J == nJ - 1:
                    epilogue(2 * p + hh, I, ob)

            infos = {}
            for idx in range(len(steps) + LOOK):
                if idx < len(steps):
                    infos[idx] = emit_S(steps[idx])
                if idx - LOOK >= 0:
                    emit_PV(steps[idx - LOOK], infos.pop(idx - LOOK))
        P.barrier()
        if DEBUG:
            dz = self.dscr("dbg_ztm", [128, c.NTB, D], BF16)
            dg = self.dscr("dbg_G", [128, c.NTB, D], BF16)
            P.op("sp", lambda e: e.dma_start(out=dz, in_=ztm), dma="dbg")
            P.op("sp", lambda e: e.dma_start(out=dg, in_=G3), dma="dbg")
            P.barrier()
        zT = self.actB.rearrange("p (c t) -> p c t", t=T)
        self.transpose_to_fm(lambda tb: ztm[:, tb, :], zT)
        P.barrier()
        nbw = 512 if D >= 512 else D
        self.gemm_tm(lambda k, a, b_: zT[:, k, a:b_], ["zT"], KC, 128, W["b_w_o"], D, nbw, self.resid_epilogue(x_ap, nbw))
        P.barrier()

    def rwkv(self, x_ap, W):
        c, P = self.c, self.P
        D, T, H, KC, TS, NCH = c.D, c.T, c.H, c.KC, c.TS, c.NCH
        rr = self.dscr("a_rr", [D, T], F32)
        kkd = self.dscr("a_kk", [D, T], F32)
        vvd = self.dscr("a_vv", [D, T], F32)
        vtm = self.dscr("a_vtm", [T, D], F32)
        lwd = self.dscr("a_lw", [D, T], F32)
        aad = self.dscr("a_aa", [D, T], F32)
        ggd = self.dscr("a_gg", [D, T], F32)
        dd = {nm_: self.dscr("a_d" + nm_, [D, T], BF16) for nm_ in ["rt", "at", "bt", "kt", "bh", "kh"]}
        d_bo = self.dscr("a_dbo", [D, T], F32)
        d_wc = self.dscr("a_dwc", [D, NCH], F32)
        d_y = self.dscr("a_dy", [D, T], F32)
        rowsA = self.sb("rowsA", [128, 128], F32)
        rowsB = self.sb("rowsB", [128, 128], F32)
        cols = self.sb("cols", [128, 256], F32)
        lora = self.sb("lora", [128, 2, T], BF16)
        wcs = self.sb("wcs", [128, NCH], F32)
        P.op("dve", lambda e: e.memset(rowsA[:], 0.0), wr=["rowsA"])
        P.op("dve", lambda e: e.memset(rowsB[:], 0.0), wr=["rowsB"])
        P.op("sp", lambda e: e.dma_start(out=rowsA[0:6 * KC, :], in_=W["a_mix"].rearrange("s (c p) -> (s c) p", p=128)), wr=["rowsA"], dma="rowsA")
        vecs = ["a_w0", "a_a0", "a_k_k", "a_k_a", "a_r_k", "a_lnx_g", "a_lnx_b"]
        for i, nm in enumerate(vecs):
            P.op("sp", lambda e, i=i, nm=nm: e.dma_start(out=rowsB[i * KC:(i + 1) * KC, :], in_=W[nm].rearrange("o (c p) -> (o c) p", p=128)),
                 wr=["rowsB"], dma="rowsB")
        b0 = self.next_ps()
        P.op("pe", lambda e: e.transpose(out=self.psA[:, b0, 0:128], in_=rowsA[:], identity=self.identf[:]), rd=["rowsA", "identf"], wr=[f"ps{b0}"])
        P.op("dve", lambda e: e.tensor_copy(out=cols[:, 0:128], in_=self.psA[:, b0, 0:128]), rd=[f"ps{b0}"], wr=["cols"])
        b1 = self.next_ps()
        P.op("pe", lambda e: e.transpose(out=self.psA[:, b1, 0:128], in_=rowsB[:], identity=self.identf[:]), rd=["rowsB", "identf"], wr=[f"ps{b1}"])
        P.op("dve", lambda e: e.tensor_copy(out=cols[:, 128:256], in_=self.psA[:, b1, 0:128]), rd=[f"ps{b1}"], wr=["cols"])
        mixc = lambda s_, k: cols[:, s_ * KC + k:s_ * KC + k + 1]
        vcol = lambda nm, k: cols[:, 128 + vecs.index(nm) * KC + k:128 + vecs.index(nm) * KC + k + 1]
        omk0 = 128 + 7 * KC
        P.op("dve", lambda e: e.tensor_scalar(out=cols[:, omk0:omk0 + KC], in0=cols[:, 128 + 3 * KC:128 + 4 * KC], scalar1=-1.0, scalar2=1.0,
                                              op0=ALU.mult, op1=ALU.add), rd=["cols"], wr=["cols"])
        self.norm_transpose(x_ap, W["a_norm_g"])
        P.barrier()
        xm = self.actB.rearrange("p (c t) -> p c t", t=T)
        xmf = lambda k, a, b_: xm[:, k, a:b_]

        def mix(s_):
            for k in range(KC):
                P.op("pool" if k % 2 == 0 else "dve", lambda e, k=k: e.tensor_tensor(out=xm[:, k, :], in0=self.actA[:, k, 1:T + 1], in1=self.actA[:, k, 2:T + 2], op=ALU.subtract),
                     rd=["actA"], wr=[f"xm{k}"])
                P.op("dve", lambda e, k=k: e.scalar_tensor_tensor(out=xm[:, k, :], in0=xm[:, k, :], scalar=mixc(s_, k), in1=self.actA[:, k, 2:T + 2],
                                                                  op0=ALU.mult, op1=ALU.add), rd=["actA", f"xm{k}", "cols"], wr=["xm", f"xm{k}"])

        fullg = lambda w_ap, n: [[(w_ap, j * 128, min(128, n - j * 128))] for j in range((n + 127) // 128)]
        for s_, dst in [(0, rr), (1, kkd), (2, vvd)]:
            mix(s_)
            self.gemm_fm(xmf, ["xm"], KC, 128, fullg(W["a_w_rkv"][s_], D), self.store_fm(dst, lambda gi, bi: gi * 128))
            if s_ == 2:
                nbw = 512 if D >= 512 else D
                self.gemm_tm(xmf, ["xm"], KC, 128, W["a_w_rkv"][2], D, nbw, self.store_tm(vtm, nbw))
            P.barrier()

        def lora_ep(func):
            def ep(gi, tt, lo, hi, banks):
                (b, ncols), = banks
                P.op("act", lambda e: e.activation(out=lora[0:ncols, gi, lo:hi], in_=self.psA[0:ncols, b, 0:hi - lo], func=func),
                     rd=[f"ps{b}"], wr=["lora"])
            return ep

        def out_ep(dst, func, bias_nm, post_scale):
            def ep(gi, tt, lo, hi, banks):
                (b, ncols), = banks
                s = self.next_stg()
                stg = self.stg[s]
                n = hi - lo
                if func is None:
                    P.op("act", lambda e: e.activation(out=stg[:, 0:n], in_=self.psA[:, b, 0:n], func=AF.Copy), rd=[f"ps{b}"], wr=[f"stg{s}"])
                else:
                    P.op("act", lambda e: e.activation(out=stg[:, 0:n], in_=self.psA[:, b, 0:n], func=func, bias=vcol(bias_nm, gi), scale=1.0),
                         rd=[f"ps{b}", "cols"], wr=[f"stg{s}"])
                if post_scale is not None:
                    P.op("dve", lambda e: e.tensor_scalar(out=stg[:, 0:n], in0=stg[:, 0:n], scalar1=post_scale, scalar2=None, op0=ALU.mult),
                         rd=[f"stg{s}"], wr=[f"stg{s}"])
                P.op("sp", lambda e: e.dma_start(out=dst[gi * 128:(gi + 1) * 128, lo:hi], in_=stg[:, 0:n]), rd=[f"stg{s}"], dma=f"stg{s}")
            return ep

        for s_, w1n, w2n, L, f1, dst, f2, bnm, psc in [
                (3, "a_w1", "a_w2", c.LW, AF.Tanh, lwd, AF.Sigmoid, "a_w0", -float(np.exp(-0.5))),
                (4, "a_a1", "a_a2", c.LA, AF.Copy, aad, AF.Sigmoid, "a_a0", None),
                (5, "a_g1", "a_g2", c.LG, AF.Sigmoid, ggd, None, None, None)]:
            mix(s_)
            self.gemm_fm(xmf, ["xm"], KC, 128, fullg(W[w1n], L), lora_ep(f1))
            kcn2 = (L + 127) // 128
            kp2 = L if L < 128 else 128
            self.gemm_fm(lambda k, a, b_, kp2=kp2: lora[0:kp2, k, a:b_], ["lora"], kcn2, kp2, fullg(W[w2n], D), out_ep(dst, f2, bnm, psc))
            P.barrier()

        off = 0
        maskA, off = self.r2view(off, 384, F32)
        maskB, off = self.r2view(off, 256, F32)
        bones, off = self.r2view(off, 128, BF16)
        fA = []
        for i in range(5):
            v_, off = self.r2view(off, T, F32)
            fA.append(v_)
        assert off <= self.R2.shape[1], (off, self.R2.shape)
        X1, X2, X3, X4, X5 = fA
        fB = []
        ob_ = 16 * T
        for i in range(5):
            v_, ob_ = self.r2view(ob_, T, F32, 128, self.R1)
            fB.append(v_)
        assert ob_ <= self.R1.shape[1], (ob_, self.R1.shape)
        Xsets = [fA, fB]
        eN2, ob_ = self.r2view(ob_, T, F32, 128, self.R1)
        S12, ob_ = self.r2view(ob_, T, BF16, 128, self.R1)
        assert ob_ <= self.R1.shape[1], (ob_, self.R1.shape)
        o1 = 0
        eN, o1 = self.r2view(o1, T, F32, 128, self.R1)
        cmask, o1 = self.r2view(o1, T, F32, 128, self.R1)
        BO, o1 = self.r2view(o1, T, F32, 128, self.R1)
        YT, o1 = self.r2view(o1, T, F32, 128, self.R1)
        RT, o1 = self.r2view(o1, T, BF16, 128, self.R1)
        AT, o1 = self.r2view(o1, T, BF16, 128, self.R1)
        BT, o1 = self.r2view(o1, T, BF16, 128, self.R1)
        KT, o1 = self.r2view(o1, T, BF16, 128, self.R1)
        BH, o1 = self.r2view(o1, T, BF16, 128, self.R1)
        KH, o1 = self.r2view(o1, T, BF16, 128, self.R1)
        S1, o1 = self.r2view(o1, T, BF16, 128, self.R1)
        V2p, o1 = self.r2view(o1, T, BF16, 128, self.R1)
        zoff = max(o1, c.KC * (c.T + 2))
        V2p = V2p.rearrange("p (c i) -> p c i", i=64)
        zT = self.R1[:, zoff:zoff + KC * T].rearrange("p (c t) -> p c t", t=T)
        P.op("dve", lambda e: e.memset(bones, 0.0), wr=["bones"])
        P.op("dve", lambda e: e.memset(bones[0:64, 0:64], 1.0), wr=["bones"])
        P.op("dve", lambda e: e.memset(bones[64:128, 64:128], 1.0), wr=["bones"])
        P.op("dve", lambda e: e.memset(cmask, 1.0), wr=["cmask"])
        P.op("dve", lambda e: e.memset(cmask.rearrange("p (c t) -> p c t", t=64)[:, :, 0:1], 0.0), wr=["cmask"])
        P.op("pool", lambda e: e.memset(maskA, 1.0), wr=["maskA"])
        P.op("pool", lambda e: e.memset(maskB, 1.0), wr=["maskB"])
        for (mk, o_, cm, base, pat) in [(maskA, 0, -1, -1, 1), (maskA, 128, 1, -1, -1), (maskA, 256, -1, -1, 1), (maskB, 0, -1, 0, 1), (maskB, 128, -1, 0, 1)]:
            P.op("pool", lambda e, mk=mk, o_=o_, cm=cm, base=base, pat=pat: e.affine_select(
                out=mk[:, o_:o_ + 128], in_=mk[:, o_:o_ + 128], pattern=[[pat, 128]], compare_op=ALU.is_ge, fill=0.0, base=base, channel_multiplier=cm),
                rd=["maskA", "maskB"], wr=["maskA", "maskB"])
            P.op("pool", lambda e, mk=mk, o_=o_: e.memset(mk[0:64, o_ + 64:o_ + 128], 0.0), rd=["maskA", "maskB"], wr=["maskA", "maskB"])
            P.op("pool", lambda e, mk=mk, o_=o_: e.memset(mk[64:128, o_:o_ + 64], 0.0), rd=["maskA", "maskB"], wr=["maskA", "maskB"])
        P.op("dve", lambda e: e.memset(self.psA[:], 0.0), wr=[f"ps{i}" for i in range(6)])
        P.barrier()
        itc = [0]

        def do_pair(p):
            r0 = p * 128
            si = p % 2
            X1, X2, X3, X4, X5 = Xsets[si]
            kx1, kx2, kx3, kx4, kx5 = [f"X{i}_{si}" for i in range(1, 6)]
            eN_ = eN if si == 0 else eN2
            S1_ = S1 if si == 0 else S12
            ken, ks1 = f"eN{si}", f"S1{si}"
            st_ = lambda nm_, til, key: P.op("pool", lambda e: e.dma_start(out=dd[nm_][r0:r0 + 128, :], in_=til), rd=[key], dma="st_" + nm_)
            ld = lambda dst, src, key: P.op("sp", lambda e: e.dma_start(out=dst, in_=src[r0:r0 + 128, :]), wr=[key], dma=key)
            tt_ = lambda out, a, b_, op, rd, wr, eng="dve": P.op(eng, lambda e: e.tensor_tensor(out=out, in0=a, in1=b_, op=op), rd=rd, wr=wr)
            v3 = lambda a_: a_.rearrange("p (c t) -> p c t", t=64)
            ld(X1, lwd, kx1)
            P.op("dve", lambda e: e.tensor_tensor_scan(out=X2, data0=cmask, data1=X1, initial=0.0, op0=ALU.mult, op1=ALU.add), rd=["cmask", kx1], wr=[kx2])
            tt_(X1, X2, X1, ALU.subtract, [kx2, kx1], [kx1], "pool")
            P.op("act", lambda e: e.activation(out=X3, in_=X2, func=AF.Exp), rd=[kx2], wr=[kx3])
            P.op("act", lambda e: e.activation(out=eN_, in_=X2, func=AF.Exp, scale=-1.0), rd=[kx2], wr=[ken])
            P.op("act", lambda e: e.activation(out=X1, in_=X1, func=AF.Exp), rd=[kx1], wr=[kx1])
            ld(X2, kkd, kx2)
            P.op("dve", lambda e: e.tensor_scalar(out=X4, in0=X2, scalar1=vcol("a_k_k", p), scalar2=None, op0=ALU.mult), rd=[kx2, "cols"], wr=[kx4])
            P.op("act", lambda e: e.activation(out=S1_, in_=X4, func=AF.Square), rd=[kx4], wr=[ks1])
            for t2 in range(c.NTT):
                b = self.next_ps()
                P.op("pe", lambda e, b=b, t2=t2: e.matmul(self.psA[:, b, 0:TS], lhsT=bones, rhs=S1_[:, t2 * TS:(t2 + 1) * TS], start=True, stop=True),
                     rd=[ks1, "bones"], wr=[f"ps{b}"])
                P.op("act", lambda e, b=b, t2=t2: e.activation(out=X5[:, t2 * TS:(t2 + 1) * TS], in_=self.psA[:, b, 0:TS], func=AF.Sqrt),
                     rd=[f"ps{b}", kx5], wr=[kx5])
            P.op("dve", lambda e: e.tensor_scalar(out=X5, in0=X5, scalar1=1e-12, scalar2=None, op0=ALU.max), rd=[kx5], wr=[kx5])
            P.op("dve", lambda e: e.reciprocal(out=X5, in_=X5), rd=[kx5], wr=[kx5])
            tt_(X4, X4, X5, ALU.mult, [kx4, kx5], [kx4])
            P.op("dve", lambda e: e.scalar_tensor_tensor(out=AT, in0=X4, scalar=-1.0, in1=X1, op0=ALU.mult, op1=ALU.mult), rd=[kx4, kx1], wr=["AT"])
            st_("at", AT, "AT")
            ld(X1, aad, kx1)
            P.op("dve", lambda e: e.tensor_scalar(out=X5, in0=X1, scalar1=vcol("a_k_a", p), scalar2=cols[:, omk0 + p:omk0 + p + 1], op0=ALU.mult, op1=ALU.add),
                 rd=[kx1, "cols"], wr=[kx5])
            tt_(X2, X2, X5, ALU.mult, [kx2, kx5], [kx2], "pool")
            tt_(X1, X4, X1, ALU.mult, [kx4, kx1], [kx1], "pool")
            WCb = X3.rearrange("p (c t) -> p c t", t=64)[:, :, 63:64].broadcast_to([128, NCH, 64])
            tt_(X5, X1, eN_, ALU.mult, [kx1, ken], [kx5])
            P.op("act", lambda e: e.activation(out=BT, in_=X5, func=AF.Copy), rd=[kx5], wr=["BT"])
            st_("bt", BT, "BT")
            tt_(v3(BH), v3(X5), WCb, ALU.mult, [kx5, kx3], ["BH"])
            st_("bh", BH, "BH")
            tt_(X5, X2, eN_, ALU.mult, [kx2, ken, "BT", "BH"], [kx5])
            P.op("act", lambda e: e.activation(out=KT, in_=X5, func=AF.Copy), rd=[kx5], wr=["KT"])
            st_("kt", KT, "KT")
            tt_(v3(KH), v3(X5), WCb, ALU.mult, [kx5, kx3], ["KH"])
            st_("kh", KH, "KH")
            ld(X1, rr, kx1)
            tt_(RT, X1, X3, ALU.mult, [kx1, kx3], ["RT"], "pool")
            st_("rt", RT, "RT")
            P.op("dve", lambda e: e.scalar_tensor_tensor(out=S1_, in0=X1, scalar=vcol("a_r_k", p), in1=X2, op0=ALU.mult, op1=ALU.mult),
                 rd=[kx1, kx2, "cols", ks1], wr=[ks1])
            ld(X4, vvd, kx4)
            for t2 in range(c.NTT):
                b = self.next_ps()
                P.op("pe", lambda e, b=b, t2=t2: e.matmul(self.psA[:, b, 0:TS], lhsT=bones, rhs=S1_[:, t2 * TS:(t2 + 1) * TS], start=True, stop=True),
                     rd=[ks1, "bones"], wr=[f"ps{b}"])
                P.op("dve", lambda e, b=b, t2=t2: e.tensor_tensor(out=BO[:, t2 * TS:(t2 + 1) * TS], in0=self.psA[:, b, 0:TS], in1=X4[:, t2 * TS:(t2 + 1) * TS], op=ALU.mult),
                     rd=[f"ps{b}", kx4, "BO"], wr=["BO"])
            eP = X3
            P.op("pool", lambda e: e.dma_start(out=d_bo[r0:r0 + 128, :], in_=BO), rd=["BO"], dma="st_bo")
            P.op("dve", lambda e: e.tensor_copy(out=wcs[:], in_=X3.rearrange("p (c t) -> p c t", t=64)[:, :, 63]), rd=[kx3], wr=["wcs"])
            P.op("pool", lambda e: e.dma_start(out=d_wc[r0:r0 + 128, :], in_=wcs[:]), rd=["wcs"], dma="st_wc")

        for p in range(KC):
            do_pair(p)
        P.barrier()

        NS = 1408
        o1 = 0
        D6 = []
        for bufi in range(2):
            lst = []
            for k6 in range(6):
                v_, o1 = self.r2view(o1, KC * 128, BF16, 128, self.R1)
                lst.append(v_.rearrange("p (q t) -> p q t", t=128))
            D6.append(lst)
        V2b = []
        for bufi in range(2):
            v_, o1 = self.r2view(o1, KC * 128, BF16, 128, self.R1)
            V2b.append(v_.rearrange("p (q c i) -> p q c i", c=2, i=64))
        Yst = []
        for bufi in range(2):
            v_, o1 = self.r2view(o1, KC * 128, F32, 128, self.R1)
            Yst.append(v_.rearrange("p (q t) -> p q t", t=128))
        WCt, o1 = self.r2view(o1, KC * NCH, F32, 128, self.R1)
        WCt = WCt.rearrange("p (q c) -> p q c", c=NCH)
        STt, o1 = self.r2view(o1, KC * 64, F32, 128, self.R1)
        STbt, o1 = self.r2view(o1, KC * 64, BF16, 128, self.R1)
        Ust, o1 = self.r2view(o1, KC * 64, BF16, 128, self.R1)
        SAt, o1 = self.r2view(o1, KC * 64, BF16, 128, self.R1)
        v4 = lambda a_: a_.rearrange("p (q i) -> p q i", i=64)
        STt, STbt, Ust, SAt = v4(STt), v4(STbt), v4(Ust), v4(SAt)
        assert o1 <= self.R1.shape[1], (o1, self.R1.shape)
        sets_off = 384 * 2 + 256 * 2 + 128
        assert sets_off + KC * NS <= self.R2.shape[1], (sets_off + KC * NS, self.R2.shape)

        def pset(p):
            o_ = sets_off + p * NS
            g = lambda a_, n: self.R2[:, o_ + a_:o_ + a_ + n]
            return dict(NN=g(0, 256), MAK=g(256, 128), MB=g(384, 256), PP=[g(640, 256), g(896, 256)], Q=g(1152, 128), BK=g(1280, 128))

        P.op("sp", lambda e: e.dma_start(out=WCt, in_=d_wc.rearrange("(q p) c -> p q c", p=128)), wr=["WCt"], dma="WCt")
        P.op("dve", lambda e: e.memset(STt, 0.0), wr=["STall"])
        P.op("dve", lambda e: e.memset(STbt, 0.0), wr=["STball"])
        P.barrier()
        hctr = [0]

        def next_half():
            b_ = self.next_ps()
            return b_, 0, f"ps{b_}"

        tctr = [0]

        def next_tslot():
            t_ = tctr[0] % 2
            tctr[0] += 1
            return t_, 0, f"psT{t_}"

        nblk = T // 128
        d6n = ["rt", "at", "bt", "kt", "bh", "kh"]

        def load_block(bi):
            bufi = bi % 2
            for k6 in range(6):
                P.op("sp", lambda e, k6=k6: e.dma_start(out=D6[bufi][k6], in_=dd[d6n[k6]][:, bi * 128:(bi + 1) * 128].rearrange("(q p) t -> p q t", p=128)),
                     wr=[f"D6_{bufi}"], dma=f"D6_{bufi}")
            for hh in range(2):
                for cl in range(2):
                    src = vtm[bi * 128 + cl * 64:bi * 128 + (cl + 1) * 64, :].rearrange("s (q h i) -> h s q i", h=2, i=64)[hh]
                    P.op("pool", lambda e, hh=hh, cl=cl, src=src: e.dma_start(out=V2b[bufi][hh * 64:(hh + 1) * 64, :, cl, :], in_=src),
                         wr=[f"V2_{bufi}"], dma=f"V2_{bufi}")

        def chunk_gen(p, ch):
            bi, cl = ch // 2, ch % 2
            bufi = bi % 2
            cs = slice(cl * 64, (cl + 1) * 64)
            RTc, ATc, BTc, KTc, BHc, KHc = [D6[bufi][k6][:, p, :] for k6 in range(6)]
            dk = [f"D6_{bufi}"] * 6
            V2c = V2b[bufi][:, p, cl, :]
            vk = f"V2_{bufi}"
            S_ = pset(p)
            NN, MAK, MBt, PPs, Q, BK = S_["NN"], S_["MAK"], S_["MB"], S_["PP"], S_["Q"], S_["BK"]
            kNN, kMAK, kMB, kQ, kBK = f"NN{p}", f"MAK{p}", f"MB{p}", f"Q{p}", f"BK{p}"
            STp, STbp, Up, SAp = STt[:, p, :], STbt[:, p, :], Ust[:, p, :], SAt[:, p, :]
            kST, kSTb, kU, kSA = f"ST{p}", f"STb{p}", f"U{p}", f"SA{p}"
            hs = [slice(0, 64), slice(64, 128)]
            b1, o1_, k1 = next_half()
            b2, o2_, k2 = b1, 256, k1
            b3, o3_, k3 = next_half()

            def mmats(e):
                ins = None
                for (bank, o_, X, Y) in [(b1, o1_, BTc, ATc), (b1, o1_ + 128, ATc, BTc), (b2, o2_, KTc, ATc), (b3, o3_, BTc, RTc), (b3, o3_ + 128, KTc, RTc)]:
                    for hh in range(2):
                        ps_ = hs[hh]
                        ins = e.matmul(self.psA[ps_, bank, o_ + hh * 64:o_ + (hh + 1) * 64], lhsT=X[ps_, cs], rhs=Y[ps_, cs],
                                       start=True, stop=True, tile_position=(hh * 64, hh * 64))
                return ins

            P.op("pe", mmats, rd=[dk[1], dk[2], dk[3], dk[0]], wr=[k1, k3])
            P.op("dve", lambda e: e.tensor_tensor(out=NN, in0=self.psA[:, b1, o1_:o1_ + 256], in1=maskA[:, 0:256], op=ALU.mult), rd=[k1, "maskA"], wr=[kNN])
            P.op("dve", lambda e: e.tensor_tensor(out=MAK, in0=self.psA[:, b2, o2_:o2_ + 128], in1=maskA[:, 256:384], op=ALU.mult), rd=[k2, "maskA"], wr=[kMAK])
            P.op("dve", lambda e: e.tensor_tensor(out=MBt, in0=self.psA[:, b3, o3_:o3_ + 256], in1=maskB, op=ALU.mult), rd=[k3, "maskB"], wr=[kMB])
            P.op("dve", lambda e: e.tensor_tensor(out=Q, in0=NN[:, 0:128], in1=self.ident[:], op=ALU.add), rd=[kNN, "ident"], wr=[kQ])
            yield
            cur, curk = NN, kNN
            for lvl in range(5):
                bL, oL, kL = next_half()
                pn = PPs[lvl % 2]
                kpn = f"PP{p}_{lvl % 2}"

                def sqm(e, cur=cur, bL=bL, oL=oL):
                    e.matmul(self.psA[:, bL, oL:oL + 128], lhsT=cur[:, 128:256], rhs=cur[:, 0:128], start=True, stop=True)
                    return e.matmul(self.psA[:, bL, oL + 128:oL + 256], lhsT=cur[:, 0:128], rhs=cur[:, 128:256], start=True, stop=True)

                P.op("pe", sqm, rd=[curk], wr=[kL])
                P.op("act", lambda e, pn=pn, bL=bL, oL=oL: e.activation(out=pn, in_=self.psA[:, bL, oL:oL + 256], func=AF.Copy), rd=[kL], wr=[kpn])
                yield
                bQ, oQ, kQb = next_half()
                P.op("pe", lambda e, pn=pn, bQ=bQ, oQ=oQ: e.matmul(self.psA[:, bQ, oQ:oQ + 128], lhsT=pn[:, 128:256], rhs=Q, start=True, stop=True),
                     rd=[kpn, kQ], wr=[kQb])
                P.op("dve", lambda e, bQ=bQ, oQ=oQ: e.tensor_tensor(out=Q, in0=Q, in1=self.psA[:, bQ, oQ:oQ + 128], op=ALU.add), rd=[kQb, kQ], wr=[kQ])
                yield
                cur, curk = pn, kpn
            tb_, to_, kt_ = next_tslot()

            def trs(e):
                ins = None
                for o_, X in [(0, BHc), (64, KHc)]:
                    for hh in range(2):
                        ps_ = hs[hh]
                        ins = e.transpose(out=self.psT[ps_, tb_, to_ + o_:to_ + o_ + 64], in_=X[ps_, cs], identity=self.ident[ps_, ps_],
                                          tile_position=(hh * 64, hh * 64))
                return ins

            P.op("pe", trs, rd=[dk[4], dk[5], "ident"], wr=[kt_])
            P.op("act", lambda e: e.activation(out=BK, in_=self.psT[:, tb_, to_:to_ + 128], func=AF.Copy), rd=[kt_], wr=[kBK])
            yield
            bU, oU, kUb = next_half()

            def mmU(e):
                for hh in range(2):
                    ps_ = hs[hh]
                    e.matmul(self.psA[ps_, bU, oU:oU + 64], lhsT=ATc[ps_, cs], rhs=STbp[ps_, :], start=True, stop=False, tile_position=(hh * 64, hh * 64))
                return e.matmul(self.psA[:, bU, oU:oU + 64], lhsT=MAK, rhs=V2c, start=False, stop=True)

            P.op("pe", mmU, rd=[dk[1], kSTb, "STball", kMAK, vk], wr=[kUb])
            P.op("act", lambda e: e.activation(out=Up, in_=self.psA[:, bU, oU:oU + 64], func=AF.Copy), rd=[kUb], wr=[kU])
            yield
            bS, oS, kSb = next_half()
            P.op("pe", lambda e: e.matmul(self.psA[:, bS, oS:oS + 64], lhsT=Q, rhs=Up, start=True, stop=True), rd=[kQ, kU], wr=[kSb])
            P.op("act", lambda e: e.activation(out=SAp, in_=self.psA[:, bS, oS:oS + 64], func=AF.Copy), rd=[kSb], wr=[kSA])
            yield
            bY, oY, kYb = next_half()

            def mmY(e):
                ins = None
                for hh in range(2):
                    ps_ = hs[hh]
                    e.matmul(self.psA[ps_, bY, oY:oY + 64], lhsT=STbp[ps_, :], rhs=RTc[ps_, cs], start=True, stop=False, tile_position=(hh * 64, hh * 64))
                for hh in range(2):
                    ps_ = hs[hh]
                    e.matmul(self.psA[ps_, bY, oY:oY + 64], lhsT=V2c[ps_, :], rhs=MBt[ps_, 128 + hh * 64:128 + (hh + 1) * 64], start=False, stop=False,
                             tile_position=(hh * 64, hh * 64))
                for hh in range(2):
                    ps_ = hs[hh]
                    ins = e.matmul(self.psA[ps_, bY, oY:oY + 64], lhsT=SAp[ps_, :], rhs=MBt[ps_, hh * 64:(hh + 1) * 64], start=False, stop=True,
                                   tile_position=(hh * 64, hh * 64))
                return ins

            P.op("pe", mmY, rd=[kSTb, "STball", dk[0], vk, kMB, kSA], wr=[kYb])
            P.op("act", lambda e: e.activation(out=Yst[bufi][:, p, cs], in_=self.psA[:, bY, oY:oY + 64], func=AF.Copy), rd=[kYb], wr=[f"Yst{bufi}"])
            yield
            bN, oN, kNb = next_half()

            def mmN(e):
                ins = None
                for hh in range(2):
                    ps_ = hs[hh]
                    e.matmul(self.psA[ps_, bN, oN:oN + 64], lhsT=BK[ps_, 0:64], rhs=SAp[ps_, :], start=True, stop=False, tile_position=(hh * 64, hh * 64))
                for hh in range(2):
                    ps_ = hs[hh]
                    ins = e.matmul(self.psA[ps_, bN, oN:oN + 64], lhsT=BK[ps_, 64:128], rhs=V2c[ps_, :], start=False, stop=True,
                                   tile_position=(hh * 64, hh * 64))
                return ins

            P.op("pe", mmN, rd=[kBK, kSA, vk], wr=[kNb])
            P.op("dve", lambda e: e.scalar_tensor_tensor(out=STp, in0=STp, scalar=WCt[:, p, ch:ch + 1], in1=self.psA[:, bN, oN:oN + 64],
                                                         op0=ALU.mult, op1=ALU.add), rd=[kST, "STall", "WCt", kNb], wr=[kST])
            P.op("act", lambda e: e.activation(out=STbp, in_=STp, func=AF.Copy), rd=[kST, "STball"], wr=[kSTb])
            yield

        load_block(0)
        for ch in range(NCH):
            bi = ch // 2
            if ch % 2 == 0 and bi + 1 < nblk:
                load_block(bi + 1)
            gens = [chunk_gen(p, ch) for p in range(KC)]
            while gens:
                alive = []
                for g in gens:
                    try:
                        next(g)
                        alive.append(g)
                    except StopIteration:
                        pass
                gens = alive
            if ch % 2 == 1:
                bufi = bi % 2
                P.op("sp", lambda e, bi=bi, bufi=bufi: e.dma_start(out=d_y[:, bi * 128:(bi + 1) * 128].rearrange("(q p) t -> p q t", p=128), in_=Yst[bufi]),
                     rd=[f"Yst{bufi}"], dma=f"Yst{bufi}")
        P.barrier()

        def do_epi(p):
            r0 = p * 128
            si = 0
            X1, X2, X3, X4, X5 = Xsets[0]
            kx1, kx2, kx3, kx4, kx5 = [f"X{i}_{si}" for i in range(1, 6)]
            eN_ = eN
            ken = "eN0"
            ld = lambda dst, src, key: P.op("sp", lambda e: e.dma_start(out=dst, in_=src[r0:r0 + 128, :]), wr=[key], dma=key)
            tt_ = lambda out, a, b_, op, rd, wr, eng="dve": P.op(eng, lambda e: e.tensor_tensor(out=out, in0=a, in1=b_, op=op), rd=rd, wr=wr)
            ld(X3, d_y, kx3)
            ld(eN_, d_bo, ken)
            ld(X4, ggd, kx4)
            YT_ = X3
            P.op("act", lambda e: e.activation(out=S1, in_=YT_, func=AF.Copy), rd=[kx3], wr=["S1"])
            P.op("act", lambda e: e.activation(out=RT, in_=YT_, func=AF.Square), rd=[kx3], wr=["RT"])
            mu, var, tG = X1, X2, X5
            for t2 in range(c.NTT):
                sl_ = slice(t2 * TS, (t2 + 1) * TS)
                b = self.next_ps()
                P.op("pe", lambda e, b=b, sl_=sl_: e.matmul(self.psA[:, b, 0:TS], lhsT=bones, rhs=S1[:, sl_], start=True, stop=True), rd=["S1", "bones"], wr=[f"ps{b}"])
                P.op("act", lambda e, b=b, sl_=sl_: e.activation(out=mu[:, sl_], in_=self.psA[:, b, 0:TS], func=AF.Copy, scale=1.0 / 64), rd=[f"ps{b}", kx1], wr=[kx1])
                b2 = self.next_ps()
                P.op("pe", lambda e, b2=b2, sl_=sl_: e.matmul(self.psA[:, b2, 0:TS], lhsT=bones, rhs=RT[:, sl_], start=True, stop=True), rd=["RT", "bones"], wr=[f"ps{b2}"])
                P.op("dve", lambda e, sl_=sl_: e.tensor_tensor(out=tG[:, sl_], in0=mu[:, sl_], in1=mu[:, sl_], op=ALU.mult), rd=[kx1, kx5], wr=[kx5])
                P.op("dve", lambda e, b2=b2, sl_=sl_: e.scalar_tensor_tensor(out=var[:, sl_], in0=self.psA[:, b2, 0:TS], scalar=1.0 / 64, in1=tG[:, sl_],
                                                                             op0=ALU.mult, op1=ALU.subtract), rd=[f"ps{b2}", kx5, kx2], wr=[kx2])
            P.op("act", lambda e: e.activation(out=var, in_=var, func=AF.Sqrt, bias=self.consts[:, 2:3], scale=1.0), rd=[kx2, "consts"], wr=[kx2])
            P.op("dve", lambda e: e.reciprocal(out=var, in_=var), rd=[kx2], wr=[kx2])
            tt_(YT_, YT_, mu, ALU.subtract, [kx3, kx1], [kx3])
            tt_(YT_, YT_, var, ALU.mult, [kx3, kx2], [kx3])
            P.op("dve", lambda e: e.tensor_scalar(out=YT_, in0=YT_, scalar1=vcol("a_lnx_g", p), scalar2=vcol("a_lnx_b", p), op0=ALU.mult, op1=ALU.add),
                 rd=[kx3, "cols"], wr=[kx3])
            tt_(YT_, YT_, eN_, ALU.add, [kx3, ken], [kx3], "pool")
            tt_(zT[:, p, :], YT_, X4, ALU.mult, [kx3, kx4], ["zT"], "pool")

        for p in range(KC):
            do_epi(p)
        P.barrier()
        nbw = 512 if D >= 512 else D
        self.gemm_tm(lambda k, a, b_: zT[:, k, a:b_], ["zT"], KC, 128, W["a_w_o"], D, nbw, self.resid_epilogue(x_ap, nbw))
        P.barrier()

    def final_norm(self, x_ap, g_ap, out_ap):
        c, P = self.c, self.P
        P.op("sp", lambda e: e.dma_start(out=self.grep[:], in_=g_ap.partition_broadcast(128)), wr=["grep"], dma="grep")
        for tb in range(c.NTB):
            s = tb % 2
            xt = self.xt[s]
            ss = self.small[:, s:s + 1]
            rs = self.small[:, 2 + s:3 + s]
            P.op("sp", lambda e, xt=xt, tb=tb: e.dma_start(out=xt[:], in_=x_ap[tb * 128:(tb + 1) * 128, :]),
                 wr=[f"xt{s}"], dma=f"xt{s}")
            P.op("act", lambda e, xt=xt, ss=ss, s=s: e.activation(out=self.hn[s][:], in_=xt[:], func=AF.Square, accum_out=ss),
                 rd=[f"xt{s}"], wr=[f"hn{s}", f"ss{s}"])
            P.op("act", lambda e, ss=ss, rs=rs: e.activation(out=rs, in_=ss, func=AF.Sqrt, scale=1.0 / c.D, bias=self.consts[:, 0:1]),
                 rd=[f"ss{s}", "consts"], wr=[f"rs{s}"])
            P.op("dve", lambda e, rs=rs: e.reciprocal(out=rs, in_=rs), rd=[f"rs{s}"], wr=[f"rsd{s}", f"rs{s}"])
            P.op("dve", lambda e, xt=xt, rs=rs: e.scalar_tensor_tensor(
                out=xt[:], in0=xt[:], scalar=rs, in1=self.grep[:], op0=ALU.mult, op1=ALU.mult),
                rd=[f"rsd{s}", f"rs{s}", "grep"], wr=[f"xt{s}"])
            P.op("sp", lambda e, xt=xt, tb=tb: e.dma_start(out=out_ap[tb * 128:(tb + 1) * 128, :], in_=xt[:]),
                 rd=[f"xt{s}"], dma=f"xt{s}")

    def finish(self):
        P, nc = self.P, self.nc
        P.barrier()
        with nc.Block() as block:
            @block.sync
            def _(e):
                for f in P.streams["sp"]:
                    f(e)

            @block.scalar
            def _(e):
                for f in P.streams["act"]:
                    f(e)

            @block.vector
            def _(e):
                for f in P.streams["dve"]:
                    f(e)

            @block.gpsimd
            def _(e):
                for f in P.streams["pool"]:
                    f(e)

            @block.tensor
            def _(e):
                for f in P.streams["pe"]:
                    f(e)
        self.es.close()
        return nc


def build_program(cfg, layers=("a", "f0", "b", "f1", "final")):
    B = Builder(cfg)
    c = cfg
    x_in = B.din("x", [c.T, c.D])
    f_norm_g = B.din("f_norm_g", [2, c.D])
    f_w_gu = B.din("f_w_gu", [2, c.D, 2 * c.DFF])
    f_w_d = B.din("f_w_d", [2, c.DFF, c.D])
    final_g = B.din("final_g", [1, c.D])
    W = {}
    for nm, shp in [("b_norm_g", [1, c.D]), ("b_w_in", [c.D, c.WIN]), ("b_b_f", [1, c.H]), ("b_qn_g", [1, 64]), ("b_kn_g", [1, 64]),
                    ("b_on_g", [1, c.D]), ("b_w_o", [c.D, c.D])]:
        if "b" in layers:
            W[nm] = B.din(nm, shp)
    for nm, shp in [("a_norm_g", [1, c.D]), ("a_mix", [6, c.D]), ("a_w_rkv", [3, c.D, c.D]), ("a_w0", [1, c.D]), ("a_w1", [c.D, c.LW]),
                    ("a_w2", [c.LW, c.D]), ("a_a0", [1, c.D]), ("a_a1", [c.D, c.LA]), ("a_a2", [c.LA, c.D]), ("a_g1", [c.D, c.LG]),
                    ("a_g2", [c.LG, c.D]), ("a_k_k", [1, c.D]), ("a_k_a", [1, c.D]), ("a_r_k", [1, c.D]), ("a_lnx_g", [1, c.D]),
                    ("a_lnx_b", [1, c.D]), ("a_w_o", [c.D, c.D])]:
        if "a" in layers:
            W[nm] = B.din(nm, shp)
    out = B.nc.dram_tensor("out", [c.T, c.D], F32, kind="ExternalOutput").ap()
    xres = B.dscr("xres", [c.T, c.D], F32)
    mid = B.dscr("mid", [c.DFF, c.T], BF16)
    B.setup_common()
    P = B.P
    P.op("sp", lambda e: e.dma_start(out=xres, in_=x_in), dma="xcopy")
    P.barrier()
    for L in layers:
        if L == "f0" or L == "f1":
            l = int(L[1])
            B.swiglu(xres, f_norm_g[l:l + 1, :], f_w_gu[l], f_w_d[l], mid)
        elif L == "a":
            B.rwkv(xres, W)
        elif L == "b":
            B.fox(xres, W)
        elif L == "final":
            B.final_norm(xres, final_g, out)
    return B.finish()


def make_in_map(cfg, inputs, core, layers=("a", "f0", "b", "f1", "final")):
    m = {"x": np.ascontiguousarray(inputs["x"][core]),
         "f_norm_g": np.ascontiguousarray(inputs["f_norm_g"]),
         "f_w_gu": np.ascontiguousarray(inputs["f_w_gu"]),
         "f_w_d": np.ascontiguousarray(inputs["f_w_d"]),
         "final_g": np.ascontiguousarray(inputs["final_g"]).reshape(1, -1)}
    if "a" in layers:
        for nm in ["a_norm_g", "a_w0", "a_a0", "a_k_k", "a_k_a", "a_r_k", "a_lnx_g", "a_lnx_b"]:
            m[nm] = np.ascontiguousarray(inputs[nm]).reshape(1, -1)
        for nm in ["a_mix", "a_w_rkv", "a_w1", "a_w2", "a_a1", "a_a2", "a_g1", "a_g2", "a_w_o"]:
            m[nm] = np.ascontiguousarray(inputs[nm][0])
    if "b" in layers:
        for nm in ["b_norm_g", "b_b_f", "b_qn_g", "b_kn_g", "b_on_g"]:
            m[nm] = np.ascontiguousarray(inputs[nm]).reshape(1, -1)
        m["b_w_in"] = np.ascontiguousarray(inputs["b_w_in"][0])
        m["b_w_o"] = np.ascontiguousarray(inputs["b_w_o"][0])
    return m


_CACHE = {}


def kernel(**inputs):
    cfg = Cfg()
    layers = ("a", "f0", "b", "f1", "final")
    if "nc" not in _CACHE:
        _CACHE["nc"] = build_program(cfg, layers)
    nc = _CACHE["nc"]
    inputs = {k: np.asarray(v) for k, v in inputs.items()}
    in_maps = [make_in_map(cfg, inputs, core, layers) for core in range(8)]
    res = run_bass_kernel_spmd(nc, in_maps, core_ids=list(range(8)))
    out = np.stack([np.asarray(r["out"]) for r in res.results], axis=0)
    return out.astype(np.float32)
```
